# Optimizing a Trainium2 kernel written in Bass

```python
import math, functools
import jax, jax.numpy as jnp
from jax import lax
import numpy as np

D_MODEL = 1024
BATCH = 16
SEQ = 4096
DEPTH = 2

HEAD_DIM = 64
NSA_Q_HEADS = 8
NSA_KV_HEADS = 2
NSA_GROUP = NSA_Q_HEADS // NSA_KV_HEADS
CMP_LEN = 32
CMP_STRIDE = 16
CMP_HIDDEN = 128
SEL_LEN = 64
SEL_TOP_N = 16
WINDOW = 512
ATTN_Q_BLOCK = 128
SEL_Q_BLOCK = 32
N_BRANCHES = 3
GMLP_GROUPS = 8
GMLP_HEAD_DIM = 64
GMLP_CHUNK = 128
CONV_WIDTH = 3
FFN_HIDDEN = -(-8 * D_MODEL // (3 * 256)) * 256

Q_DIM = NSA_Q_HEADS * HEAD_DIM
KV_DIM = NSA_KV_HEADS * HEAD_DIM
GATE_DIM = NSA_Q_HEADS * N_BRANCHES
GMLP_DIM = GMLP_GROUPS * GMLP_HEAD_DIM
IN0_DIM = Q_DIM + 6 * KV_DIM + GATE_DIM + 2 * GMLP_DIM
MIX0_DIM = Q_DIM + GMLP_DIM

ALPHA = (2 * DEPTH) ** 0.25
BETA = (8 * DEPTH) ** -0.25
N_EVEN = (DEPTH + 1) // 2
N_ODD = DEPTH // 2
LN_EPS = 1e-5
NEG_INF = -1e30
FORCE_SCORE = 1e4

kernel_name = "hybrid_nsa_gmlp_shortconv_deepnorm_adaln"


def layer_norm(x, g, b):
    xf = x.astype(jnp.float32)
    mu = jnp.mean(xf, axis=-1, keepdims=True)
    var = jnp.mean(jnp.square(xf - mu), axis=-1, keepdims=True)
    y = (xf - mu) * lax.rsqrt(var + LN_EPS)
    return (y * g.astype(jnp.float32) + b.astype(jnp.float32)).astype(x.dtype)


def masked_softmax(s, mask):
    s = jnp.where(mask, s.astype(jnp.float32), NEG_INF)
    return jax.nn.softmax(s, axis=-1) * mask


def ada_modulation(c, w, b):
    m = jax.nn.silu(c) @ w + b
    shift, scale, gate = jnp.split(m[:, None, :], 3, axis=-1)
    return shift, scale, gate


def residual_update(x, c, ada_w, ada_b, ln_g, ln_b, sublayer):
    shift, scale, gate = ada_modulation(c, ada_w, ada_b)
    out = sublayer(x * (1 + scale) + shift)
    return layer_norm(ALPHA * x + (1 + gate) * out, ln_g, ln_b)


def compress_blocks(kv, pos, w1, w2):
    b_, g_, s_, dk = kv.shape
    r = CMP_LEN // CMP_STRIDE
    ch = kv.reshape(b_, g_, s_ // CMP_STRIDE, CMP_STRIDE, dk)
    n_c = s_ // CMP_STRIDE - r + 1
    blk = jnp.concatenate([ch[:, :, i:i + n_c] for i in range(r)], axis=3)
    blk = (blk + pos).reshape(b_, g_, n_c, CMP_LEN * dk)
    return jax.nn.gelu(blk @ w1) @ w2


def nsa_attention(q, k_c, v_c, k_s, v_s, k_w, v_w, gates, cmp_pos, cmp_w1, cmp_w2):
    b_, g_, r_, s_, dk = q.shape
    q = q * dk ** -0.5
    kc = compress_blocks(k_c, cmp_pos[0], cmp_w1[0], cmp_w2[0])
    vc = compress_blocks(v_c, cmp_pos[1], cmp_w1[1], cmp_w2[1])
    n_cmp = kc.shape[2]
    n_sel = s_ // SEL_LEN
    top_n = min(SEL_TOP_N, n_sel)
    cmp_end = jnp.arange(n_cmp) * CMP_STRIDE + CMP_LEN - 1
    kk = jnp.arange(n_cmp)[:, None]
    jj = jnp.arange(n_sel)[None, :]
    cover = ((kk * CMP_STRIDE < (jj + 1) * SEL_LEN)
             & (kk * CMP_STRIDE + CMP_LEN > jj * SEL_LEN)).astype(jnp.float32)

    def cmp_block(i):
        q_blk = lax.dynamic_slice_in_dim(q, i * ATTN_Q_BLOCK, ATTN_Q_BLOCK, axis=3)
        qpos = i * ATTN_Q_BLOCK + jnp.arange(ATTN_Q_BLOCK)
        s = jnp.einsum('bgrqd,bgkd->bgrqk', q_blk, kc)
        mask = cmp_end[None, :] <= qpos[:, None]
        p = masked_softmax(s, mask)
        o = jnp.einsum('bgrqk,bgkd->bgrqd', p.astype(vc.dtype), vc)
        imp = jnp.einsum('bgqk,kj->bgqj', p.sum(axis=2), cover)
        cur = (qpos // SEL_LEN)[:, None]
        valid = jj * SEL_LEN <= qpos[:, None]
        forced = (jj == 0) | (jj == cur) | (jj == cur - 1)
        imp = jnp.where(forced, FORCE_SCORE, jnp.where(valid, imp, -FORCE_SCORE))
        _, idx = lax.top_k(imp, top_n)
        return o, idx.astype(jnp.int32)

    o_cmp, sel_idx = lax.map(cmp_block, jnp.arange(s_ // ATTN_Q_BLOCK))
    o_cmp = jnp.moveaxis(o_cmp, 0, 3).reshape(b_, g_, r_, s_, dk)
    sel_idx = jnp.moveaxis(sel_idx, 0, 2).reshape(b_, g_, s_, top_n)

    ks_blocks = k_s.reshape(b_, g_, n_sel, SEL_LEN, dk)
    vs_blocks = v_s.reshape(b_, g_, n_sel, SEL_LEN, dk)
    bi = jnp.arange(b_)[:, None, None, None]
    gi = jnp.arange(g_)[None, :, None, None]

    def sel_block(i):
        q_blk = lax.dynamic_slice_in_dim(q, i * SEL_Q_BLOCK, SEL_Q_BLOCK, axis=3)
        idx = lax.dynamic_slice_in_dim(sel_idx, i * SEL_Q_BLOCK, SEL_Q_BLOCK, axis=2)
        qpos = i * SEL_Q_BLOCK + jnp.arange(SEL_Q_BLOCK)
        kg = ks_blocks[bi, gi, idx].reshape(b_, g_, SEL_Q_BLOCK, top_n * SEL_LEN, dk)
        vg = vs_blocks[bi, gi, idx].reshape(b_, g_, SEL_Q_BLOCK, top_n * SEL_LEN, dk)
        kpos = (idx[..., None] * SEL_LEN + jnp.arange(SEL_LEN)).reshape(
            b_, g_, SEL_Q_BLOCK, top_n * SEL_LEN)
        mask = (kpos <= qpos[:, None])[:, :, None]
        s = jnp.einsum('bgrqd,bgqkd->bgrqk', q_blk, kg)
        p = masked_softmax(s, mask)
        return jnp.einsum('bgrqk,bgqkd->bgrqd', p.astype(vg.dtype), vg)

    o_sel = lax.map(sel_block, jnp.arange(s_ // SEL_Q_BLOCK))
    o_sel = jnp.moveaxis(o_sel, 0, 3).reshape(b_, g_, r_, s_, dk)

    kw_pad = jnp.pad(k_w, ((0, 0), (0, 0), (WINDOW, 0), (0, 0)))
    vw_pad = jnp.pad(v_w, ((0, 0), (0, 0), (WINDOW, 0), (0, 0)))
    span = ATTN_Q_BLOCK + WINDOW

    def win_block(i):
        start = i * ATTN_Q_BLOCK
        q_blk = lax.dynamic_slice_in_dim(q, start, ATTN_Q_BLOCK, axis=3)
        kb = lax.dynamic_slice_in_dim(kw_pad, start, span, axis=2)
        vb = lax.dynamic_slice_in_dim(vw_pad, start, span, axis=2)
        qpos = start + jnp.arange(ATTN_Q_BLOCK)
        kpos = start - WINDOW + jnp.arange(span)
        dist = qpos[:, None] - kpos[None, :]
        mask = (dist >= 0) & (dist < WINDOW) & (kpos >= 0)[None, :]
        s = jnp.einsum('bgrqd,bgkd->bgrqk', q_blk, kb)
        p = masked_softmax(s, mask)
        return jnp.einsum('bgrqk,bgkd->bgrqd', p.astype(vb.dtype), vb)

    o_win = lax.map(win_block, jnp.arange(s_ // ATTN_Q_BLOCK))
    o_win = jnp.moveaxis(o_win, 0, 3).reshape(b_, g_, r_, s_, dk)

    return gates[..., 0:1] * o_cmp + gates[..., 1:2] * o_sel + gates[..., 2:3] * o_win


def chunked_gmlp(u, v, norm_g, w_s, b_s):
    b_, s_, gm, dm = u.shape
    u = jax.nn.gelu(u)
    vf = jax.nn.gelu(v).astype(jnp.float32)
    mu = jnp.mean(vf, axis=-1, keepdims=True)
    var = jnp.mean(jnp.square(vf - mu), axis=-1, keepdims=True)
    v = ((vf - mu) * lax.rsqrt(var + LN_EPS) * norm_g.astype(jnp.float32)).astype(u.dtype)
    v = v.reshape(b_, s_ // GMLP_CHUNK, GMLP_CHUNK, gm, dm)
    causal = jnp.tril(jnp.ones((GMLP_CHUNK, GMLP_CHUNK), dtype=bool))
    w = jnp.where(causal, w_s, jnp.zeros_like(w_s))
    mixed = jnp.einsum('gts,bnsgd->bntgd', w, v) + b_s.T[:, :, None]
    return (u * mixed.reshape(b_, s_, gm, dm)).reshape(b_, s_, gm * dm)


def hybrid_mixer(h, w_in, cmp_pos, cmp_w1, cmp_w2, gmlp_norm_g, gmlp_ws, gmlp_bs, w_out):
    b_, s_, _ = h.shape
    sizes = [Q_DIM] + [KV_DIM] * 6 + [GATE_DIM, GMLP_DIM, GMLP_DIM]
    offsets = np.cumsum(sizes)[:-1].tolist()
    q, kc, vc, ks, vs, kw, vw, g, u, v = jnp.split(h @ w_in, offsets, axis=-1)
    q = q.reshape(b_, s_, NSA_KV_HEADS, NSA_GROUP, HEAD_DIM).transpose(0, 2, 3, 1, 4)
    kv = [t.reshape(b_, s_, NSA_KV_HEADS, HEAD_DIM).transpose(0, 2, 1, 3)
          for t in (kc, vc, ks, vs, kw, vw)]
    gates = jax.nn.sigmoid(g).reshape(b_, s_, NSA_KV_HEADS, NSA_GROUP, N_BRANCHES)
    gates = gates.transpose(0, 2, 3, 1, 4)
    o_nsa = nsa_attention(q, *kv, gates, cmp_pos, cmp_w1, cmp_w2)
    o_nsa = o_nsa.transpose(0, 3, 1, 2, 4).reshape(b_, s_, Q_DIM)
    o_gmlp = chunked_gmlp(u.reshape(b_, s_, GMLP_GROUPS, GMLP_HEAD_DIM),
                          v.reshape(b_, s_, GMLP_GROUPS, GMLP_HEAD_DIM),
                          gmlp_norm_g, gmlp_ws, gmlp_bs)
    return jnp.concatenate([o_nsa, o_gmlp], axis=-1) @ w_out


def short_conv_mixer(h, w_in, conv_w, w_out):
    b_gate, c_gate, z = jnp.split(h @ w_in, 3, axis=-1)
    y = lax.conv_general_dilated(
        c_gate * z, conv_w[:, None, :], window_strides=(1,),
        padding=[(CONV_WIDTH - 1, 0)], dimension_numbers=('NWC', 'WIO', 'NWC'),
        feature_group_count=conv_w.shape[-1])
    return (b_gate * y) @ w_out


def swiglu(h, w_in, w_out):
    gate, up = jnp.split(h @ w_in, 2, axis=-1)
    return (jax.nn.silu(gate) * up) @ w_out


def setup_inputs(seed: int = 0) -> dict:
    key = jax.random.key(seed)
    ks = jax.random.split(key, 24)
    f32 = jnp.float32
    nrm = lambda k, shape, s: jax.random.normal(k, shape, f32) * s
    D = D_MODEL
    return {
        "x": nrm(ks[0], (BATCH, SEQ, D), 1.0),
        "c": nrm(ks[1], (BATCH, D), 1.0),
        "ada_w": nrm(ks[2], (DEPTH, 2, D, 3 * D), 0.1 * D ** -0.5),
        "ada_b": nrm(ks[3], (DEPTH, 2, 3 * D), 0.01),
        "ln_g": 1.0 + nrm(ks[4], (DEPTH, 2, D), 0.01),
        "ln_b": nrm(ks[5], (DEPTH, 2, D), 0.01),
        "even_w_in": nrm(ks[6], (N_EVEN, D, IN0_DIM), D ** -0.5),
        "even_cmp_pos": nrm(ks[7], (N_EVEN, 2, CMP_LEN, HEAD_DIM), 0.02),
        "even_cmp_w1": nrm(ks[8], (N_EVEN, 2, CMP_LEN * HEAD_DIM, CMP_HIDDEN), (CMP_LEN * HEAD_DIM) ** -0.5),
        "even_cmp_w2": nrm(ks[9], (N_EVEN, 2, CMP_HIDDEN, HEAD_DIM), CMP_HIDDEN ** -0.5),
        "even_gmlp_norm_g": 1.0 + nrm(ks[10], (N_EVEN, GMLP_GROUPS, GMLP_HEAD_DIM), 0.01),
        "even_gmlp_ws": nrm(ks[11], (N_EVEN, GMLP_GROUPS, GMLP_CHUNK, GMLP_CHUNK), GMLP_CHUNK ** -0.5),
        "even_gmlp_bs": 1.0 + nrm(ks[12], (N_EVEN, GMLP_GROUPS, GMLP_CHUNK), 0.01),
        "even_w_out": nrm(ks[13], (N_EVEN, MIX0_DIM, D), BETA * MIX0_DIM ** -0.5),
        "odd_w_in": nrm(ks[14], (N_ODD, D, 3 * D), D ** -0.5),
        "odd_conv_w": nrm(ks[15], (N_ODD, CONV_WIDTH, D), CONV_WIDTH ** -0.5),
        "odd_w_out": nrm(ks[16], (N_ODD, D, D), BETA * D ** -0.5),
        "ffn_w_in": nrm(ks[17], (DEPTH, D, 2 * FFN_HIDDEN), D ** -0.5),
        "ffn_w_out": nrm(ks[18], (DEPTH, FFN_HIDDEN, D), BETA * FFN_HIDDEN ** -0.5),
    }


def reference(x, c, ada_w, ada_b, ln_g, ln_b, even_w_in, even_cmp_pos, even_cmp_w1,
              even_cmp_w2, even_gmlp_norm_g, even_gmlp_ws, even_gmlp_bs, even_w_out,
              odd_w_in, odd_conv_w, odd_w_out, ffn_w_in, ffn_w_out):
    for layer in range(DEPTH):
        j = layer // 2
        if layer % 2 == 0:
            mixer = functools.partial(
                hybrid_mixer, w_in=even_w_in[j], cmp_pos=even_cmp_pos[j],
                cmp_w1=even_cmp_w1[j], cmp_w2=even_cmp_w2[j],
                gmlp_norm_g=even_gmlp_norm_g[j], gmlp_ws=even_gmlp_ws[j],
                gmlp_bs=even_gmlp_bs[j], w_out=even_w_out[j])
        else:
            mixer = functools.partial(
                short_conv_mixer, w_in=odd_w_in[j], conv_w=odd_conv_w[j], w_out=odd_w_out[j])
        x = residual_update(x, c, ada_w[layer, 0], ada_b[layer, 0],
                            ln_g[layer, 0], ln_b[layer, 0], mixer)
        ffn = functools.partial(swiglu, w_in=ffn_w_in[layer], w_out=ffn_w_out[layer])
        x = residual_update(x, c, ada_w[layer, 1], ada_b[layer, 1],
                            ln_g[layer, 1], ln_b[layer, 1], ffn)
    return x
```

```python
import numpy as np
import concourse.bass as bass
import concourse.mybir as mybir
from concourse.bass_utils import run_bass_kernel_spmd
from contextlib import ExitStack

F32 = mybir.dt.float32; BF16 = mybir.dt.bfloat16; I32 = mybir.dt.int32
ALU = mybir.AluOpType; AF = mybir.ActivationFunctionType; AX = mybir.AxisListType

S = 4096; D = 1024; NB = 2; NT = S // 128
FH = 2816
NEG = -30000.0
ALPHA = 4.0 ** 0.25
EPS = 1e-5
N_CORES = 8
P3_TILES = NT
P1_MT = 8
SELMODE = 0
NB_RUN = NB
P3_PARTS = ('cmp', 'topk', 'sel', 'win', 'comb', 'out')


class Buf:
    __slots__ = ("name", "w", "r", "dsem")

    def __init__(self, name):
        self.name = name; self.w = {}; self.r = {}; self.dsem = None


class T:
    def __init__(self, h, buf):
        self.h = h; self.b = buf

    def __getitem__(self, k):
        return self.h[k]


class FW:
    def __init__(self, nc, ges, n_dsem=56):
        self.nc = nc; self.es = ges
        self.eng = {"pe": nc.tensor, "act": nc.scalar, "dve": nc.vector, "pool": nc.gpsimd, "sp": nc.sync}
        self.sems = {}; self.cnt = {}
        for e in ("pe", "act", "dve", "pool"):
            self.sems[e] = ges.enter_context(nc.semaphore("s_" + e)); self.cnt[e] = 0
        self.dpool = []; self.dpool_sw = []
        for i in range(n_dsem):
            k = "d%d" % i
            self.sems[k] = ges.enter_context(nc.semaphore("s_" + k)); self.cnt[k] = 0
            (self.dpool_sw if i < 10 else self.dpool).append(k)
        self.checked = []
        self.scopes = []
        self.seen = {e: {} for e in self.eng}
        self.uid = 0

    def push(self):
        st = ExitStack(); st.__enter__()
        self.scopes.append((self.es, self.checked, st))
        self.es = st; self.checked = []

    def pop(self):
        self.barrier()
        old_es, old_checked, st = self.scopes.pop()
        st.__exit__(None, None, None)
        for k in self.checked:
            (self.dpool_sw if int(k[1:]) < 10 else self.dpool).append(k)
        self.es = old_es; self.checked = old_checked

    def sb(self, name, shape, dt, dma=False):
        self.uid += 1
        h = self.es.enter_context(self.nc.sbuf_tensor("%s_%d" % (name, self.uid), shape, dt))
        b = Buf(name)
        if dma:
            b.dsem = (self.dpool_sw if dma == "sw" else self.dpool).pop(0); self.checked.append(b.dsem)
        return T(h, b)

    def ps(self, name, shape, dt):
        self.uid += 1
        h = self.es.enter_context(self.nc.psum_tensor("%s_%d" % (name, self.uid), shape, dt))
        return T(h, Buf(name))

    def _deps(self, reads, writes):
        deps = {}
        for b in reads:
            for k, v in b.w.items():
                if deps.get(k, 0) < v: deps[k] = v
        for b in writes:
            for k, v in b.w.items():
                if deps.get(k, 0) < v: deps[k] = v
            for k, v in b.r.items():
                if deps.get(k, 0) < v: deps[k] = v
        return deps

    def _wait(self, e, deps):
        seen = self.seen[e]; eng = self.eng[e]
        for k, v in deps.items():
            if k == "pe" and e == "pe":
                continue
            if seen.get(k, 0) < v:
                eng.wait_ge(self.sems[k], v); seen[k] = v

    def _record(self, reads, writes, key, val):
        for b in writes:
            b.w = {key: val}; b.r = {}
        for b in reads:
            if b.r.get(key, 0) < val: b.r[key] = val

    @staticmethod
    def _bl(ts):
        return [t.b if isinstance(t, T) else t for t in ts]

    def op(self, e, fn, reads=(), writes=()):
        reads = self._bl(reads); writes = self._bl(writes)
        self._wait(e, self._deps(reads, writes))
        ins = fn()
        self.cnt[e] += 1
        ins.then_inc(self.sems[e], 1)
        self._record(reads, writes, e, self.cnt[e])
        return ins

    def dma(self, q, out, in_, sbuf_t, reads=(), writes=()):
        reads = self._bl(reads); writes = self._bl(writes)
        sbb = sbuf_t.b if isinstance(sbuf_t, T) else sbuf_t
        assert sbb.dsem is not None, sbb.name
        self._wait(q, self._deps(reads, writes))
        ins = self.eng[q].dma_start(out=out, in_=in_)
        key = sbb.dsem
        self.cnt[key] += 16
        ins.then_inc(self.sems[key], 16)
        self._record(reads, writes, key, self.cnt[key])
        return ins

    def dma_group(self, q, pairs, sbuf_t, reads=(), writes=()):
        reads = self._bl(reads); writes = self._bl(writes)
        sbb = sbuf_t.b if isinstance(sbuf_t, T) else sbuf_t
        self._wait(q, self._deps(reads, writes))
        key = sbb.dsem
        base = self.cnt[key]
        for n_, (o, i_) in enumerate(pairs):
            if n_ >= 4 and n_ % 4 == 0:
                self.eng[q].wait_ge(self.sems[key], base + 16 * n_)
            ins = self.eng[q].dma_start(out=o, in_=i_)
            self.cnt[key] += 16
            ins.then_inc(self.sems[key], 16)
        self._record(reads, writes, key, self.cnt[key])

    def barrier(self):
        deps = {k: v for k, v in self.cnt.items() if v > 0}
        for e in self.eng:
            seen = self.seen[e]
            for k, v in deps.items():
                if seen.get(k, 0) < v:
                    self.eng[e].wait_ge(self.sems[k], v); seen[k] = v


def build_program(upto=99, debug=False):
    nc = bass.Bass("TRN2", target_bir_lowering=False)
    kS = "ExternalOutput" if debug else "Internal"

    def din(name, shape):
        return nc.dram_tensor(name, shape, F32, kind="ExternalInput").ap()

    x_d = din("x", [NB, S, D]); c_d = din("c", [NB, D])
    adaw_d = din("ada_w", [4, D, 3 * D]); adab_d = din("ada_b", [4, 3 * D])
    lng_d = din("ln_g", [4, D]); lnb_d = din("ln_b", [4, D])
    win0_d = din("even_w_in", [D, 2328]); pos_d = din("even_cmp_pos", [2, 32, 64])
    w1_d = din("even_cmp_w1", [2, 2048, 128]); w2_d = din("even_cmp_w2", [2, 128, 64])
    gng_d = din("even_gmlp_norm_g", [512]); gws_d = din("even_gmlp_ws", [8, 128, 128]); gbs_d = din("even_gmlp_bs", [8, 128])
    wout0_d = din("even_w_out", [D, D])
    win1_d = din("odd_w_in", [D, 3 * D]); cw_d = din("odd_conv_w", [3, D]); wout1_d = din("odd_w_out", [D, D])
    fwi_d = din("ffn_w_in", [2, D, 2 * FH]); fwo_d = din("ffn_w_out", [2, FH, D])
    out_d = nc.dram_tensor("out", [NB, S, D], F32, kind="ExternalOutput").ap()
    mod_d = nc.dram_tensor("mod_s", [4, NB, 3 * D], F32, kind=kS).ap()
    x1_d = nc.dram_tensor("x1_s", [NB, S, D], F32, kind=kS).ap()
    x2_d = nc.dram_tensor("x2_s", [NB, S, D], F32, kind=kS).ap()
    x3_d = nc.dram_tensor("x3_s", [NB, S, D], F32, kind=kS).ap()
    ogm_d = nc.dram_tensor("ogm_s", [NB, S, 512], BF16, kind=kS).ap()
    dbg = {}
    if debug:
        dbg["onsa"] = nc.dram_tensor("onsa_s", [NB, S, 512], F32, kind=kS).ap()
        dbg["kcc"] = nc.dram_tensor("kcc_s", [NB, 128, 256], F32, kind=kS).ap()

    dbufs = {}

    def dbuf(*key):
        if key not in dbufs:
            dbufs[key] = Buf(str(key))
        return dbufs[key]

    with ExitStack() as ges, nc.allow_non_contiguous_dma(reason="small param layouts"):
        fw = FW(nc, ges)
        op = fw.op
        V = nc.vector; A = nc.scalar; G = nc.gpsimd; PE = nc.tensor
        VENG = {"dve": V, "pool": G}

        ident = fw.sb("ident", [128, 128], BF16)
        identf = fw.sb("identf", [128, 128], F32)
        epsb = fw.sb("epsb", [128, 1], F32)
        op("pool", lambda: G.memset(identf[:], 1.0), writes=[identf])
        op("pool", lambda: G.affine_select(out=identf[:], in_=identf[:], pattern=[[1, 128]], compare_op=ALU.is_equal,
                                           fill=0.0, base=0, channel_multiplier=-1), reads=[identf], writes=[identf])
        op("pool", lambda: G.tensor_copy(out=ident[:], in_=identf[:]), reads=[identf], writes=[ident])
        op("pool", lambda: G.memset(epsb[:], EPS), writes=[epsb])

        def tt(e, out, in0, in1, o, rd, wr):
            return op(e, lambda: VENG[e].tensor_tensor(out=out, in0=in0, in1=in1, op=o), reads=rd, writes=wr)

        def load_bc(dst, src_1d, q="sp"):
            fw.dma(q, dst[:], src_1d.partition_broadcast(128), dst, writes=[dst])

        def load_w_bf16(dst, kidx, src2d):
            fw.dma("pool", dst[:, kidx, :], src2d, dst, writes=[dst])

        class HPrep:
            def __init__(self, sc1, sh, nhb=2, tmp_dt=F32):
                self.sc1 = sc1; self.sh = sh; self.nhb = nhb
                self.tmp = [fw.sb("hp_tmp%d" % i, [128, D], tmp_dt) for i in range(1)]
                self.hb = [fw.sb("hp_hb%d" % i, [128, D], BF16) for i in range(nhb)]
                self.psT = fw.ps("hp_psT", [128, 8, 128], BF16)
                self.n = 0

            def run_a(self, xt):
                tmp = self.tmp[0]; hb = self.hb[self.n % self.nhb]; self.n += 1
                tt("dve", tmp[:], xt[:], self.sc1[:], ALU.mult, [xt, self.sc1], [tmp])
                tt("pool", hb[:], tmp[:], self.sh[:], ALU.add, [tmp, self.sh], [hb])
                return hb

            def run_b(self, hb, hT, off):
                psT = self.psT
                for c in range(8):
                    op("pe", lambda: PE.transpose(psT[:, c, :], hb[:, c * 128:(c + 1) * 128], ident[:]),
                       reads=[hb, ident], writes=[psT])
                op("act", lambda: A.copy(out=hT[:, :, off:off + 128], in_=psT[:]), reads=[psT], writes=[hT])

            def run(self, xt, hT, off):
                self.run_b(self.run_a(xt), hT, off)

        class Epilogue:
            def __init__(self, g1, lng, lnb, nt=1):
                self.g1 = g1; self.lng = lng; self.lnb = lnb
                self.ts = [fw.sb("ep_t%d" % i, [128, D], F32) for i in range(nt)]
                self.xo = [fw.sb("ep_xo%d" % i, [128, D], F32, dma=True) for i in range(2)]
                self.sts = [fw.sb("ep_st%d" % i, [128, 2, 6], F32) for i in range(nt)]
                self.mvs = [fw.sb("ep_mv%d" % i, [128, 8], F32) for i in range(nt)]
                self.k = 0

            def steps(self, halves, xres, out_ap, out_db):
                t = self.ts[self.k % len(self.ts)]; st = self.sts[self.k % len(self.ts)]; mv = self.mvs[self.k % len(self.ts)]
                xo = self.xo[self.k % 2]; self.k += 1
                g1 = self.g1; lng = self.lng; lnb = self.lnb
                L = []
                for hi, (pt, pap) in enumerate(halves):
                    L.append(lambda hi=hi, pt=pt, pap=pap: tt("dve", t[:, hi * 512:(hi + 1) * 512], pap, g1[:, hi * 512:(hi + 1) * 512], ALU.mult, [pt, g1], [t]))
                L.append(lambda: op("dve", lambda: V.scalar_tensor_tensor(out=t[:], in0=xres[:], scalar=ALPHA, in1=t[:], op0=ALU.mult, op1=ALU.add),
                                    reads=[xres, t], writes=[t]))
                L.append(lambda: op("dve", lambda: V.bn_stats(out=st[:, 0, :], in_=t[:, 0:512]), reads=[t], writes=[st]))
                L.append(lambda: op("dve", lambda: V.bn_stats(out=st[:, 1, :], in_=t[:, 512:1024]), reads=[t, st], writes=[st]))
                L.append(lambda: op("dve", lambda: V.bn_aggr(out=mv[:, 0:2], in_=st[:].rearrange("p a b -> p (a b)")), reads=[st], writes=[mv]))
                L.append(lambda: op("act", lambda: A.activation(out=mv[:, 2:3], in_=mv[:, 1:2], func=AF.Sqrt, bias=epsb[:], scale=1.0),
                                    reads=[mv, epsb], writes=[mv]))
                L.append(lambda: op("dve", lambda: V.reciprocal(out=mv[:, 3:4], in_=mv[:, 2:3]), reads=[mv], writes=[mv]))
                L.append(lambda: op("dve", lambda: V.tensor_scalar(out=mv[:, 4:5], in0=mv[:, 0:1], scalar1=mv[:, 3:4], scalar2=-1.0,
                                                                   op0=ALU.mult, op1=ALU.mult), reads=[mv], writes=[mv]))
                L.append(lambda: op("act", lambda: A.activation(out=xo[:], in_=t[:], func=AF.Identity, bias=mv[:, 4:5], scale=mv[:, 3:4]),
                                    reads=[t, mv], writes=[xo]))
                L.append(lambda: tt("pool", xo[:], xo[:], lng[:], ALU.mult, [xo, lng], [xo]))
                L.append(lambda: tt("pool", xo[:], xo[:], lnb[:], ALU.add, [xo, lnb], [xo]))
                L.append(lambda: fw.dma("sp", out_ap, xo[:], xo, reads=[xo], writes=[out_db]))
                return L

            def run(self, halves, xres, out_ap, out_db):
                for f in self.steps(halves, xres, out_ap, out_db):
                    f()

        def load_mod(ls, b, want):
            for tile, which in want:
                load_bc(tile, mod_d[ls, b, which * D:(which + 1) * D])

        def transpose_small(dst_ap, dstT, src_ap, srcT, npart, pst, pst_ap):
            op("pe", lambda: PE.transpose(pst_ap, src_ap, identf[0:npart, 0:npart]), reads=[srcT, identf], writes=[pst])
            op("dve", lambda: V.tensor_copy(out=dst_ap, in_=pst_ap), reads=[pst], writes=[dstT])

        fw.push()
        if True:
            csb_ = fw.sb("csb", [NB, D], F32, dma=True)
            scs = fw.sb("scs", [NB, D], F32)
            scT = fw.sb("scT", [128, 8, NB], F32)
            stage = [fw.sb("adastage%d" % i, [128, 3 * D], F32, dma=True) for i in range(2)]
            adab = fw.sb("adab", [NB, 3 * D], F32, dma=True)
            msb = fw.sb("msb", [NB, 3 * D], F32, dma=True)
            psm = [fw.ps("psm%d" % i, [NB, 512], F32) for i in range(6)]
            pst = fw.ps("pst_p0", [128, 8, NB], F32)
            fw.dma("sp", csb_[:], c_d, csb_, writes=[csb_])
            op("act", lambda: A.activation(out=scs[:], in_=csb_[:], func=AF.Silu), reads=[csb_], writes=[scs])
            for k in range(8):
                op("pe", lambda: PE.transpose(pst[:, k, :], scs[:, k * 128:(k + 1) * 128], identf[0:NB, 0:NB]), reads=[scs, identf], writes=[pst])
            op("dve", lambda: V.tensor_copy(out=scT[:], in_=pst[:]), reads=[pst], writes=[scT])
            si = 0
            for ls in range(4):
                fw.dma("sp", adab[:], adab_d[ls].partition_broadcast(NB), adab, writes=[adab])
                for k in range(8):
                    stg = stage[si % 2]; si += 1
                    fw.dma("sp", stg[:], adaw_d[ls, k * 128:(k + 1) * 128, :], stg, writes=[stg])
                    for ct in range(6):
                        op("pe", lambda: PE.matmul(psm[ct][:], lhsT=scT[:, k, :], rhs=stg[:, ct * 512:(ct + 1) * 512],
                                                   start=(k == 0), stop=(k == 7)), reads=[scT, stg], writes=[psm[ct]])
                for ct in range(6):
                    tt("dve", msb[:, ct * 512:(ct + 1) * 512], psm[ct][:], adab[:, ct * 512:(ct + 1) * 512], ALU.add,
                       [psm[ct], adab], [msb])
                op("dve", lambda: V.tensor_scalar_add(out=msb[:, D:3 * D], in0=msb[:, D:3 * D], scalar1=1.0), reads=[msb], writes=[msb])
                fw.dma("sp", mod_d[ls], msb[:], msb, reads=[msb], writes=[dbuf("mod")])
        fw.pop()
        if upto < 1:
            return nc

        fw.push()
        if True:
            WsT = fw.sb("WsT", [128, 8, 128], BF16)
            bsT = fw.sb("bsT", [128, 8], F32)
            ngb = fw.sb("ngb", [128, 512], F32, dma=True)
            load_bc(ngb, gng_d)
            cbias = fw.sb("cbias", [128, 2], F32)
            w2 = fw.sb("cw2", [128, 2, 64], BF16, dma="sw")
            fw.dma("pool", w2[:], w2_d.rearrange("kv h d -> h kv d"), w2, writes=[w2])
            w2pad = fw.sb("cw2pad", [128, 2, 128], BF16)
            op("pool", lambda: G.memset(w2pad[:], 0.0), writes=[w2pad])
            for g in range(2):
                op("pool", lambda: G.tensor_copy(out=w2pad[:, g, g * 64:(g + 1) * 64], in_=w2[:, 0, :]), reads=[w2, w2pad], writes=[w2pad])
            McNeg = fw.sb("McNeg", [128, 128], BF16)
            MwNeg = fw.sb("MwNeg", [128, 128], BF16)
            cm_all = fw.sb("cm_all", [128, 33, 128], BF16)
            TMm = fw.sb("TMm", [128, 126], F32); TB = fw.sb("TB", [128, 126], F32)
            qT = fw.sb("qT", [128, 4, S], BF16)
            ksE = [fw.sb("ksE%d" % g_, [128, S], BF16) for g_ in range(2)]; kwT = fw.sb("kwT", [128, S], BF16)
            kcT = fw.sb("kcT", [128, S], BF16); vcT = fw.sb("vcT", [128, S], BF16)
            vsA = fw.sb("vsA", [128, NT, 2, 65], BF16); vwA = fw.sb("vwA", [128, NT, 2, 65], BF16)
            gat = fw.sb("gat", [128, NT, 24], F32)
            kcmpT = fw.sb("kcmpT", [128, 256], BF16)
            vcx = fw.sb("vcx", [128, 2, 2, 129], BF16)
            cmpmask = {}
            if P1_MT < 8:
                for tcache in (qT, ksE[0], ksE[1], kwT, kcT, vcT, gat):
                    op("pool", lambda: G.memset(tcache[:], 0.0), writes=[tcache])
                op("pool", lambda: G.memset(vsA[:], 0.0), writes=[vsA]); op("pool", lambda: G.memset(vwA[:], 0.0), writes=[vwA])
            fw.push()
            if True:
                w1t = fw.sb("cw1t", [64, 2, 32, 128], BF16, dma="sw")
                for kv in range(2):
                    fw.dma("pool", w1t[:, kv, :, :], w1_d[kv].rearrange("(l d) h -> d l h", d=64), w1t, writes=[w1t])
                posf = fw.sb("posf", [32, 2, 64], F32, dma=True)
                fw.dma("sp", posf[:], pos_d.rearrange("kv l d -> l kv d"), posf, writes=[posf])
                posT = fw.sb("posT", [64, 2, 32], BF16)
                pss = fw.ps("pss", [128, 64], F32)
                for kv in range(2):
                    transpose_small(posT[:, kv, :], posT, posf[:, kv, :], posf, 32, pss, pss[0:64, 0:32])
                psb = fw.ps("psb", [128, 2], F32)
                for kv in range(2):
                    for l in range(32):
                        op("pe", lambda: PE.matmul(psb[:, kv:kv + 1], lhsT=w1t[:, kv, l, :], rhs=posT[:, kv, l:l + 1],
                                                   start=(l == 0), stop=(l == 31)), reads=[w1t, posT], writes=[psb])
                op("dve", lambda: V.tensor_copy(out=cbias[:], in_=psb[:]), reads=[psb], writes=[cbias])
                gbsf = fw.sb("gbsf", [8, 128], F32, dma=True)
                fw.dma("sp", gbsf[:], gbs_d, gbsf, writes=[gbsf])
                transpose_small(bsT[:], bsT, gbsf[:], gbsf, 8, pss, pss[:, 0:8])
                wsf = fw.sb("wsf", [128, 8, 128], F32, dma=True)
                wsb = fw.sb("wsb", [128, 8, 128], BF16)
                fw.dma("sp", wsf[:], gws_d.rearrange("g t s -> t g s"), wsf, writes=[wsf])
                op("pool", lambda: G.affine_select(out=wsf[:], in_=wsf[:], pattern=[[0, 8], [-1, 128]], compare_op=ALU.is_ge,
                                                   fill=0.0, base=0, channel_multiplier=1), reads=[wsf], writes=[wsf])
                op("pool", lambda: G.tensor_copy(out=wsb[:], in_=wsf[:]), reads=[wsf], writes=[wsb])
                pst = fw.ps("pst0", [128, 8, 128], BF16)
                for gm in range(8):
                    op("pe", lambda: PE.transpose(pst[:, gm, :], wsb[:, gm, :], ident[:]), reads=[wsb, ident], writes=[pst])
                op("dve", lambda: V.tensor_copy(out=WsT[:], in_=pst[:]), reads=[pst], writes=[WsT])
                zf = fw.sb("zf", [128, 128], F32)
                mtmp = fw.sb("mtmp", [128, 128], F32)
                op("pool", lambda: G.memset(zf[:], 0.0), writes=[zf])
                op("pool", lambda: G.affine_select(out=mtmp[:], in_=zf[:], pattern=[[1, 128]], compare_op=ALU.is_ge, fill=NEG,
                                                   base=0, channel_multiplier=-1), reads=[zf], writes=[mtmp])
                op("pool", lambda: G.tensor_copy(out=McNeg[:], in_=mtmp[:]), reads=[mtmp], writes=[McNeg])
                op("pool", lambda: G.affine_select(out=mtmp[:], in_=zf[:], pattern=[[-1, 128]], compare_op=ALU.is_gt, fill=NEG,
                                                   base=0, channel_multiplier=1), reads=[zf], writes=[mtmp])
                op("pool", lambda: G.tensor_copy(out=MwNeg[:], in_=mtmp[:]), reads=[mtmp], writes=[MwNeg])
                mi = 0
                for c in range(2):
                    for i in (range(0, 17) if c == 0 else range(16, 32)):
                        op("pool", lambda: G.affine_select(out=mtmp[:], in_=zf[:], pattern=[[1, 128]], compare_op=ALU.is_ge, fill=NEG,
                                                           base=128 * i - 31 - 2048 * c, channel_multiplier=-16),
                           reads=[zf], writes=[mtmp])
                        op("pool", lambda: G.tensor_copy(out=cm_all[:, mi, :], in_=mtmp[:]), reads=[mtmp], writes=[cm_all])
                        cmpmask[(c, i)] = mi; mi += 1
                ebf = fw.sb("ebf", [64, S], F32)
                op("pool", lambda: G.memset(ebf[:], 1.0), writes=[ebf])
                op("pool", lambda: G.affine_select(out=ebf[:], in_=ebf[:], pattern=[[1, S]], compare_op=ALU.is_ge, fill=0.0,
                                                   base=0, channel_multiplier=-64), reads=[ebf], writes=[ebf])
                op("pool", lambda: G.affine_select(out=ebf[:], in_=ebf[:], pattern=[[-1, S]], compare_op=ALU.is_ge, fill=0.0,
                                                   base=63, channel_multiplier=64), reads=[ebf], writes=[ebf])
                op("pool", lambda: G.tensor_copy(out=ksE[1][0:64, :], in_=ebf[:]), reads=[ebf, ksE[1]], writes=[ksE[1]])
                ebf2 = fw.sb("ebf2", [128, S], F32)
                op("pool", lambda: G.memset(ebf2[:], 1.0), writes=[ebf2])
                op("pool", lambda: G.affine_select(out=ebf2[:], in_=ebf2[:], pattern=[[1, S]], compare_op=ALU.is_ge, fill=0.0,
                                                   base=4096, channel_multiplier=-64), reads=[ebf2], writes=[ebf2])
                op("pool", lambda: G.affine_select(out=ebf2[:], in_=ebf2[:], pattern=[[-1, S]], compare_op=ALU.is_ge, fill=0.0,
                                                   base=-4033, channel_multiplier=64), reads=[ebf2], writes=[ebf2])
                op("pool", lambda: G.tensor_copy(out=ksE[0][64:128, :], in_=ebf2[64:128, :]), reads=[ebf2, ksE[0]], writes=[ksE[0]])
                for hh in range(2):
                    hs = slice(hh * 64, (hh + 1) * 64)
                    op("pool", lambda: G.memset(TMm[hs, 0:61 + hh], 1.0), reads=[TMm], writes=[TMm])
                    op("pool", lambda: G.memset(TMm[hs, 61 + hh:126], 0.0), reads=[TMm], writes=[TMm])
                    op("pool", lambda: G.memset(TB[hs, 0:61 + hh], 0.0), reads=[TB], writes=[TB])
                    op("pool", lambda: G.memset(TB[hs, 61 + hh:63 + hh], 1e4), reads=[TB], writes=[TB])
                    op("pool", lambda: G.memset(TB[hs, 63 + hh:126], -1e4), reads=[TB], writes=[TB])
                op("pool", lambda: G.memset(vsA[:, :, :, 64:65], 1.0), writes=[vsA])
                op("pool", lambda: G.memset(vwA[:, :, :, 64:65], 1.0), writes=[vwA])
                op("pool", lambda: G.memset(vcx[:], 0.0), writes=[vcx])
                op("pool", lambda: G.memset(vcx[:, :, :, 64:65], 1.0), reads=[vcx], writes=[vcx])
                covf = fw.sb("covf", [128, 2, 64], F32)
                op("pool", lambda: G.memset(covf[:], 1.0), writes=[covf])
                op("pool", lambda: G.affine_select(out=covf[:], in_=covf[:], pattern=[[-128, 2], [4, 64]], compare_op=ALU.is_ge,
                                                   fill=0.0, base=3, channel_multiplier=-1), reads=[covf], writes=[covf])
                op("pool", lambda: G.affine_select(out=covf[:], in_=covf[:], pattern=[[128, 2], [-4, 64]], compare_op=ALU.is_ge,
                                                   fill=0.0, base=1, channel_multiplier=1), reads=[covf], writes=[covf])
                for g in range(2):
                    op("pool", lambda: G.tensor_copy(out=vcx[:, :, g, 65:129], in_=covf[:]), reads=[covf, vcx], writes=[vcx])
                op("pool", lambda: G.memset(kcmpT[:], 0.0), writes=[kcmpT])
            fw.pop()

            v3 = lambda ap: ap.rearrange("p (g d) -> p g d", d=64)
            b8 = lambda ap: ap.unsqueeze(2).to_broadcast([128, 8, 64])

            for b in range(NB_RUN):
                fw.push()
                if True:
                    W0 = fw.sb("W0", [128, 8, 2328], BF16, dma="sw")
                    Wq = fw.sb("Wq", [128, 8, 512], BF16, dma="sw")
                    fw.dma_group("pool", [(W0[:, k, :], win0_d[k * 128:(k + 1) * 128, :]) for k in range(8)], W0, writes=[W0])
                    fw.dma_group("pool", [(Wq[:, k, r * 128:(r + 1) * 128].rearrange("p (g d) -> p g d", g=2),
                                           win0_d[k * 128:(k + 1) * 128, 0:512].rearrange("p (g r d) -> p g r d", g=2, r=4)[:, :, r, :])
                                          for k in range(8) for r in range(4)], Wq, writes=[Wq])
                    sc1 = fw.sb("sc1", [128, D], F32, dma=True); sh = fw.sb("sh", [128, D], F32, dma=True)
                    load_mod(0, b, [(sh, 0), (sc1, 1)])
                    xin = [fw.sb("xin%d" % i, [128, D], F32, dma=True) for i in range(3)]
                    hT = [fw.sb("hT%d" % i, [128, 8, 512], BF16) for i in range(2)]
                    hp = HPrep(sc1, sh)
                    psF = [fw.ps("psF%d" % i, [128, 512], F32) for i in range(2)]
                    psA = fw.ps("psA", [128, 512], F32); psB = fw.ps("psB", [128, 512], F32)
                    psC = fw.ps("psC", [128, 512], F32); psM = fw.ps("psM", [128, 512], F32)
                    ug = fw.sb("ug", [128, 512], F32); vg = fw.sb("vg", [128, 512], F32)
                    sq = fw.sb("sq", [128, 512], F32)
                    gst = fw.sb("gst", [128, 6, 8], F32)
                    vn = fw.sb("vn", [128, 512], BF16)
                    og = [fw.sb("og%d" % i, [128, 512], BF16, dma=True) for i in range(2)]
                    xcount = 0
                    for mt in range(P1_MT):
                        hTm = hT[mt % 2]
                        for sub in range(4):
                            i = mt * 4 + sub
                            xt = xin[xcount % 3]; xcount += 1
                            fw.dma("sp", xt[:], x_d[b, i * 128:(i + 1) * 128, :], xt, writes=[xt])
                            hp.run(xt, hTm, sub * 128)
                        fm = []
                        for r in range(4):
                            fm.append((Wq[:, :, r * 128:(r + 1) * 128], qT[:, r, mt * 512:(mt + 1) * 512], qT))
                        fm.append((W0[:, :, 512:640], kcT[:, mt * 512:(mt + 1) * 512], kcT))
                        fm.append((W0[:, :, 640:768], vcT[:, mt * 512:(mt + 1) * 512], vcT))
                        fm.append((W0[:, :, 768:896], None, "ks"))
                        fm.append((W0[:, :, 1024:1152], kwT[:, mt * 512:(mt + 1) * 512], kwT))
                        for fi, (wap, dst, dstT) in enumerate(fm):
                            pf = psF[fi % 2]
                            for k in range(8):
                                op("pe", lambda: PE.matmul(pf[:], lhsT=wap[:, k], rhs=hTm[:, k, :], start=(k == 0), stop=(k == 7)),
                                   reads=[W0, Wq, hTm], writes=[pf])
                            if dstT == "ks":
                                op("act", lambda: A.copy(out=ksE[0][0:64, mt * 512:(mt + 1) * 512], in_=pf[0:64, :]), reads=[pf, ksE[0]], writes=[ksE[0]])
                                op("dve", lambda: V.tensor_copy(out=ksE[1][64:128, mt * 512:(mt + 1) * 512], in_=pf[64:128, :]), reads=[pf, ksE[1]], writes=[ksE[1]])
                            elif fi % 2 == 0:
                                op("act", lambda: A.copy(out=dst, in_=pf[:]), reads=[pf], writes=[dstT])
                            else:
                                op("dve", lambda: V.tensor_copy(out=dst, in_=pf[:]), reads=[pf], writes=[dstT])
                        for sub in range(4):
                            i = mt * 4 + sub
                            lh = lambda k: hTm[:, k, sub * 128:(sub + 1) * 128]
                            for k in range(8):
                                op("pe", lambda: PE.matmul(psA[:, 0:408], lhsT=lh(k), rhs=W0[:, k, 896:1304], start=(k == 0), stop=(k == 7)),
                                   reads=[W0, hTm], writes=[psA])
                            for k in range(8):
                                op("pe", lambda: PE.matmul(psB[:], lhsT=lh(k), rhs=W0[:, k, 1304:1816], start=(k == 0), stop=(k == 7)),
                                   reads=[W0, hTm], writes=[psB])
                            for k in range(8):
                                op("pe", lambda: PE.matmul(psC[:], lhsT=lh(k), rhs=W0[:, k, 1816:2328], start=(k == 0), stop=(k == 7)),
                                   reads=[W0, hTm], writes=[psC])
                            op("dve", lambda: V.tensor_copy(out=vsA[:, i, :, 0:64], in_=v3(psA[:, 0:128])), reads=[psA], writes=[vsA])
                            op("dve", lambda: V.tensor_copy(out=vwA[:, i, :, 0:64], in_=v3(psA[:, 256:384])), reads=[psA], writes=[vwA])
                            op("act", lambda: A.activation(out=gat[:, i, :], in_=psA[:, 384:408], func=AF.Sigmoid), reads=[psA], writes=[gat])
                            op("act", lambda: A.activation(out=ug[:], in_=psB[:], func=AF.Gelu_apprx_tanh), reads=[psB], writes=[ug])
                            op("act", lambda: A.activation(out=vg[:], in_=psC[:], func=AF.Gelu_apprx_tanh), reads=[psC], writes=[vg])
                            op("dve", lambda: V.tensor_reduce(out=gst[:, 0, :], in_=v3(vg[:]), axis=AX.X, op=ALU.add), reads=[vg], writes=[gst])
                            tt("pool", sq[:], vg[:], vg[:], ALU.mult, [vg], [sq])
                            op("dve", lambda: V.tensor_reduce(out=gst[:, 1, :], in_=v3(sq[:]), axis=AX.X, op=ALU.add), reads=[sq, gst], writes=[gst])
                            op("dve", lambda: V.tensor_scalar(out=gst[:, 2, :], in0=gst[:, 0, :], scalar1=1.0 / 64, scalar2=None, op0=ALU.mult),
                               reads=[gst], writes=[gst])
                            tt("dve", gst[:, 5, :], gst[:, 2, :], gst[:, 2, :], ALU.mult, [gst], [gst])
                            op("dve", lambda: V.scalar_tensor_tensor(out=gst[:, 3, :], in0=gst[:, 1, :], scalar=1.0 / 64, in1=gst[:, 5, :],
                                                                     op0=ALU.mult, op1=ALU.subtract), reads=[gst], writes=[gst])
                            op("act", lambda: A.activation(out=gst[:, 3, :], in_=gst[:, 3, :], func=AF.Sqrt, bias=epsb[:], scale=1.0),
                               reads=[gst, epsb], writes=[gst])
                            op("dve", lambda: V.reciprocal(out=gst[:, 4, :], in_=gst[:, 3, :]), reads=[gst], writes=[gst])
                            tt("dve", v3(sq[:]), v3(vg[:]), b8(gst[:, 2, :]), ALU.subtract, [vg, gst], [sq])
                            tt("dve", v3(sq[:]), v3(sq[:]), b8(gst[:, 4, :]), ALU.mult, [sq, gst], [sq])
                            tt("pool", vn[:], sq[:], ngb[:], ALU.mult, [sq, ngb], [vn])
                            for gm in range(8):
                                op("pe", lambda: PE.matmul(psM[:, gm * 64:(gm + 1) * 64], lhsT=WsT[:, gm, :], rhs=vn[:, gm * 64:(gm + 1) * 64],
                                                           start=True, stop=True), reads=[WsT, vn], writes=[psM])
                            tt("dve", v3(sq[:]), v3(psM[:]), b8(bsT[:]), ALU.add, [psM, bsT], [sq])
                            ogt = og[i % 2]
                            tt("dve", ogt[:], sq[:], ug[:], ALU.mult, [sq, ug], [ogt])
                            fw.dma("sp", ogm_d[b, i * 128:(i + 1) * 128, :], ogt[:], ogt, reads=[ogt], writes=[dbuf("ogm", b, i)])
                fw.pop()
                if upto < 2:
                    continue
                fw.push()
                if True:
                    w1 = fw.sb("cw1", [128, 2, 32, 128], BF16, dma="sw")
                    for hh in range(2):
                        for kv in range(2):
                            fw.dma("pool", w1[hh * 64:(hh + 1) * 64, kv, :, :], w1_d[kv].rearrange("(l d) h -> d l h", d=64), w1, writes=[w1])
                    psH = [fw.ps("psH%d" % i, [128, 256], F32) for i in range(2)]
                    psK = fw.ps("psK", [128, 256], F32); psV = fw.ps("psV", [128, 256], F32)
                    for kv in range(2):
                        src = kcT if kv == 0 else vcT
                        hbs = []
                        for g in range(2):
                            ph = psH[g]
                            for l in range(32):
                                op("pe", lambda: PE.matmul(ph[:, 0:255], lhsT=w1[g * 64:(g + 1) * 64, kv, l, :],
                                                           rhs=src[g * 64:(g + 1) * 64, l:l + 16 * 254 + 1:16], start=(l == 0), stop=(l == 31)),
                                   reads=[w1, src], writes=[ph])
                            hb_g = fw.sb("hbg_%d%d" % (kv, g), [128, 256], BF16)
                            op("pool", lambda: G.memset(hb_g[:], 0.0), writes=[hb_g])
                            op("act", lambda: A.activation(out=hb_g[:, 0:255], in_=ph[:, 0:255], func=AF.Gelu_apprx_tanh,
                                                           bias=cbias[:, kv:kv + 1], scale=1.0), reads=[ph, cbias, hb_g], writes=[hb_g])
                            hbs.append(hb_g)
                        if kv == 0:
                            for g in range(2):
                                op("pe", lambda: PE.matmul(psK[:], lhsT=w2pad[:, g, :], rhs=hbs[g][:], start=(g == 0), stop=(g == 1)),
                                   reads=[w2pad, hbs[g]], writes=[psK])
                            op("dve", lambda: V.tensor_copy(out=kcmpT[:], in_=psK[:]), reads=[psK], writes=[kcmpT])
                            if debug:
                                kd = fw.sb("kd", [128, 256], F32, dma=True)
                                op("dve", lambda: V.tensor_copy(out=kd[:], in_=psK[:]), reads=[psK], writes=[kd])
                                fw.dma("sp", dbg["kcc"][b], kd[:], kd, reads=[kd], writes=[dbuf("kcc", b)])
                        else:
                            for g in range(2):
                                for c in range(2):
                                    op("pe", lambda: PE.matmul(psV[:, (g * 2 + c) * 64:(g * 2 + c + 1) * 64], lhsT=hbs[g][:, c * 128:(c + 1) * 128],
                                                               rhs=w2[:, 1, :], start=True, stop=True), reads=[w2, hbs[g]], writes=[psV])
                            for g in range(2):
                                op("dve", lambda: V.tensor_copy(out=vcx[:, :, g, 0:64],
                                                                in_=psV[:, g * 128:(g + 1) * 128].rearrange("p (c d) -> p c d", d=64)),
                                   reads=[psV], writes=[vcx])
                fw.pop()
                if upto < 3:
                    continue
                fw.push()
                if True:
                    Wo0 = fw.sb("Wo0", [128, 8, D], BF16, dma="sw")
                    fw.dma_group("pool", [(Wo0[:, k, :], wout0_d[k * 128:(k + 1) * 128, :]) for k in range(8)], Wo0, writes=[Wo0])
                    lng = fw.sb("lng", [128, D], F32, dma=True); lnb = fw.sb("lnb", [128, D], F32, dma=True)
                    load_bc(lng, lng_d[0]); load_bc(lnb, lnb_d[0])
                    g1 = fw.sb("g1", [128, D], F32, dma=True)
                    load_mod(0, b, [(g1, 2)])
                    ep = Epilogue(g1, lng, lnb)
                    psS = [fw.ps("psS%d" % i, [128, 512], F32) for i in range(3)]
                    psTt = fw.ps("psTt", [128, 8, 128], BF16)
                    psOc = fw.ps("psOc", [128, 512], F32); psImp = fw.ps("psImp", [128, 512], F32)
                    psOs = fw.ps("psOs", [128, 512], F32); psOw = fw.ps("psOw", [128, 512], F32)
                    o4 = lambda ps_: ps_[:, 0:260].rearrange("p (r e) -> p r e", e=65)
                    ebuf = [fw.sb("ebuf%d" % i, [128, 4, 128], BF16) for i in range(4)]
                    ecount = [0]
                    cats = [fw.sb("cat%d" % i, [128, D], BF16, dma=True) for i in range(2)]
                    ocS = [fw.sb("ocS%d" % i, [128, 4, 65], F32) for i in range(2)]
                    nsel2s = [fw.sb("nsel2_%d" % i, [128, 2, 64], BF16) for i in range(2)]
                    catT = fw.sb("catT", [128, 8, 128], BF16)
                    sm = fw.sb("sm", [128, 3, 4], F32)
                    ff = fw.sb("ff", [128, 3, 4], F32)
                    rs4 = fw.sb("rs4", [128, 4], F32)
                    impn = fw.sb("impn", [128, 4, 64], F32)
                    imp = fw.sb("imp", [128, 64], F32)
                    impa = fw.sb("impa", [128, 64], F32)
                    wk = fw.sb("wk", [128, 64], F32)
                    m8 = fw.sb("m8", [128, 8], F32)
                    nsel = fw.sb("nsel", [128, 64], BF16)
                    nsel2 = fw.sb("nsel2", [128, 2, 64], BF16)
                    qzs = [[fw.sb("qzs%d_%d" % (g_, k_), [128, 4, 128], BF16) for k_ in range(2)] for g_ in range(2)]
                    qz = [[fw.sb("qz%d_%d" % (g_, k_), [128, 4, 128], BF16) for k_ in range(2)] for g_ in range(2)]
                    for g_ in range(2):
                        for k_ in range(2):
                            op("pool", lambda: G.memset(qz[g_][k_][:], 0.0), writes=[qz[g_][k_]])
                    o1 = fw.sb("o1", [128, 4, 64], F32); o2 = fw.sb("o2", [128, 4, 64], F32)
                    xres = [fw.sb("xres%d" % i, [128, D], F32, dma=True) for i in range(2)]
                    if debug:
                        ond = fw.sb("ond", [128, 512], F32, dma=True)
                    bc4 = lambda ap: ap.unsqueeze(1).to_broadcast([ap.shape[0], 4, 128])

                    qzt_cur = [None]

                    def emit_S(job):
                        if job.get("pre"):
                            job["pre"]()
                        pS = psS[job["k"] % 3]
                        pS4 = pS[:].rearrange("p (r t) -> p r t", r=4)
                        extra = job["extra"]
                        op("pe", lambda: PE.matmul(pS4, lhsT=job["kT_ap"], rhs=job["qs"], start=True, stop=(len(extra) == 0)),
                           reads=[job["kT_t"], job["qzt"]], writes=[pS])
                        for xi, (la, ra, rd) in enumerate(extra):
                            op("pe", lambda: PE.matmul(pS4, lhsT=la, rhs=ra, start=False, stop=(xi == len(extra) - 1)),
                               reads=rd, writes=[pS])

                    def emit_EXP_PV(job):
                        pS = psS[job["k"] % 3]; e = ebuf[job["k"] % 4]
                        op("act", lambda: A.activation(out=e[:].rearrange("p r t -> p (r t)"), in_=pS[:], func=AF.Exp, scale=0.125),
                           reads=[pS], writes=[e])
                        for r in range(4):
                            for (va, po_fn, pot) in job["pvs"]:
                                op("pe", lambda: PE.matmul(po_fn(r), lhsT=e[:, r, :], rhs=va, start=(job["first"] and r == 0), stop=job["last"],
                                                           skip_group_check=True), reads=[e, job["vaug_t"]], writes=[pot])
                        if job.get("post"):
                            job["post"]()

                    def run_jobs(jobs):
                        n = len(jobs)
                        for idx, job in enumerate(jobs):
                            job["k"] = ecount[0] + idx
                        ecount[0] += n
                        nextS = 0
                        for k in range(n):
                            while nextS < n and nextS <= k + 2:
                                jb = jobs[nextS]
                                if jb.get("pre") and jb.get("needs", -1) > k - 1:
                                    break
                                emit_S(jb); nextS += 1
                            assert nextS > k
                            emit_EXP_PV(jobs[k])
                            for _ in range(3):
                                if deferred:
                                    deferred.pop(0)()

                    deferred = []

                    def topk_dve(i, g):
                        oc = ocS[g]; nsel2 = nsel2s[g]
                        op("dve", lambda: V.tensor_copy(out=oc[:], in_=o4(psOc)), reads=[psOc], writes=[oc])
                        op("dve", lambda: V.tensor_scalar_max(out=rs4[:], in0=oc[:, :, 64], scalar1=1e-30), reads=[oc], writes=[rs4])
                        op("dve", lambda: V.reciprocal(out=rs4[:], in_=rs4[:]), reads=[rs4], writes=[rs4])
                        tt("dve", impn[:], psImp[:, 0:256].rearrange("p (r j) -> p r j", j=64), rs4[:].unsqueeze(2).to_broadcast([128, 4, 64]),
                           ALU.mult, [psImp, rs4], [impn])
                        op("dve", lambda: V.tensor_reduce(out=imp[:], in_=impn[:].rearrange("p r j -> p j r"), axis=AX.X, op=ALU.add),
                           reads=[impn], writes=[imp])
                        so = 62 - 2 * i
                        tt("dve", impa[:], imp[:], TMm[:, so:so + 64], ALU.mult, [imp, TMm], [impa])
                        tt("dve", impa[:], impa[:], TB[:, so:so + 64], ALU.add, [impa, TB], [impa])
                        op("dve", lambda: V.memset(impa[:, 0:1], 1e4), reads=[impa], writes=[impa])
                        op("dve", lambda: V.max(out=m8[:], in_=impa[:]), reads=[impa], writes=[m8])
                        op("dve", lambda: V.match_replace(out=wk[:], in_to_replace=m8[:], in_values=impa[:], imm_value=-1e9),
                           reads=[m8, impa], writes=[wk])
                        op("dve", lambda: V.max(out=m8[:], in_=wk[:]), reads=[wk], writes=[m8])
                        op("dve", lambda: V.tensor_scalar(out=wk[:], in0=impa[:], scalar1=m8[:, 7:8], scalar2=-NEG, op0=ALU.is_ge, op1=ALU.mult),
                           reads=[impa, m8], writes=[wk])
                        op("dve", lambda: V.tensor_scalar_add(out=nsel2[:], in0=wk[:].unsqueeze(1).to_broadcast([128, 2, 64]), scalar1=NEG),
                           reads=[wk], writes=[nsel2])

                    def negT_pe(g, qzst):
                        og_ = slice((1 - g) * 64, (2 - g) * 64)
                        nsel2 = nsel2s[g]
                        op("pe", lambda: PE.transpose(psTt[:, 0, :], nsel2[:].rearrange("p a b -> p (a b)"), ident[:]), reads=[nsel2, ident], writes=[psTt])
                        op("dve", lambda: V.tensor_copy(out=qzst[og_, :, :], in_=psTt[og_, 0, :].unsqueeze(1).to_broadcast([64, 4, 128])),
                           reads=[psTt, qzst], writes=[qzst])

                    def tile_start(i):
                        xr = xres[i % 2]; cat = cats[i % 2]
                        fw.dma("sp", xr[:], x_d[b, i * 128:(i + 1) * 128, :], xr, writes=[xr])
                        fw.dma("sp", cat[:, 512:1024], ogm_d[b, i * 128:(i + 1) * 128, :], cat, reads=[dbuf("ogm", b, i)], writes=[cat])
                        for ii in ([0, 1] if i == 0 else [i + 1]):
                            if ii >= P3_TILES:
                                continue
                            for g_ in range(2):
                                gs_ = slice(g_ * 64, (g_ + 1) * 64)
                                a_ = qz[g_][ii % 2]; b_ = qzs[g_][ii % 2]
                                op("dve", lambda: V.tensor_copy(out=a_[gs_, :, :], in_=qT[gs_, :, ii * 128:(ii + 1) * 128]), reads=[qT, a_], writes=[a_])
                                op("pool", lambda: G.tensor_copy(out=b_[gs_, :, :], in_=qT[gs_, :, ii * 128:(ii + 1) * 128]), reads=[qT, b_], writes=[b_])

                    def combine(i, g):
                        oc = ocS[g]; cat = cats[i % 2]
                        op("dve", lambda: V.tensor_scalar_max(out=sm[:, 0, :], in0=oc[:, :, 64], scalar1=1e-30), reads=[oc, sm], writes=[sm])
                        for br, pso in ((1, psOs), (2, psOw)):
                            op("dve", lambda: V.tensor_scalar_max(out=sm[:, br, :], in0=o4(pso)[:, :, 64], scalar1=1e-30), reads=[pso, sm], writes=[sm])
                        op("dve", lambda: V.reciprocal(out=sm[:], in_=sm[:]), reads=[sm], writes=[sm])
                        tt("dve", ff[:], sm[:], gat[:, i, g * 12:(g + 1) * 12].rearrange("p (r b) -> p b r", b=3), ALU.mult, [sm, gat], [ff])
                        fb = lambda br: ff[:, br, :].unsqueeze(2).to_broadcast([128, 4, 64])
                        tt("dve", o2[:], o4(psOw)[:, :, 0:64], fb(2), ALU.mult, [psOw, ff], [o2])
                        tt("dve", o1[:], o4(psOs)[:, :, 0:64], fb(1), ALU.mult, [psOs, ff], [o1])
                        tt("dve", o1[:], o1[:], o2[:], ALU.add, [o1, o2], [o1])
                        tt("dve", o2[:], oc[:, :, 0:64], fb(0), ALU.mult, [oc, ff], [o2])
                        if debug:
                            tt("pool", ond[:, g * 256:(g + 1) * 256].rearrange("p (r d) -> p r d", d=64), o1[:], o2[:], ALU.add, [o1, o2], [ond])
                            if g == 1:
                                fw.dma("sp", dbg["onsa"][b, i * 128:(i + 1) * 128, :], ond[:], ond, reads=[ond], writes=[dbuf("onsa", b, i)])
                        tt("dve", cat[:, g * 256:(g + 1) * 256].rearrange("p (r d) -> p r d", d=64), o1[:], o2[:], ALU.add, [o1, o2], [cat])

                    def outproj(i):
                        while deferred:
                            deferred.pop(0)()
                        cat = cats[i % 2]; xr = xres[i % 2]
                        for c in range(8):
                            op("pe", lambda: PE.transpose(psTt[:, c, :], cat[:, c * 128:(c + 1) * 128], ident[:]), reads=[cat, ident], writes=[psTt])
                        op("act", lambda: A.copy(out=catT[:], in_=psTt[:]), reads=[psTt], writes=[catT])
                        halves = [(psOs, psOs[:]), (psOw, psOw[:])]
                        for half in range(2):
                            pyt, pya = halves[half]
                            for c in range(8):
                                op("pe", lambda: PE.matmul(pya, lhsT=catT[:, c, :], rhs=Wo0[:, c, half * 512:(half + 1) * 512],
                                                           start=(c == 0), stop=(c == 7)), reads=[catT, Wo0], writes=[pyt])
                        st_ = ep.steps(halves, xr, x1_d[b, i * 128:(i + 1) * 128, :], dbuf("x1", b, i))
                        for f_ in st_[:3]:
                            f_()
                        deferred.extend(st_[3:])

                    jobs = []
                    last_sel_g1 = -1
                    for i in range(P3_TILES):
                        def mk(kT_ap, kT_t, extra, pvs, vaug_t, first, last, qt):
                            return dict(kT_ap=kT_ap, kT_t=kT_t, qs=qt[:], qzt=qt, extra=extra, pvs=pvs, vaug_t=vaug_t, first=first, last=last)
                        chunks = [0] if i < 16 else [0, 1]
                        last_cmp = {}
                        for g in range(2):
                            qzt = qz[g][i % 2]
                            for ci, c in enumerate(chunks):
                                extra = []
                                if (c, i) in cmpmask:
                                    extra.append((ident[:], bc4(cm_all[:, cmpmask[(c, i)], :]), [ident, cm_all]))
                                jb = mk(kcmpT[:, c * 128:(c + 1) * 128], kcmpT, extra,
                                        [(vcx[:, c, g, 0:65], lambda r: o4(psOc)[:, r, :], psOc),
                                         (vcx[:, c, g, 65:129], lambda r: psImp[:, r * 64:(r + 1) * 64], psImp)],
                                        vcx, ci == 0, ci == len(chunks) - 1, qzt)
                                if g == 0 and ci == 0:
                                    jb["pre"] = (lambda i_=i: tile_start(i_)); jb["needs"] = -1
                                jobs.append(jb)
                            jobs[-1]["post"] = (lambda i_=i, g_=g: topk_dve(i_, g_))
                            last_cmp[g] = len(jobs) - 1
                        for g in range(2):
                            qzt = qz[g][i % 2]; qzst = qzs[g][i % 2]
                            j0 = max(0, i - 4)
                            for j in range(j0, i + 1):
                                extra = []
                                if j == i:
                                    extra.append((ident[:], bc4(McNeg[:]), [ident, McNeg]))
                                elif j == i - 4:
                                    extra.append((ident[:], bc4(MwNeg[:]), [ident, MwNeg]))
                                jb = mk(kwT[:, j * 128:(j + 1) * 128], kwT, extra,
                                        [(vwA[:, j, g, :], lambda r: o4(psOw)[:, r, :], psOw)], vwA, j == j0, j == i, qzt)
                                if g == 0 and j == j0 and i > 0:
                                    jb["pre"] = (lambda i_=i: outproj(i_ - 1)); jb["needs"] = last_sel_g1
                                jobs.append(jb)
                            for j in range(i + 1):
                                extra = []
                                if j == i:
                                    extra.append((ident[:], bc4(McNeg[:]), [ident, McNeg]))
                                jb = mk(ksE[g][:, j * 128:(j + 1) * 128], ksE[g], extra,
                                        [(vsA[:, j, g, :], lambda r: o4(psOs)[:, r, :], psOs)], vsA, j == 0, j == i, qzst)
                                if j == 0:
                                    jb["pre"] = (lambda g_=g, q_=qzst: negT_pe(g_, q_)); jb["needs"] = last_cmp[g]
                                jobs.append(jb)
                            jobs[-1]["post"] = (lambda i_=i, g_=g: combine(i_, g_))
                            if g == 1:
                                last_sel_g1 = len(jobs) - 1
                    run_jobs(jobs)
                    outproj(P3_TILES - 1)
                    while deferred:
                        deferred.pop(0)()
                fw.pop()
        fw.pop()
        if upto < 4:
            return nc

        def ffn_phase(ls, layer, src_d, src_name, dst_d, dst_name):
            fw.push()
            if True:
                Wi = fw.sb("Wi", [128, 8, 2 * FH], BF16, dma="sw")
                Wo = fw.sb("Wo", [128, 22, D], BF16, dma="sw")
                fw.dma_group("pool", [(Wi[:, k, :], fwi_d[layer, k * 128:(k + 1) * 128, :]) for k in range(8)], Wi, writes=[Wi])
                fw.dma_group("pool", [(Wo[:, k, :], fwo_d[layer, k * 128:(k + 1) * 128, :]) for k in range(22)], Wo, writes=[Wo])
                lng = fw.sb("lng", [128, D], F32, dma=True); lnb = fw.sb("lnb", [128, D], F32, dma=True)
                load_bc(lng, lng_d[ls]); load_bc(lnb, lnb_d[ls])
                sc1 = fw.sb("sc1", [128, D], F32, dma=True); sh = fw.sb("sh", [128, D], F32, dma=True); g1 = fw.sb("g1", [128, D], F32, dma=True)
                NX = 4
                xin = [fw.sb("xin%d" % i, [128, D], F32, dma=True) for i in range(NX)]
                hT = [fw.sb("hT%d" % i, [128, 8, 256], BF16) for i in range(2)]
                actT = fw.sb("actT", [128, 22, 256], BF16)
                sg = [fw.sb("sg%d" % i, [128, 256], BF16) for i in range(2)]
                hp = HPrep(sc1, sh, nhb=2, tmp_dt=BF16)
                ep = Epilogue(g1, lng, lnb)
                psGU = [fw.ps("psGU%d" % i, [128, 2, 256], F32) for i in range(2)]
                psYs = [fw.ps("psY%d" % i, [128, 2, 512], F32) for i in range(2)]
                tiles = [(b, mt) for b in range(NB) for mt in range(S // 256)]
                state = {}
                xc = [0]

                def prep_a(idx):
                    b, mt = tiles[idx]
                    if mt == 0:
                        load_mod(ls, b, [(sh, 0), (sc1, 1)])
                    hTm = hT[idx % 2]
                    xts = []; hbs = []
                    for sub in range(2):
                        i = mt * 2 + sub
                        xt = xin[xc[0] % NX]; xc[0] += 1
                        fw.dma("sp", xt[:], src_d[b, i * 128:(i + 1) * 128, :], xt, reads=[dbuf(src_name, b, i)], writes=[xt])
                        hbs.append(hp.run_a(xt))
                        xts.append(xt)
                    state[idx] = (hTm, xts, hbs)

                def prep_b(idx):
                    hTm, xts, hbs = state[idx]
                    for sub in range(2):
                        hp.run_b(hbs[sub], hTm, sub * 128)

                def up(idx, hcs):
                    hTm = state[idx][0]
                    for hc in hcs:
                        pg = psGU[hc % 2]; sgt = sg[hc % 2]
                        for k in range(8):
                            op("pe", lambda: PE.matmul(pg[:, 0, :], lhsT=Wi[:, k, hc * 128:(hc + 1) * 128], rhs=hTm[:, k, :],
                                                       start=(k == 0), stop=(k == 7)), reads=[Wi, hTm], writes=[pg])
                        for k in range(8):
                            op("pe", lambda: PE.matmul(pg[:, 1, :], lhsT=Wi[:, k, FH + hc * 128:FH + (hc + 1) * 128], rhs=hTm[:, k, :],
                                                       start=(k == 0), stop=(k == 7)), reads=[Wi, hTm], writes=[pg])
                        op("act", lambda: A.activation(out=sgt[:], in_=pg[:, 0, :], func=AF.Silu), reads=[pg], writes=[sgt])
                        tt("dve", actT[:, hc, :], pg[:, 1, :], sgt[:], ALU.mult, [pg, sgt], [actT])
                        for _ in range(2):
                            if deferred:
                                deferred.pop(0)()

                def down_ep(idx):
                    b, mt = tiles[idx]
                    xts = state.pop(idx)[1]
                    if mt == 0:
                        load_mod(ls, b, [(g1, 2)])
                    for sub in range(2):
                        i = mt * 2 + sub
                        psY = psYs[sub]
                        for half in range(2):
                            for hc in range(22):
                                op("pe", lambda: PE.matmul(psY[:, half, :], lhsT=actT[:, hc, sub * 128:(sub + 1) * 128],
                                                           rhs=Wo[:, hc, half * 512:(half + 1) * 512], start=(hc == 0), stop=(hc == 21)),
                                   reads=[actT, Wo], writes=[psY])
                        st_ = ep.steps([(psY, psY[:, 0, :]), (psY, psY[:, 1, :])], xts[sub], dst_d[b, i * 128:(i + 1) * 128, :], dbuf(dst_name, b, i))
                        nim = 3 if sub == 1 else len(st_)
                        for f_ in st_[:nim]:
                            f_()
                        deferred.extend(st_[nim:])

                deferred = []
                prep_a(0); prep_b(0)
                for idx in range(len(tiles)):
                    up(idx, range(0, 2))
                    if idx + 1 < len(tiles):
                        prep_a(idx + 1)
                    up(idx, range(2, 12))
                    if idx + 1 < len(tiles):
                        prep_b(idx + 1)
                    up(idx, range(12, 22))
                    while deferred:
                        deferred.pop(0)()
                    down_ep(idx)
                while deferred:
                    deferred.pop(0)()
            fw.pop()

        ffn_phase(1, 0, x1_d, "x1", x2_d, "x2")
        if upto < 5:
            return nc

        fw.push()
        if True:
            W1 = fw.sb("W1", [128, 8, 3 * D], BF16, dma="sw")
            Wo1 = fw.sb("Wo1", [128, 8, D], BF16, dma="sw")
            fw.dma_group("pool", [(W1[:, k, :], win1_d[k * 128:(k + 1) * 128, :]) for k in range(8)], W1, writes=[W1])
            fw.dma_group("pool", [(Wo1[:, k, :], wout1_d[k * 128:(k + 1) * 128, :]) for k in range(8)], Wo1, writes=[Wo1])
            cwf = fw.sb("cwf", [3, D], F32, dma=True)
            fw.dma("sp", cwf[:], cw_d, cwf, writes=[cwf])
            cw = fw.sb("cw", [128, 8, 3], F32)
            pscw = fw.ps("pscw", [128, 8, 3], F32)
            for cc in range(8):
                op("pe", lambda: PE.transpose(pscw[:, cc, :], cwf[:, cc * 128:(cc + 1) * 128], identf[0:3, 0:3]), reads=[cwf, identf], writes=[pscw])
            op("dve", lambda: V.tensor_copy(out=cw[:], in_=pscw[:]), reads=[pscw], writes=[cw])
            lng = fw.sb("lng", [128, D], F32, dma=True); lnb = fw.sb("lnb", [128, D], F32, dma=True)
            load_bc(lng, lng_d[2]); load_bc(lnb, lnb_d[2])
            sc1 = fw.sb("sc1", [128, D], F32, dma=True); sh = fw.sb("sh", [128, D], F32, dma=True); g1 = fw.sb("g1", [128, D], F32, dma=True)
            xin = [fw.sb("xin%d" % i, [128, D], F32, dma=True) for i in range(4)]
            hT = [fw.sb("hT%d" % i, [128, 8, 256], BF16) for i in range(2)]
            cz = fw.sb("cz", [128, 8, 258], F32)
            csb = [fw.sb("csb%d" % i, [128, 256], F32) for i in range(2)]
            y1 = [fw.sb("y1_%d" % i, [128, 256], F32) for i in range(2)]
            byT = fw.sb("byT", [128, 8, 256], BF16)
            hp = HPrep(sc1, sh)
            ep = Epilogue(g1, lng, lnb)
            psP = [fw.ps("psP%d" % i, [128, 4, 256], F32) for i in range(2)]
            psY = fw.ps("psY", [128, 2, 512], F32)
            tiles = [(b, mt) for b in range(NB) for mt in range(S // 256)]
            state = {}
            xc = [0]

            def prep(idx):
                b, mt = tiles[idx]
                if mt == 0:
                    load_mod(2, b, [(sh, 0), (sc1, 1)])
                hTm = hT[idx % 2]
                xts = []
                for sub in range(2):
                    i = mt * 2 + sub
                    xt = xin[xc[0] % 4]; xc[0] += 1
                    fw.dma("sp", xt[:], x2_d[b, i * 128:(i + 1) * 128, :], xt, reads=[dbuf("x2", b, i)], writes=[xt])
                    hp.run(xt, hTm, sub * 128)
                    xts.append(xt)
                state[idx] = (hTm, xts)

            def compute(idx, ccs):
                b, mt = tiles[idx]
                hTm, _ = state[idx]
                for cc in ccs:
                    if cc == 0:
                        if mt == 0:
                            op("pool", lambda: G.memset(cz[:, :, 0:2], 0.0), reads=[cz], writes=[cz])
                        else:
                            op("pool", lambda: G.tensor_copy(out=cz[:, :, 0:2], in_=cz[:, :, 256:258]), reads=[cz], writes=[cz])
                    pp = psP[cc % 2]; cs = csb[cc % 2]; yt = y1[cc % 2]
                    for part in range(3):
                        for k in range(8):
                            op("pe", lambda: PE.matmul(pp[:, part, :], lhsT=W1[:, k, part * D + cc * 128:part * D + (cc + 1) * 128],
                                                       rhs=hTm[:, k, :], start=(k == 0), stop=(k == 7)), reads=[W1, hTm], writes=[pp])
                    op("act", lambda: A.copy(out=cs[:], in_=pp[:, 1, :]), reads=[pp], writes=[cs])
                    tt("dve", cz[:, cc, 2:258], pp[:, 2, :], cs[:], ALU.mult, [pp, cs, cz], [cz])
                    op("dve", lambda: V.tensor_scalar(out=yt[:], in0=cz[:, cc, 0:256], scalar1=cw[:, cc, 0:1], scalar2=None, op0=ALU.mult),
                       reads=[cz, cw], writes=[yt])
                    op("dve", lambda: V.scalar_tensor_tensor(out=yt[:], in0=cz[:, cc, 1:257], scalar=cw[:, cc, 1:2], in1=yt[:],
                                                             op0=ALU.mult, op1=ALU.add), reads=[cz, cw, yt], writes=[yt])
                    op("dve", lambda: V.scalar_tensor_tensor(out=yt[:], in0=cz[:, cc, 2:258], scalar=cw[:, cc, 2:3], in1=yt[:],
                                                             op0=ALU.mult, op1=ALU.add), reads=[cz, cw, yt], writes=[yt])
                    tt("dve", byT[:, cc, :], pp[:, 0, :], yt[:], ALU.mult, [pp, yt], [byT])

            def out_ep(idx):
                b, mt = tiles[idx]
                _, xts = state.pop(idx)
                if mt == 0:
                    load_mod(2, b, [(g1, 2)])
                for sub in range(2):
                    i = mt * 2 + sub
                    for half in range(2):
                        for cc in range(8):
                            op("pe", lambda: PE.matmul(psY[:, half, :], lhsT=byT[:, cc, sub * 128:(sub + 1) * 128],
                                                       rhs=Wo1[:, cc, half * 512:(half + 1) * 512], start=(cc == 0), stop=(cc == 7)),
                               reads=[byT, Wo1], writes=[psY])
                    ep.run([(psY, psY[:, 0, :]), (psY, psY[:, 1, :])], xts[sub], x3_d[b, i * 128:(i + 1) * 128, :], dbuf("x3", b, i))

            prep(0)
            for idx in range(len(tiles)):
                compute(idx, range(0, 4))
                if idx + 1 < len(tiles):
                    prep(idx + 1)
                compute(idx, range(4, 8))
                out_ep(idx)
        fw.pop()
        if upto < 6:
            return nc

        ffn_phase(3, 1, x3_d, "x3", out_d, "out")
        fw.barrier()
    return nc


_CACHE = {}


def _prep_inputs(inputs):
    f = lambda a: np.ascontiguousarray(np.asarray(a, dtype=np.float32))
    common = {
        "ada_w": f(inputs["ada_w"]).reshape(4, D, 3 * D), "ada_b": f(inputs["ada_b"]).reshape(4, 3 * D),
        "ln_g": f(inputs["ln_g"]).reshape(4, D), "ln_b": f(inputs["ln_b"]).reshape(4, D),
        "even_w_in": f(inputs["even_w_in"])[0], "even_cmp_pos": f(inputs["even_cmp_pos"])[0],
        "even_cmp_w1": f(inputs["even_cmp_w1"])[0], "even_cmp_w2": f(inputs["even_cmp_w2"])[0],
        "even_gmlp_norm_g": f(inputs["even_gmlp_norm_g"])[0].reshape(512), "even_gmlp_ws": f(inputs["even_gmlp_ws"])[0],
        "even_gmlp_bs": f(inputs["even_gmlp_bs"])[0], "even_w_out": f(inputs["even_w_out"])[0],
        "odd_w_in": f(inputs["odd_w_in"])[0], "odd_conv_w": f(inputs["odd_conv_w"])[0], "odd_w_out": f(inputs["odd_w_out"])[0],
        "ffn_w_in": f(inputs["ffn_w_in"]), "ffn_w_out": f(inputs["ffn_w_out"]),
    }
    x = f(inputs["x"]); c = f(inputs["c"])
    maps = []
    for i in range(N_CORES):
        m = dict(common)
        m["x"] = x[NB * i:NB * (i + 1)]; m["c"] = c[NB * i:NB * (i + 1)]
        maps.append(m)
    return maps


def kernel(**inputs):
    if "nc" not in _CACHE:
        _CACHE["nc"] = build_program()
    nc = _CACHE["nc"]
    maps = _prep_inputs(inputs)
    res = run_bass_kernel_spmd(nc, maps, core_ids=list(range(N_CORES)))
    return np.concatenate([r["out"] for r in res.results], axis=0).astype(np.float32)
```

```python
import numpy as np
import concourse.bass as bass
import concourse.mybir as mybir
from concourse.bass_utils import run_bass_kernel_spmd
from contextlib import ExitStack

F32 = mybir.dt.float32; BF16 = mybir.dt.bfloat16; I32 = mybir.dt.int32
ALU = mybir.AluOpType; AF = mybir.ActivationFunctionType; AX = mybir.AxisListType

S = 4096; D = 1024; NB = 2; NT = S // 128
FH = 2816
NEG = -30000.0
ALPHA = 4.0 ** 0.25
EPS = 1e-5
N_CORES = 8
P3_TILES = NT
P1_MT = 8
SELMODE = 0
NB_RUN = NB
P3_PARTS = ('cmp', 'topk', 'sel', 'win', 'comb', 'out')


class Buf:
    __slots__ = ("name", "w", "r", "dsem")

    def __init__(self, name):
        self.name = name; self.w = {}; self.r = {}; self.dsem = None


class T:
    def __init__(self, h, buf):
        self.h = h; self.b = buf

    def __getitem__(self, k):
        return self.h[k]


class FW:
    def __init__(self, nc, ges, n_dsem=56):
        self.nc = nc; self.es = ges
        self.eng = {"pe": nc.tensor, "act": nc.scalar, "dve": nc.vector, "pool": nc.gpsimd, "sp": nc.sync}
        self.sems = {}; self.cnt = {}
        for e in ("pe", "act", "dve", "pool"):
            self.sems[e] = ges.enter_context(nc.semaphore("s_" + e)); self.cnt[e] = 0
        self.dpool = []; self.dpool_sw = []
        for i in range(n_dsem):
            k = "d%d" % i
            self.sems[k] = ges.enter_context(nc.semaphore("s_" + k)); self.cnt[k] = 0
            (self.dpool_sw if i < 10 else self.dpool).append(k)
        self.checked = []
        self.scopes = []
        self.seen = {e: {} for e in self.eng}
        self.uid = 0

    def push(self):
        st = ExitStack(); st.__enter__()
        self.scopes.append((self.es, self.checked, st))
        self.es = st; self.checked = []

    def pop(self):
        self.barrier()
        old_es, old_checked, st = self.scopes.pop()
        st.__exit__(None, None, None)
        for k in self.checked:
            (self.dpool_sw if int(k[1:]) < 10 else self.dpool).append(k)
        self.es = old_es; self.checked = old_checked

    def sb(self, name, shape, dt, dma=False):
        self.uid += 1
        h = self.es.enter_context(self.nc.sbuf_tensor("%s_%d" % (name, self.uid), shape, dt))
        b = Buf(name)
        if dma:
            b.dsem = (self.dpool_sw if dma == "sw" else self.dpool).pop(0); self.checked.append(b.dsem)
        return T(h, b)

    def ps(self, name, shape, dt):
        self.uid += 1
        h = self.es.enter_context(self.nc.psum_tensor("%s_%d" % (name, self.uid), shape, dt))
        return T(h, Buf(name))

    def _deps(self, reads, writes):
        deps = {}
        for b in reads:
            for k, v in b.w.items():
                if deps.get(k, 0) < v: deps[k] = v
        for b in writes:
            for k, v in b.w.items():
                if deps.get(k, 0) < v: deps[k] = v
            for k, v in b.r.items():
                if deps.get(k, 0) < v: deps[k] = v
        return deps

    def _wait(self, e, deps):
        seen = self.seen[e]; eng = self.eng[e]
        for k, v in deps.items():
            if k == "pe" and e == "pe":
                continue
            if seen.get(k, 0) < v:
                eng.wait_ge(self.sems[k], v); seen[k] = v

    def _record(self, reads, writes, key, val):
        for b in writes:
            b.w = {key: val}; b.r = {}
        for b in reads:
            if b.r.get(key, 0) < val: b.r[key] = val

    @staticmethod
    def _bl(ts):
        return [t.b if isinstance(t, T) else t for t in ts]

    def op(self, e, fn, reads=(), writes=()):
        reads = self._bl(reads); writes = self._bl(writes)
        self._wait(e, self._deps(reads, writes))
        ins = fn()
        self.cnt[e] += 1
        ins.then_inc(self.sems[e], 1)
        self._record(reads, writes, e, self.cnt[e])
        return ins

    def dma(self, q, out, in_, sbuf_t, reads=(), writes=()):
        reads = self._bl(reads); writes = self._bl(writes)
        sbb = sbuf_t.b if isinstance(sbuf_t, T) else sbuf_t
        assert sbb.dsem is not None, sbb.name
        self._wait(q, self._deps(reads, writes))
        ins = self.eng[q].dma_start(out=out, in_=in_)
        key = sbb.dsem
        self.cnt[key] += 16
        ins.then_inc(self.sems[key], 16)
        self._record(reads, writes, key, self.cnt[key])
        return ins

    def dma_group(self, q, pairs, sbuf_t, reads=(), writes=()):
        reads = self._bl(reads); writes = self._bl(writes)
        sbb = sbuf_t.b if isinstance(sbuf_t, T) else sbuf_t
        self._wait(q, self._deps(reads, writes))
        key = sbb.dsem
        base = self.cnt[key]
        for n_, (o, i_) in enumerate(pairs):
            if n_ >= 4 and n_ % 4 == 0:
                self.eng[q].wait_ge(self.sems[key], base + 16 * n_)
            ins = self.eng[q].dma_start(out=o, in_=i_)
            self.cnt[key] += 16
            ins.then_inc(self.sems[key], 16)
        self._record(reads, writes, key, self.cnt[key])

    def barrier(self):
        deps = {k: v for k, v in self.cnt.items() if v > 0}
        for e in self.eng:
            seen = self.seen[e]
            for k, v in deps.items():
                if seen.get(k, 0) < v:
                    self.eng[e].wait_ge(self.sems[k], v); seen[k] = v


def build_program(upto=99, debug=False):
    nc = bass.Bass("TRN2", target_bir_lowering=False)
    kS = "ExternalOutput" if debug else "Internal"

    def din(name, shape):
        return nc.dram_tensor(name, shape, F32, kind="ExternalInput").ap()

    x_d = din("x", [NB, S, D]); c_d = din("c", [NB, D])
    adaw_d = din("ada_w", [4, D, 3 * D]); adab_d = din("ada_b", [4, 3 * D])
    lng_d = din("ln_g", [4, D]); lnb_d = din("ln_b", [4, D])
    win0_d = din("even_w_in", [D, 2328]); pos_d = din("even_cmp_pos", [2, 32, 64])
    w1_d = din("even_cmp_w1", [2, 2048, 128]); w2_d = din("even_cmp_w2", [2, 128, 64])
    gng_d = din("even_gmlp_norm_g", [512]); gws_d = din("even_gmlp_ws", [8, 128, 128]); gbs_d = din("even_gmlp_bs", [8, 128])
    wout0_d = din("even_w_out", [D, D])
    win1_d = din("odd_w_in", [D, 3 * D]); cw_d = din("odd_conv_w", [3, D]); wout1_d = din("odd_w_out", [D, D])
    fwi_d = din("ffn_w_in", [2, D, 2 * FH]); fwo_d = din("ffn_w_out", [2, FH, D])
    out_d = nc.dram_tensor("out", [NB, S, D], F32, kind="ExternalOutput").ap()
    mod_d = nc.dram_tensor("mod_s", [4, NB, 3 * D], F32, kind=kS).ap()
    x1_d = nc.dram_tensor("x1_s", [NB, S, D], F32, kind=kS).ap()
    x2_d = nc.dram_tensor("x2_s", [NB, S, D], F32, kind=kS).ap()
    x3_d = nc.dram_tensor("x3_s", [NB, S, D], F32, kind=kS).ap()
    ogm_d = nc.dram_tensor("ogm_s", [NB, S, 512], BF16, kind=kS).ap()
    dbg = {}
    if debug:
        dbg["onsa"] = nc.dram_tensor("onsa_s", [NB, S, 512], F32, kind=kS).ap()
        dbg["kcc"] = nc.dram_tensor("kcc_s", [NB, 128, 256], F32, kind=kS).ap()

    dbufs = {}

    def dbuf(*key):
        if key not in dbufs:
            dbufs[key] = Buf(str(key))
        return dbufs[key]

    with ExitStack() as ges, nc.allow_non_contiguous_dma(reason="small param layouts"):
        fw = FW(nc, ges)
        op = fw.op
        V = nc.vector; A = nc.scalar; G = nc.gpsimd; PE = nc.tensor
        VENG = {"dve": V, "pool": G}

        ident = fw.sb("ident", [128, 128], BF16)
        identf = fw.sb("identf", [128, 128], F32)
        epsb = fw.sb("epsb", [128, 1], F32)
        op("pool", lambda: G.memset(identf[:], 1.0), writes=[identf])
        op("pool", lambda: G.affine_select(out=identf[:], in_=identf[:], pattern=[[1, 128]], compare_op=ALU.is_equal,
                                           fill=0.0, base=0, channel_multiplier=-1), reads=[identf], writes=[identf])
        op("pool", lambda: G.tensor_copy(out=ident[:], in_=identf[:]), reads=[identf], writes=[ident])
        op("pool", lambda: G.memset(epsb[:], EPS), writes=[epsb])
        mhalf = fw.sb("mhalf", [128, 8], F32)
        op("pool", lambda: G.memset(mhalf[:], -0.5), writes=[mhalf])

        def tt(e, out, in0, in1, o, rd, wr):
            return op(e, lambda: VENG[e].tensor_tensor(out=out, in0=in0, in1=in1, op=o), reads=rd, writes=wr)

        def load_bc(dst, src_1d, q="sp"):
            fw.dma(q, dst[:], src_1d.partition_broadcast(128), dst, writes=[dst])

        def load_w_bf16(dst, kidx, src2d):
            fw.dma("pool", dst[:, kidx, :], src2d, dst, writes=[dst])

        class HPrep:
            def __init__(self, sc1, sh, nhb=2, tmp_dt=F32):
                self.sc1 = sc1; self.sh = sh; self.nhb = nhb
                self.tmp = [fw.sb("hp_tmp%d" % i, [128, D], tmp_dt) for i in range(1)]
                self.hb = [fw.sb("hp_hb%d" % i, [128, D], BF16) for i in range(nhb)]
                self.psT = fw.ps("hp_psT", [128, 8, 128], BF16)
                self.n = 0

            def run_a(self, xt):
                tmp = self.tmp[0]; hb = self.hb[self.n % self.nhb]; self.n += 1
                tt("dve", tmp[:], xt[:], self.sc1[:], ALU.mult, [xt, self.sc1], [tmp])
                tt("pool", hb[:], tmp[:], self.sh[:], ALU.add, [tmp, self.sh], [hb])
                return hb

            def run_b(self, hb, hT, off):
                psT = self.psT
                for c in range(8):
                    op("pe", lambda: PE.transpose(psT[:, c, :], hb[:, c * 128:(c + 1) * 128], ident[:]),
                       reads=[hb, ident], writes=[psT])
                op("act", lambda: A.copy(out=hT[:, :, off:off + 128], in_=psT[:]), reads=[psT], writes=[hT])

            def run(self, xt, hT, off):
                self.run_b(self.run_a(xt), hT, off)

        class Epilogue:
            def __init__(self, g1, lng, lnb, nt=1):
                self.g1 = g1; self.lng = lng; self.lnb = lnb
                self.ts = [fw.sb("ep_t%d" % i, [128, D], F32) for i in range(nt)]
                self.xo = [fw.sb("ep_xo%d" % i, [128, D], F32, dma=True) for i in range(2)]
                self.sts = [fw.sb("ep_st%d" % i, [128, 2, 6], F32) for i in range(nt)]
                self.mvs = [fw.sb("ep_mv%d" % i, [128, 8], F32) for i in range(nt)]
                self.k = 0

            def steps(self, halves, xres, out_ap, out_db):
                t = self.ts[self.k % len(self.ts)]; st = self.sts[self.k % len(self.ts)]; mv = self.mvs[self.k % len(self.ts)]
                xo = self.xo[self.k % 2]; self.k += 1
                g1 = self.g1; lng = self.lng; lnb = self.lnb
                L = []
                for hi, (pt, pap) in enumerate(halves):
                    L.append(lambda hi=hi, pt=pt, pap=pap: tt("dve", t[:, hi * 512:(hi + 1) * 512], pap, g1[:, hi * 512:(hi + 1) * 512], ALU.mult, [pt, g1], [t]))
                L.append(lambda: op("dve", lambda: V.scalar_tensor_tensor(out=t[:], in0=xres[:], scalar=ALPHA, in1=t[:], op0=ALU.mult, op1=ALU.add),
                                    reads=[xres, t], writes=[t]))
                L.append(lambda: op("dve", lambda: V.bn_stats(out=st[:, 0, :], in_=t[:, 0:512]), reads=[t], writes=[st]))
                L.append(lambda: op("dve", lambda: V.bn_stats(out=st[:, 1, :], in_=t[:, 512:1024]), reads=[t, st], writes=[st]))
                L.append(lambda: op("dve", lambda: V.bn_aggr(out=mv[:, 0:2], in_=st[:].rearrange("p a b -> p (a b)")), reads=[st], writes=[mv]))
                L.append(lambda: op("dve", lambda: V.tensor_scalar_add(out=mv[:, 2:3], in0=mv[:, 1:2], scalar1=EPS), reads=[mv], writes=[mv]))
                L.append(lambda: op("pool", lambda: G.tensor_tensor(out=mv[:, 3:4], in0=mv[:, 2:3], in1=mhalf[:, 0:1], op=ALU.pow),
                                    reads=[mv, mhalf], writes=[mv]))
                L.append(lambda: op("dve", lambda: V.tensor_scalar(out=mv[:, 4:5], in0=mv[:, 0:1], scalar1=mv[:, 3:4], scalar2=-1.0,
                                                                   op0=ALU.mult, op1=ALU.mult), reads=[mv], writes=[mv]))
                L.append(lambda: op("act", lambda: A.activation(out=xo[:], in_=t[:], func=AF.Identity, bias=mv[:, 4:5], scale=mv[:, 3:4]),
                                    reads=[t, mv], writes=[xo]))
                L.append(lambda: tt("pool", xo[:], xo[:], lng[:], ALU.mult, [xo, lng], [xo]))
                L.append(lambda: tt("pool", xo[:], xo[:], lnb[:], ALU.add, [xo, lnb], [xo]))
                L.append(lambda: fw.dma("sp", out_ap, xo[:], xo, reads=[xo], writes=[out_db]))
                return L

            def run(self, halves, xres, out_ap, out_db):
                for f in self.steps(halves, xres, out_ap, out_db):
                    f()

        def load_mod(ls, b, want):
            for tile, which in want:
                load_bc(tile, mod_d[ls, b, which * D:(which + 1) * D])

        def transpose_small(dst_ap, dstT, src_ap, srcT, npart, pst, pst_ap):
            op("pe", lambda: PE.transpose(pst_ap, src_ap, identf[0:npart, 0:npart]), reads=[srcT, identf], writes=[pst])
            op("dve", lambda: V.tensor_copy(out=dst_ap, in_=pst_ap), reads=[pst], writes=[dstT])

        fw.push()
        if True:
            csb_ = fw.sb("csb", [NB, D], F32, dma=True)
            scs = fw.sb("scs", [NB, D], F32)
            scT = fw.sb("scT", [128, 8, NB], F32)
            stage = [fw.sb("adastage%d" % i, [128, 3 * D], F32, dma=True) for i in range(2)]
            adab = fw.sb("adab", [NB, 3 * D], F32, dma=True)
            msb = fw.sb("msb", [NB, 3 * D], F32, dma=True)
            psm = [fw.ps("psm%d" % i, [NB, 512], F32) for i in range(6)]
            pst = fw.ps("pst_p0", [128, 8, NB], F32)
            fw.dma("sp", csb_[:], c_d, csb_, writes=[csb_])
            op("act", lambda: A.activation(out=scs[:], in_=csb_[:], func=AF.Silu), reads=[csb_], writes=[scs])
            for k in range(8):
                op("pe", lambda: PE.transpose(pst[:, k, :], scs[:, k * 128:(k + 1) * 128], identf[0:NB, 0:NB]), reads=[scs, identf], writes=[pst])
            op("dve", lambda: V.tensor_copy(out=scT[:], in_=pst[:]), reads=[pst], writes=[scT])
            si = 0
            for ls in range(4):
                fw.dma("sp", adab[:], adab_d[ls].partition_broadcast(NB), adab, writes=[adab])
                for k in range(8):
                    stg = stage[si % 2]; si += 1
                    fw.dma("sp", stg[:], adaw_d[ls, k * 128:(k + 1) * 128, :], stg, writes=[stg])
                    for ct in range(6):
                        op("pe", lambda: PE.matmul(psm[ct][:], lhsT=scT[:, k, :], rhs=stg[:, ct * 512:(ct + 1) * 512],
                                                   start=(k == 0), stop=(k == 7)), reads=[scT, stg], writes=[psm[ct]])
                for ct in range(6):
                    tt("dve", msb[:, ct * 512:(ct + 1) * 512], psm[ct][:], adab[:, ct * 512:(ct + 1) * 512], ALU.add,
                       [psm[ct], adab], [msb])
                op("dve", lambda: V.tensor_scalar_add(out=msb[:, D:3 * D], in0=msb[:, D:3 * D], scalar1=1.0), reads=[msb], writes=[msb])
                fw.dma("sp", mod_d[ls], msb[:], msb, reads=[msb], writes=[dbuf("mod")])
        fw.pop()
        if upto < 1:
            return nc

        fw.push()
        if True:
            WsT = fw.sb("WsT", [128, 8, 128], BF16)
            bsT = fw.sb("bsT", [128, 8], F32)
            ngb = fw.sb("ngb", [128, 512], F32, dma=True)
            load_bc(ngb, gng_d)
            cbias = fw.sb("cbias", [128, 2], F32)
            w2 = fw.sb("cw2", [128, 2, 64], BF16, dma="sw")
            fw.dma("pool", w2[:], w2_d.rearrange("kv h d -> h kv d"), w2, writes=[w2])
            w2pad = fw.sb("cw2pad", [128, 2, 128], BF16)
            op("pool", lambda: G.memset(w2pad[:], 0.0), writes=[w2pad])
            for g in range(2):
                op("pool", lambda: G.tensor_copy(out=w2pad[:, g, g * 64:(g + 1) * 64], in_=w2[:, 0, :]), reads=[w2, w2pad], writes=[w2pad])
            McNeg = fw.sb("McNeg", [128, 128], BF16)
            MwNeg = fw.sb("MwNeg", [128, 128], BF16)
            cm_all = fw.sb("cm_all", [128, 33, 128], BF16)
            TMm = fw.sb("TMm", [128, 126], F32); TB = fw.sb("TB", [128, 126], F32)
            qT = fw.sb("qT", [128, 4, S], BF16)
            ksE = [fw.sb("ksE%d" % g_, [128, S], BF16) for g_ in range(2)]; kwT = fw.sb("kwT", [128, S], BF16)
            kcT = fw.sb("kcT", [128, S], BF16); vcT = fw.sb("vcT", [128, S], BF16)
            vsA = fw.sb("vsA", [128, NT, 2, 65], BF16); vwA = fw.sb("vwA", [128, NT, 2, 65], BF16)
            gat = fw.sb("gat", [128, NT, 24], F32)
            kcmpT = fw.sb("kcmpT", [128, 256], BF16)
            vcx = fw.sb("vcx", [128, 2, 2, 129], BF16)
            cmpmask = {}
            if P1_MT < 8:
                for tcache in (qT, ksE[0], ksE[1], kwT, kcT, vcT, gat):
                    op("pool", lambda: G.memset(tcache[:], 0.0), writes=[tcache])
                op("pool", lambda: G.memset(vsA[:], 0.0), writes=[vsA]); op("pool", lambda: G.memset(vwA[:], 0.0), writes=[vwA])
            fw.push()
            if True:
                w1t = fw.sb("cw1t", [64, 2, 32, 128], BF16, dma="sw")
                for kv in range(2):
                    fw.dma("pool", w1t[:, kv, :, :], w1_d[kv].rearrange("(l d) h -> d l h", d=64), w1t, writes=[w1t])
                posf = fw.sb("posf", [32, 2, 64], F32, dma=True)
                fw.dma("sp", posf[:], pos_d.rearrange("kv l d -> l kv d"), posf, writes=[posf])
                posT = fw.sb("posT", [64, 2, 32], BF16)
                pss = fw.ps("pss", [128, 64], F32)
                for kv in range(2):
                    transpose_small(posT[:, kv, :], posT, posf[:, kv, :], posf, 32, pss, pss[0:64, 0:32])
                psb = fw.ps("psb", [128, 2], F32)
                for kv in range(2):
                    for l in range(32):
                        op("pe", lambda: PE.matmul(psb[:, kv:kv + 1], lhsT=w1t[:, kv, l, :], rhs=posT[:, kv, l:l + 1],
                                                   start=(l == 0), stop=(l == 31)), reads=[w1t, posT], writes=[psb])
                op("dve", lambda: V.tensor_copy(out=cbias[:], in_=psb[:]), reads=[psb], writes=[cbias])
                gbsf = fw.sb("gbsf", [8, 128], F32, dma=True)
                fw.dma("sp", gbsf[:], gbs_d, gbsf, writes=[gbsf])
                transpose_small(bsT[:], bsT, gbsf[:], gbsf, 8, pss, pss[:, 0:8])
                wsf = fw.sb("wsf", [128, 8, 128], F32, dma=True)
                wsb = fw.sb("wsb", [128, 8, 128], BF16)
                fw.dma("sp", wsf[:], gws_d.rearrange("g t s -> t g s"), wsf, writes=[wsf])
                op("pool", lambda: G.affine_select(out=wsf[:], in_=wsf[:], pattern=[[0, 8], [-1, 128]], compare_op=ALU.is_ge,
                                                   fill=0.0, base=0, channel_multiplier=1), reads=[wsf], writes=[wsf])
                op("pool", lambda: G.tensor_copy(out=wsb[:], in_=wsf[:]), reads=[wsf], writes=[wsb])
                pst = fw.ps("pst0", [128, 8, 128], BF16)
                for gm in range(8):
                    op("pe", lambda: PE.transpose(pst[:, gm, :], wsb[:, gm, :], ident[:]), reads=[wsb, ident], writes=[pst])
                op("dve", lambda: V.tensor_copy(out=WsT[:], in_=pst[:]), reads=[pst], writes=[WsT])
                zf = fw.sb("zf", [128, 128], F32)
                mtmp = fw.sb("mtmp", [128, 128], F32)
                op("pool", lambda: G.memset(zf[:], 0.0), writes=[zf])
                op("pool", lambda: G.affine_select(out=mtmp[:], in_=zf[:], pattern=[[1, 128]], compare_op=ALU.is_ge, fill=NEG,
                                                   base=0, channel_multiplier=-1), reads=[zf], writes=[mtmp])
                op("pool", lambda: G.tensor_copy(out=McNeg[:], in_=mtmp[:]), reads=[mtmp], writes=[McNeg])
                op("pool", lambda: G.affine_select(out=mtmp[:], in_=zf[:], pattern=[[-1, 128]], compare_op=ALU.is_gt, fill=NEG,
                                                   base=0, channel_multiplier=1), reads=[zf], writes=[mtmp])
                op("pool", lambda: G.tensor_copy(out=MwNeg[:], in_=mtmp[:]), reads=[mtmp], writes=[MwNeg])
                mi = 0
                for c in range(2):
                    for i in (range(0, 17) if c == 0 else range(16, 32)):
                        op("pool", lambda: G.affine_select(out=mtmp[:], in_=zf[:], pattern=[[1, 128]], compare_op=ALU.is_ge, fill=NEG,
                                                           base=128 * i - 31 - 2048 * c, channel_multiplier=-16),
                           reads=[zf], writes=[mtmp])
                        op("pool", lambda: G.tensor_copy(out=cm_all[:, mi, :], in_=mtmp[:]), reads=[mtmp], writes=[cm_all])
                        cmpmask[(c, i)] = mi; mi += 1
                ebf = fw.sb("ebf", [64, S], F32)
                op("pool", lambda: G.memset(ebf[:], 1.0), writes=[ebf])
                op("pool", lambda: G.affine_select(out=ebf[:], in_=ebf[:], pattern=[[1, S]], compare_op=ALU.is_ge, fill=0.0,
                                                   base=0, channel_multiplier=-64), reads=[ebf], writes=[ebf])
                op("pool", lambda: G.affine_select(out=ebf[:], in_=ebf[:], pattern=[[-1, S]], compare_op=ALU.is_ge, fill=0.0,
                                                   base=63, channel_multiplier=64), reads=[ebf], writes=[ebf])
                op("pool", lambda: G.tensor_copy(out=ksE[1][0:64, :], in_=ebf[:]), reads=[ebf, ksE[1]], writes=[ksE[1]])
                ebf2 = fw.sb("ebf2", [128, S], F32)
                op("pool", lambda: G.memset(ebf2[:], 1.0), writes=[ebf2])
                op("pool", lambda: G.affine_select(out=ebf2[:], in_=ebf2[:], pattern=[[1, S]], compare_op=ALU.is_ge, fill=0.0,
                                                   base=4096, channel_multiplier=-64), reads=[ebf2], writes=[ebf2])
                op("pool", lambda: G.affine_select(out=ebf2[:], in_=ebf2[:], pattern=[[-1, S]], compare_op=ALU.is_ge, fill=0.0,
                                                   base=-4033, channel_multiplier=64), reads=[ebf2], writes=[ebf2])
                op("pool", lambda: G.tensor_copy(out=ksE[0][64:128, :], in_=ebf2[64:128, :]), reads=[ebf2, ksE[0]], writes=[ksE[0]])
                for hh in range(2):
                    hs = slice(hh * 64, (hh + 1) * 64)
                    op("pool", lambda: G.memset(TMm[hs, 0:61 + hh], 1.0), reads=[TMm], writes=[TMm])
                    op("pool", lambda: G.memset(TMm[hs, 61 + hh:126], 0.0), reads=[TMm], writes=[TMm])
                    op("pool", lambda: G.memset(TB[hs, 0:61 + hh], 0.0), reads=[TB], writes=[TB])
                    op("pool", lambda: G.memset(TB[hs, 61 + hh:63 + hh], 1e4), reads=[TB], writes=[TB])
                    op("pool", lambda: G.memset(TB[hs, 63 + hh:126], -1e4), reads=[TB], writes=[TB])
                op("pool", lambda: G.memset(vsA[:, :, :, 64:65], 1.0), writes=[vsA])
                op("pool", lambda: G.memset(vwA[:, :, :, 64:65], 1.0), writes=[vwA])
                op("pool", lambda: G.memset(vcx[:], 0.0), writes=[vcx])
                op("pool", lambda: G.memset(vcx[:, :, :, 64:65], 1.0), reads=[vcx], writes=[vcx])
                covf = fw.sb("covf", [128, 2, 64], F32)
                op("pool", lambda: G.memset(covf[:], 1.0), writes=[covf])
                op("pool", lambda: G.affine_select(out=covf[:], in_=covf[:], pattern=[[-128, 2], [4, 64]], compare_op=ALU.is_ge,
                                                   fill=0.0, base=3, channel_multiplier=-1), reads=[covf], writes=[covf])
                op("pool", lambda: G.affine_select(out=covf[:], in_=covf[:], pattern=[[128, 2], [-4, 64]], compare_op=ALU.is_ge,
                                                   fill=0.0, base=1, channel_multiplier=1), reads=[covf], writes=[covf])
                for g in range(2):
                    op("pool", lambda: G.tensor_copy(out=vcx[:, :, g, 65:129], in_=covf[:]), reads=[covf, vcx], writes=[vcx])
                op("pool", lambda: G.memset(kcmpT[:], 0.0), writes=[kcmpT])
            fw.pop()

            v3 = lambda ap: ap.rearrange("p (g d) -> p g d", d=64)
            b8 = lambda ap: ap.unsqueeze(2).to_broadcast([128, 8, 64])

            for b in range(NB_RUN):
                fw.push()
                if True:
                    W0 = fw.sb("W0", [128, 8, 2328], BF16, dma="sw")
                    Wq = fw.sb("Wq", [128, 8, 512], BF16, dma="sw")
                    fw.dma_group("pool", [(W0[:, k, :], win0_d[k * 128:(k + 1) * 128, :]) for k in range(8)], W0, writes=[W0])
                    fw.dma_group("pool", [(Wq[:, k, r * 128:(r + 1) * 128].rearrange("p (g d) -> p g d", g=2),
                                           win0_d[k * 128:(k + 1) * 128, 0:512].rearrange("p (g r d) -> p g r d", g=2, r=4)[:, :, r, :])
                                          for k in range(8) for r in range(4)], Wq, writes=[Wq])
                    sc1 = fw.sb("sc1", [128, D], F32, dma=True); sh = fw.sb("sh", [128, D], F32, dma=True)
                    load_mod(0, b, [(sh, 0), (sc1, 1)])
                    xin = [fw.sb("xin%d" % i, [128, D], F32, dma=True) for i in range(3)]
                    hT = [fw.sb("hT%d" % i, [128, 8, 512], BF16) for i in range(2)]
                    hp = HPrep(sc1, sh)
                    psF = [fw.ps("psF%d" % i, [128, 512], F32) for i in range(2)]
                    psA = fw.ps("psA", [128, 512], F32); psB = fw.ps("psB", [128, 512], F32)
                    psC = fw.ps("psC", [128, 512], F32); psM = fw.ps("psM", [128, 512], F32)
                    ug = fw.sb("ug", [128, 512], F32); vg = fw.sb("vg", [128, 512], F32)
                    sq = fw.sb("sq", [128, 512], F32)
                    gst = fw.sb("gst", [128, 6, 8], F32)
                    vn = fw.sb("vn", [128, 512], BF16)
                    og = [fw.sb("og%d" % i, [128, 512], BF16, dma=True) for i in range(2)]
                    xcount = [0]
                    pend = {}

                    def p1_prep_a(mt_, sub_):
                        if mt_ >= P1_MT:
                            return
                        i_ = mt_ * 4 + sub_
                        xt = xin[xcount[0] % 3]; xcount[0] += 1
                        fw.dma("sp", xt[:], x_d[b, i_ * 128:(i_ + 1) * 128, :], xt, writes=[xt])
                        pend[(mt_, sub_)] = hp.run_a(xt)

                    def p1_prep_b(mt_, sub_):
                        if mt_ >= P1_MT:
                            return
                        hp.run_b(pend.pop((mt_, sub_)), hT[mt_ % 2], sub_ * 128)

                    for sub in range(4):
                        p1_prep_a(0, sub); p1_prep_b(0, sub)
                    for mt in range(P1_MT):
                        hTm = hT[mt % 2]
                        fm = []
                        for r in range(4):
                            fm.append((Wq[:, :, r * 128:(r + 1) * 128], qT[:, r, mt * 512:(mt + 1) * 512], qT))
                        fm.append((W0[:, :, 512:640], kcT[:, mt * 512:(mt + 1) * 512], kcT))
                        fm.append((W0[:, :, 640:768], vcT[:, mt * 512:(mt + 1) * 512], vcT))
                        fm.append((W0[:, :, 768:896], None, "ks"))
                        fm.append((W0[:, :, 1024:1152], kwT[:, mt * 512:(mt + 1) * 512], kwT))
                        for fi, (wap, dst, dstT) in enumerate(fm):
                            pf = psF[fi % 2]
                            for k in range(8):
                                op("pe", lambda: PE.matmul(pf[:], lhsT=wap[:, k], rhs=hTm[:, k, :], start=(k == 0), stop=(k == 7)),
                                   reads=[W0, Wq, hTm], writes=[pf])
                            if dstT == "ks":
                                op("act", lambda: A.copy(out=ksE[0][0:64, mt * 512:(mt + 1) * 512], in_=pf[0:64, :]), reads=[pf, ksE[0]], writes=[ksE[0]])
                                op("dve", lambda: V.tensor_copy(out=ksE[1][64:128, mt * 512:(mt + 1) * 512], in_=pf[64:128, :]), reads=[pf, ksE[1]], writes=[ksE[1]])
                            elif fi % 2 == 0:
                                op("act", lambda: A.copy(out=dst, in_=pf[:]), reads=[pf], writes=[dstT])
                            else:
                                op("dve", lambda: V.tensor_copy(out=dst, in_=pf[:]), reads=[pf], writes=[dstT])
                        for sub in range(4):
                            i = mt * 4 + sub
                            lh = lambda k: hTm[:, k, sub * 128:(sub + 1) * 128]
                            for k in range(8):
                                op("pe", lambda: PE.matmul(psA[:, 0:408], lhsT=lh(k), rhs=W0[:, k, 896:1304], start=(k == 0), stop=(k == 7)),
                                   reads=[W0, hTm], writes=[psA])
                            for k in range(8):
                                op("pe", lambda: PE.matmul(psB[:], lhsT=lh(k), rhs=W0[:, k, 1304:1816], start=(k == 0), stop=(k == 7)),
                                   reads=[W0, hTm], writes=[psB])
                            for k in range(8):
                                op("pe", lambda: PE.matmul(psC[:], lhsT=lh(k), rhs=W0[:, k, 1816:2328], start=(k == 0), stop=(k == 7)),
                                   reads=[W0, hTm], writes=[psC])
                            p1_prep_a(mt + 1, sub)
                            op("dve", lambda: V.tensor_copy(out=vsA[:, i, :, 0:64], in_=v3(psA[:, 0:128])), reads=[psA], writes=[vsA])
                            op("dve", lambda: V.tensor_copy(out=vwA[:, i, :, 0:64], in_=v3(psA[:, 256:384])), reads=[psA], writes=[vwA])
                            op("dve", lambda: V.tensor_copy(out=gat[:, i, :], in_=psA[:, 384:408]), reads=[psA], writes=[gat])
                            op("act", lambda: A.activation(out=ug[:], in_=psB[:], func=AF.Gelu_apprx_tanh), reads=[psB], writes=[ug])
                            op("act", lambda: A.activation(out=vg[:], in_=psC[:], func=AF.Gelu_apprx_tanh), reads=[psC], writes=[vg])
                            op("dve", lambda: V.tensor_reduce(out=gst[:, 0, :], in_=v3(vg[:]), axis=AX.X, op=ALU.add), reads=[vg], writes=[gst])
                            tt("dve", sq[:], vg[:], vg[:], ALU.mult, [vg], [sq])
                            op("dve", lambda: V.tensor_reduce(out=gst[:, 1, :], in_=v3(sq[:]), axis=AX.X, op=ALU.add), reads=[sq, gst], writes=[gst])
                            op("dve", lambda: V.tensor_scalar(out=gst[:, 2, :], in0=gst[:, 0, :], scalar1=1.0 / 64, scalar2=None, op0=ALU.mult),
                               reads=[gst], writes=[gst])
                            tt("dve", gst[:, 5, :], gst[:, 2, :], gst[:, 2, :], ALU.mult, [gst], [gst])
                            op("dve", lambda: V.scalar_tensor_tensor(out=gst[:, 3, :], in0=gst[:, 1, :], scalar=1.0 / 64, in1=gst[:, 5, :],
                                                                     op0=ALU.mult, op1=ALU.subtract), reads=[gst], writes=[gst])
                            op("dve", lambda: V.tensor_scalar_add(out=gst[:, 3, :], in0=gst[:, 3, :], scalar1=EPS), reads=[gst], writes=[gst])
                            op("pool", lambda: G.tensor_tensor(out=gst[:, 4, :], in0=gst[:, 3, :], in1=mhalf[:], op=ALU.pow),
                               reads=[gst, mhalf], writes=[gst])
                            tt("dve", v3(sq[:]), v3(vg[:]), b8(gst[:, 2, :]), ALU.subtract, [vg, gst], [sq])
                            tt("dve", v3(sq[:]), v3(sq[:]), b8(gst[:, 4, :]), ALU.mult, [sq, gst], [sq])
                            tt("dve", vn[:], sq[:], ngb[:], ALU.mult, [sq, ngb], [vn])
                            for gm in range(8):
                                op("pe", lambda: PE.matmul(psM[:, gm * 64:(gm + 1) * 64], lhsT=WsT[:, gm, :], rhs=vn[:, gm * 64:(gm + 1) * 64],
                                                           start=True, stop=True), reads=[WsT, vn], writes=[psM])
                            p1_prep_b(mt + 1, sub)
                            tt("dve", v3(sq[:]), v3(psM[:]), b8(bsT[:]), ALU.add, [psM, bsT], [sq])
                            ogt = og[i % 2]
                            tt("dve", ogt[:], sq[:], ug[:], ALU.mult, [sq, ug], [ogt])
                            fw.dma("sp", ogm_d[b, i * 128:(i + 1) * 128, :], ogt[:], ogt, reads=[ogt], writes=[dbuf("ogm", b, i)])
                fw.pop()
                if upto < 2:
                    continue
                fw.push()
                if True:
                    w1 = fw.sb("cw1", [128, 2, 32, 128], BF16, dma="sw")
                    for hh in range(2):
                        for kv in range(2):
                            fw.dma("pool", w1[hh * 64:(hh + 1) * 64, kv, :, :], w1_d[kv].rearrange("(l d) h -> d l h", d=64), w1, writes=[w1])
                    psH = [fw.ps("psH%d" % i, [128, 256], F32) for i in range(2)]
                    psK = fw.ps("psK", [128, 256], F32); psV = fw.ps("psV", [128, 256], F32)
                    for kv in range(2):
                        src = kcT if kv == 0 else vcT
                        hbs = []
                        for g in range(2):
                            ph = psH[g]
                            for l in range(32):
                                op("pe", lambda: PE.matmul(ph[:, 0:255], lhsT=w1[g * 64:(g + 1) * 64, kv, l, :],
                                                           rhs=src[g * 64:(g + 1) * 64, l:l + 16 * 254 + 1:16], start=(l == 0), stop=(l == 31)),
                                   reads=[w1, src], writes=[ph])
                            hb_g = fw.sb("hbg_%d%d" % (kv, g), [128, 256], BF16)
                            op("pool", lambda: G.memset(hb_g[:], 0.0), writes=[hb_g])
                            op("act", lambda: A.activation(out=hb_g[:, 0:255], in_=ph[:, 0:255], func=AF.Gelu_apprx_tanh,
                                                           bias=cbias[:, kv:kv + 1], scale=1.0), reads=[ph, cbias, hb_g], writes=[hb_g])
                            hbs.append(hb_g)
                        if kv == 0:
                            for g in range(2):
                                op("pe", lambda: PE.matmul(psK[:], lhsT=w2pad[:, g, :], rhs=hbs[g][:], start=(g == 0), stop=(g == 1)),
                                   reads=[w2pad, hbs[g]], writes=[psK])
                            op("dve", lambda: V.tensor_copy(out=kcmpT[:], in_=psK[:]), reads=[psK], writes=[kcmpT])
                            if debug:
                                kd = fw.sb("kd", [128, 256], F32, dma=True)
                                op("dve", lambda: V.tensor_copy(out=kd[:], in_=psK[:]), reads=[psK], writes=[kd])
                                fw.dma("sp", dbg["kcc"][b], kd[:], kd, reads=[kd], writes=[dbuf("kcc", b)])
                        else:
                            for g in range(2):
                                for c in range(2):
                                    op("pe", lambda: PE.matmul(psV[:, (g * 2 + c) * 64:(g * 2 + c + 1) * 64], lhsT=hbs[g][:, c * 128:(c + 1) * 128],
                                                               rhs=w2[:, 1, :], start=True, stop=True), reads=[w2, hbs[g]], writes=[psV])
                            for g in range(2):
                                op("dve", lambda: V.tensor_copy(out=vcx[:, :, g, 0:64],
                                                                in_=psV[:, g * 128:(g + 1) * 128].rearrange("p (c d) -> p c d", d=64)),
                                   reads=[psV], writes=[vcx])
                fw.pop()
                if upto < 3:
                    continue
                fw.push()
                if True:
                    Wo0 = fw.sb("Wo0", [128, 8, D], BF16, dma="sw")
                    fw.dma_group("pool", [(Wo0[:, k, :], wout0_d[k * 128:(k + 1) * 128, :]) for k in range(8)], Wo0, writes=[Wo0])
                    lng = fw.sb("lng", [128, D], F32, dma=True); lnb = fw.sb("lnb", [128, D], F32, dma=True)
                    load_bc(lng, lng_d[0]); load_bc(lnb, lnb_d[0])
                    g1 = fw.sb("g1", [128, D], F32, dma=True)
                    load_mod(0, b, [(g1, 2)])
                    ep = Epilogue(g1, lng, lnb)
                    op("act", lambda: A.activation(out=gat[:].rearrange("p a b -> p (a b)"), in_=gat[:].rearrange("p a b -> p (a b)"), func=AF.Sigmoid),
                       reads=[gat], writes=[gat])
                    psS = [fw.ps("psS%d" % i, [128, 512], F32) for i in range(3)]
                    psTt = fw.ps("psTt", [128, 8, 128], BF16)
                    psOc = fw.ps("psOc", [128, 512], F32); psImp = fw.ps("psImp", [128, 512], F32)
                    psOs = fw.ps("psOs", [128, 512], F32); psOw = fw.ps("psOw", [128, 512], F32)
                    o4 = lambda ps_: ps_[:, 0:260].rearrange("p (r e) -> p r e", e=65)
                    ebuf = [fw.sb("ebuf%d" % i, [128, 4, 128], BF16) for i in range(4)]
                    ecount = [0]
                    cats = [fw.sb("cat%d" % i, [128, D], BF16, dma=True) for i in range(2)]
                    ocS = [fw.sb("ocS%d" % i, [128, 4, 65], F32) for i in range(2)]
                    nsel2s = [fw.sb("nsel2_%d" % i, [128, 2, 64], BF16) for i in range(2)]
                    catT = fw.sb("catT", [128, 8, 128], BF16)
                    sm = fw.sb("sm", [128, 3, 4], F32)
                    ff = fw.sb("ff", [128, 3, 4], F32)
                    rs4 = fw.sb("rs4", [128, 4], F32)
                    impn = fw.sb("impn", [128, 4, 64], F32)
                    imp = fw.sb("imp", [128, 64], F32)
                    impa = fw.sb("impa", [128, 64], F32)
                    wk = fw.sb("wk", [128, 64], F32)
                    m8 = fw.sb("m8", [128, 8], F32)
                    nsel = fw.sb("nsel", [128, 64], BF16)
                    nsel2 = fw.sb("nsel2", [128, 2, 64], BF16)
                    qzs = [[fw.sb("qzs%d_%d" % (g_, k_), [128, 4, 128], BF16) for k_ in range(2)] for g_ in range(2)]
                    qz = [[fw.sb("qz%d_%d" % (g_, k_), [128, 4, 128], BF16) for k_ in range(2)] for g_ in range(2)]
                    for g_ in range(2):
                        for k_ in range(2):
                            op("pool", lambda: G.memset(qz[g_][k_][:], 0.0), writes=[qz[g_][k_]])
                    o1 = fw.sb("o1", [128, 4, 64], F32); o2 = fw.sb("o2", [128, 4, 64], F32)
                    xres = [fw.sb("xres%d" % i, [128, D], F32, dma=True) for i in range(2)]
                    if debug:
                        ond = fw.sb("ond", [128, 512], F32, dma=True)
                    bc4 = lambda ap: ap.unsqueeze(1).to_broadcast([ap.shape[0], 4, 128])

                    qzt_cur = [None]

                    def emit_S(job):
                        if job.get("pre"):
                            job["pre"]()
                        pS = psS[job["k"] % 3]
                        pS4 = pS[:].rearrange("p (r t) -> p r t", r=4)
                        extra = job["extra"]
                        op("pe", lambda: PE.matmul(pS4, lhsT=job["kT_ap"], rhs=job["qs"], start=True, stop=(len(extra) == 0)),
                           reads=[job["kT_t"], job["qzt"]], writes=[pS])
                        for xi, (la, ra, rd) in enumerate(extra):
                            op("pe", lambda: PE.matmul(pS4, lhsT=la, rhs=ra, start=False, stop=(xi == len(extra) - 1)),
                               reads=rd, writes=[pS])

                    def emit_EXP_PV(job):
                        pS = psS[job["k"] % 3]; e = ebuf[job["k"] % 4]
                        op("act", lambda: A.activation(out=e[:].rearrange("p r t -> p (r t)"), in_=pS[:], func=AF.Exp, scale=0.125),
                           reads=[pS], writes=[e])
                        for r in range(4):
                            for (va, po_fn, pot) in job["pvs"]:
                                op("pe", lambda: PE.matmul(po_fn(r), lhsT=e[:, r, :], rhs=va, start=(job["first"] and r == 0), stop=job["last"],
                                                           skip_group_check=True), reads=[e, job["vaug_t"]], writes=[pot])
                        if job.get("post"):
                            job["post"]()

                    def run_jobs(jobs):
                        n = len(jobs)
                        for idx, job in enumerate(jobs):
                            job["k"] = ecount[0] + idx
                        ecount[0] += n
                        nextS = 0
                        for k in range(n):
                            while nextS < n and nextS <= k + 2:
                                jb = jobs[nextS]
                                if jb.get("pre") and jb.get("needs", -1) > k - 1:
                                    break
                                emit_S(jb); nextS += 1
                            assert nextS > k
                            emit_EXP_PV(jobs[k])
                            for _ in range(3):
                                if deferred:
                                    deferred.pop(0)()

                    deferred = []

                    def topk_dve(i, g):
                        oc = ocS[g]; nsel2 = nsel2s[g]
                        op("dve", lambda: V.tensor_copy(out=oc[:], in_=o4(psOc)), reads=[psOc], writes=[oc])
                        op("dve", lambda: V.tensor_scalar_max(out=rs4[:], in0=oc[:, :, 64], scalar1=1e-30), reads=[oc], writes=[rs4])
                        op("dve", lambda: V.reciprocal(out=rs4[:], in_=rs4[:]), reads=[rs4], writes=[rs4])
                        tt("dve", impn[:], psImp[:, 0:256].rearrange("p (r j) -> p r j", j=64), rs4[:].unsqueeze(2).to_broadcast([128, 4, 64]),
                           ALU.mult, [psImp, rs4], [impn])
                        op("dve", lambda: V.tensor_reduce(out=imp[:], in_=impn[:].rearrange("p r j -> p j r"), axis=AX.X, op=ALU.add),
                           reads=[impn], writes=[imp])
                        so = 62 - 2 * i
                        tt("dve", impa[:], imp[:], TMm[:, so:so + 64], ALU.mult, [imp, TMm], [impa])
                        tt("dve", impa[:], impa[:], TB[:, so:so + 64], ALU.add, [impa, TB], [impa])
                        op("dve", lambda: V.memset(impa[:, 0:1], 1e4), reads=[impa], writes=[impa])
                        op("dve", lambda: V.max(out=m8[:], in_=impa[:]), reads=[impa], writes=[m8])
                        op("dve", lambda: V.match_replace(out=wk[:], in_to_replace=m8[:], in_values=impa[:], imm_value=-1e9),
                           reads=[m8, impa], writes=[wk])
                        op("dve", lambda: V.max(out=m8[:], in_=wk[:]), reads=[wk], writes=[m8])
                        op("dve", lambda: V.tensor_scalar(out=wk[:], in0=impa[:], scalar1=m8[:, 7:8], scalar2=-NEG, op0=ALU.is_ge, op1=ALU.mult),
                           reads=[impa, m8], writes=[wk])
                        op("dve", lambda: V.tensor_scalar_add(out=nsel2[:], in0=wk[:].unsqueeze(1).to_broadcast([128, 2, 64]), scalar1=NEG),
                           reads=[wk], writes=[nsel2])

                    def negT_pe(g, qzst):
                        og_ = slice((1 - g) * 64, (2 - g) * 64)
                        nsel2 = nsel2s[g]
                        op("pe", lambda: PE.transpose(psTt[:, 0, :], nsel2[:].rearrange("p a b -> p (a b)"), ident[:]), reads=[nsel2, ident], writes=[psTt])
                        op("dve", lambda: V.tensor_copy(out=qzst[og_, :, :], in_=psTt[og_, 0, :].unsqueeze(1).to_broadcast([64, 4, 128])),
                           reads=[psTt, qzst], writes=[qzst])

                    def tile_start(i):
                        xr = xres[i % 2]; cat = cats[i % 2]
                        fw.dma("sp", xr[:], x_d[b, i * 128:(i + 1) * 128, :], xr, writes=[xr])
                        fw.dma("sp", cat[:, 512:1024], ogm_d[b, i * 128:(i + 1) * 128, :], cat, reads=[dbuf("ogm", b, i)], writes=[cat])
                        for ii in ([0, 1] if i == 0 else [i + 1]):
                            if ii >= P3_TILES:
                                continue
                            for g_ in range(2):
                                gs_ = slice(g_ * 64, (g_ + 1) * 64)
                                a_ = qz[g_][ii % 2]; b_ = qzs[g_][ii % 2]
                                op("dve", lambda: V.tensor_copy(out=a_[gs_, :, :], in_=qT[gs_, :, ii * 128:(ii + 1) * 128]), reads=[qT, a_], writes=[a_])
                                op("pool", lambda: G.tensor_copy(out=b_[gs_, :, :], in_=qT[gs_, :, ii * 128:(ii + 1) * 128]), reads=[qT, b_], writes=[b_])

                    def combine(i, g):
                        oc = ocS[g]; cat = cats[i % 2]
                        op("dve", lambda: V.tensor_scalar_max(out=sm[:, 0, :], in0=oc[:, :, 64], scalar1=1e-30), reads=[oc, sm], writes=[sm])
                        for br, pso in ((1, psOs), (2, psOw)):
                            op("dve", lambda: V.tensor_scalar_max(out=sm[:, br, :], in0=o4(pso)[:, :, 64], scalar1=1e-30), reads=[pso, sm], writes=[sm])
                        op("dve", lambda: V.reciprocal(out=sm[:], in_=sm[:]), reads=[sm], writes=[sm])
                        tt("dve", ff[:], sm[:], gat[:, i, g * 12:(g + 1) * 12].rearrange("p (r b) -> p b r", b=3), ALU.mult, [sm, gat], [ff])
                        fb = lambda br: ff[:, br, :].unsqueeze(2).to_broadcast([128, 4, 64])
                        tt("dve", o2[:], o4(psOw)[:, :, 0:64], fb(2), ALU.mult, [psOw, ff], [o2])
                        tt("dve", o1[:], o4(psOs)[:, :, 0:64], fb(1), ALU.mult, [psOs, ff], [o1])
                        tt("dve", o1[:], o1[:], o2[:], ALU.add, [o1, o2], [o1])
                        tt("dve", o2[:], oc[:, :, 0:64], fb(0), ALU.mult, [oc, ff], [o2])
                        if debug:
                            tt("pool", ond[:, g * 256:(g + 1) * 256].rearrange("p (r d) -> p r d", d=64), o1[:], o2[:], ALU.add, [o1, o2], [ond])
                            if g == 1:
                                fw.dma("sp", dbg["onsa"][b, i * 128:(i + 1) * 128, :], ond[:], ond, reads=[ond], writes=[dbuf("onsa", b, i)])
                        tt("dve", cat[:, g * 256:(g + 1) * 256].rearrange("p (r d) -> p r d", d=64), o1[:], o2[:], ALU.add, [o1, o2], [cat])

                    def outproj(i):
                        while deferred:
                            deferred.pop(0)()
                        cat = cats[i % 2]; xr = xres[i % 2]
                        for c in range(8):
                            op("pe", lambda: PE.transpose(psTt[:, c, :], cat[:, c * 128:(c + 1) * 128], ident[:]), reads=[cat, ident], writes=[psTt])
                        op("act", lambda: A.copy(out=catT[:], in_=psTt[:]), reads=[psTt], writes=[catT])
                        halves = [(psOs, psOs[:]), (psOw, psOw[:])]
                        for half in range(2):
                            pyt, pya = halves[half]
                            for c in range(8):
                                op("pe", lambda: PE.matmul(pya, lhsT=catT[:, c, :], rhs=Wo0[:, c, half * 512:(half + 1) * 512],
                                                           start=(c == 0), stop=(c == 7)), reads=[catT, Wo0], writes=[pyt])
                        st_ = ep.steps(halves, xr, x1_d[b, i * 128:(i + 1) * 128, :], dbuf("x1", b, i))
                        for f_ in st_[:3]:
                            f_()
                        deferred.extend(st_[3:])

                    jobs = []
                    last_sel_g1 = -1
                    for i in range(P3_TILES):
                        def mk(kT_ap, kT_t, extra, pvs, vaug_t, first, last, qt):
                            return dict(kT_ap=kT_ap, kT_t=kT_t, qs=qt[:], qzt=qt, extra=extra, pvs=pvs, vaug_t=vaug_t, first=first, last=last)
                        chunks = [0] if i < 16 else [0, 1]
                        last_cmp = {}
                        for g in range(2):
                            qzt = qz[g][i % 2]
                            for ci, c in enumerate(chunks):
                                extra = []
                                if (c, i) in cmpmask:
                                    extra.append((ident[:], bc4(cm_all[:, cmpmask[(c, i)], :]), [ident, cm_all]))
                                jb = mk(kcmpT[:, c * 128:(c + 1) * 128], kcmpT, extra,
                                        [(vcx[:, c, g, 0:65], lambda r: o4(psOc)[:, r, :], psOc),
                                         (vcx[:, c, g, 65:129], lambda r: psImp[:, r * 64:(r + 1) * 64], psImp)],
                                        vcx, ci == 0, ci == len(chunks) - 1, qzt)
                                if g == 0 and ci == 0:
                                    jb["pre"] = (lambda i_=i: tile_start(i_)); jb["needs"] = -1
                                jobs.append(jb)
                            jobs[-1]["post"] = (lambda i_=i, g_=g: topk_dve(i_, g_))
                            last_cmp[g] = len(jobs) - 1
                        for g in range(2):
                            qzt = qz[g][i % 2]; qzst = qzs[g][i % 2]
                            j0 = max(0, i - 4)
                            for j in range(j0, i + 1):
                                extra = []
                                if j == i:
                                    extra.append((ident[:], bc4(McNeg[:]), [ident, McNeg]))
                                elif j == i - 4:
                                    extra.append((ident[:], bc4(MwNeg[:]), [ident, MwNeg]))
                                jb = mk(kwT[:, j * 128:(j + 1) * 128], kwT, extra,
                                        [(vwA[:, j, g, :], lambda r: o4(psOw)[:, r, :], psOw)], vwA, j == j0, j == i, qzt)
                                if g == 0 and j == j0 and i > 0:
                                    jb["pre"] = (lambda i_=i: outproj(i_ - 1)); jb["needs"] = last_sel_g1
                                jobs.append(jb)
                            for j in range(i + 1):
                                extra = []
                                if j == i:
                                    extra.append((ident[:], bc4(McNeg[:]), [ident, McNeg]))
                                jb = mk(ksE[g][:, j * 128:(j + 1) * 128], ksE[g], extra,
                                        [(vsA[:, j, g, :], lambda r: o4(psOs)[:, r, :], psOs)], vsA, j == 0, j == i, qzst)
                                if j == 0:
                                    jb["pre"] = (lambda g_=g, q_=qzst: negT_pe(g_, q_)); jb["needs"] = last_cmp[g]
                                jobs.append(jb)
                            jobs[-1]["post"] = (lambda i_=i, g_=g: combine(i_, g_))
                            if g == 1:
                                last_sel_g1 = len(jobs) - 1
                    run_jobs(jobs)
                    outproj(P3_TILES - 1)
                    while deferred:
                        deferred.pop(0)()
                fw.pop()
        fw.pop()
        if upto < 4:
            return nc

        def ffn_phase(ls, layer, src_d, src_name, dst_d, dst_name):
            fw.push()
            if True:
                Wi = fw.sb("Wi", [128, 8, 2 * FH], BF16, dma="sw")
                Wo = fw.sb("Wo", [128, 22, D], BF16, dma="sw")
                fw.dma_group("pool", [(Wi[:, k, :], fwi_d[layer, k * 128:(k + 1) * 128, :]) for k in range(8)], Wi, writes=[Wi])
                fw.dma_group("pool", [(Wo[:, k, :], fwo_d[layer, k * 128:(k + 1) * 128, :]) for k in range(22)], Wo, writes=[Wo])
                lng = fw.sb("lng", [128, D], F32, dma=True); lnb = fw.sb("lnb", [128, D], F32, dma=True)
                load_bc(lng, lng_d[ls]); load_bc(lnb, lnb_d[ls])
                sc1 = fw.sb("sc1", [128, D], F32, dma=True); sh = fw.sb("sh", [128, D], F32, dma=True); g1 = fw.sb("g1", [128, D], F32, dma=True)
                NX = 4
                xin = [fw.sb("xin%d" % i, [128, D], F32, dma=True) for i in range(NX)]
                hT = [fw.sb("hT%d" % i, [128, 8, 256], BF16) for i in range(2)]
                actT = fw.sb("actT", [128, 22, 256], BF16)
                sg = [fw.sb("sg%d" % i, [128, 256], BF16) for i in range(2)]
                hp = HPrep(sc1, sh, nhb=2, tmp_dt=BF16)
                ep = Epilogue(g1, lng, lnb)
                psGU = [fw.ps("psGU%d" % i, [128, 2, 256], F32) for i in range(2)]
                psYs = [fw.ps("psY%d" % i, [128, 2, 512], F32) for i in range(2)]
                tiles = [(b, mt) for b in range(NB) for mt in range(S // 256)]
                state = {}
                xc = [0]

                def prep_a(idx):
                    b, mt = tiles[idx]
                    if mt == 0:
                        load_mod(ls, b, [(sh, 0), (sc1, 1)])
                    hTm = hT[idx % 2]
                    xts = []; hbs = []
                    for sub in range(2):
                        i = mt * 2 + sub
                        xt = xin[xc[0] % NX]; xc[0] += 1
                        fw.dma("sp", xt[:], src_d[b, i * 128:(i + 1) * 128, :], xt, reads=[dbuf(src_name, b, i)], writes=[xt])
                        hbs.append(hp.run_a(xt))
                        xts.append(xt)
                    state[idx] = (hTm, xts, hbs)

                def prep_b(idx):
                    hTm, xts, hbs = state[idx]
                    for sub in range(2):
                        hp.run_b(hbs[sub], hTm, sub * 128)

                def up(idx, hcs):
                    hTm = state[idx][0]
                    for hc in hcs:
                        pg = psGU[hc % 2]; sgt = sg[hc % 2]
                        for k in range(8):
                            op("pe", lambda: PE.matmul(pg[:, 0, :], lhsT=Wi[:, k, hc * 128:(hc + 1) * 128], rhs=hTm[:, k, :],
                                                       start=(k == 0), stop=(k == 7)), reads=[Wi, hTm], writes=[pg])
                        for k in range(8):
                            op("pe", lambda: PE.matmul(pg[:, 1, :], lhsT=Wi[:, k, FH + hc * 128:FH + (hc + 1) * 128], rhs=hTm[:, k, :],
                                                       start=(k == 0), stop=(k == 7)), reads=[Wi, hTm], writes=[pg])
                        op("act", lambda: A.activation(out=sgt[:], in_=pg[:, 0, :], func=AF.Silu), reads=[pg], writes=[sgt])
                        tt("dve", actT[:, hc, :], pg[:, 1, :], sgt[:], ALU.mult, [pg, sgt], [actT])
                        for _ in range(2):
                            if deferred:
                                deferred.pop(0)()

                def down_ep(idx):
                    b, mt = tiles[idx]
                    xts = state.pop(idx)[1]
                    if mt == 0:
                        load_mod(ls, b, [(g1, 2)])
                    for sub in range(2):
                        i = mt * 2 + sub
                        psY = psYs[sub]
                        for half in range(2):
                            for hc in range(22):
                                op("pe", lambda: PE.matmul(psY[:, half, :], lhsT=actT[:, hc, sub * 128:(sub + 1) * 128],
                                                           rhs=Wo[:, hc, half * 512:(half + 1) * 512], start=(hc == 0), stop=(hc == 21)),
                                   reads=[actT, Wo], writes=[psY])
                        st_ = ep.steps([(psY, psY[:, 0, :]), (psY, psY[:, 1, :])], xts[sub], dst_d[b, i * 128:(i + 1) * 128, :], dbuf(dst_name, b, i))
                        nim = 3 if sub == 1 else len(st_)
                        for f_ in st_[:nim]:
                            f_()
                        deferred.extend(st_[nim:])

                deferred = []
                prep_a(0); prep_b(0)
                for idx in range(len(tiles)):
                    up(idx, range(0, 2))
                    if idx + 1 < len(tiles):
                        prep_a(idx + 1)
                    up(idx, range(2, 12))
                    if idx + 1 < len(tiles):
                        prep_b(idx + 1)
                    up(idx, range(12, 22))
                    while deferred:
                        deferred.pop(0)()
                    down_ep(idx)
                while deferred:
                    deferred.pop(0)()
            fw.pop()

        ffn_phase(1, 0, x1_d, "x1", x2_d, "x2")
        if upto < 5:
            return nc

        fw.push()
        if True:
            W1 = fw.sb("W1", [128, 8, 3 * D], BF16, dma="sw")
            Wo1 = fw.sb("Wo1", [128, 8, D], BF16, dma="sw")
            fw.dma_group("pool", [(W1[:, k, :], win1_d[k * 128:(k + 1) * 128, :]) for k in range(8)], W1, writes=[W1])
            fw.dma_group("pool", [(Wo1[:, k, :], wout1_d[k * 128:(k + 1) * 128, :]) for k in range(8)], Wo1, writes=[Wo1])
            cwf = fw.sb("cwf", [3, D], F32, dma=True)
            fw.dma("sp", cwf[:], cw_d, cwf, writes=[cwf])
            cw = fw.sb("cw", [128, 8, 3], F32)
            pscw = fw.ps("pscw", [128, 8, 3], F32)
            for cc in range(8):
                op("pe", lambda: PE.transpose(pscw[:, cc, :], cwf[:, cc * 128:(cc + 1) * 128], identf[0:3, 0:3]), reads=[cwf, identf], writes=[pscw])
            op("dve", lambda: V.tensor_copy(out=cw[:], in_=pscw[:]), reads=[pscw], writes=[cw])
            lng = fw.sb("lng", [128, D], F32, dma=True); lnb = fw.sb("lnb", [128, D], F32, dma=True)
            load_bc(lng, lng_d[2]); load_bc(lnb, lnb_d[2])
            sc1 = fw.sb("sc1", [128, D], F32, dma=True); sh = fw.sb("sh", [128, D], F32, dma=True); g1 = fw.sb("g1", [128, D], F32, dma=True)
            xin = [fw.sb("xin%d" % i, [128, D], F32, dma=True) for i in range(4)]
            hT = [fw.sb("hT%d" % i, [128, 8, 256], BF16) for i in range(2)]
            cz = fw.sb("cz", [128, 8, 258], F32)
            csb = [fw.sb("csb%d" % i, [128, 256], F32) for i in range(2)]
            y1 = [fw.sb("y1_%d" % i, [128, 256], F32) for i in range(2)]
            byT = fw.sb("byT", [128, 8, 256], BF16)
            hp = HPrep(sc1, sh)
            ep = Epilogue(g1, lng, lnb)
            psP = [fw.ps("psP%d" % i, [128, 4, 256], F32) for i in range(2)]
            psY = fw.ps("psY", [128, 2, 512], F32)
            tiles = [(b, mt) for b in range(NB) for mt in range(S // 256)]
            state = {}
            xc = [0]

            def prep(idx):
                b, mt = tiles[idx]
                if mt == 0:
                    load_mod(2, b, [(sh, 0), (sc1, 1)])
                hTm = hT[idx % 2]
                xts = []
                for sub in range(2):
                    i = mt * 2 + sub
                    xt = xin[xc[0] % 4]; xc[0] += 1
                    fw.dma("sp", xt[:], x2_d[b, i * 128:(i + 1) * 128, :], xt, reads=[dbuf("x2", b, i)], writes=[xt])
                    hp.run(xt, hTm, sub * 128)
                    xts.append(xt)
                state[idx] = (hTm, xts)

            def compute(idx, ccs):
                b, mt = tiles[idx]
                hTm, _ = state[idx]
                for cc in ccs:
                    if cc == 0:
                        if mt == 0:
                            op("pool", lambda: G.memset(cz[:, :, 0:2], 0.0), reads=[cz], writes=[cz])
                        else:
                            op("pool", lambda: G.tensor_copy(out=cz[:, :, 0:2], in_=cz[:, :, 256:258]), reads=[cz], writes=[cz])
                    pp = psP[cc % 2]; cs = csb[cc % 2]; yt = y1[cc % 2]
                    for part in range(3):
                        for k in range(8):
                            op("pe", lambda: PE.matmul(pp[:, part, :], lhsT=W1[:, k, part * D + cc * 128:part * D + (cc + 1) * 128],
                                                       rhs=hTm[:, k, :], start=(k == 0), stop=(k == 7)), reads=[W1, hTm], writes=[pp])
                    op("act", lambda: A.copy(out=cs[:], in_=pp[:, 1, :]), reads=[pp], writes=[cs])
                    tt("dve", cz[:, cc, 2:258], pp[:, 2, :], cs[:], ALU.mult, [pp, cs, cz], [cz])
                    op("dve", lambda: V.tensor_scalar(out=yt[:], in0=cz[:, cc, 0:256], scalar1=cw[:, cc, 0:1], scalar2=None, op0=ALU.mult),
                       reads=[cz, cw], writes=[yt])
                    op("dve", lambda: V.scalar_tensor_tensor(out=yt[:], in0=cz[:, cc, 1:257], scalar=cw[:, cc, 1:2], in1=yt[:],
                                                             op0=ALU.mult, op1=ALU.add), reads=[cz, cw, yt], writes=[yt])
                    op("dve", lambda: V.scalar_tensor_tensor(out=yt[:], in0=cz[:, cc, 2:258], scalar=cw[:, cc, 2:3], in1=yt[:],
                                                             op0=ALU.mult, op1=ALU.add), reads=[cz, cw, yt], writes=[yt])
                    tt("dve", byT[:, cc, :], pp[:, 0, :], yt[:], ALU.mult, [pp, yt], [byT])

            def out_ep(idx):
                b, mt = tiles[idx]
                _, xts = state.pop(idx)
                if mt == 0:
                    load_mod(2, b, [(g1, 2)])
                for sub in range(2):
                    i = mt * 2 + sub
                    for half in range(2):
                        for cc in range(8):
                            op("pe", lambda: PE.matmul(psY[:, half, :], lhsT=byT[:, cc, sub * 128:(sub + 1) * 128],
                                                       rhs=Wo1[:, cc, half * 512:(half + 1) * 512], start=(cc == 0), stop=(cc == 7)),
                               reads=[byT, Wo1], writes=[psY])
                    ep.run([(psY, psY[:, 0, :]), (psY, psY[:, 1, :])], xts[sub], x3_d[b, i * 128:(i + 1) * 128, :], dbuf("x3", b, i))

            prep(0)
            for idx in range(len(tiles)):
                compute(idx, range(0, 4))
                if idx + 1 < len(tiles):
                    prep(idx + 1)
                compute(idx, range(4, 8))
                out_ep(idx)
        fw.pop()
        if upto < 6:
            return nc

        ffn_phase(3, 1, x3_d, "x3", out_d, "out")
        fw.barrier()
    return nc


_CACHE = {}


def _prep_inputs(inputs):
    f = lambda a: np.ascontiguousarray(np.asarray(a, dtype=np.float32))
    common = {
        "ada_w": f(inputs["ada_w"]).reshape(4, D, 3 * D), "ada_b": f(inputs["ada_b"]).reshape(4, 3 * D),
        "ln_g": f(inputs["ln_g"]).reshape(4, D), "ln_b": f(inputs["ln_b"]).reshape(4, D),
        "even_w_in": f(inputs["even_w_in"])[0], "even_cmp_pos": f(inputs["even_cmp_pos"])[0],
        "even_cmp_w1": f(inputs["even_cmp_w1"])[0], "even_cmp_w2": f(inputs["even_cmp_w2"])[0],
        "even_gmlp_norm_g": f(inputs["even_gmlp_norm_g"])[0].reshape(512), "even_gmlp_ws": f(inputs["even_gmlp_ws"])[0],
        "even_gmlp_bs": f(inputs["even_gmlp_bs"])[0], "even_w_out": f(inputs["even_w_out"])[0],
        "odd_w_in": f(inputs["odd_w_in"])[0], "odd_conv_w": f(inputs["odd_conv_w"])[0], "odd_w_out": f(inputs["odd_w_out"])[0],
        "ffn_w_in": f(inputs["ffn_w_in"]), "ffn_w_out": f(inputs["ffn_w_out"]),
    }
    x = f(inputs["x"]); c = f(inputs["c"])
    maps = []
    for i in range(N_CORES):
        m = dict(common)
        m["x"] = x[NB * i:NB * (i + 1)]; m["c"] = c[NB * i:NB * (i + 1)]
        maps.append(m)
    return maps


def kernel(**inputs):
    if "nc" not in _CACHE:
        _CACHE["nc"] = build_program()
    nc = _CACHE["nc"]
    maps = _prep_inputs(inputs)
    res = run_bass_kernel_spmd(nc, maps, core_ids=list(range(N_CORES)))
    return np.concatenate([r["out"] for r in res.results], axis=0).astype(np.float32)
```

```python
import numpy as np
import concourse.bass as bass
import concourse.mybir as mybir
from concourse.bass_utils import run_bass_kernel_spmd
from contextlib import ExitStack

F32 = mybir.dt.float32; BF16 = mybir.dt.bfloat16; I32 = mybir.dt.int32
ALU = mybir.AluOpType; AF = mybir.ActivationFunctionType; AX = mybir.AxisListType

S = 4096; D = 1024; NB = 2; NT = S // 128
FH = 2816
NEG = -30000.0
ALPHA = 4.0 ** 0.25
EPS = 1e-5
N_CORES = 8
P3_TILES = NT
P1_MT = 8
SELMODE = 0
NB_RUN = NB
P3_PARTS = ('cmp', 'topk', 'sel', 'win', 'comb', 'out')


class Buf:
    __slots__ = ("name", "w", "r", "dsem")

    def __init__(self, name):
        self.name = name; self.w = {}; self.r = {}; self.dsem = None


class T:
    def __init__(self, h, buf):
        self.h = h; self.b = buf

    def __getitem__(self, k):
        return self.h[k]


class FW:
    def __init__(self, nc, ges, n_dsem=56):
        self.nc = nc; self.es = ges
        self.eng = {"pe": nc.tensor, "act": nc.scalar, "dve": nc.vector, "pool": nc.gpsimd, "sp": nc.sync}
        self.sems = {}; self.cnt = {}
        for e in ("pe", "act", "dve", "pool"):
            self.sems[e] = ges.enter_context(nc.semaphore("s_" + e)); self.cnt[e] = 0
        self.dpool = []; self.dpool_sw = []
        for i in range(n_dsem):
            k = "d%d" % i
            self.sems[k] = ges.enter_context(nc.semaphore("s_" + k)); self.cnt[k] = 0
            (self.dpool_sw if i < 10 else self.dpool).append(k)
        self.checked = []
        self.scopes = []
        self.seen = {e: {} for e in self.eng}
        self.uid = 0

    def push(self):
        st = ExitStack(); st.__enter__()
        self.scopes.append((self.es, self.checked, st))
        self.es = st; self.checked = []

    def pop(self):
        self.barrier()
        old_es, old_checked, st = self.scopes.pop()
        st.__exit__(None, None, None)
        for k in self.checked:
            (self.dpool_sw if int(k[1:]) < 10 else self.dpool).append(k)
        self.es = old_es; self.checked = old_checked

    def sb(self, name, shape, dt, dma=False):
        self.uid += 1
        h = self.es.enter_context(self.nc.sbuf_tensor("%s_%d" % (name, self.uid), shape, dt))
        b = Buf(name)
        if dma:
            b.dsem = (self.dpool_sw if dma == "sw" else self.dpool).pop(0); self.checked.append(b.dsem)
        return T(h, b)

    def ps(self, name, shape, dt):
        self.uid += 1
        h = self.es.enter_context(self.nc.psum_tensor("%s_%d" % (name, self.uid), shape, dt))
        return T(h, Buf(name))

    def _deps(self, reads, writes):
        deps = {}
        for b in reads:
            for k, v in b.w.items():
                if deps.get(k, 0) < v: deps[k] = v
        for b in writes:
            for k, v in b.w.items():
                if deps.get(k, 0) < v: deps[k] = v
            for k, v in b.r.items():
                if deps.get(k, 0) < v: deps[k] = v
        return deps

    def _wait(self, e, deps):
        seen = self.seen[e]; eng = self.eng[e]
        for k, v in deps.items():
            if k == "pe" and e == "pe":
                continue
            if seen.get(k, 0) < v:
                eng.wait_ge(self.sems[k], v); seen[k] = v

    def _record(self, reads, writes, key, val):
        for b in writes:
            b.w = {key: val}; b.r = {}
        for b in reads:
            if b.r.get(key, 0) < val: b.r[key] = val

    @staticmethod
    def _bl(ts):
        return [t.b if isinstance(t, T) else t for t in ts]

    def op(self, e, fn, reads=(), writes=()):
        reads = self._bl(reads); writes = self._bl(writes)
        self._wait(e, self._deps(reads, writes))
        ins = fn()
        self.cnt[e] += 1
        ins.then_inc(self.sems[e], 1)
        self._record(reads, writes, e, self.cnt[e])
        return ins

    def dma(self, q, out, in_, sbuf_t, reads=(), writes=()):
        reads = self._bl(reads); writes = self._bl(writes)
        sbb = sbuf_t.b if isinstance(sbuf_t, T) else sbuf_t
        assert sbb.dsem is not None, sbb.name
        self._wait(q, self._deps(reads, writes))
        ins = self.eng[q].dma_start(out=out, in_=in_)
        key = sbb.dsem
        self.cnt[key] += 16
        ins.then_inc(self.sems[key], 16)
        self._record(reads, writes, key, self.cnt[key])
        return ins

    def dma_group(self, q, pairs, sbuf_t, reads=(), writes=()):
        reads = self._bl(reads); writes = self._bl(writes)
        sbb = sbuf_t.b if isinstance(sbuf_t, T) else sbuf_t
        self._wait(q, self._deps(reads, writes))
        key = sbb.dsem
        base = self.cnt[key]
        for n_, (o, i_) in enumerate(pairs):
            if n_ >= 4 and n_ % 4 == 0:
                self.eng[q].wait_ge(self.sems[key], base + 16 * n_)
            ins = self.eng[q].dma_start(out=o, in_=i_)
            self.cnt[key] += 16
            ins.then_inc(self.sems[key], 16)
        self._record(reads, writes, key, self.cnt[key])

    def barrier(self):
        deps = {k: v for k, v in self.cnt.items() if v > 0}
        for e in self.eng:
            seen = self.seen[e]
            for k, v in deps.items():
                if seen.get(k, 0) < v:
                    self.eng[e].wait_ge(self.sems[k], v); seen[k] = v


def build_program(upto=99, debug=False):
    nc = bass.Bass("TRN2", target_bir_lowering=False)
    kS = "ExternalOutput" if debug else "Internal"

    def din(name, shape):
        return nc.dram_tensor(name, shape, F32, kind="ExternalInput").ap()

    x_d = din("x", [NB, S, D]); c_d = din("c", [NB, D])
    adaw_d = din("ada_w", [4, D, 3 * D]); adab_d = din("ada_b", [4, 3 * D])
    lng_d = din("ln_g", [4, D]); lnb_d = din("ln_b", [4, D])
    win0_d = din("even_w_in", [D, 2328]); pos_d = din("even_cmp_pos", [2, 32, 64])
    w1_d = din("even_cmp_w1", [2, 2048, 128]); w2_d = din("even_cmp_w2", [2, 128, 64])
    gng_d = din("even_gmlp_norm_g", [512]); gws_d = din("even_gmlp_ws", [8, 128, 128]); gbs_d = din("even_gmlp_bs", [8, 128])
    wout0_d = din("even_w_out", [D, D])
    win1_d = din("odd_w_in", [D, 3 * D]); cw_d = din("odd_conv_w", [3, D]); wout1_d = din("odd_w_out", [D, D])
    fwi_d = din("ffn_w_in", [2, D, 2 * FH]); fwo_d = din("ffn_w_out", [2, FH, D])
    out_d = nc.dram_tensor("out", [NB, S, D], F32, kind="ExternalOutput").ap()
    mod_d = nc.dram_tensor("mod_s", [4, NB, 3 * D], F32, kind=kS).ap()
    x1_d = nc.dram_tensor("x1_s", [NB, S, D], F32, kind=kS).ap()
    x2_d = nc.dram_tensor("x2_s", [NB, S, D], F32, kind=kS).ap()
    x3_d = nc.dram_tensor("x3_s", [NB, S, D], F32, kind=kS).ap()
    ogm_d = nc.dram_tensor("ogm_s", [NB, S, 512], BF16, kind=kS).ap()
    dbg = {}
    if debug:
        dbg["onsa"] = nc.dram_tensor("onsa_s", [NB, S, 512], F32, kind=kS).ap()
        dbg["kcc"] = nc.dram_tensor("kcc_s", [NB, 128, 256], F32, kind=kS).ap()

    dbufs = {}

    def dbuf(*key):
        if key not in dbufs:
            dbufs[key] = Buf(str(key))
        return dbufs[key]

    with ExitStack() as ges, nc.allow_non_contiguous_dma(reason="small param layouts"):
        fw = FW(nc, ges)
        op = fw.op
        V = nc.vector; A = nc.scalar; G = nc.gpsimd; PE = nc.tensor
        VENG = {"dve": V, "pool": G}

        ident = fw.sb("ident", [128, 128], BF16)
        identf = fw.sb("identf", [128, 128], F32)
        epsb = fw.sb("epsb", [128, 1], F32)
        op("pool", lambda: G.memset(identf[:], 1.0), writes=[identf])
        op("pool", lambda: G.affine_select(out=identf[:], in_=identf[:], pattern=[[1, 128]], compare_op=ALU.is_equal,
                                           fill=0.0, base=0, channel_multiplier=-1), reads=[identf], writes=[identf])
        op("pool", lambda: G.tensor_copy(out=ident[:], in_=identf[:]), reads=[identf], writes=[ident])
        op("pool", lambda: G.memset(epsb[:], EPS), writes=[epsb])
        mhalf = fw.sb("mhalf", [128, 8], F32)
        op("pool", lambda: G.memset(mhalf[:], -0.5), writes=[mhalf])

        def tt(e, out, in0, in1, o, rd, wr):
            return op(e, lambda: VENG[e].tensor_tensor(out=out, in0=in0, in1=in1, op=o), reads=rd, writes=wr)

        def load_bc(dst, src_1d, q="sp"):
            fw.dma(q, dst[:], src_1d.partition_broadcast(128), dst, writes=[dst])

        def load_w_bf16(dst, kidx, src2d):
            fw.dma("pool", dst[:, kidx, :], src2d, dst, writes=[dst])

        class HPrep:
            def __init__(self, sc1, sh, nhb=2, tmp_dt=F32):
                self.sc1 = sc1; self.sh = sh; self.nhb = nhb
                self.tmp = [fw.sb("hp_tmp%d" % i, [128, D], tmp_dt) for i in range(1)]
                self.hb = [fw.sb("hp_hb%d" % i, [128, D], BF16) for i in range(nhb)]
                self.psT = fw.ps("hp_psT", [128, 8, 128], BF16)
                self.n = 0

            def run_a(self, xt):
                tmp = self.tmp[0]; hb = self.hb[self.n % self.nhb]; self.n += 1
                tt("dve", tmp[:], xt[:], self.sc1[:], ALU.mult, [xt, self.sc1], [tmp])
                tt("pool", hb[:], tmp[:], self.sh[:], ALU.add, [tmp, self.sh], [hb])
                return hb

            def run_b(self, hb, hT, off):
                psT = self.psT
                for c in range(8):
                    op("pe", lambda: PE.transpose(psT[:, c, :], hb[:, c * 128:(c + 1) * 128], ident[:]),
                       reads=[hb, ident], writes=[psT])
                op("act", lambda: A.copy(out=hT[:, :, off:off + 128], in_=psT[:]), reads=[psT], writes=[hT])

            def run(self, xt, hT, off):
                self.run_b(self.run_a(xt), hT, off)

        class Epilogue:
            def __init__(self, g1, lng, lnb, nt=1):
                self.g1 = g1; self.lng = lng; self.lnb = lnb
                self.ts = [fw.sb("ep_t%d" % i, [128, D], F32) for i in range(nt)]
                self.xo = [fw.sb("ep_xo%d" % i, [128, D], F32, dma=True) for i in range(2)]
                self.sts = [fw.sb("ep_st%d" % i, [128, 2, 6], F32) for i in range(nt)]
                self.mvs = [fw.sb("ep_mv%d" % i, [128, 8], F32) for i in range(nt)]
                self.k = 0

            def steps(self, halves, xres, out_ap, out_db):
                t = self.ts[self.k % len(self.ts)]; st = self.sts[self.k % len(self.ts)]; mv = self.mvs[self.k % len(self.ts)]
                xo = self.xo[self.k % 2]; self.k += 1
                g1 = self.g1; lng = self.lng; lnb = self.lnb
                L = []
                for hi, (pt, pap) in enumerate(halves):
                    L.append(lambda hi=hi, pt=pt, pap=pap: tt("dve", t[:, hi * 512:(hi + 1) * 512], pap, g1[:, hi * 512:(hi + 1) * 512], ALU.mult, [pt, g1], [t]))
                L.append(lambda: op("dve", lambda: V.scalar_tensor_tensor(out=t[:], in0=xres[:], scalar=ALPHA, in1=t[:], op0=ALU.mult, op1=ALU.add),
                                    reads=[xres, t], writes=[t]))
                L.append(lambda: op("dve", lambda: V.bn_stats(out=st[:, 0, :], in_=t[:, 0:512]), reads=[t], writes=[st]))
                L.append(lambda: op("dve", lambda: V.bn_stats(out=st[:, 1, :], in_=t[:, 512:1024]), reads=[t, st], writes=[st]))
                L.append(lambda: op("dve", lambda: V.bn_aggr(out=mv[:, 0:2], in_=st[:].rearrange("p a b -> p (a b)")), reads=[st], writes=[mv]))
                L.append(lambda: op("dve", lambda: V.tensor_scalar_add(out=mv[:, 2:3], in0=mv[:, 1:2], scalar1=EPS), reads=[mv], writes=[mv]))
                L.append(lambda: op("pool", lambda: G.tensor_tensor(out=mv[:, 3:4], in0=mv[:, 2:3], in1=mhalf[:, 0:1], op=ALU.pow),
                                    reads=[mv, mhalf], writes=[mv]))
                L.append(lambda: op("dve", lambda: V.tensor_scalar(out=mv[:, 4:5], in0=mv[:, 0:1], scalar1=mv[:, 3:4], scalar2=-1.0,
                                                                   op0=ALU.mult, op1=ALU.mult), reads=[mv], writes=[mv]))
                L.append(lambda: op("act", lambda: A.activation(out=xo[:], in_=t[:], func=AF.Identity, bias=mv[:, 4:5], scale=mv[:, 3:4]),
                                    reads=[t, mv], writes=[xo]))
                L.append(lambda: tt("pool", xo[:], xo[:], lng[:], ALU.mult, [xo, lng], [xo]))
                L.append(lambda: tt("pool", xo[:], xo[:], lnb[:], ALU.add, [xo, lnb], [xo]))
                L.append(lambda: fw.dma("sp", out_ap, xo[:], xo, reads=[xo], writes=[out_db]))
                return L

            def run(self, halves, xres, out_ap, out_db):
                for f in self.steps(halves, xres, out_ap, out_db):
                    f()

        def load_mod(ls, b, want):
            for tile, which in want:
                load_bc(tile, mod_d[ls, b, which * D:(which + 1) * D])

        def transpose_small(dst_ap, dstT, src_ap, srcT, npart, pst, pst_ap):
            op("pe", lambda: PE.transpose(pst_ap, src_ap, identf[0:npart, 0:npart]), reads=[srcT, identf], writes=[pst])
            op("dve", lambda: V.tensor_copy(out=dst_ap, in_=pst_ap), reads=[pst], writes=[dstT])

        fw.push()
        if True:
            csb_ = fw.sb("csb", [NB, D], F32, dma=True)
            scs = fw.sb("scs", [NB, D], F32)
            scT = fw.sb("scT", [128, 8, NB], F32)
            stage = [fw.sb("adastage%d" % i, [128, 3 * D], F32, dma=True) for i in range(2)]
            adab = fw.sb("adab", [NB, 3 * D], F32, dma=True)
            msb = fw.sb("msb", [NB, 3 * D], F32, dma=True)
            psm = [fw.ps("psm%d" % i, [NB, 512], F32) for i in range(6)]
            pst = fw.ps("pst_p0", [128, 8, NB], F32)
            fw.dma("sp", csb_[:], c_d, csb_, writes=[csb_])
            op("act", lambda: A.activation(out=scs[:], in_=csb_[:], func=AF.Silu), reads=[csb_], writes=[scs])
            for k in range(8):
                op("pe", lambda: PE.transpose(pst[:, k, :], scs[:, k * 128:(k + 1) * 128], identf[0:NB, 0:NB]), reads=[scs, identf], writes=[pst])
            op("dve", lambda: V.tensor_copy(out=scT[:], in_=pst[:]), reads=[pst], writes=[scT])
            si = 0
            for ls in range(4):
                fw.dma("sp", adab[:], adab_d[ls].partition_broadcast(NB), adab, writes=[adab])
                for k in range(8):
                    stg = stage[si % 2]; si += 1
                    fw.dma("sp", stg[:], adaw_d[ls, k * 128:(k + 1) * 128, :], stg, writes=[stg])
                    for ct in range(6):
                        op("pe", lambda: PE.matmul(psm[ct][:], lhsT=scT[:, k, :], rhs=stg[:, ct * 512:(ct + 1) * 512],
                                                   start=(k == 0), stop=(k == 7)), reads=[scT, stg], writes=[psm[ct]])
                for ct in range(6):
                    tt("dve", msb[:, ct * 512:(ct + 1) * 512], psm[ct][:], adab[:, ct * 512:(ct + 1) * 512], ALU.add,
                       [psm[ct], adab], [msb])
                op("dve", lambda: V.tensor_scalar_add(out=msb[:, D:3 * D], in0=msb[:, D:3 * D], scalar1=1.0), reads=[msb], writes=[msb])
                fw.dma("sp", mod_d[ls], msb[:], msb, reads=[msb], writes=[dbuf("mod")])
        fw.pop()
        if upto < 1:
            return nc

        fw.push()
        if True:
            WsT = fw.sb("WsT", [128, 8, 128], BF16)
            bsT = fw.sb("bsT", [128, 8], F32)
            ngb = fw.sb("ngb", [128, 512], F32, dma=True)
            load_bc(ngb, gng_d)
            cbias = fw.sb("cbias", [128, 2], F32)
            w2 = fw.sb("cw2", [128, 2, 64], BF16, dma="sw")
            fw.dma("pool", w2[:], w2_d.rearrange("kv h d -> h kv d"), w2, writes=[w2])
            w2pad = fw.sb("cw2pad", [128, 2, 128], BF16)
            op("pool", lambda: G.memset(w2pad[:], 0.0), writes=[w2pad])
            for g in range(2):
                op("pool", lambda: G.tensor_copy(out=w2pad[:, g, g * 64:(g + 1) * 64], in_=w2[:, 0, :]), reads=[w2, w2pad], writes=[w2pad])
            McNeg = fw.sb("McNeg", [128, 128], BF16)
            MwNeg = fw.sb("MwNeg", [128, 128], BF16)
            cm_all = fw.sb("cm_all", [128, 33, 128], BF16)
            TMm = fw.sb("TMm", [128, 126], F32); TB = fw.sb("TB", [128, 126], F32)
            qT = fw.sb("qT", [128, 4, S], BF16)
            ksE = [fw.sb("ksE%d" % g_, [128, S], BF16) for g_ in range(2)]; kwT = fw.sb("kwT", [128, S], BF16)
            kcT = fw.sb("kcT", [128, S], BF16); vcT = fw.sb("vcT", [128, S], BF16)
            vsA = fw.sb("vsA", [128, NT, 2, 65], BF16); vwA = fw.sb("vwA", [128, NT, 2, 65], BF16)
            gat = fw.sb("gat", [128, NT, 24], F32)
            kcmpT = fw.sb("kcmpT", [128, 256], BF16)
            vcx = fw.sb("vcx", [128, 2, 2, 129], BF16)
            cmpmask = {}
            if P1_MT < 8:
                for tcache in (qT, ksE[0], ksE[1], kwT, kcT, vcT, gat):
                    op("pool", lambda: G.memset(tcache[:], 0.0), writes=[tcache])
                op("pool", lambda: G.memset(vsA[:], 0.0), writes=[vsA]); op("pool", lambda: G.memset(vwA[:], 0.0), writes=[vwA])
            fw.push()
            if True:
                w1t = fw.sb("cw1t", [64, 2, 32, 128], BF16, dma="sw")
                for kv in range(2):
                    fw.dma("pool", w1t[:, kv, :, :], w1_d[kv].rearrange("(l d) h -> d l h", d=64), w1t, writes=[w1t])
                posf = fw.sb("posf", [32, 2, 64], F32, dma=True)
                fw.dma("sp", posf[:], pos_d.rearrange("kv l d -> l kv d"), posf, writes=[posf])
                posT = fw.sb("posT", [64, 2, 32], BF16)
                pss = fw.ps("pss", [128, 64], F32)
                for kv in range(2):
                    transpose_small(posT[:, kv, :], posT, posf[:, kv, :], posf, 32, pss, pss[0:64, 0:32])
                psb = fw.ps("psb", [128, 2], F32)
                for kv in range(2):
                    for l in range(32):
                        op("pe", lambda: PE.matmul(psb[:, kv:kv + 1], lhsT=w1t[:, kv, l, :], rhs=posT[:, kv, l:l + 1],
                                                   start=(l == 0), stop=(l == 31)), reads=[w1t, posT], writes=[psb])
                op("dve", lambda: V.tensor_copy(out=cbias[:], in_=psb[:]), reads=[psb], writes=[cbias])
                gbsf = fw.sb("gbsf", [8, 128], F32, dma=True)
                fw.dma("sp", gbsf[:], gbs_d, gbsf, writes=[gbsf])
                transpose_small(bsT[:], bsT, gbsf[:], gbsf, 8, pss, pss[:, 0:8])
                wsf = fw.sb("wsf", [128, 8, 128], F32, dma=True)
                wsb = fw.sb("wsb", [128, 8, 128], BF16)
                fw.dma("sp", wsf[:], gws_d.rearrange("g t s -> t g s"), wsf, writes=[wsf])
                op("pool", lambda: G.affine_select(out=wsf[:], in_=wsf[:], pattern=[[0, 8], [-1, 128]], compare_op=ALU.is_ge,
                                                   fill=0.0, base=0, channel_multiplier=1), reads=[wsf], writes=[wsf])
                op("pool", lambda: G.tensor_copy(out=wsb[:], in_=wsf[:]), reads=[wsf], writes=[wsb])
                pst = fw.ps("pst0", [128, 8, 128], BF16)
                for gm in range(8):
                    op("pe", lambda: PE.transpose(pst[:, gm, :], wsb[:, gm, :], ident[:]), reads=[wsb, ident], writes=[pst])
                op("dve", lambda: V.tensor_copy(out=WsT[:], in_=pst[:]), reads=[pst], writes=[WsT])
                zf = fw.sb("zf", [128, 128], F32)
                mtmp = fw.sb("mtmp", [128, 128], F32)
                op("pool", lambda: G.memset(zf[:], 0.0), writes=[zf])
                op("pool", lambda: G.affine_select(out=mtmp[:], in_=zf[:], pattern=[[1, 128]], compare_op=ALU.is_ge, fill=NEG,
                                                   base=0, channel_multiplier=-1), reads=[zf], writes=[mtmp])
                op("pool", lambda: G.tensor_copy(out=McNeg[:], in_=mtmp[:]), reads=[mtmp], writes=[McNeg])
                op("pool", lambda: G.affine_select(out=mtmp[:], in_=zf[:], pattern=[[-1, 128]], compare_op=ALU.is_gt, fill=NEG,
                                                   base=0, channel_multiplier=1), reads=[zf], writes=[mtmp])
                op("pool", lambda: G.tensor_copy(out=MwNeg[:], in_=mtmp[:]), reads=[mtmp], writes=[MwNeg])
                mi = 0
                for c in range(2):
                    for i in (range(0, 17) if c == 0 else range(16, 32)):
                        op("pool", lambda: G.affine_select(out=mtmp[:], in_=zf[:], pattern=[[1, 128]], compare_op=ALU.is_ge, fill=NEG,
                                                           base=128 * i - 31 - 2048 * c, channel_multiplier=-16),
                           reads=[zf], writes=[mtmp])
                        op("pool", lambda: G.tensor_copy(out=cm_all[:, mi, :], in_=mtmp[:]), reads=[mtmp], writes=[cm_all])
                        cmpmask[(c, i)] = mi; mi += 1
                ebf = fw.sb("ebf", [64, S], F32)
                op("pool", lambda: G.memset(ebf[:], 1.0), writes=[ebf])
                op("pool", lambda: G.affine_select(out=ebf[:], in_=ebf[:], pattern=[[1, S]], compare_op=ALU.is_ge, fill=0.0,
                                                   base=0, channel_multiplier=-64), reads=[ebf], writes=[ebf])
                op("pool", lambda: G.affine_select(out=ebf[:], in_=ebf[:], pattern=[[-1, S]], compare_op=ALU.is_ge, fill=0.0,
                                                   base=63, channel_multiplier=64), reads=[ebf], writes=[ebf])
                op("pool", lambda: G.tensor_copy(out=ksE[1][0:64, :], in_=ebf[:]), reads=[ebf, ksE[1]], writes=[ksE[1]])
                ebf2 = fw.sb("ebf2", [128, S], F32)
                op("pool", lambda: G.memset(ebf2[:], 1.0), writes=[ebf2])
                op("pool", lambda: G.affine_select(out=ebf2[:], in_=ebf2[:], pattern=[[1, S]], compare_op=ALU.is_ge, fill=0.0,
                                                   base=4096, channel_multiplier=-64), reads=[ebf2], writes=[ebf2])
                op("pool", lambda: G.affine_select(out=ebf2[:], in_=ebf2[:], pattern=[[-1, S]], compare_op=ALU.is_ge, fill=0.0,
                                                   base=-4033, channel_multiplier=64), reads=[ebf2], writes=[ebf2])
                op("pool", lambda: G.tensor_copy(out=ksE[0][64:128, :], in_=ebf2[64:128, :]), reads=[ebf2, ksE[0]], writes=[ksE[0]])
                for hh in range(2):
                    hs = slice(hh * 64, (hh + 1) * 64)
                    op("pool", lambda: G.memset(TMm[hs, 0:61 + hh], 1.0), reads=[TMm], writes=[TMm])
                    op("pool", lambda: G.memset(TMm[hs, 61 + hh:126], 0.0), reads=[TMm], writes=[TMm])
                    op("pool", lambda: G.memset(TB[hs, 0:61 + hh], 0.0), reads=[TB], writes=[TB])
                    op("pool", lambda: G.memset(TB[hs, 61 + hh:63 + hh], 1e4), reads=[TB], writes=[TB])
                    op("pool", lambda: G.memset(TB[hs, 63 + hh:126], -1e4), reads=[TB], writes=[TB])
                op("pool", lambda: G.memset(vsA[:, :, :, 64:65], 1.0), writes=[vsA])
                op("pool", lambda: G.memset(vwA[:, :, :, 64:65], 1.0), writes=[vwA])
                op("pool", lambda: G.memset(vcx[:], 0.0), writes=[vcx])
                op("pool", lambda: G.memset(vcx[:, :, :, 64:65], 1.0), reads=[vcx], writes=[vcx])
                covf = fw.sb("covf", [128, 2, 64], F32)
                op("pool", lambda: G.memset(covf[:], 1.0), writes=[covf])
                op("pool", lambda: G.affine_select(out=covf[:], in_=covf[:], pattern=[[-128, 2], [4, 64]], compare_op=ALU.is_ge,
                                                   fill=0.0, base=3, channel_multiplier=-1), reads=[covf], writes=[covf])
                op("pool", lambda: G.affine_select(out=covf[:], in_=covf[:], pattern=[[128, 2], [-4, 64]], compare_op=ALU.is_ge,
                                                   fill=0.0, base=1, channel_multiplier=1), reads=[covf], writes=[covf])
                for g in range(2):
                    op("pool", lambda: G.tensor_copy(out=vcx[:, :, g, 65:129], in_=covf[:]), reads=[covf, vcx], writes=[vcx])
                op("pool", lambda: G.memset(kcmpT[:], 0.0), writes=[kcmpT])
            fw.pop()

            v3 = lambda ap: ap.rearrange("p (g d) -> p g d", d=64)
            b8 = lambda ap: ap.unsqueeze(2).to_broadcast([128, 8, 64])

            for b in range(NB_RUN):
                fw.push()
                if True:
                    W0 = fw.sb("W0", [128, 8, 2328], BF16, dma="sw")
                    Wq = fw.sb("Wq", [128, 8, 512], BF16, dma="sw")
                    fw.dma_group("pool", [(W0[:, k, :], win0_d[k * 128:(k + 1) * 128, :]) for k in range(8)], W0, writes=[W0])
                    fw.dma_group("pool", [(Wq[:, k, r * 128:(r + 1) * 128].rearrange("p (g d) -> p g d", g=2),
                                           win0_d[k * 128:(k + 1) * 128, 0:512].rearrange("p (g r d) -> p g r d", g=2, r=4)[:, :, r, :])
                                          for k in range(8) for r in range(4)], Wq, writes=[Wq])
                    sc1 = fw.sb("sc1", [128, D], F32, dma=True); sh = fw.sb("sh", [128, D], F32, dma=True)
                    load_mod(0, b, [(sh, 0), (sc1, 1)])
                    xin = [fw.sb("xin%d" % i, [128, D], F32, dma=True) for i in range(3)]
                    hT = [fw.sb("hT%d" % i, [128, 8, 512], BF16) for i in range(2)]
                    hp = HPrep(sc1, sh)
                    psF = [fw.ps("psF%d" % i, [128, 512], F32) for i in range(2)]
                    psA = fw.ps("psA", [128, 512], F32); psB = fw.ps("psB", [128, 512], F32)
                    psC = fw.ps("psC", [128, 512], F32); psM = fw.ps("psM", [128, 512], F32)
                    ug = fw.sb("ug", [128, 512], F32); vg = fw.sb("vg", [128, 512], F32)
                    sq = fw.sb("sq", [128, 512], F32)
                    gst = fw.sb("gst", [128, 6, 8], F32)
                    vn = fw.sb("vn", [128, 512], BF16)
                    og = [fw.sb("og%d" % i, [128, 512], BF16, dma=True) for i in range(2)]
                    xcount = [0]
                    pend = {}

                    def p1_prep_a(mt_, sub_):
                        if mt_ >= P1_MT:
                            return
                        i_ = mt_ * 4 + sub_
                        xt = xin[xcount[0] % 3]; xcount[0] += 1
                        fw.dma("sp", xt[:], x_d[b, i_ * 128:(i_ + 1) * 128, :], xt, writes=[xt])
                        pend[(mt_, sub_)] = hp.run_a(xt)

                    def p1_prep_b(mt_, sub_):
                        if mt_ >= P1_MT:
                            return
                        hp.run_b(pend.pop((mt_, sub_)), hT[mt_ % 2], sub_ * 128)

                    for sub in range(4):
                        p1_prep_a(0, sub); p1_prep_b(0, sub)
                    for mt in range(P1_MT):
                        hTm = hT[mt % 2]
                        fm = []
                        for r in range(4):
                            fm.append((Wq[:, :, r * 128:(r + 1) * 128], qT[:, r, mt * 512:(mt + 1) * 512], qT))
                        fm.append((W0[:, :, 512:640], kcT[:, mt * 512:(mt + 1) * 512], kcT))
                        fm.append((W0[:, :, 640:768], vcT[:, mt * 512:(mt + 1) * 512], vcT))
                        fm.append((W0[:, :, 768:896], None, "ks"))
                        fm.append((W0[:, :, 1024:1152], kwT[:, mt * 512:(mt + 1) * 512], kwT))
                        for fi, (wap, dst, dstT) in enumerate(fm):
                            pf = psF[fi % 2]
                            for k in range(8):
                                op("pe", lambda: PE.matmul(pf[:], lhsT=wap[:, k], rhs=hTm[:, k, :], start=(k == 0), stop=(k == 7)),
                                   reads=[W0, Wq, hTm], writes=[pf])
                            if dstT == "ks":
                                op("act", lambda: A.copy(out=ksE[0][0:64, mt * 512:(mt + 1) * 512], in_=pf[0:64, :]), reads=[pf, ksE[0]], writes=[ksE[0]])
                                op("dve", lambda: V.tensor_copy(out=ksE[1][64:128, mt * 512:(mt + 1) * 512], in_=pf[64:128, :]), reads=[pf, ksE[1]], writes=[ksE[1]])
                            elif fi % 2 == 0:
                                op("act", lambda: A.copy(out=dst, in_=pf[:]), reads=[pf], writes=[dstT])
                            else:
                                op("dve", lambda: V.tensor_copy(out=dst, in_=pf[:]), reads=[pf], writes=[dstT])
                        for sub in range(4):
                            i = mt * 4 + sub
                            lh = lambda k: hTm[:, k, sub * 128:(sub + 1) * 128]
                            for k in range(8):
                                op("pe", lambda: PE.matmul(psA[:, 0:408], lhsT=lh(k), rhs=W0[:, k, 896:1304], start=(k == 0), stop=(k == 7)),
                                   reads=[W0, hTm], writes=[psA])
                            for k in range(8):
                                op("pe", lambda: PE.matmul(psB[:], lhsT=lh(k), rhs=W0[:, k, 1304:1816], start=(k == 0), stop=(k == 7)),
                                   reads=[W0, hTm], writes=[psB])
                            for k in range(8):
                                op("pe", lambda: PE.matmul(psC[:], lhsT=lh(k), rhs=W0[:, k, 1816:2328], start=(k == 0), stop=(k == 7)),
                                   reads=[W0, hTm], writes=[psC])
                            p1_prep_a(mt + 1, sub)
                            op("dve", lambda: V.tensor_copy(out=vsA[:, i, :, 0:64], in_=v3(psA[:, 0:128])), reads=[psA], writes=[vsA])
                            op("dve", lambda: V.tensor_copy(out=vwA[:, i, :, 0:64], in_=v3(psA[:, 256:384])), reads=[psA], writes=[vwA])
                            op("dve", lambda: V.tensor_copy(out=gat[:, i, :], in_=psA[:, 384:408]), reads=[psA], writes=[gat])
                            op("act", lambda: A.activation(out=ug[:], in_=psB[:], func=AF.Gelu_apprx_tanh), reads=[psB], writes=[ug])
                            op("act", lambda: A.activation(out=vg[:], in_=psC[:], func=AF.Gelu_apprx_tanh), reads=[psC], writes=[vg])
                            op("dve", lambda: V.tensor_reduce(out=gst[:, 0, :], in_=v3(vg[:]), axis=AX.X, op=ALU.add), reads=[vg], writes=[gst])
                            tt("dve", sq[:], vg[:], vg[:], ALU.mult, [vg], [sq])
                            op("dve", lambda: V.tensor_reduce(out=gst[:, 1, :], in_=v3(sq[:]), axis=AX.X, op=ALU.add), reads=[sq, gst], writes=[gst])
                            op("dve", lambda: V.tensor_scalar(out=gst[:, 2, :], in0=gst[:, 0, :], scalar1=1.0 / 64, scalar2=None, op0=ALU.mult),
                               reads=[gst], writes=[gst])
                            tt("dve", gst[:, 5, :], gst[:, 2, :], gst[:, 2, :], ALU.mult, [gst], [gst])
                            op("dve", lambda: V.scalar_tensor_tensor(out=gst[:, 3, :], in0=gst[:, 1, :], scalar=1.0 / 64, in1=gst[:, 5, :],
                                                                     op0=ALU.mult, op1=ALU.subtract), reads=[gst], writes=[gst])
                            op("dve", lambda: V.tensor_scalar_add(out=gst[:, 3, :], in0=gst[:, 3, :], scalar1=EPS), reads=[gst], writes=[gst])
                            op("pool", lambda: G.tensor_tensor(out=gst[:, 4, :], in0=gst[:, 3, :], in1=mhalf[:], op=ALU.pow),
                               reads=[gst, mhalf], writes=[gst])
                            tt("dve", v3(sq[:]), v3(vg[:]), b8(gst[:, 2, :]), ALU.subtract, [vg, gst], [sq])
                            tt("dve", v3(sq[:]), v3(sq[:]), b8(gst[:, 4, :]), ALU.mult, [sq, gst], [sq])
                            tt("dve", vn[:], sq[:], ngb[:], ALU.mult, [sq, ngb], [vn])
                            for gm in range(8):
                                op("pe", lambda: PE.matmul(psM[:, gm * 64:(gm + 1) * 64], lhsT=WsT[:, gm, :], rhs=vn[:, gm * 64:(gm + 1) * 64],
                                                           start=True, stop=True), reads=[WsT, vn], writes=[psM])
                            p1_prep_b(mt + 1, sub)
                            tt("dve", v3(sq[:]), v3(psM[:]), b8(bsT[:]), ALU.add, [psM, bsT], [sq])
                            ogt = og[i % 2]
                            tt("dve", ogt[:], sq[:], ug[:], ALU.mult, [sq, ug], [ogt])
                            fw.dma("sp", ogm_d[b, i * 128:(i + 1) * 128, :], ogt[:], ogt, reads=[ogt], writes=[dbuf("ogm", b, i)])
                fw.pop()
                if upto < 2:
                    continue
                fw.push()
                if True:
                    w1 = fw.sb("cw1", [128, 2, 32, 128], BF16, dma="sw")
                    for hh in range(2):
                        for kv in range(2):
                            fw.dma("pool", w1[hh * 64:(hh + 1) * 64, kv, :, :], w1_d[kv].rearrange("(l d) h -> d l h", d=64), w1, writes=[w1])
                    psH = [fw.ps("psH%d" % i, [128, 256], F32) for i in range(2)]
                    psK = fw.ps("psK", [128, 256], F32); psV = fw.ps("psV", [128, 256], F32)
                    for kv in range(2):
                        src = kcT if kv == 0 else vcT
                        hbs = []
                        for g in range(2):
                            ph = psH[g]
                            for l in range(32):
                                op("pe", lambda: PE.matmul(ph[:, 0:255], lhsT=w1[g * 64:(g + 1) * 64, kv, l, :],
                                                           rhs=src[g * 64:(g + 1) * 64, l:l + 16 * 254 + 1:16], start=(l == 0), stop=(l == 31)),
                                   reads=[w1, src], writes=[ph])
                            hb_g = fw.sb("hbg_%d%d" % (kv, g), [128, 256], BF16)
                            op("pool", lambda: G.memset(hb_g[:], 0.0), writes=[hb_g])
                            op("act", lambda: A.activation(out=hb_g[:, 0:255], in_=ph[:, 0:255], func=AF.Gelu_apprx_tanh,
                                                           bias=cbias[:, kv:kv + 1], scale=1.0), reads=[ph, cbias, hb_g], writes=[hb_g])
                            hbs.append(hb_g)
                        if kv == 0:
                            for g in range(2):
                                op("pe", lambda: PE.matmul(psK[:], lhsT=w2pad[:, g, :], rhs=hbs[g][:], start=(g == 0), stop=(g == 1)),
                                   reads=[w2pad, hbs[g]], writes=[psK])
                            op("dve", lambda: V.tensor_copy(out=kcmpT[:], in_=psK[:]), reads=[psK], writes=[kcmpT])
                            if debug:
                                kd = fw.sb("kd", [128, 256], F32, dma=True)
                                op("dve", lambda: V.tensor_copy(out=kd[:], in_=psK[:]), reads=[psK], writes=[kd])
                                fw.dma("sp", dbg["kcc"][b], kd[:], kd, reads=[kd], writes=[dbuf("kcc", b)])
                        else:
                            for g in range(2):
                                for c in range(2):
                                    op("pe", lambda: PE.matmul(psV[:, (g * 2 + c) * 64:(g * 2 + c + 1) * 64], lhsT=hbs[g][:, c * 128:(c + 1) * 128],
                                                               rhs=w2[:, 1, :], start=True, stop=True), reads=[w2, hbs[g]], writes=[psV])
                            for g in range(2):
                                op("dve", lambda: V.tensor_copy(out=vcx[:, :, g, 0:64],
                                                                in_=psV[:, g * 128:(g + 1) * 128].rearrange("p (c d) -> p c d", d=64)),
                                   reads=[psV], writes=[vcx])
                fw.pop()
                if upto < 3:
                    continue
                fw.push()
                if True:
                    Wo0 = fw.sb("Wo0", [128, 8, D], BF16, dma="sw")
                    fw.dma_group("pool", [(Wo0[:, k, :], wout0_d[k * 128:(k + 1) * 128, :]) for k in range(8)], Wo0, writes=[Wo0])
                    lng = fw.sb("lng", [128, D], F32, dma=True); lnb = fw.sb("lnb", [128, D], F32, dma=True)
                    load_bc(lng, lng_d[0]); load_bc(lnb, lnb_d[0])
                    g1 = fw.sb("g1", [128, D], F32, dma=True)
                    load_mod(0, b, [(g1, 2)])
                    ep = Epilogue(g1, lng, lnb)
                    op("act", lambda: A.activation(out=gat[:].rearrange("p a b -> p (a b)"), in_=gat[:].rearrange("p a b -> p (a b)"), func=AF.Sigmoid),
                       reads=[gat], writes=[gat])
                    psS = [fw.ps("psS%d" % i, [128, 512], F32) for i in range(3)]
                    psTt = fw.ps("psTt", [128, 8, 128], BF16)
                    psOc = fw.ps("psOc", [128, 512], F32); psImp = fw.ps("psImp", [128, 512], F32)
                    psOs = fw.ps("psOs", [128, 512], F32); psOw = fw.ps("psOw", [128, 512], F32)
                    o4 = lambda ps_: ps_[:, 0:260].rearrange("p (r e) -> p r e", e=65)
                    ebuf = [fw.sb("ebuf%d" % i, [128, 4, 128], BF16) for i in range(4)]
                    ecount = [0]
                    cats = [fw.sb("cat%d" % i, [128, D], BF16, dma=True) for i in range(2)]
                    ocS = [fw.sb("ocS%d" % i, [128, 4, 65], F32) for i in range(2)]
                    nsel2s = [fw.sb("nsel2_%d" % i, [128, 2, 64], BF16) for i in range(2)]
                    catT = fw.sb("catT", [128, 8, 128], BF16)
                    sm = fw.sb("sm", [128, 3, 4], F32)
                    ff = fw.sb("ff", [128, 3, 4], F32)
                    rs4 = fw.sb("rs4", [128, 4], F32)
                    impn = fw.sb("impn", [128, 4, 64], F32)
                    imp = fw.sb("imp", [128, 64], F32)
                    impa = fw.sb("impa", [128, 64], F32)
                    wk = fw.sb("wk", [128, 64], F32)
                    m8 = fw.sb("m8", [128, 8], F32)
                    nsel = fw.sb("nsel", [128, 64], BF16)
                    nsel2 = fw.sb("nsel2", [128, 2, 64], BF16)
                    qzs = [[fw.sb("qzs%d_%d" % (g_, k_), [128, 4, 128], BF16) for k_ in range(2)] for g_ in range(2)]
                    qz = [[fw.sb("qz%d_%d" % (g_, k_), [128, 4, 128], BF16) for k_ in range(2)] for g_ in range(2)]
                    for g_ in range(2):
                        for k_ in range(2):
                            op("pool", lambda: G.memset(qz[g_][k_][:], 0.0), writes=[qz[g_][k_]])
                    o1 = fw.sb("o1", [128, 4, 64], F32); o2 = fw.sb("o2", [128, 4, 64], F32)
                    xres = [fw.sb("xres%d" % i, [128, D], F32, dma=True) for i in range(2)]
                    if debug:
                        ond = fw.sb("ond", [128, 512], F32, dma=True)
                    bc4 = lambda ap: ap.unsqueeze(1).to_broadcast([ap.shape[0], 4, 128])

                    qzt_cur = [None]

                    def emit_S(job):
                        if job.get("pre"):
                            job["pre"]()
                        pS = psS[job["k"] % 3]
                        pS4 = pS[:].rearrange("p (r t) -> p r t", r=4)
                        extra = job["extra"]
                        op("pe", lambda: PE.matmul(pS4, lhsT=job["kT_ap"], rhs=job["qs"], start=True, stop=(len(extra) == 0)),
                           reads=[job["kT_t"], job["qzt"]], writes=[pS])
                        for xi, (la, ra, rd) in enumerate(extra):
                            op("pe", lambda: PE.matmul(pS4, lhsT=la, rhs=ra, start=False, stop=(xi == len(extra) - 1)),
                               reads=rd, writes=[pS])

                    def emit_EXP_PV(job):
                        pS = psS[job["k"] % 3]; e = ebuf[job["k"] % 4]
                        op("act", lambda: A.activation(out=e[:].rearrange("p r t -> p (r t)"), in_=pS[:], func=AF.Exp, scale=0.125),
                           reads=[pS], writes=[e])
                        for r in range(4):
                            for (va, po_fn, pot) in job["pvs"]:
                                op("pe", lambda: PE.matmul(po_fn(r), lhsT=e[:, r, :], rhs=va, start=(job["first"] and r == 0), stop=job["last"],
                                                           skip_group_check=True), reads=[e, job["vaug_t"]], writes=[pot])
                        if job.get("post"):
                            job["post"]()

                    def run_jobs(jobs):
                        n = len(jobs)
                        for idx, job in enumerate(jobs):
                            job["k"] = ecount[0] + idx
                        ecount[0] += n
                        nextS = 0
                        for k in range(n):
                            while nextS < n and nextS <= k + 2:
                                jb = jobs[nextS]
                                if jb.get("pre") and jb.get("needs", -1) > k - 1:
                                    break
                                emit_S(jb); nextS += 1
                            assert nextS > k
                            emit_EXP_PV(jobs[k])
                            for _ in range(3):
                                if deferred:
                                    deferred.pop(0)()

                    deferred = []

                    def topk_dve(i, g):
                        oc = ocS[g]; nsel2 = nsel2s[g]
                        op("dve", lambda: V.tensor_copy(out=oc[:], in_=o4(psOc)), reads=[psOc], writes=[oc])
                        op("dve", lambda: V.tensor_scalar_max(out=rs4[:], in0=oc[:, :, 64], scalar1=1e-30), reads=[oc], writes=[rs4])
                        op("dve", lambda: V.reciprocal(out=rs4[:], in_=rs4[:]), reads=[rs4], writes=[rs4])
                        tt("dve", impn[:], psImp[:, 0:256].rearrange("p (r j) -> p r j", j=64), rs4[:].unsqueeze(2).to_broadcast([128, 4, 64]),
                           ALU.mult, [psImp, rs4], [impn])
                        op("dve", lambda: V.tensor_reduce(out=imp[:], in_=impn[:].rearrange("p r j -> p j r"), axis=AX.X, op=ALU.add),
                           reads=[impn], writes=[imp])
                        so = 62 - 2 * i
                        tt("dve", impa[:], imp[:], TMm[:, so:so + 64], ALU.mult, [imp, TMm], [impa])
                        tt("dve", impa[:], impa[:], TB[:, so:so + 64], ALU.add, [impa, TB], [impa])
                        op("dve", lambda: V.memset(impa[:, 0:1], 1e4), reads=[impa], writes=[impa])
                        op("dve", lambda: V.max(out=m8[:], in_=impa[:]), reads=[impa], writes=[m8])
                        op("dve", lambda: V.match_replace(out=wk[:], in_to_replace=m8[:], in_values=impa[:], imm_value=-1e9),
                           reads=[m8, impa], writes=[wk])
                        op("dve", lambda: V.max(out=m8[:], in_=wk[:]), reads=[wk], writes=[m8])
                        op("dve", lambda: V.tensor_scalar(out=wk[:], in0=impa[:], scalar1=m8[:, 7:8], scalar2=-NEG, op0=ALU.is_ge, op1=ALU.mult),
                           reads=[impa, m8], writes=[wk])
                        op("dve", lambda: V.tensor_scalar_add(out=nsel2[:], in0=wk[:].unsqueeze(1).to_broadcast([128, 2, 64]), scalar1=NEG),
                           reads=[wk], writes=[nsel2])

                    def negT_pe(g, qzst):
                        og_ = slice((1 - g) * 64, (2 - g) * 64)
                        nsel2 = nsel2s[g]
                        op("pe", lambda: PE.transpose(psTt[:, 0, :], nsel2[:].rearrange("p a b -> p (a b)"), ident[:]), reads=[nsel2, ident], writes=[psTt])
                        op("dve", lambda: V.tensor_copy(out=qzst[og_, :, :], in_=psTt[og_, 0, :].unsqueeze(1).to_broadcast([64, 4, 128])),
                           reads=[psTt, qzst], writes=[qzst])

                    def tile_start(i):
                        xr = xres[i % 2]; cat = cats[i % 2]
                        fw.dma("sp", xr[:], x_d[b, i * 128:(i + 1) * 128, :], xr, writes=[xr])
                        fw.dma("sp", cat[:, 512:1024], ogm_d[b, i * 128:(i + 1) * 128, :], cat, reads=[dbuf("ogm", b, i)], writes=[cat])
                        for ii in ([0, 1] if i == 0 else [i + 1]):
                            if ii >= P3_TILES:
                                continue
                            for g_ in range(2):
                                gs_ = slice(g_ * 64, (g_ + 1) * 64)
                                a_ = qz[g_][ii % 2]; b_ = qzs[g_][ii % 2]
                                op("dve", lambda: V.tensor_copy(out=a_[gs_, :, :], in_=qT[gs_, :, ii * 128:(ii + 1) * 128]), reads=[qT, a_], writes=[a_])
                                op("pool", lambda: G.tensor_copy(out=b_[gs_, :, :], in_=qT[gs_, :, ii * 128:(ii + 1) * 128]), reads=[qT, b_], writes=[b_])

                    def combine(i, g):
                        oc = ocS[g]; cat = cats[i % 2]
                        op("dve", lambda: V.tensor_scalar_max(out=sm[:, 0, :], in0=oc[:, :, 64], scalar1=1e-30), reads=[oc, sm], writes=[sm])
                        for br, pso in ((1, psOs), (2, psOw)):
                            op("dve", lambda: V.tensor_scalar_max(out=sm[:, br, :], in0=o4(pso)[:, :, 64], scalar1=1e-30), reads=[pso, sm], writes=[sm])
                        op("dve", lambda: V.reciprocal(out=sm[:], in_=sm[:]), reads=[sm], writes=[sm])
                        tt("dve", ff[:], sm[:], gat[:, i, g * 12:(g + 1) * 12].rearrange("p (r b) -> p b r", b=3), ALU.mult, [sm, gat], [ff])
                        fb = lambda br: ff[:, br, :].unsqueeze(2).to_broadcast([128, 4, 64])
                        tt("dve", o2[:], o4(psOw)[:, :, 0:64], fb(2), ALU.mult, [psOw, ff], [o2])
                        tt("dve", o1[:], o4(psOs)[:, :, 0:64], fb(1), ALU.mult, [psOs, ff], [o1])
                        tt("dve", o1[:], o1[:], o2[:], ALU.add, [o1, o2], [o1])
                        tt("dve", o2[:], oc[:, :, 0:64], fb(0), ALU.mult, [oc, ff], [o2])
                        if debug:
                            tt("pool", ond[:, g * 256:(g + 1) * 256].rearrange("p (r d) -> p r d", d=64), o1[:], o2[:], ALU.add, [o1, o2], [ond])
                            if g == 1:
                                fw.dma("sp", dbg["onsa"][b, i * 128:(i + 1) * 128, :], ond[:], ond, reads=[ond], writes=[dbuf("onsa", b, i)])
                        tt("dve", cat[:, g * 256:(g + 1) * 256].rearrange("p (r d) -> p r d", d=64), o1[:], o2[:], ALU.add, [o1, o2], [cat])

                    def outproj(i):
                        while deferred:
                            deferred.pop(0)()
                        cat = cats[i % 2]; xr = xres[i % 2]
                        for c in range(8):
                            op("pe", lambda: PE.transpose(psTt[:, c, :], cat[:, c * 128:(c + 1) * 128], ident[:]), reads=[cat, ident], writes=[psTt])
                        op("act", lambda: A.copy(out=catT[:], in_=psTt[:]), reads=[psTt], writes=[catT])
                        halves = [(psOs, psOs[:]), (psOw, psOw[:])]
                        for half in range(2):
                            pyt, pya = halves[half]
                            for c in range(8):
                                op("pe", lambda: PE.matmul(pya, lhsT=catT[:, c, :], rhs=Wo0[:, c, half * 512:(half + 1) * 512],
                                                           start=(c == 0), stop=(c == 7)), reads=[catT, Wo0], writes=[pyt])
                        st_ = ep.steps(halves, xr, x1_d[b, i * 128:(i + 1) * 128, :], dbuf("x1", b, i))
                        for f_ in st_[:3]:
                            f_()
                        deferred.extend(st_[3:])

                    jobs = []
                    last_sel_g1 = -1
                    for i in range(P3_TILES):
                        def mk(kT_ap, kT_t, extra, pvs, vaug_t, first, last, qt):
                            return dict(kT_ap=kT_ap, kT_t=kT_t, qs=qt[:], qzt=qt, extra=extra, pvs=pvs, vaug_t=vaug_t, first=first, last=last)
                        chunks = [0] if i < 16 else [0, 1]
                        last_cmp = {}
                        for g in range(2):
                            qzt = qz[g][i % 2]
                            for ci, c in enumerate(chunks):
                                extra = []
                                if (c, i) in cmpmask:
                                    extra.append((ident[:], bc4(cm_all[:, cmpmask[(c, i)], :]), [ident, cm_all]))
                                jb = mk(kcmpT[:, c * 128:(c + 1) * 128], kcmpT, extra,
                                        [(vcx[:, c, g, 0:65], lambda r: o4(psOc)[:, r, :], psOc),
                                         (vcx[:, c, g, 65:129], lambda r: psImp[:, r * 64:(r + 1) * 64], psImp)],
                                        vcx, ci == 0, ci == len(chunks) - 1, qzt)
                                if g == 0 and ci == 0:
                                    jb["pre"] = (lambda i_=i: tile_start(i_)); jb["needs"] = -1
                                jobs.append(jb)
                            jobs[-1]["post"] = (lambda i_=i, g_=g: topk_dve(i_, g_))
                            last_cmp[g] = len(jobs) - 1
                        for g in range(2):
                            qzt = qz[g][i % 2]; qzst = qzs[g][i % 2]
                            j0 = max(0, i - 4)
                            for j in range(j0, i + 1):
                                extra = []
                                if j == i:
                                    extra.append((ident[:], bc4(McNeg[:]), [ident, McNeg]))
                                elif j == i - 4:
                                    extra.append((ident[:], bc4(MwNeg[:]), [ident, MwNeg]))
                                jb = mk(kwT[:, j * 128:(j + 1) * 128], kwT, extra,
                                        [(vwA[:, j, g, :], lambda r: o4(psOw)[:, r, :], psOw)], vwA, j == j0, j == i, qzt)
                                if g == 0 and j == j0 and i > 0:
                                    jb["pre"] = (lambda i_=i: outproj(i_ - 1)); jb["needs"] = last_sel_g1
                                jobs.append(jb)
                            for j in range(i + 1):
                                extra = []
                                if j == i:
                                    extra.append((ident[:], bc4(McNeg[:]), [ident, McNeg]))
                                jb = mk(ksE[g][:, j * 128:(j + 1) * 128], ksE[g], extra,
                                        [(vsA[:, j, g, :], lambda r: o4(psOs)[:, r, :], psOs)], vsA, j == 0, j == i, qzst)
                                if j == 0:
                                    jb["pre"] = (lambda g_=g, q_=qzst: negT_pe(g_, q_)); jb["needs"] = last_cmp[g]
                                jobs.append(jb)
                            jobs[-1]["post"] = (lambda i_=i, g_=g: combine(i_, g_))
                            if g == 1:
                                last_sel_g1 = len(jobs) - 1
                    run_jobs(jobs)
                    outproj(P3_TILES - 1)
                    while deferred:
                        deferred.pop(0)()
                fw.pop()
        fw.pop()
        if upto < 4:
            return nc

        def ffn_phase(ls, layer, src_d, src_name, dst_d, dst_name):
            fw.push()
            if True:
                Wi = fw.sb("Wi", [128, 8, 2 * FH], BF16, dma="sw")
                Wo = fw.sb("Wo", [128, 22, D], BF16, dma="sw")
                fw.dma_group("pool", [(Wi[:, k, :], fwi_d[layer, k * 128:(k + 1) * 128, :]) for k in range(8)], Wi, writes=[Wi])
                fw.dma_group("pool", [(Wo[:, k, :], fwo_d[layer, k * 128:(k + 1) * 128, :]) for k in range(22)], Wo, writes=[Wo])
                lng = fw.sb("lng", [128, D], F32, dma=True); lnb = fw.sb("lnb", [128, D], F32, dma=True)
                load_bc(lng, lng_d[ls]); load_bc(lnb, lnb_d[ls])
                sc1 = fw.sb("sc1", [128, D], F32, dma=True); sh = fw.sb("sh", [128, D], F32, dma=True); g1 = fw.sb("g1", [128, D], F32, dma=True)
                NX = 4
                xin = [fw.sb("xin%d" % i, [128, D], F32, dma=True) for i in range(NX)]
                hT = [fw.sb("hT%d" % i, [128, 8, 256], BF16) for i in range(2)]
                actT = fw.sb("actT", [128, 22, 256], BF16)
                sg = [fw.sb("sg%d" % i, [128, 256], BF16) for i in range(2)]
                hp = HPrep(sc1, sh, nhb=2, tmp_dt=BF16)
                ep = Epilogue(g1, lng, lnb)
                psGU = [fw.ps("psGU%d" % i, [128, 2, 256], F32) for i in range(2)]
                psYs = [fw.ps("psY%d" % i, [128, 2, 512], F32) for i in range(2)]
                tiles = [(b, mt) for b in range(NB) for mt in range(S // 256)]
                state = {}
                xc = [0]

                def prep_a(idx):
                    b, mt = tiles[idx]
                    if mt == 0:
                        load_mod(ls, b, [(sh, 0), (sc1, 1)])
                    hTm = hT[idx % 2]
                    xts = []; hbs = []
                    for sub in range(2):
                        i = mt * 2 + sub
                        xt = xin[xc[0] % NX]; xc[0] += 1
                        fw.dma("sp", xt[:], src_d[b, i * 128:(i + 1) * 128, :], xt, reads=[dbuf(src_name, b, i)], writes=[xt])
                        hbs.append(hp.run_a(xt))
                        xts.append(xt)
                    state[idx] = (hTm, xts, hbs)

                def prep_b(idx):
                    hTm, xts, hbs = state[idx]
                    for sub in range(2):
                        hp.run_b(hbs[sub], hTm, sub * 128)

                def up(idx, hcs):
                    hTm = state[idx][0]
                    for hc in hcs:
                        pg = psGU[hc % 2]; sgt = sg[hc % 2]
                        for k in range(8):
                            op("pe", lambda: PE.matmul(pg[:, 0, :], lhsT=Wi[:, k, hc * 128:(hc + 1) * 128], rhs=hTm[:, k, :],
                                                       start=(k == 0), stop=(k == 7)), reads=[Wi, hTm], writes=[pg])
                        for k in range(8):
                            op("pe", lambda: PE.matmul(pg[:, 1, :], lhsT=Wi[:, k, FH + hc * 128:FH + (hc + 1) * 128], rhs=hTm[:, k, :],
                                                       start=(k == 0), stop=(k == 7)), reads=[Wi, hTm], writes=[pg])
                        op("act", lambda: A.activation(out=sgt[:], in_=pg[:, 0, :], func=AF.Silu), reads=[pg], writes=[sgt])
                        tt("dve", actT[:, hc, :], pg[:, 1, :], sgt[:], ALU.mult, [pg, sgt], [actT])
                        for _ in range(2):
                            if deferred:
                                deferred.pop(0)()

                def down_ep(idx):
                    b, mt = tiles[idx]
                    xts = state.pop(idx)[1]
                    if mt == 0:
                        load_mod(ls, b, [(g1, 2)])
                    for sub in range(2):
                        i = mt * 2 + sub
                        psY = psYs[sub]
                        for half in range(2):
                            for hc in range(22):
                                op("pe", lambda: PE.matmul(psY[:, half, :], lhsT=actT[:, hc, sub * 128:(sub + 1) * 128],
                                                           rhs=Wo[:, hc, half * 512:(half + 1) * 512], start=(hc == 0), stop=(hc == 21)),
                                   reads=[actT, Wo], writes=[psY])
                        st_ = ep.steps([(psY, psY[:, 0, :]), (psY, psY[:, 1, :])], xts[sub], dst_d[b, i * 128:(i + 1) * 128, :], dbuf(dst_name, b, i))
                        nim = 3 if sub == 1 else len(st_)
                        for f_ in st_[:nim]:
                            f_()
                        deferred.extend(st_[nim:])

                deferred = []
                prep_a(0); prep_b(0)
                for idx in range(len(tiles)):
                    up(idx, range(0, 2))
                    if idx + 1 < len(tiles):
                        prep_a(idx + 1)
                    up(idx, range(2, 12))
                    if idx + 1 < len(tiles):
                        prep_b(idx + 1)
                    up(idx, range(12, 22))
                    while deferred:
                        deferred.pop(0)()
                    down_ep(idx)
                while deferred:
                    deferred.pop(0)()
            fw.pop()

        ffn_phase(1, 0, x1_d, "x1", x2_d, "x2")
        if upto < 5:
            return nc

        fw.push()
        if True:
            W1 = fw.sb("W1", [128, 8, 3 * D], BF16, dma="sw")
            Wo1 = fw.sb("Wo1", [128, 8, D], BF16, dma="sw")
            fw.dma_group("pool", [(W1[:, k, :], win1_d[k * 128:(k + 1) * 128, :]) for k in range(8)], W1, writes=[W1])
            fw.dma_group("pool", [(Wo1[:, k, :], wout1_d[k * 128:(k + 1) * 128, :]) for k in range(8)], Wo1, writes=[Wo1])
            cwf = fw.sb("cwf", [3, D], F32, dma=True)
            fw.dma("sp", cwf[:], cw_d, cwf, writes=[cwf])
            cw = fw.sb("cw", [128, 8, 3], F32)
            pscw = fw.ps("pscw", [128, 8, 3], F32)
            for cc in range(8):
                op("pe", lambda: PE.transpose(pscw[:, cc, :], cwf[:, cc * 128:(cc + 1) * 128], identf[0:3, 0:3]), reads=[cwf, identf], writes=[pscw])
            op("dve", lambda: V.tensor_copy(out=cw[:], in_=pscw[:]), reads=[pscw], writes=[cw])
            lng = fw.sb("lng", [128, D], F32, dma=True); lnb = fw.sb("lnb", [128, D], F32, dma=True)
            load_bc(lng, lng_d[2]); load_bc(lnb, lnb_d[2])
            sc1 = fw.sb("sc1", [128, D], F32, dma=True); sh = fw.sb("sh", [128, D], F32, dma=True); g1 = fw.sb("g1", [128, D], F32, dma=True)
            xin = [fw.sb("xin%d" % i, [128, D], F32, dma=True) for i in range(4)]
            hT = [fw.sb("hT%d" % i, [128, 8, 256], BF16) for i in range(2)]
            cz = fw.sb("cz", [128, 8, 258], F32)
            csb = [fw.sb("csb%d" % i, [128, 256], F32) for i in range(2)]
            y1 = [fw.sb("y1_%d" % i, [128, 256], F32) for i in range(2)]
            byT = fw.sb("byT", [128, 8, 256], BF16)
            hp = HPrep(sc1, sh)
            ep = Epilogue(g1, lng, lnb, nt=2)
            psP = [fw.ps("psP%d" % i, [128, 4, 256], F32) for i in range(2)]
            psY_ = fw.ps("psY", [128, 2, 512], F32)
            psYs = [psY_, psY_]
            tiles = [(b, mt) for b in range(NB) for mt in range(S // 256)]
            state = {}
            xc = [0]

            def prep_a(idx):
                b, mt = tiles[idx]
                if mt == 0:
                    load_mod(2, b, [(sh, 0), (sc1, 1)])
                hTm = hT[idx % 2]
                xts = []; hbs = []
                for sub in range(2):
                    i = mt * 2 + sub
                    xt = xin[xc[0] % 4]; xc[0] += 1
                    fw.dma("sp", xt[:], x2_d[b, i * 128:(i + 1) * 128, :], xt, reads=[dbuf("x2", b, i)], writes=[xt])
                    hbs.append(hp.run_a(xt))
                    xts.append(xt)
                state[idx] = (hTm, xts, hbs)

            def prep_b(idx):
                hTm, xts, hbs = state[idx]
                for sub in range(2):
                    hp.run_b(hbs[sub], hTm, sub * 128)

            deferred = []

            def compute(idx, ccs):
                b, mt = tiles[idx]
                hTm = state[idx][0]
                for cc in ccs:
                    if cc == 0:
                        if mt == 0:
                            op("pool", lambda: G.memset(cz[:, :, 0:2], 0.0), reads=[cz], writes=[cz])
                        else:
                            op("pool", lambda: G.tensor_copy(out=cz[:, :, 0:2], in_=cz[:, :, 256:258]), reads=[cz], writes=[cz])
                    pp = psP[cc % 2]; cs = csb[cc % 2]; yt = y1[cc % 2]
                    for part in range(3):
                        for k in range(8):
                            op("pe", lambda: PE.matmul(pp[:, part, :], lhsT=W1[:, k, part * D + cc * 128:part * D + (cc + 1) * 128],
                                                       rhs=hTm[:, k, :], start=(k == 0), stop=(k == 7)), reads=[W1, hTm], writes=[pp])
                    op("act", lambda: A.copy(out=cs[:], in_=pp[:, 1, :]), reads=[pp], writes=[cs])
                    tt("dve", cz[:, cc, 2:258], pp[:, 2, :], cs[:], ALU.mult, [pp, cs, cz], [cz])
                    op("act", lambda: A.activation(out=yt[:], in_=cz[:, cc, 0:256], func=AF.Copy, scale=cw[:, cc, 0:1]),
                       reads=[cz, cw], writes=[yt])
                    op("dve", lambda: V.scalar_tensor_tensor(out=yt[:], in0=cz[:, cc, 1:257], scalar=cw[:, cc, 1:2], in1=yt[:],
                                                             op0=ALU.mult, op1=ALU.add), reads=[cz, cw, yt], writes=[yt])
                    op("dve", lambda: V.scalar_tensor_tensor(out=yt[:], in0=cz[:, cc, 2:258], scalar=cw[:, cc, 2:3], in1=yt[:],
                                                             op0=ALU.mult, op1=ALU.add), reads=[cz, cw, yt], writes=[yt])
                    tt("dve", byT[:, cc, :], pp[:, 0, :], yt[:], ALU.mult, [pp, yt], [byT])
                    for _ in range(4):
                        if deferred:
                            deferred.pop(0)()

            def out_ep(idx):
                b, mt = tiles[idx]
                xts = state.pop(idx)[1]
                if mt == 0:
                    load_mod(2, b, [(g1, 2)])
                for sub in range(2):
                    i = mt * 2 + sub
                    psY = psYs[sub]
                    for half in range(2):
                        for cc in range(8):
                            op("pe", lambda: PE.matmul(psY[:, half, :], lhsT=byT[:, cc, sub * 128:(sub + 1) * 128],
                                                       rhs=Wo1[:, cc, half * 512:(half + 1) * 512], start=(cc == 0), stop=(cc == 7)),
                               reads=[byT, Wo1], writes=[psY])
                    st_ = ep.steps([(psY, psY[:, 0, :]), (psY, psY[:, 1, :])], xts[sub], x3_d[b, i * 128:(i + 1) * 128, :], dbuf("x3", b, i))
                    nim = 3
                    for f_ in st_[:nim]:
                        f_()
                    deferred.extend(st_[nim:])

            prep_a(0); prep_b(0)
            for idx in range(len(tiles)):
                compute(idx, range(0, 1))
                if idx + 1 < len(tiles):
                    prep_a(idx + 1)
                compute(idx, range(1, 5))
                if idx + 1 < len(tiles):
                    prep_b(idx + 1)
                compute(idx, range(5, 8))
                while deferred:
                    deferred.pop(0)()
                out_ep(idx)
            while deferred:
                deferred.pop(0)()
        fw.pop()
        if upto < 6:
            return nc

        ffn_phase(3, 1, x3_d, "x3", out_d, "out")
        fw.barrier()
    return nc


_CACHE = {}


def _prep_inputs(inputs):
    f = lambda a: np.ascontiguousarray(np.asarray(a, dtype=np.float32))
    common = {
        "ada_w": f(inputs["ada_w"]).reshape(4, D, 3 * D), "ada_b": f(inputs["ada_b"]).reshape(4, 3 * D),
        "ln_g": f(inputs["ln_g"]).reshape(4, D), "ln_b": f(inputs["ln_b"]).reshape(4, D),
        "even_w_in": f(inputs["even_w_in"])[0], "even_cmp_pos": f(inputs["even_cmp_pos"])[0],
        "even_cmp_w1": f(inputs["even_cmp_w1"])[0], "even_cmp_w2": f(inputs["even_cmp_w2"])[0],
        "even_gmlp_norm_g": f(inputs["even_gmlp_norm_g"])[0].reshape(512), "even_gmlp_ws": f(inputs["even_gmlp_ws"])[0],
        "even_gmlp_bs": f(inputs["even_gmlp_bs"])[0], "even_w_out": f(inputs["even_w_out"])[0],
        "odd_w_in": f(inputs["odd_w_in"])[0], "odd_conv_w": f(inputs["odd_conv_w"])[0], "odd_w_out": f(inputs["odd_w_out"])[0],
        "ffn_w_in": f(inputs["ffn_w_in"]), "ffn_w_out": f(inputs["ffn_w_out"]),
    }
    x = f(inputs["x"]); c = f(inputs["c"])
    maps = []
    for i in range(N_CORES):
        m = dict(common)
        m["x"] = x[NB * i:NB * (i + 1)]; m["c"] = c[NB * i:NB * (i + 1)]
        maps.append(m)
    return maps


def kernel(**inputs):
    if "nc" not in _CACHE:
        _CACHE["nc"] = build_program()
    nc = _CACHE["nc"]
    maps = _prep_inputs(inputs)
    res = run_bass_kernel_spmd(nc, maps, core_ids=list(range(N_CORES)))
    return np.concatenate([r["out"] for r in res.results], axis=0).astype(np.float32)
```

```python
import numpy as np
import concourse.bass as bass
import concourse.mybir as mybir
from concourse.bass_utils import run_bass_kernel_spmd
from contextlib import ExitStack

F32 = mybir.dt.float32; BF16 = mybir.dt.bfloat16; I32 = mybir.dt.int32
ALU = mybir.AluOpType; AF = mybir.ActivationFunctionType; AX = mybir.AxisListType

S = 4096; D = 1024; NB = 2; NT = S // 128
FH = 2816
NEG = -30000.0
ALPHA = 4.0 ** 0.25
EPS = 1e-5
N_CORES = 8
P3_TILES = NT
P1_MT = 8
SELMODE = 0
NB_RUN = NB
P3_PARTS = ('cmp', 'topk', 'sel', 'win', 'comb', 'out')


class Buf:
    __slots__ = ("name", "w", "r", "dsem")

    def __init__(self, name):
        self.name = name; self.w = {}; self.r = {}; self.dsem = None


class T:
    def __init__(self, h, buf):
        self.h = h; self.b = buf

    def __getitem__(self, k):
        return self.h[k]


class FW:
    def __init__(self, nc, ges, n_dsem=56):
        self.nc = nc; self.es = ges
        self.eng = {"pe": nc.tensor, "act": nc.scalar, "dve": nc.vector, "pool": nc.gpsimd, "sp": nc.sync}
        self.sems = {}; self.cnt = {}
        for e in ("pe", "act", "dve", "pool"):
            self.sems[e] = ges.enter_context(nc.semaphore("s_" + e)); self.cnt[e] = 0
        self.dpool = []; self.dpool_sw = []
        for i in range(n_dsem):
            k = "d%d" % i
            self.sems[k] = ges.enter_context(nc.semaphore("s_" + k)); self.cnt[k] = 0
            (self.dpool_sw if i < 10 else self.dpool).append(k)
        self.checked = []
        self.scopes = []
        self.seen = {e: {} for e in self.eng}
        self.uid = 0

    def push(self):
        st = ExitStack(); st.__enter__()
        self.scopes.append((self.es, self.checked, st))
        self.es = st; self.checked = []

    def pop(self):
        self.barrier()
        old_es, old_checked, st = self.scopes.pop()
        st.__exit__(None, None, None)
        for k in self.checked:
            (self.dpool_sw if int(k[1:]) < 10 else self.dpool).append(k)
        self.es = old_es; self.checked = old_checked

    def sb(self, name, shape, dt, dma=False):
        self.uid += 1
        h = self.es.enter_context(self.nc.sbuf_tensor("%s_%d" % (name, self.uid), shape, dt))
        b = Buf(name)
        if dma:
            b.dsem = (self.dpool_sw if dma == "sw" else self.dpool).pop(0); self.checked.append(b.dsem)
        return T(h, b)

    def ps(self, name, shape, dt):
        self.uid += 1
        h = self.es.enter_context(self.nc.psum_tensor("%s_%d" % (name, self.uid), shape, dt))
        return T(h, Buf(name))

    def _deps(self, reads, writes):
        deps = {}
        for b in reads:
            for k, v in b.w.items():
                if deps.get(k, 0) < v: deps[k] = v
        for b in writes:
            for k, v in b.w.items():
                if deps.get(k, 0) < v: deps[k] = v
            for k, v in b.r.items():
                if deps.get(k, 0) < v: deps[k] = v
        return deps

    def _wait(self, e, deps):
        seen = self.seen[e]; eng = self.eng[e]
        for k, v in deps.items():
            if k == "pe" and e == "pe":
                continue
            if seen.get(k, 0) < v:
                eng.wait_ge(self.sems[k], v); seen[k] = v

    def _record(self, reads, writes, key, val):
        for b in writes:
            b.w = {key: val}; b.r = {}
        for b in reads:
            if b.r.get(key, 0) < val: b.r[key] = val

    @staticmethod
    def _bl(ts):
        return [t.b if isinstance(t, T) else t for t in ts]

    def op(self, e, fn, reads=(), writes=()):
        reads = self._bl(reads); writes = self._bl(writes)
        self._wait(e, self._deps(reads, writes))
        ins = fn()
        self.cnt[e] += 1
        ins.then_inc(self.sems[e], 1)
        self._record(reads, writes, e, self.cnt[e])
        return ins

    def dma(self, q, out, in_, sbuf_t, reads=(), writes=()):
        reads = self._bl(reads); writes = self._bl(writes)
        sbb = sbuf_t.b if isinstance(sbuf_t, T) else sbuf_t
        assert sbb.dsem is not None, sbb.name
        self._wait(q, self._deps(reads, writes))
        ins = self.eng[q].dma_start(out=out, in_=in_)
        key = sbb.dsem
        self.cnt[key] += 16
        ins.then_inc(self.sems[key], 16)
        self._record(reads, writes, key, self.cnt[key])
        return ins

    def dma_group(self, q, pairs, sbuf_t, reads=(), writes=()):
        reads = self._bl(reads); writes = self._bl(writes)
        sbb = sbuf_t.b if isinstance(sbuf_t, T) else sbuf_t
        self._wait(q, self._deps(reads, writes))
        key = sbb.dsem
        base = self.cnt[key]
        for n_, (o, i_) in enumerate(pairs):
            if n_ >= 4 and n_ % 4 == 0:
                self.eng[q].wait_ge(self.sems[key], base + 16 * n_)
            ins = self.eng[q].dma_start(out=o, in_=i_)
            self.cnt[key] += 16
            ins.then_inc(self.sems[key], 16)
        self._record(reads, writes, key, self.cnt[key])

    def barrier(self):
        deps = {k: v for k, v in self.cnt.items() if v > 0}
        for e in self.eng:
            seen = self.seen[e]
            for k, v in deps.items():
                if seen.get(k, 0) < v:
                    self.eng[e].wait_ge(self.sems[k], v); seen[k] = v


def build_program(upto=99, debug=False):
    nc = bass.Bass("TRN2", target_bir_lowering=False)
    kS = "ExternalOutput" if debug else "Internal"

    def din(name, shape):
        return nc.dram_tensor(name, shape, F32, kind="ExternalInput").ap()

    x_d = din("x", [NB, S, D]); c_d = din("c", [NB, D])
    adaw_d = din("ada_w", [4, D, 3 * D]); adab_d = din("ada_b", [4, 3 * D])
    lng_d = din("ln_g", [4, D]); lnb_d = din("ln_b", [4, D])
    win0_d = din("even_w_in", [D, 2328]); pos_d = din("even_cmp_pos", [2, 32, 64])
    w1_d = din("even_cmp_w1", [2, 2048, 128]); w2_d = din("even_cmp_w2", [2, 128, 64])
    gng_d = din("even_gmlp_norm_g", [512]); gws_d = din("even_gmlp_ws", [8, 128, 128]); gbs_d = din("even_gmlp_bs", [8, 128])
    wout0_d = din("even_w_out", [D, D])
    win1_d = din("odd_w_in", [D, 3 * D]); cw_d = din("odd_conv_w", [3, D]); wout1_d = din("odd_w_out", [D, D])
    fwi_d = din("ffn_w_in", [2, D, 2 * FH]); fwo_d = din("ffn_w_out", [2, FH, D])
    out_d = nc.dram_tensor("out", [NB, S, D], F32, kind="ExternalOutput").ap()
    mod_d = nc.dram_tensor("mod_s", [4, NB, 3 * D], F32, kind=kS).ap()
    x1_d = nc.dram_tensor("x1_s", [NB, S, D], F32, kind=kS).ap()
    x2_d = nc.dram_tensor("x2_s", [NB, S, D], F32, kind=kS).ap()
    x3_d = nc.dram_tensor("x3_s", [NB, S, D], F32, kind=kS).ap()
    ogm_d = nc.dram_tensor("ogm_s", [NB, S, 512], BF16, kind=kS).ap()
    dbg = {}
    if debug:
        dbg["onsa"] = nc.dram_tensor("onsa_s", [NB, S, 512], F32, kind=kS).ap()
        dbg["kcc"] = nc.dram_tensor("kcc_s", [NB, 128, 256], F32, kind=kS).ap()

    dbufs = {}

    def dbuf(*key):
        if key not in dbufs:
            dbufs[key] = Buf(str(key))
        return dbufs[key]

    with ExitStack() as ges, nc.allow_non_contiguous_dma(reason="small param layouts"):
        fw = FW(nc, ges)
        op = fw.op
        V = nc.vector; A = nc.scalar; G = nc.gpsimd; PE = nc.tensor
        VENG = {"dve": V, "pool": G}

        ident = fw.sb("ident", [128, 128], BF16)
        identf = fw.sb("identf", [128, 128], F32)
        epsb = fw.sb("epsb", [128, 1], F32)
        op("pool", lambda: G.memset(identf[:], 1.0), writes=[identf])
        op("pool", lambda: G.affine_select(out=identf[:], in_=identf[:], pattern=[[1, 128]], compare_op=ALU.is_equal,
                                           fill=0.0, base=0, channel_multiplier=-1), reads=[identf], writes=[identf])
        op("pool", lambda: G.tensor_copy(out=ident[:], in_=identf[:]), reads=[identf], writes=[ident])
        op("pool", lambda: G.memset(epsb[:], EPS), writes=[epsb])
        mhalf = fw.sb("mhalf", [128, 8], F32)
        op("pool", lambda: G.memset(mhalf[:], -0.5), writes=[mhalf])

        def tt(e, out, in0, in1, o, rd, wr):
            return op(e, lambda: VENG[e].tensor_tensor(out=out, in0=in0, in1=in1, op=o), reads=rd, writes=wr)

        def load_bc(dst, src_1d, q="sp"):
            fw.dma(q, dst[:], src_1d.partition_broadcast(128), dst, writes=[dst])

        def load_w_bf16(dst, kidx, src2d):
            fw.dma("pool", dst[:, kidx, :], src2d, dst, writes=[dst])

        class HPrep:
            def __init__(self, sc1, sh, nhb=2, tmp_dt=F32):
                self.sc1 = sc1; self.sh = sh; self.nhb = nhb
                self.tmp = [fw.sb("hp_tmp%d" % i, [128, D], tmp_dt) for i in range(1)]
                self.hb = [fw.sb("hp_hb%d" % i, [128, D], BF16) for i in range(nhb)]
                self.psT = fw.ps("hp_psT", [128, 8, 128], BF16)
                self.n = 0

            def run_a(self, xt):
                tmp = self.tmp[0]; hb = self.hb[self.n % self.nhb]; self.n += 1
                tt("dve", tmp[:], xt[:], self.sc1[:], ALU.mult, [xt, self.sc1], [tmp])
                tt("pool", hb[:], tmp[:], self.sh[:], ALU.add, [tmp, self.sh], [hb])
                return hb

            def run_b(self, hb, hT, off):
                psT = self.psT
                for c in range(8):
                    op("pe", lambda: PE.transpose(psT[:, c, :], hb[:, c * 128:(c + 1) * 128], ident[:]),
                       reads=[hb, ident], writes=[psT])
                op("act", lambda: A.copy(out=hT[:, :, off:off + 128], in_=psT[:]), reads=[psT], writes=[hT])

            def run(self, xt, hT, off):
                self.run_b(self.run_a(xt), hT, off)

        class Epilogue:
            def __init__(self, g1, lng, lnb, nt=1):
                self.g1 = g1; self.lng = lng; self.lnb = lnb
                self.ts = [fw.sb("ep_t%d" % i, [128, D], F32) for i in range(nt)]
                self.xo = [fw.sb("ep_xo%d" % i, [128, D], F32, dma=True) for i in range(2)]
                self.sts = [fw.sb("ep_st%d" % i, [128, 2, 6], F32) for i in range(nt)]
                self.mvs = [fw.sb("ep_mv%d" % i, [128, 8], F32) for i in range(nt)]
                self.k = 0

            def steps(self, halves, xres, out_ap, out_db):
                t = self.ts[self.k % len(self.ts)]; st = self.sts[self.k % len(self.ts)]; mv = self.mvs[self.k % len(self.ts)]
                xo = self.xo[self.k % 2]; self.k += 1
                g1 = self.g1; lng = self.lng; lnb = self.lnb
                L = []
                for hi, (pt, pap) in enumerate(halves):
                    L.append(lambda hi=hi, pt=pt, pap=pap: tt("dve", t[:, hi * 512:(hi + 1) * 512], pap, g1[:, hi * 512:(hi + 1) * 512], ALU.mult, [pt, g1], [t]))
                L.append(lambda: op("dve", lambda: V.scalar_tensor_tensor(out=t[:], in0=xres[:], scalar=ALPHA, in1=t[:], op0=ALU.mult, op1=ALU.add),
                                    reads=[xres, t], writes=[t]))
                L.append(lambda: op("dve", lambda: V.bn_stats(out=st[:, 0, :], in_=t[:, 0:512]), reads=[t], writes=[st]))
                L.append(lambda: op("dve", lambda: V.bn_stats(out=st[:, 1, :], in_=t[:, 512:1024]), reads=[t, st], writes=[st]))
                L.append(lambda: op("dve", lambda: V.bn_aggr(out=mv[:, 0:2], in_=st[:].rearrange("p a b -> p (a b)")), reads=[st], writes=[mv]))
                L.append(lambda: op("dve", lambda: V.tensor_scalar_add(out=mv[:, 2:3], in0=mv[:, 1:2], scalar1=EPS), reads=[mv], writes=[mv]))
                L.append(lambda: op("pool", lambda: G.tensor_tensor(out=mv[:, 3:4], in0=mv[:, 2:3], in1=mhalf[:, 0:1], op=ALU.pow),
                                    reads=[mv, mhalf], writes=[mv]))
                L.append(lambda: op("dve", lambda: V.tensor_scalar(out=mv[:, 4:5], in0=mv[:, 0:1], scalar1=mv[:, 3:4], scalar2=-1.0,
                                                                   op0=ALU.mult, op1=ALU.mult), reads=[mv], writes=[mv]))
                L.append(lambda: op("act", lambda: A.activation(out=xo[:], in_=t[:], func=AF.Identity, bias=mv[:, 4:5], scale=mv[:, 3:4]),
                                    reads=[t, mv], writes=[xo]))
                L.append(lambda: tt("pool", xo[:], xo[:], lng[:], ALU.mult, [xo, lng], [xo]))
                L.append(lambda: tt("pool", xo[:], xo[:], lnb[:], ALU.add, [xo, lnb], [xo]))
                L.append(lambda: fw.dma("sp", out_ap, xo[:], xo, reads=[xo], writes=[out_db]))
                return L

            def run(self, halves, xres, out_ap, out_db):
                for f in self.steps(halves, xres, out_ap, out_db):
                    f()

        def load_mod(ls, b, want):
            for tile, which in want:
                load_bc(tile, mod_d[ls, b, which * D:(which + 1) * D])

        def transpose_small(dst_ap, dstT, src_ap, srcT, npart, pst, pst_ap):
            op("pe", lambda: PE.transpose(pst_ap, src_ap, identf[0:npart, 0:npart]), reads=[srcT, identf], writes=[pst])
            op("dve", lambda: V.tensor_copy(out=dst_ap, in_=pst_ap), reads=[pst], writes=[dstT])

        fw.push()
        if True:
            csb_ = fw.sb("csb", [NB, D], F32, dma=True)
            scs = fw.sb("scs", [NB, D], F32)
            scT = fw.sb("scT", [128, 8, NB], F32)
            stage = [fw.sb("adastage%d" % i, [128, 3 * D], F32, dma=True) for i in range(2)]
            adab = fw.sb("adab", [NB, 3 * D], F32, dma=True)
            msb = fw.sb("msb", [NB, 3 * D], F32, dma=True)
            psm = [fw.ps("psm%d" % i, [NB, 512], F32) for i in range(6)]
            pst = fw.ps("pst_p0", [128, 8, NB], F32)
            fw.dma("sp", csb_[:], c_d, csb_, writes=[csb_])
            op("act", lambda: A.activation(out=scs[:], in_=csb_[:], func=AF.Silu), reads=[csb_], writes=[scs])
            for k in range(8):
                op("pe", lambda: PE.transpose(pst[:, k, :], scs[:, k * 128:(k + 1) * 128], identf[0:NB, 0:NB]), reads=[scs, identf], writes=[pst])
            op("dve", lambda: V.tensor_copy(out=scT[:], in_=pst[:]), reads=[pst], writes=[scT])
            si = 0
            for ls in range(4):
                fw.dma("sp", adab[:], adab_d[ls].partition_broadcast(NB), adab, writes=[adab])
                for k in range(8):
                    stg = stage[si % 2]; si += 1
                    fw.dma("sp", stg[:], adaw_d[ls, k * 128:(k + 1) * 128, :], stg, writes=[stg])
                    for ct in range(6):
                        op("pe", lambda: PE.matmul(psm[ct][:], lhsT=scT[:, k, :], rhs=stg[:, ct * 512:(ct + 1) * 512],
                                                   start=(k == 0), stop=(k == 7)), reads=[scT, stg], writes=[psm[ct]])
                for ct in range(6):
                    tt("dve", msb[:, ct * 512:(ct + 1) * 512], psm[ct][:], adab[:, ct * 512:(ct + 1) * 512], ALU.add,
                       [psm[ct], adab], [msb])
                op("dve", lambda: V.tensor_scalar_add(out=msb[:, D:3 * D], in0=msb[:, D:3 * D], scalar1=1.0), reads=[msb], writes=[msb])
                fw.dma("sp", mod_d[ls], msb[:], msb, reads=[msb], writes=[dbuf("mod")])
        fw.pop()
        if upto < 1:
            return nc

        fw.push()
        if True:
            WsT = fw.sb("WsT", [128, 8, 128], BF16)
            bsT = fw.sb("bsT", [128, 8], F32)
            ngb = fw.sb("ngb", [128, 512], F32, dma=True)
            load_bc(ngb, gng_d)
            cbias = fw.sb("cbias", [128, 2], F32)
            w2 = fw.sb("cw2", [128, 2, 64], BF16, dma="sw")
            fw.dma("pool", w2[:], w2_d.rearrange("kv h d -> h kv d"), w2, writes=[w2])
            w2pad = fw.sb("cw2pad", [128, 2, 128], BF16)
            op("pool", lambda: G.memset(w2pad[:], 0.0), writes=[w2pad])
            for g in range(2):
                op("pool", lambda: G.tensor_copy(out=w2pad[:, g, g * 64:(g + 1) * 64], in_=w2[:, 0, :]), reads=[w2, w2pad], writes=[w2pad])
            McNeg = fw.sb("McNeg", [128, 128], BF16)
            MwNeg = fw.sb("MwNeg", [128, 128], BF16)
            cm_all = fw.sb("cm_all", [128, 33, 128], BF16)
            TMm = fw.sb("TMm", [128, 126], F32); TB = fw.sb("TB", [128, 126], F32)
            qT = fw.sb("qT", [128, 4, S], BF16)
            ksE = [fw.sb("ksE%d" % g_, [128, S], BF16) for g_ in range(2)]; kwT = fw.sb("kwT", [128, S], BF16)
            kcT = fw.sb("kcT", [128, S], BF16); vcT = fw.sb("vcT", [128, S], BF16)
            vsA = fw.sb("vsA", [128, NT, 2, 65], BF16); vwA = fw.sb("vwA", [128, NT, 2, 65], BF16)
            gat = fw.sb("gat", [128, NT, 24], F32)
            kcmpT = fw.sb("kcmpT", [128, 256], BF16)
            vcx = fw.sb("vcx", [128, 2, 2, 129], BF16)
            cmpmask = {}
            if P1_MT < 8:
                for tcache in (qT, ksE[0], ksE[1], kwT, kcT, vcT, gat):
                    op("pool", lambda: G.memset(tcache[:], 0.0), writes=[tcache])
                op("pool", lambda: G.memset(vsA[:], 0.0), writes=[vsA]); op("pool", lambda: G.memset(vwA[:], 0.0), writes=[vwA])
            fw.push()
            if True:
                w1t = fw.sb("cw1t", [64, 2, 32, 128], BF16, dma="sw")
                for kv in range(2):
                    fw.dma("pool", w1t[:, kv, :, :], w1_d[kv].rearrange("(l d) h -> d l h", d=64), w1t, writes=[w1t])
                posf = fw.sb("posf", [32, 2, 64], F32, dma=True)
                fw.dma("sp", posf[:], pos_d.rearrange("kv l d -> l kv d"), posf, writes=[posf])
                posT = fw.sb("posT", [64, 2, 32], BF16)
                pss = fw.ps("pss", [128, 64], F32)
                for kv in range(2):
                    transpose_small(posT[:, kv, :], posT, posf[:, kv, :], posf, 32, pss, pss[0:64, 0:32])
                psb = fw.ps("psb", [128, 2], F32)
                for kv in range(2):
                    for l in range(32):
                        op("pe", lambda: PE.matmul(psb[:, kv:kv + 1], lhsT=w1t[:, kv, l, :], rhs=posT[:, kv, l:l + 1],
                                                   start=(l == 0), stop=(l == 31)), reads=[w1t, posT], writes=[psb])
                op("dve", lambda: V.tensor_copy(out=cbias[:], in_=psb[:]), reads=[psb], writes=[cbias])
                gbsf = fw.sb("gbsf", [8, 128], F32, dma=True)
                fw.dma("sp", gbsf[:], gbs_d, gbsf, writes=[gbsf])
                transpose_small(bsT[:], bsT, gbsf[:], gbsf, 8, pss, pss[:, 0:8])
                wsf = fw.sb("wsf", [128, 8, 128], F32, dma=True)
                wsb = fw.sb("wsb", [128, 8, 128], BF16)
                fw.dma("sp", wsf[:], gws_d.rearrange("g t s -> t g s"), wsf, writes=[wsf])
                op("pool", lambda: G.affine_select(out=wsf[:], in_=wsf[:], pattern=[[0, 8], [-1, 128]], compare_op=ALU.is_ge,
                                                   fill=0.0, base=0, channel_multiplier=1), reads=[wsf], writes=[wsf])
                op("pool", lambda: G.tensor_copy(out=wsb[:], in_=wsf[:]), reads=[wsf], writes=[wsb])
                pst = fw.ps("pst0", [128, 8, 128], BF16)
                for gm in range(8):
                    op("pe", lambda: PE.transpose(pst[:, gm, :], wsb[:, gm, :], ident[:]), reads=[wsb, ident], writes=[pst])
                op("dve", lambda: V.tensor_copy(out=WsT[:], in_=pst[:]), reads=[pst], writes=[WsT])
                zf = fw.sb("zf", [128, 128], F32)
                mtmp = fw.sb("mtmp", [128, 128], F32)
                op("pool", lambda: G.memset(zf[:], 0.0), writes=[zf])
                op("pool", lambda: G.affine_select(out=mtmp[:], in_=zf[:], pattern=[[1, 128]], compare_op=ALU.is_ge, fill=NEG,
                                                   base=0, channel_multiplier=-1), reads=[zf], writes=[mtmp])
                op("pool", lambda: G.tensor_copy(out=McNeg[:], in_=mtmp[:]), reads=[mtmp], writes=[McNeg])
                op("pool", lambda: G.affine_select(out=mtmp[:], in_=zf[:], pattern=[[-1, 128]], compare_op=ALU.is_gt, fill=NEG,
                                                   base=0, channel_multiplier=1), reads=[zf], writes=[mtmp])
                op("pool", lambda: G.tensor_copy(out=MwNeg[:], in_=mtmp[:]), reads=[mtmp], writes=[MwNeg])
                mi = 0
                for c in range(2):
                    for i in (range(0, 17) if c == 0 else range(16, 32)):
                        op("pool", lambda: G.affine_select(out=mtmp[:], in_=zf[:], pattern=[[1, 128]], compare_op=ALU.is_ge, fill=NEG,
                                                           base=128 * i - 31 - 2048 * c, channel_multiplier=-16),
                           reads=[zf], writes=[mtmp])
                        op("pool", lambda: G.tensor_copy(out=cm_all[:, mi, :], in_=mtmp[:]), reads=[mtmp], writes=[cm_all])
                        cmpmask[(c, i)] = mi; mi += 1
                ebf = fw.sb("ebf", [64, S], F32)
                op("pool", lambda: G.memset(ebf[:], 1.0), writes=[ebf])
                op("pool", lambda: G.affine_select(out=ebf[:], in_=ebf[:], pattern=[[1, S]], compare_op=ALU.is_ge, fill=0.0,
                                                   base=0, channel_multiplier=-64), reads=[ebf], writes=[ebf])
                op("pool", lambda: G.affine_select(out=ebf[:], in_=ebf[:], pattern=[[-1, S]], compare_op=ALU.is_ge, fill=0.0,
                                                   base=63, channel_multiplier=64), reads=[ebf], writes=[ebf])
                op("pool", lambda: G.tensor_copy(out=ksE[1][0:64, :], in_=ebf[:]), reads=[ebf, ksE[1]], writes=[ksE[1]])
                ebf2 = fw.sb("ebf2", [128, S], F32)
                op("pool", lambda: G.memset(ebf2[:], 1.0), writes=[ebf2])
                op("pool", lambda: G.affine_select(out=ebf2[:], in_=ebf2[:], pattern=[[1, S]], compare_op=ALU.is_ge, fill=0.0,
                                                   base=4096, channel_multiplier=-64), reads=[ebf2], writes=[ebf2])
                op("pool", lambda: G.affine_select(out=ebf2[:], in_=ebf2[:], pattern=[[-1, S]], compare_op=ALU.is_ge, fill=0.0,
                                                   base=-4033, channel_multiplier=64), reads=[ebf2], writes=[ebf2])
                op("pool", lambda: G.tensor_copy(out=ksE[0][64:128, :], in_=ebf2[64:128, :]), reads=[ebf2, ksE[0]], writes=[ksE[0]])
                for hh in range(2):
                    hs = slice(hh * 64, (hh + 1) * 64)
                    op("pool", lambda: G.memset(TMm[hs, 0:61 + hh], 1.0), reads=[TMm], writes=[TMm])
                    op("pool", lambda: G.memset(TMm[hs, 61 + hh:126], 0.0), reads=[TMm], writes=[TMm])
                    op("pool", lambda: G.memset(TB[hs, 0:61 + hh], 0.0), reads=[TB], writes=[TB])
                    op("pool", lambda: G.memset(TB[hs, 61 + hh:63 + hh], 1e4), reads=[TB], writes=[TB])
                    op("pool", lambda: G.memset(TB[hs, 63 + hh:126], -1e4), reads=[TB], writes=[TB])
                op("pool", lambda: G.memset(vsA[:, :, :, 64:65], 1.0), writes=[vsA])
                op("pool", lambda: G.memset(vwA[:, :, :, 64:65], 1.0), writes=[vwA])
                op("pool", lambda: G.memset(vcx[:], 0.0), writes=[vcx])
                op("pool", lambda: G.memset(vcx[:, :, :, 64:65], 1.0), reads=[vcx], writes=[vcx])
                covf = fw.sb("covf", [128, 2, 64], F32)
                op("pool", lambda: G.memset(covf[:], 1.0), writes=[covf])
                op("pool", lambda: G.affine_select(out=covf[:], in_=covf[:], pattern=[[-128, 2], [4, 64]], compare_op=ALU.is_ge,
                                                   fill=0.0, base=3, channel_multiplier=-1), reads=[covf], writes=[covf])
                op("pool", lambda: G.affine_select(out=covf[:], in_=covf[:], pattern=[[128, 2], [-4, 64]], compare_op=ALU.is_ge,
                                                   fill=0.0, base=1, channel_multiplier=1), reads=[covf], writes=[covf])
                for g in range(2):
                    op("pool", lambda: G.tensor_copy(out=vcx[:, :, g, 65:129], in_=covf[:]), reads=[covf, vcx], writes=[vcx])
                op("pool", lambda: G.memset(kcmpT[:], 0.0), writes=[kcmpT])
            fw.pop()

            v3 = lambda ap: ap.rearrange("p (g d) -> p g d", d=64)
            b8 = lambda ap: ap.unsqueeze(2).to_broadcast([128, 8, 64])

            for b in range(NB_RUN):
                fw.push()
                if True:
                    W0 = fw.sb("W0", [128, 8, 2328], BF16, dma="sw")
                    Wq = fw.sb("Wq", [128, 8, 512], BF16, dma="sw")
                    fw.dma_group("pool", [(W0[:, k, :], win0_d[k * 128:(k + 1) * 128, :]) for k in range(8)], W0, writes=[W0])
                    fw.dma_group("pool", [(Wq[:, k, r * 128:(r + 1) * 128].rearrange("p (g d) -> p g d", g=2),
                                           win0_d[k * 128:(k + 1) * 128, 0:512].rearrange("p (g r d) -> p g r d", g=2, r=4)[:, :, r, :])
                                          for k in range(8) for r in range(4)], Wq, writes=[Wq])
                    sc1 = fw.sb("sc1", [128, D], F32, dma=True); sh = fw.sb("sh", [128, D], F32, dma=True)
                    load_mod(0, b, [(sh, 0), (sc1, 1)])
                    xin = [fw.sb("xin%d" % i, [128, D], F32, dma=True) for i in range(3)]
                    hT = [fw.sb("hT%d" % i, [128, 8, 512], BF16) for i in range(2)]
                    hp = HPrep(sc1, sh)
                    psF = [fw.ps("psF%d" % i, [128, 512], F32) for i in range(2)]
                    psA = fw.ps("psA", [128, 512], F32); psB = fw.ps("psB", [128, 512], F32)
                    psC = fw.ps("psC", [128, 512], F32); psM = fw.ps("psM", [128, 512], F32)
                    ug = fw.sb("ug", [128, 512], F32); vg = fw.sb("vg", [128, 512], F32)
                    sq = fw.sb("sq", [128, 512], F32)
                    gst = fw.sb("gst", [128, 6, 8], F32)
                    vn = fw.sb("vn", [128, 512], BF16)
                    og = [fw.sb("og%d" % i, [128, 512], BF16, dma=True) for i in range(2)]
                    xcount = [0]
                    pend = {}

                    def p1_prep_a(mt_, sub_):
                        if mt_ >= P1_MT:
                            return
                        i_ = mt_ * 4 + sub_
                        xt = xin[xcount[0] % 3]; xcount[0] += 1
                        fw.dma("sp", xt[:], x_d[b, i_ * 128:(i_ + 1) * 128, :], xt, writes=[xt])
                        pend[(mt_, sub_)] = hp.run_a(xt)

                    def p1_prep_b(mt_, sub_):
                        if mt_ >= P1_MT:
                            return
                        hp.run_b(pend.pop((mt_, sub_)), hT[mt_ % 2], sub_ * 128)

                    for sub in range(4):
                        p1_prep_a(0, sub); p1_prep_b(0, sub)
                    for mt in range(P1_MT):
                        hTm = hT[mt % 2]
                        fm = []
                        for r in range(4):
                            fm.append((Wq[:, :, r * 128:(r + 1) * 128], qT[:, r, mt * 512:(mt + 1) * 512], qT))
                        fm.append((W0[:, :, 512:640], kcT[:, mt * 512:(mt + 1) * 512], kcT))
                        fm.append((W0[:, :, 640:768], vcT[:, mt * 512:(mt + 1) * 512], vcT))
                        fm.append((W0[:, :, 768:896], None, "ks"))
                        fm.append((W0[:, :, 1024:1152], kwT[:, mt * 512:(mt + 1) * 512], kwT))
                        for fi, (wap, dst, dstT) in enumerate(fm):
                            pf = psF[fi % 2]
                            for k in range(8):
                                op("pe", lambda: PE.matmul(pf[:], lhsT=wap[:, k], rhs=hTm[:, k, :], start=(k == 0), stop=(k == 7)),
                                   reads=[W0, Wq, hTm], writes=[pf])
                            if dstT == "ks":
                                op("act", lambda: A.copy(out=ksE[0][0:64, mt * 512:(mt + 1) * 512], in_=pf[0:64, :]), reads=[pf, ksE[0]], writes=[ksE[0]])
                                op("dve", lambda: V.tensor_copy(out=ksE[1][64:128, mt * 512:(mt + 1) * 512], in_=pf[64:128, :]), reads=[pf, ksE[1]], writes=[ksE[1]])
                            elif fi % 2 == 0:
                                op("act", lambda: A.copy(out=dst, in_=pf[:]), reads=[pf], writes=[dstT])
                            else:
                                op("dve", lambda: V.tensor_copy(out=dst, in_=pf[:]), reads=[pf], writes=[dstT])
                        for sub in range(4):
                            i = mt * 4 + sub
                            lh = lambda k: hTm[:, k, sub * 128:(sub + 1) * 128]
                            for k in range(8):
                                op("pe", lambda: PE.matmul(psA[:, 0:408], lhsT=lh(k), rhs=W0[:, k, 896:1304], start=(k == 0), stop=(k == 7)),
                                   reads=[W0, hTm], writes=[psA])
                            for k in range(8):
                                op("pe", lambda: PE.matmul(psB[:], lhsT=lh(k), rhs=W0[:, k, 1304:1816], start=(k == 0), stop=(k == 7)),
                                   reads=[W0, hTm], writes=[psB])
                            for k in range(8):
                                op("pe", lambda: PE.matmul(psC[:], lhsT=lh(k), rhs=W0[:, k, 1816:2328], start=(k == 0), stop=(k == 7)),
                                   reads=[W0, hTm], writes=[psC])
                            p1_prep_a(mt + 1, sub)
                            op("dve", lambda: V.tensor_copy(out=vsA[:, i, :, 0:64], in_=v3(psA[:, 0:128])), reads=[psA], writes=[vsA])
                            op("dve", lambda: V.tensor_copy(out=vwA[:, i, :, 0:64], in_=v3(psA[:, 256:384])), reads=[psA], writes=[vwA])
                            op("dve", lambda: V.tensor_copy(out=gat[:, i, :], in_=psA[:, 384:408]), reads=[psA], writes=[gat])
                            op("act", lambda: A.activation(out=ug[:], in_=psB[:], func=AF.Gelu_apprx_tanh), reads=[psB], writes=[ug])
                            op("act", lambda: A.activation(out=vg[:], in_=psC[:], func=AF.Gelu_apprx_tanh), reads=[psC], writes=[vg])
                            op("dve", lambda: V.tensor_reduce(out=gst[:, 0, :], in_=v3(vg[:]), axis=AX.X, op=ALU.add), reads=[vg], writes=[gst])
                            tt("dve", sq[:], vg[:], vg[:], ALU.mult, [vg], [sq])
                            op("dve", lambda: V.tensor_reduce(out=gst[:, 1, :], in_=v3(sq[:]), axis=AX.X, op=ALU.add), reads=[sq, gst], writes=[gst])
                            op("dve", lambda: V.tensor_scalar(out=gst[:, 2, :], in0=gst[:, 0, :], scalar1=1.0 / 64, scalar2=None, op0=ALU.mult),
                               reads=[gst], writes=[gst])
                            tt("dve", gst[:, 5, :], gst[:, 2, :], gst[:, 2, :], ALU.mult, [gst], [gst])
                            op("dve", lambda: V.scalar_tensor_tensor(out=gst[:, 3, :], in0=gst[:, 1, :], scalar=1.0 / 64, in1=gst[:, 5, :],
                                                                     op0=ALU.mult, op1=ALU.subtract), reads=[gst], writes=[gst])
                            op("dve", lambda: V.tensor_scalar_add(out=gst[:, 3, :], in0=gst[:, 3, :], scalar1=EPS), reads=[gst], writes=[gst])
                            op("pool", lambda: G.tensor_tensor(out=gst[:, 4, :], in0=gst[:, 3, :], in1=mhalf[:], op=ALU.pow),
                               reads=[gst, mhalf], writes=[gst])
                            tt("dve", v3(sq[:]), v3(vg[:]), b8(gst[:, 2, :]), ALU.subtract, [vg, gst], [sq])
                            tt("dve", v3(sq[:]), v3(sq[:]), b8(gst[:, 4, :]), ALU.mult, [sq, gst], [sq])
                            tt("dve", vn[:], sq[:], ngb[:], ALU.mult, [sq, ngb], [vn])
                            for gm in range(8):
                                op("pe", lambda: PE.matmul(psM[:, gm * 64:(gm + 1) * 64], lhsT=WsT[:, gm, :], rhs=vn[:, gm * 64:(gm + 1) * 64],
                                                           start=True, stop=True), reads=[WsT, vn], writes=[psM])
                            p1_prep_b(mt + 1, sub)
                            tt("dve", v3(sq[:]), v3(psM[:]), b8(bsT[:]), ALU.add, [psM, bsT], [sq])
                            ogt = og[i % 2]
                            tt("dve", ogt[:], sq[:], ug[:], ALU.mult, [sq, ug], [ogt])
                            fw.dma("sp", ogm_d[b, i * 128:(i + 1) * 128, :], ogt[:], ogt, reads=[ogt], writes=[dbuf("ogm", b, i)])
                fw.pop()
                if upto < 2:
                    continue
                fw.push()
                if True:
                    w1 = fw.sb("cw1", [128, 2, 32, 128], BF16, dma="sw")
                    for hh in range(2):
                        for kv in range(2):
                            fw.dma("pool", w1[hh * 64:(hh + 1) * 64, kv, :, :], w1_d[kv].rearrange("(l d) h -> d l h", d=64), w1, writes=[w1])
                    psH = [fw.ps("psH%d" % i, [128, 256], F32) for i in range(2)]
                    psK = fw.ps("psK", [128, 256], F32); psV = fw.ps("psV", [128, 256], F32)
                    for kv in range(2):
                        src = kcT if kv == 0 else vcT
                        hbs = []
                        for g in range(2):
                            ph = psH[g]
                            for l in range(32):
                                op("pe", lambda: PE.matmul(ph[:, 0:255], lhsT=w1[g * 64:(g + 1) * 64, kv, l, :],
                                                           rhs=src[g * 64:(g + 1) * 64, l:l + 16 * 254 + 1:16], start=(l == 0), stop=(l == 31)),
                                   reads=[w1, src], writes=[ph])
                            hb_g = fw.sb("hbg_%d%d" % (kv, g), [128, 256], BF16)
                            op("pool", lambda: G.memset(hb_g[:], 0.0), writes=[hb_g])
                            op("act", lambda: A.activation(out=hb_g[:, 0:255], in_=ph[:, 0:255], func=AF.Gelu_apprx_tanh,
                                                           bias=cbias[:, kv:kv + 1], scale=1.0), reads=[ph, cbias, hb_g], writes=[hb_g])
                            hbs.append(hb_g)
                        if kv == 0:
                            for g in range(2):
                                op("pe", lambda: PE.matmul(psK[:], lhsT=w2pad[:, g, :], rhs=hbs[g][:], start=(g == 0), stop=(g == 1)),
                                   reads=[w2pad, hbs[g]], writes=[psK])
                            op("dve", lambda: V.tensor_copy(out=kcmpT[:], in_=psK[:]), reads=[psK], writes=[kcmpT])
                            if debug:
                                kd = fw.sb("kd", [128, 256], F32, dma=True)
                                op("dve", lambda: V.tensor_copy(out=kd[:], in_=psK[:]), reads=[psK], writes=[kd])
                                fw.dma("sp", dbg["kcc"][b], kd[:], kd, reads=[kd], writes=[dbuf("kcc", b)])
                        else:
                            for g in range(2):
                                for c in range(2):
                                    op("pe", lambda: PE.matmul(psV[:, (g * 2 + c) * 64:(g * 2 + c + 1) * 64], lhsT=hbs[g][:, c * 128:(c + 1) * 128],
                                                               rhs=w2[:, 1, :], start=True, stop=True), reads=[w2, hbs[g]], writes=[psV])
                            for g in range(2):
                                op("dve", lambda: V.tensor_copy(out=vcx[:, :, g, 0:64],
                                                                in_=psV[:, g * 128:(g + 1) * 128].rearrange("p (c d) -> p c d", d=64)),
                                   reads=[psV], writes=[vcx])
                fw.pop()
                if upto < 3:
                    continue
                fw.push()
                if True:
                    Wo0 = fw.sb("Wo0", [128, 8, D], BF16, dma="sw")
                    fw.dma_group("pool", [(Wo0[:, k, :], wout0_d[k * 128:(k + 1) * 128, :]) for k in range(8)], Wo0, writes=[Wo0])
                    lng = fw.sb("lng", [128, D], F32, dma=True); lnb = fw.sb("lnb", [128, D], F32, dma=True)
                    load_bc(lng, lng_d[0]); load_bc(lnb, lnb_d[0])
                    g1 = fw.sb("g1", [128, D], F32, dma=True)
                    load_mod(0, b, [(g1, 2)])
                    ep = Epilogue(g1, lng, lnb)
                    op("act", lambda: A.activation(out=gat[:].rearrange("p a b -> p (a b)"), in_=gat[:].rearrange("p a b -> p (a b)"), func=AF.Sigmoid),
                       reads=[gat], writes=[gat])
                    psS = [fw.ps("psS%d" % i, [128, 512], F32) for i in range(3)]
                    psTt = fw.ps("psTt", [128, 8, 128], BF16)
                    psOc = fw.ps("psOc", [128, 512], F32); psImp = fw.ps("psImp", [128, 512], F32)
                    psOs = fw.ps("psOs", [128, 512], F32); psOw = fw.ps("psOw", [128, 512], F32)
                    o4 = lambda ps_: ps_[:, 0:260].rearrange("p (r e) -> p r e", e=65)
                    ebuf = [fw.sb("ebuf%d" % i, [128, 4, 128], BF16) for i in range(4)]
                    ecount = [0]
                    cats = [fw.sb("cat%d" % i, [128, D], BF16, dma=True) for i in range(2)]
                    ocS = [fw.sb("ocS%d" % i, [128, 4, 65], F32) for i in range(2)]
                    nsel2s = [fw.sb("nsel2_%d" % i, [128, 2, 64], BF16) for i in range(2)]
                    catT = fw.sb("catT", [128, 8, 128], BF16)
                    sm = fw.sb("sm", [128, 3, 4], F32)
                    ff = fw.sb("ff", [128, 3, 4], F32)
                    rs4 = fw.sb("rs4", [128, 4], F32)
                    impn = fw.sb("impn", [128, 4, 64], F32)
                    imp = fw.sb("imp", [128, 64], F32)
                    impa = fw.sb("impa", [128, 64], F32)
                    wk = fw.sb("wk", [128, 64], F32)
                    m8 = fw.sb("m8", [128, 8], F32)
                    nsel = fw.sb("nsel", [128, 64], BF16)
                    nsel2 = fw.sb("nsel2", [128, 2, 64], BF16)
                    qzs = [[fw.sb("qzs%d_%d" % (g_, k_), [128, 4, 128], BF16) for k_ in range(2)] for g_ in range(2)]
                    qz = [[fw.sb("qz%d_%d" % (g_, k_), [128, 4, 128], BF16) for k_ in range(2)] for g_ in range(2)]
                    for g_ in range(2):
                        for k_ in range(2):
                            op("pool", lambda: G.memset(qz[g_][k_][:], 0.0), writes=[qz[g_][k_]])
                    o1 = fw.sb("o1", [128, 4, 64], F32); o2 = fw.sb("o2", [128, 4, 64], F32)
                    xres = [fw.sb("xres%d" % i, [128, D], F32, dma=True) for i in range(2)]
                    if debug:
                        ond = fw.sb("ond", [128, 512], F32, dma=True)
                    bc4 = lambda ap: ap.unsqueeze(1).to_broadcast([ap.shape[0], 4, 128])

                    qzt_cur = [None]

                    def emit_S(job):
                        if job.get("pre"):
                            job["pre"]()
                        pS = psS[job["k"] % 3]
                        pS4 = pS[:].rearrange("p (r t) -> p r t", r=4)
                        extra = job["extra"]
                        op("pe", lambda: PE.matmul(pS4, lhsT=job["kT_ap"], rhs=job["qs"], start=True, stop=(len(extra) == 0)),
                           reads=[job["kT_t"], job["qzt"]], writes=[pS])
                        for xi, (la, ra, rd) in enumerate(extra):
                            op("pe", lambda: PE.matmul(pS4, lhsT=la, rhs=ra, start=False, stop=(xi == len(extra) - 1)),
                               reads=rd, writes=[pS])

                    def emit_EXP_PV(job):
                        if job.get("pre_pv"):
                            job["pre_pv"]()
                        pS = psS[job["k"] % 3]; e = ebuf[job["k"] % 4]
                        op("act", lambda: A.activation(out=e[:].rearrange("p r t -> p (r t)"), in_=pS[:], func=AF.Exp, scale=0.125),
                           reads=[pS], writes=[e])
                        for r in range(4):
                            for (va, po_fn, pot) in job["pvs"]:
                                op("pe", lambda: PE.matmul(po_fn(r), lhsT=e[:, r, :], rhs=va, start=(job["first"] and r == 0), stop=job["last"],
                                                           skip_group_check=True), reads=[e, job["vaug_t"]], writes=[pot])
                        if job.get("post"):
                            job["post"]()

                    def run_jobs(jobs):
                        n = len(jobs)
                        for idx, job in enumerate(jobs):
                            job["k"] = ecount[0] + idx
                        ecount[0] += n
                        nextS = 0
                        for k in range(n):
                            while nextS < n and nextS <= k + 2:
                                jb = jobs[nextS]
                                if jb.get("pre") and jb.get("needs", -1) > k - 1:
                                    break
                                emit_S(jb); nextS += 1
                            assert nextS > k
                            emit_EXP_PV(jobs[k])
                            for _ in range(3):
                                if deferred:
                                    deferred.pop(0)()

                    deferred = []

                    def topk_dve(i, g):
                        oc = ocS[g]; nsel2 = nsel2s[g]
                        op("dve", lambda: V.tensor_copy(out=oc[:], in_=o4(psOc)), reads=[psOc], writes=[oc])
                        op("dve", lambda: V.tensor_scalar_max(out=rs4[:], in0=oc[:, :, 64], scalar1=1e-30), reads=[oc], writes=[rs4])
                        op("dve", lambda: V.reciprocal(out=rs4[:], in_=rs4[:]), reads=[rs4], writes=[rs4])
                        tt("dve", impn[:], psImp[:, 0:256].rearrange("p (r j) -> p r j", j=64), rs4[:].unsqueeze(2).to_broadcast([128, 4, 64]),
                           ALU.mult, [psImp, rs4], [impn])
                        op("dve", lambda: V.tensor_reduce(out=imp[:], in_=impn[:].rearrange("p r j -> p j r"), axis=AX.X, op=ALU.add),
                           reads=[impn], writes=[imp])
                        so = 62 - 2 * i
                        tt("dve", impa[:], imp[:], TMm[:, so:so + 64], ALU.mult, [imp, TMm], [impa])
                        tt("dve", impa[:], impa[:], TB[:, so:so + 64], ALU.add, [impa, TB], [impa])
                        op("dve", lambda: V.memset(impa[:, 0:1], 1e4), reads=[impa], writes=[impa])
                        op("dve", lambda: V.max(out=m8[:], in_=impa[:]), reads=[impa], writes=[m8])
                        op("dve", lambda: V.match_replace(out=wk[:], in_to_replace=m8[:], in_values=impa[:], imm_value=-1e9),
                           reads=[m8, impa], writes=[wk])
                        op("dve", lambda: V.max(out=m8[:], in_=wk[:]), reads=[wk], writes=[m8])
                        op("dve", lambda: V.tensor_scalar(out=wk[:], in0=impa[:], scalar1=m8[:, 7:8], scalar2=-NEG, op0=ALU.is_ge, op1=ALU.mult),
                           reads=[impa, m8], writes=[wk])
                        op("dve", lambda: V.tensor_scalar_add(out=nsel2[:], in0=wk[:].unsqueeze(1).to_broadcast([128, 2, 64]), scalar1=NEG),
                           reads=[wk], writes=[nsel2])

                    def negT_pe(g, qzst):
                        og_ = slice((1 - g) * 64, (2 - g) * 64)
                        nsel2 = nsel2s[g]
                        op("pe", lambda: PE.transpose(psTt[:, 0, :], nsel2[:].rearrange("p a b -> p (a b)"), ident[:]), reads=[nsel2, ident], writes=[psTt])
                        op("dve", lambda: V.tensor_copy(out=qzst[og_, :, :], in_=psTt[og_, 0, :].unsqueeze(1).to_broadcast([64, 4, 128])),
                           reads=[psTt, qzst], writes=[qzst])

                    def tile_start(i):
                        xr = xres[i % 2]; cat = cats[i % 2]
                        fw.dma("sp", xr[:], x_d[b, i * 128:(i + 1) * 128, :], xr, writes=[xr])
                        fw.dma("sp", cat[:, 512:1024], ogm_d[b, i * 128:(i + 1) * 128, :], cat, reads=[dbuf("ogm", b, i)], writes=[cat])
                        for ii in ([0, 1] if i == 0 else [i + 1]):
                            if ii >= P3_TILES:
                                continue
                            for g_ in range(2):
                                gs_ = slice(g_ * 64, (g_ + 1) * 64)
                                a_ = qz[g_][ii % 2]; b_ = qzs[g_][ii % 2]
                                op("dve", lambda: V.tensor_copy(out=a_[gs_, :, :], in_=qT[gs_, :, ii * 128:(ii + 1) * 128]), reads=[qT, a_], writes=[a_])
                                op("pool", lambda: G.tensor_copy(out=b_[gs_, :, :], in_=qT[gs_, :, ii * 128:(ii + 1) * 128]), reads=[qT, b_], writes=[b_])

                    def combine(i, g):
                        oc = ocS[g]; cat = cats[i % 2]
                        op("dve", lambda: V.tensor_scalar_max(out=sm[:, 0, :], in0=oc[:, :, 64], scalar1=1e-30), reads=[oc, sm], writes=[sm])
                        for br, pso in ((1, psOs), (2, psOw)):
                            op("dve", lambda: V.tensor_scalar_max(out=sm[:, br, :], in0=o4(pso)[:, :, 64], scalar1=1e-30), reads=[pso, sm], writes=[sm])
                        op("dve", lambda: V.reciprocal(out=sm[:], in_=sm[:]), reads=[sm], writes=[sm])
                        tt("dve", ff[:], sm[:], gat[:, i, g * 12:(g + 1) * 12].rearrange("p (r b) -> p b r", b=3), ALU.mult, [sm, gat], [ff])
                        fb = lambda br: ff[:, br, :].unsqueeze(2).to_broadcast([128, 4, 64])
                        tt("dve", o2[:], o4(psOw)[:, :, 0:64], fb(2), ALU.mult, [psOw, ff], [o2])
                        tt("dve", o1[:], o4(psOs)[:, :, 0:64], fb(1), ALU.mult, [psOs, ff], [o1])
                        tt("dve", o1[:], o1[:], o2[:], ALU.add, [o1, o2], [o1])
                        tt("dve", o2[:], oc[:, :, 0:64], fb(0), ALU.mult, [oc, ff], [o2])
                        if debug:
                            tt("pool", ond[:, g * 256:(g + 1) * 256].rearrange("p (r d) -> p r d", d=64), o1[:], o2[:], ALU.add, [o1, o2], [ond])
                            if g == 1:
                                fw.dma("sp", dbg["onsa"][b, i * 128:(i + 1) * 128, :], ond[:], ond, reads=[ond], writes=[dbuf("onsa", b, i)])
                        tt("dve", cat[:, g * 256:(g + 1) * 256].rearrange("p (r d) -> p r d", d=64), o1[:], o2[:], ALU.add, [o1, o2], [cat])

                    def outproj(i):
                        while deferred:
                            deferred.pop(0)()
                        cat = cats[i % 2]; xr = xres[i % 2]
                        for c in range(8):
                            op("pe", lambda: PE.transpose(psTt[:, c, :], cat[:, c * 128:(c + 1) * 128], ident[:]), reads=[cat, ident], writes=[psTt])
                        op("act", lambda: A.copy(out=catT[:], in_=psTt[:]), reads=[psTt], writes=[catT])
                        halves = [(psOs, psOs[:]), (psOw, psOw[:])]
                        for half in range(2):
                            pyt, pya = halves[half]
                            for c in range(8):
                                op("pe", lambda: PE.matmul(pya, lhsT=catT[:, c, :], rhs=Wo0[:, c, half * 512:(half + 1) * 512],
                                                           start=(c == 0), stop=(c == 7)), reads=[catT, Wo0], writes=[pyt])
                        st_ = ep.steps(halves, xr, x1_d[b, i * 128:(i + 1) * 128, :], dbuf("x1", b, i))
                        for f_ in st_[:3]:
                            f_()
                        deferred.extend(st_[3:])

                    jobs = []
                    last_sel_g1 = -1
                    for i in range(P3_TILES):
                        def mk(kT_ap, kT_t, extra, pvs, vaug_t, first, last, qt):
                            return dict(kT_ap=kT_ap, kT_t=kT_t, qs=qt[:], qzt=qt, extra=extra, pvs=pvs, vaug_t=vaug_t, first=first, last=last)
                        chunks = [0] if i < 16 else [0, 1]
                        last_cmp = {}
                        for g in range(2):
                            qzt = qz[g][i % 2]
                            for ci, c in enumerate(chunks):
                                extra = []
                                if (c, i) in cmpmask:
                                    extra.append((ident[:], bc4(cm_all[:, cmpmask[(c, i)], :]), [ident, cm_all]))
                                jb = mk(kcmpT[:, c * 128:(c + 1) * 128], kcmpT, extra,
                                        [(vcx[:, c, g, 0:65], lambda r: o4(psOc)[:, r, :], psOc),
                                         (vcx[:, c, g, 65:129], lambda r: psImp[:, r * 64:(r + 1) * 64], psImp)],
                                        vcx, ci == 0, ci == len(chunks) - 1, qzt)
                                if g == 0 and ci == 0:
                                    jb["pre"] = (lambda i_=i: tile_start(i_)); jb["needs"] = -1
                                jobs.append(jb)
                            jobs[-1]["post"] = (lambda i_=i, g_=g: topk_dve(i_, g_))
                            last_cmp[g] = len(jobs) - 1
                        for g in range(2):
                            qzt = qz[g][i % 2]; qzst = qzs[g][i % 2]
                            j0 = max(0, i - 4)
                            for j in range(j0, i + 1):
                                extra = []
                                if j == i:
                                    extra.append((ident[:], bc4(McNeg[:]), [ident, McNeg]))
                                elif j == i - 4:
                                    extra.append((ident[:], bc4(MwNeg[:]), [ident, MwNeg]))
                                jb = mk(kwT[:, j * 128:(j + 1) * 128], kwT, extra,
                                        [(vwA[:, j, g, :], lambda r: o4(psOw)[:, r, :], psOw)], vwA, j == j0, j == i, qzt)
                                if g == 0 and j == j0 and i > 0:
                                    jb["pre_pv"] = (lambda i_=i: outproj(i_ - 1))
                                jobs.append(jb)
                            for j in range(i + 1):
                                extra = []
                                if j == i:
                                    extra.append((ident[:], bc4(McNeg[:]), [ident, McNeg]))
                                jb = mk(ksE[g][:, j * 128:(j + 1) * 128], ksE[g], extra,
                                        [(vsA[:, j, g, :], lambda r: o4(psOs)[:, r, :], psOs)], vsA, j == 0, j == i, qzst)
                                if j == 0:
                                    jb["pre"] = (lambda g_=g, q_=qzst: negT_pe(g_, q_)); jb["needs"] = last_cmp[g]
                                jobs.append(jb)
                            jobs[-1]["post"] = (lambda i_=i, g_=g: combine(i_, g_))
                            if g == 1:
                                last_sel_g1 = len(jobs) - 1
                    run_jobs(jobs)
                    outproj(P3_TILES - 1)
                    while deferred:
                        deferred.pop(0)()
                fw.pop()
        fw.pop()
        if upto < 4:
            return nc

        def ffn_phase(ls, layer, src_d, src_name, dst_d, dst_name):
            fw.push()
            if True:
                Wi = fw.sb("Wi", [128, 8, 2 * FH], BF16, dma="sw")
                Wo = fw.sb("Wo", [128, 22, D], BF16, dma="sw")
                fw.dma_group("pool", [(Wi[:, k, :], fwi_d[layer, k * 128:(k + 1) * 128, :]) for k in range(8)], Wi, writes=[Wi])
                fw.dma_group("pool", [(Wo[:, k, :], fwo_d[layer, k * 128:(k + 1) * 128, :]) for k in range(22)], Wo, writes=[Wo])
                lng = fw.sb("lng", [128, D], F32, dma=True); lnb = fw.sb("lnb", [128, D], F32, dma=True)
                load_bc(lng, lng_d[ls]); load_bc(lnb, lnb_d[ls])
                sc1 = fw.sb("sc1", [128, D], F32, dma=True); sh = fw.sb("sh", [128, D], F32, dma=True); g1 = fw.sb("g1", [128, D], F32, dma=True)
                NX = 4
                xin = [fw.sb("xin%d" % i, [128, D], F32, dma=True) for i in range(NX)]
                hT = [fw.sb("hT%d" % i, [128, 8, 256], BF16) for i in range(2)]
                actT = fw.sb("actT", [128, 22, 256], BF16)
                sg = [fw.sb("sg%d" % i, [128, 256], BF16) for i in range(2)]
                hp = HPrep(sc1, sh, nhb=2, tmp_dt=BF16)
                ep = Epilogue(g1, lng, lnb)
                psGU = [fw.ps("psGU%d" % i, [128, 2, 256], F32) for i in range(2)]
                psYs = [fw.ps("psY%d" % i, [128, 2, 512], F32) for i in range(2)]
                tiles = [(b, mt) for b in range(NB) for mt in range(S // 256)]
                state = {}
                xc = [0]

                def prep_a(idx):
                    b, mt = tiles[idx]
                    if mt == 0:
                        load_mod(ls, b, [(sh, 0), (sc1, 1)])
                    hTm = hT[idx % 2]
                    xts = []; hbs = []
                    for sub in range(2):
                        i = mt * 2 + sub
                        xt = xin[xc[0] % NX]; xc[0] += 1
                        fw.dma("sp", xt[:], src_d[b, i * 128:(i + 1) * 128, :], xt, reads=[dbuf(src_name, b, i)], writes=[xt])
                        hbs.append(hp.run_a(xt))
                        xts.append(xt)
                    state[idx] = (hTm, xts, hbs)

                def prep_b(idx):
                    hTm, xts, hbs = state[idx]
                    for sub in range(2):
                        hp.run_b(hbs[sub], hTm, sub * 128)

                def up(idx, hcs):
                    hTm = state[idx][0]
                    for hc in hcs:
                        pg = psGU[hc % 2]; sgt = sg[hc % 2]
                        for k in range(8):
                            op("pe", lambda: PE.matmul(pg[:, 0, :], lhsT=Wi[:, k, hc * 128:(hc + 1) * 128], rhs=hTm[:, k, :],
                                                       start=(k == 0), stop=(k == 7)), reads=[Wi, hTm], writes=[pg])
                        for k in range(8):
                            op("pe", lambda: PE.matmul(pg[:, 1, :], lhsT=Wi[:, k, FH + hc * 128:FH + (hc + 1) * 128], rhs=hTm[:, k, :],
                                                       start=(k == 0), stop=(k == 7)), reads=[Wi, hTm], writes=[pg])
                        op("act", lambda: A.activation(out=sgt[:], in_=pg[:, 0, :], func=AF.Silu), reads=[pg], writes=[sgt])
                        tt("dve", actT[:, hc, :], pg[:, 1, :], sgt[:], ALU.mult, [pg, sgt], [actT])
                        for _ in range(2):
                            if deferred:
                                deferred.pop(0)()

                def down_ep(idx):
                    b, mt = tiles[idx]
                    xts = state.pop(idx)[1]
                    if mt == 0:
                        load_mod(ls, b, [(g1, 2)])
                    for sub in range(2):
                        i = mt * 2 + sub
                        psY = psYs[sub]
                        for half in range(2):
                            for hc in range(22):
                                op("pe", lambda: PE.matmul(psY[:, half, :], lhsT=actT[:, hc, sub * 128:(sub + 1) * 128],
                                                           rhs=Wo[:, hc, half * 512:(half + 1) * 512], start=(hc == 0), stop=(hc == 21)),
                                   reads=[actT, Wo], writes=[psY])
                        st_ = ep.steps([(psY, psY[:, 0, :]), (psY, psY[:, 1, :])], xts[sub], dst_d[b, i * 128:(i + 1) * 128, :], dbuf(dst_name, b, i))
                        nim = 3 if sub == 1 else len(st_)
                        for f_ in st_[:nim]:
                            f_()
                        deferred.extend(st_[nim:])

                deferred = []
                prep_a(0); prep_b(0)
                for idx in range(len(tiles)):
                    up(idx, range(0, 2))
                    if idx + 1 < len(tiles):
                        prep_a(idx + 1)
                    up(idx, range(2, 12))
                    if idx + 1 < len(tiles):
                        prep_b(idx + 1)
                    up(idx, range(12, 22))
                    while deferred:
                        deferred.pop(0)()
                    down_ep(idx)
                while deferred:
                    deferred.pop(0)()
            fw.pop()

        ffn_phase(1, 0, x1_d, "x1", x2_d, "x2")
        if upto < 5:
            return nc

        fw.push()
        if True:
            W1 = fw.sb("W1", [128, 8, 3 * D], BF16, dma="sw")
            Wo1 = fw.sb("Wo1", [128, 8, D], BF16, dma="sw")
            fw.dma_group("pool", [(W1[:, k, :], win1_d[k * 128:(k + 1) * 128, :]) for k in range(8)], W1, writes=[W1])
            fw.dma_group("pool", [(Wo1[:, k, :], wout1_d[k * 128:(k + 1) * 128, :]) for k in range(8)], Wo1, writes=[Wo1])
            cwf = fw.sb("cwf", [3, D], F32, dma=True)
            fw.dma("sp", cwf[:], cw_d, cwf, writes=[cwf])
            cw = fw.sb("cw", [128, 8, 3], F32)
            pscw = fw.ps("pscw", [128, 8, 3], F32)
            for cc in range(8):
                op("pe", lambda: PE.transpose(pscw[:, cc, :], cwf[:, cc * 128:(cc + 1) * 128], identf[0:3, 0:3]), reads=[cwf, identf], writes=[pscw])
            op("dve", lambda: V.tensor_copy(out=cw[:], in_=pscw[:]), reads=[pscw], writes=[cw])
            lng = fw.sb("lng", [128, D], F32, dma=True); lnb = fw.sb("lnb", [128, D], F32, dma=True)
            load_bc(lng, lng_d[2]); load_bc(lnb, lnb_d[2])
            sc1 = fw.sb("sc1", [128, D], F32, dma=True); sh = fw.sb("sh", [128, D], F32, dma=True); g1 = fw.sb("g1", [128, D], F32, dma=True)
            xin = [fw.sb("xin%d" % i, [128, D], F32, dma=True) for i in range(4)]
            hT = [fw.sb("hT%d" % i, [128, 8, 256], BF16) for i in range(2)]
            cz = fw.sb("cz", [128, 8, 258], F32)
            csb = [fw.sb("csb%d" % i, [128, 256], F32) for i in range(2)]
            y1 = [fw.sb("y1_%d" % i, [128, 256], F32) for i in range(2)]
            byT = fw.sb("byT", [128, 8, 256], BF16)
            hp = HPrep(sc1, sh)
            ep = Epilogue(g1, lng, lnb, nt=2)
            psP = [fw.ps("psP%d" % i, [128, 4, 256], F32) for i in range(2)]
            psY_ = fw.ps("psY", [128, 2, 512], F32)
            psYs = [psY_, psY_]
            tiles = [(b, mt) for b in range(NB) for mt in range(S // 256)]
            state = {}
            xc = [0]

            def prep_a(idx):
                b, mt = tiles[idx]
                if mt == 0:
                    load_mod(2, b, [(sh, 0), (sc1, 1)])
                hTm = hT[idx % 2]
                xts = []; hbs = []
                for sub in range(2):
                    i = mt * 2 + sub
                    xt = xin[xc[0] % 4]; xc[0] += 1
                    fw.dma("sp", xt[:], x2_d[b, i * 128:(i + 1) * 128, :], xt, reads=[dbuf("x2", b, i)], writes=[xt])
                    hbs.append(hp.run_a(xt))
                    xts.append(xt)
                state[idx] = (hTm, xts, hbs)

            def prep_b(idx):
                hTm, xts, hbs = state[idx]
                for sub in range(2):
                    hp.run_b(hbs[sub], hTm, sub * 128)

            deferred = []

            def compute(idx, ccs):
                b, mt = tiles[idx]
                hTm = state[idx][0]
                for cc in ccs:
                    if cc == 0:
                        if mt == 0:
                            op("pool", lambda: G.memset(cz[:, :, 0:2], 0.0), reads=[cz], writes=[cz])
                        else:
                            op("pool", lambda: G.tensor_copy(out=cz[:, :, 0:2], in_=cz[:, :, 256:258]), reads=[cz], writes=[cz])
                    pp = psP[cc % 2]; cs = csb[cc % 2]; yt = y1[cc % 2]
                    for part in range(3):
                        for k in range(8):
                            op("pe", lambda: PE.matmul(pp[:, part, :], lhsT=W1[:, k, part * D + cc * 128:part * D + (cc + 1) * 128],
                                                       rhs=hTm[:, k, :], start=(k == 0), stop=(k == 7)), reads=[W1, hTm], writes=[pp])
                    op("act", lambda: A.copy(out=cs[:], in_=pp[:, 1, :]), reads=[pp], writes=[cs])
                    tt("dve", cz[:, cc, 2:258], pp[:, 2, :], cs[:], ALU.mult, [pp, cs, cz], [cz])
                    op("act", lambda: A.activation(out=yt[:], in_=cz[:, cc, 0:256], func=AF.Copy, scale=cw[:, cc, 0:1]),
                       reads=[cz, cw], writes=[yt])
                    op("dve", lambda: V.scalar_tensor_tensor(out=yt[:], in0=cz[:, cc, 1:257], scalar=cw[:, cc, 1:2], in1=yt[:],
                                                             op0=ALU.mult, op1=ALU.add), reads=[cz, cw, yt], writes=[yt])
                    op("dve", lambda: V.scalar_tensor_tensor(out=yt[:], in0=cz[:, cc, 2:258], scalar=cw[:, cc, 2:3], in1=yt[:],
                                                             op0=ALU.mult, op1=ALU.add), reads=[cz, cw, yt], writes=[yt])
                    tt("dve", byT[:, cc, :], pp[:, 0, :], yt[:], ALU.mult, [pp, yt], [byT])
                    for _ in range(4):
                        if deferred:
                            deferred.pop(0)()

            def out_ep(idx):
                b, mt = tiles[idx]
                xts = state.pop(idx)[1]
                if mt == 0:
                    load_mod(2, b, [(g1, 2)])
                for sub in range(2):
                    i = mt * 2 + sub
                    psY = psYs[sub]
                    for half in range(2):
                        for cc in range(8):
                            op("pe", lambda: PE.matmul(psY[:, half, :], lhsT=byT[:, cc, sub * 128:(sub + 1) * 128],
                                                       rhs=Wo1[:, cc, half * 512:(half + 1) * 512], start=(cc == 0), stop=(cc == 7)),
                               reads=[byT, Wo1], writes=[psY])
                    st_ = ep.steps([(psY, psY[:, 0, :]), (psY, psY[:, 1, :])], xts[sub], x3_d[b, i * 128:(i + 1) * 128, :], dbuf("x3", b, i))
                    nim = 3
                    for f_ in st_[:nim]:
                        f_()
                    deferred.extend(st_[nim:])

            prep_a(0); prep_b(0)
            for idx in range(len(tiles)):
                compute(idx, range(0, 1))
                if idx + 1 < len(tiles):
                    prep_a(idx + 1)
                compute(idx, range(1, 5))
                if idx + 1 < len(tiles):
                    prep_b(idx + 1)
                compute(idx, range(5, 8))
                while deferred:
                    deferred.pop(0)()
                out_ep(idx)
            while deferred:
                deferred.pop(0)()
        fw.pop()
        if upto < 6:
            return nc

        ffn_phase(3, 1, x3_d, "x3", out_d, "out")
        fw.barrier()
    return nc


_CACHE = {}


def _prep_inputs(inputs):
    f = lambda a: np.ascontiguousarray(np.asarray(a, dtype=np.float32))
    common = {
        "ada_w": f(inputs["ada_w"]).reshape(4, D, 3 * D), "ada_b": f(inputs["ada_b"]).reshape(4, 3 * D),
        "ln_g": f(inputs["ln_g"]).reshape(4, D), "ln_b": f(inputs["ln_b"]).reshape(4, D),
        "even_w_in": f(inputs["even_w_in"])[0], "even_cmp_pos": f(inputs["even_cmp_pos"])[0],
        "even_cmp_w1": f(inputs["even_cmp_w1"])[0], "even_cmp_w2": f(inputs["even_cmp_w2"])[0],
        "even_gmlp_norm_g": f(inputs["even_gmlp_norm_g"])[0].reshape(512), "even_gmlp_ws": f(inputs["even_gmlp_ws"])[0],
        "even_gmlp_bs": f(inputs["even_gmlp_bs"])[0], "even_w_out": f(inputs["even_w_out"])[0],
        "odd_w_in": f(inputs["odd_w_in"])[0], "odd_conv_w": f(inputs["odd_conv_w"])[0], "odd_w_out": f(inputs["odd_w_out"])[0],
        "ffn_w_in": f(inputs["ffn_w_in"]), "ffn_w_out": f(inputs["ffn_w_out"]),
    }
    x = f(inputs["x"]); c = f(inputs["c"])
    maps = []
    for i in range(N_CORES):
        m = dict(common)
        m["x"] = x[NB * i:NB * (i + 1)]; m["c"] = c[NB * i:NB * (i + 1)]
        maps.append(m)
    return maps


def kernel(**inputs):
    if "nc" not in _CACHE:
        _CACHE["nc"] = build_program()
    nc = _CACHE["nc"]
    maps = _prep_inputs(inputs)
    res = run_bass_kernel_spmd(nc, maps, core_ids=list(range(N_CORES)))
    return np.concatenate([r["out"] for r in res.results], axis=0).astype(np.float32)
```

```python
import numpy as np
import concourse.bass as bass
import concourse.mybir as mybir
from concourse.bass_utils import run_bass_kernel_spmd
from contextlib import ExitStack

F32 = mybir.dt.float32; BF16 = mybir.dt.bfloat16; I32 = mybir.dt.int32
ALU = mybir.AluOpType; AF = mybir.ActivationFunctionType; AX = mybir.AxisListType

S = 4096; D = 1024; NB = 2; NT = S // 128
FH = 2816
NEG = -30000.0
ALPHA = 4.0 ** 0.25
EPS = 1e-5
N_CORES = 8
P3_TILES = NT
P1_MT = 8
SELMODE = 0
NB_RUN = NB
P3_PARTS = ('cmp', 'topk', 'sel', 'win', 'comb', 'out')


class Buf:
    __slots__ = ("name", "w", "r", "dsem")

    def __init__(self, name):
        self.name = name; self.w = {}; self.r = {}; self.dsem = None


class T:
    def __init__(self, h, buf):
        self.h = h; self.b = buf

    def __getitem__(self, k):
        return self.h[k]


class FW:
    def __init__(self, nc, ges, n_dsem=56):
        self.nc = nc; self.es = ges
        self.eng = {"pe": nc.tensor, "act": nc.scalar, "dve": nc.vector, "pool": nc.gpsimd, "sp": nc.sync}
        self.sems = {}; self.cnt = {}
        for e in ("pe", "act", "dve", "pool"):
            self.sems[e] = ges.enter_context(nc.semaphore("s_" + e)); self.cnt[e] = 0
        self.dpool = []; self.dpool_sw = []
        for i in range(n_dsem):
            k = "d%d" % i
            self.sems[k] = ges.enter_context(nc.semaphore("s_" + k)); self.cnt[k] = 0
            (self.dpool_sw if i < 10 else self.dpool).append(k)
        self.checked = []
        self.scopes = []
        self.seen = {e: {} for e in self.eng}
        self.uid = 0

    def push(self):
        st = ExitStack(); st.__enter__()
        self.scopes.append((self.es, self.checked, st))
        self.es = st; self.checked = []

    def pop(self):
        self.barrier()
        old_es, old_checked, st = self.scopes.pop()
        st.__exit__(None, None, None)
        for k in self.checked:
            (self.dpool_sw if int(k[1:]) < 10 else self.dpool).append(k)
        self.es = old_es; self.checked = old_checked

    def sb(self, name, shape, dt, dma=False):
        self.uid += 1
        h = self.es.enter_context(self.nc.sbuf_tensor("%s_%d" % (name, self.uid), shape, dt))
        b = Buf(name)
        if dma:
            b.dsem = (self.dpool_sw if dma == "sw" else self.dpool).pop(0); self.checked.append(b.dsem)
        return T(h, b)

    def ps(self, name, shape, dt):
        self.uid += 1
        h = self.es.enter_context(self.nc.psum_tensor("%s_%d" % (name, self.uid), shape, dt))
        return T(h, Buf(name))

    def _deps(self, reads, writes):
        deps = {}
        for b in reads:
            for k, v in b.w.items():
                if deps.get(k, 0) < v: deps[k] = v
        for b in writes:
            for k, v in b.w.items():
                if deps.get(k, 0) < v: deps[k] = v
            for k, v in b.r.items():
                if deps.get(k, 0) < v: deps[k] = v
        return deps

    def _wait(self, e, deps):
        seen = self.seen[e]; eng = self.eng[e]
        for k, v in deps.items():
            if k == "pe" and e == "pe":
                continue
            if seen.get(k, 0) < v:
                eng.wait_ge(self.sems[k], v); seen[k] = v

    def _record(self, reads, writes, key, val):
        for b in writes:
            b.w = {key: val}; b.r = {}
        for b in reads:
            if b.r.get(key, 0) < val: b.r[key] = val

    @staticmethod
    def _bl(ts):
        return [t.b if isinstance(t, T) else t for t in ts]

    def op(self, e, fn, reads=(), writes=()):
        reads = self._bl(reads); writes = self._bl(writes)
        self._wait(e, self._deps(reads, writes))
        ins = fn()
        self.cnt[e] += 1
        ins.then_inc(self.sems[e], 1)
        self._record(reads, writes, e, self.cnt[e])
        return ins

    def dma(self, q, out, in_, sbuf_t, reads=(), writes=()):
        reads = self._bl(reads); writes = self._bl(writes)
        sbb = sbuf_t.b if isinstance(sbuf_t, T) else sbuf_t
        assert sbb.dsem is not None, sbb.name
        self._wait(q, self._deps(reads, writes))
        ins = self.eng[q].dma_start(out=out, in_=in_)
        key = sbb.dsem
        self.cnt[key] += 16
        ins.then_inc(self.sems[key], 16)
        self._record(reads, writes, key, self.cnt[key])
        return ins

    def dma_group(self, q, pairs, sbuf_t, reads=(), writes=()):
        reads = self._bl(reads); writes = self._bl(writes)
        sbb = sbuf_t.b if isinstance(sbuf_t, T) else sbuf_t
        self._wait(q, self._deps(reads, writes))
        key = sbb.dsem
        base = self.cnt[key]
        for n_, (o, i_) in enumerate(pairs):
            if n_ >= 4 and n_ % 4 == 0:
                self.eng[q].wait_ge(self.sems[key], base + 16 * n_)
            ins = self.eng[q].dma_start(out=o, in_=i_)
            self.cnt[key] += 16
            ins.then_inc(self.sems[key], 16)
        self._record(reads, writes, key, self.cnt[key])

    def barrier(self):
        deps = {k: v for k, v in self.cnt.items() if v > 0}
        for e in self.eng:
            seen = self.seen[e]
            for k, v in deps.items():
                if seen.get(k, 0) < v:
                    self.eng[e].wait_ge(self.sems[k], v); seen[k] = v


def build_program(upto=99, debug=False):
    nc = bass.Bass("TRN2", target_bir_lowering=False)
    kS = "ExternalOutput" if debug else "Internal"

    def din(name, shape):
        return nc.dram_tensor(name, shape, F32, kind="ExternalInput").ap()

    x_d = din("x", [NB, S, D]); c_d = din("c", [NB, D])
    adaw_d = din("ada_w", [4, D, 3 * D]); adab_d = din("ada_b", [4, 3 * D])
    lng_d = din("ln_g", [4, D]); lnb_d = din("ln_b", [4, D])
    win0_d = din("even_w_in", [D, 2328]); pos_d = din("even_cmp_pos", [2, 32, 64])
    w1_d = din("even_cmp_w1", [2, 2048, 128]); w2_d = din("even_cmp_w2", [2, 128, 64])
    gng_d = din("even_gmlp_norm_g", [512]); gws_d = din("even_gmlp_ws", [8, 128, 128]); gbs_d = din("even_gmlp_bs", [8, 128])
    wout0_d = din("even_w_out", [D, D])
    win1_d = din("odd_w_in", [D, 3 * D]); cw_d = din("odd_conv_w", [3, D]); wout1_d = din("odd_w_out", [D, D])
    fwi_d = din("ffn_w_in", [2, D, 2 * FH]); fwo_d = din("ffn_w_out", [2, FH, D])
    out_d = nc.dram_tensor("out", [NB, S, D], F32, kind="ExternalOutput").ap()
    mod_d = nc.dram_tensor("mod_s", [4, NB, 3 * D], F32, kind=kS).ap()
    x1_d = nc.dram_tensor("x1_s", [NB, S, D], F32, kind=kS).ap()
    x2_d = nc.dram_tensor("x2_s", [NB, S, D], F32, kind=kS).ap()
    x3_d = nc.dram_tensor("x3_s", [NB, S, D], F32, kind=kS).ap()
    ogm_d = nc.dram_tensor("ogm_s", [NB, S, 512], BF16, kind=kS).ap()
    dbg = {}
    if debug:
        dbg["onsa"] = nc.dram_tensor("onsa_s", [NB, S, 512], F32, kind=kS).ap()
        dbg["kcc"] = nc.dram_tensor("kcc_s", [NB, 128, 256], F32, kind=kS).ap()

    dbufs = {}

    def dbuf(*key):
        if key not in dbufs:
            dbufs[key] = Buf(str(key))
        return dbufs[key]

    with ExitStack() as ges, nc.allow_non_contiguous_dma(reason="small param layouts"):
        fw = FW(nc, ges)
        op = fw.op
        V = nc.vector; A = nc.scalar; G = nc.gpsimd; PE = nc.tensor
        VENG = {"dve": V, "pool": G}

        ident = fw.sb("ident", [128, 128], BF16)
        identf = fw.sb("identf", [128, 128], F32)
        epsb = fw.sb("epsb", [128, 1], F32)
        op("pool", lambda: G.memset(identf[:], 1.0), writes=[identf])
        op("pool", lambda: G.affine_select(out=identf[:], in_=identf[:], pattern=[[1, 128]], compare_op=ALU.is_equal,
                                           fill=0.0, base=0, channel_multiplier=-1), reads=[identf], writes=[identf])
        op("pool", lambda: G.tensor_copy(out=ident[:], in_=identf[:]), reads=[identf], writes=[ident])
        op("pool", lambda: G.memset(epsb[:], EPS), writes=[epsb])
        mhalf = fw.sb("mhalf", [128, 8], F32)
        op("pool", lambda: G.memset(mhalf[:], -0.5), writes=[mhalf])

        def tt(e, out, in0, in1, o, rd, wr):
            return op(e, lambda: VENG[e].tensor_tensor(out=out, in0=in0, in1=in1, op=o), reads=rd, writes=wr)

        def load_bc(dst, src_1d, q="sp"):
            fw.dma(q, dst[:], src_1d.partition_broadcast(128), dst, writes=[dst])

        def load_w_bf16(dst, kidx, src2d):
            fw.dma("pool", dst[:, kidx, :], src2d, dst, writes=[dst])

        class HPrep:
            def __init__(self, sc1, sh, nhb=2, tmp_dt=F32):
                self.sc1 = sc1; self.sh = sh; self.nhb = nhb
                self.tmp = [fw.sb("hp_tmp%d" % i, [128, D], tmp_dt) for i in range(1)]
                self.hb = [fw.sb("hp_hb%d" % i, [128, D], BF16) for i in range(nhb)]
                self.psT = fw.ps("hp_psT", [128, 8, 128], BF16)
                self.n = 0

            def run_a(self, xt):
                tmp = self.tmp[0]; hb = self.hb[self.n % self.nhb]; self.n += 1
                tt("dve", tmp[:], xt[:], self.sc1[:], ALU.mult, [xt, self.sc1], [tmp])
                tt("pool", hb[:], tmp[:], self.sh[:], ALU.add, [tmp, self.sh], [hb])
                return hb

            def run_b(self, hb, hT, off):
                psT = self.psT
                for c in range(8):
                    op("pe", lambda: PE.transpose(psT[:, c, :], hb[:, c * 128:(c + 1) * 128], ident[:]),
                       reads=[hb, ident], writes=[psT])
                op("act", lambda: A.copy(out=hT[:, :, off:off + 128], in_=psT[:]), reads=[psT], writes=[hT])

            def run(self, xt, hT, off):
                self.run_b(self.run_a(xt), hT, off)

        class Epilogue:
            def __init__(self, g1, lng, lnb, nt=1):
                self.g1 = g1; self.lng = lng; self.lnb = lnb
                self.ts = [fw.sb("ep_t%d" % i, [128, D], F32) for i in range(nt)]
                self.xo = [fw.sb("ep_xo%d" % i, [128, D], F32, dma=True) for i in range(2)]
                self.sts = [fw.sb("ep_st%d" % i, [128, 2, 6], F32) for i in range(nt)]
                self.mvs = [fw.sb("ep_mv%d" % i, [128, 8], F32) for i in range(nt)]
                self.k = 0

            def steps(self, halves, xres, out_ap, out_db):
                t = self.ts[self.k % len(self.ts)]; st = self.sts[self.k % len(self.ts)]; mv = self.mvs[self.k % len(self.ts)]
                xo = self.xo[self.k % 2]; self.k += 1
                g1 = self.g1; lng = self.lng; lnb = self.lnb
                L = []
                for hi, (pt, pap) in enumerate(halves):
                    L.append(lambda hi=hi, pt=pt, pap=pap: tt("dve", t[:, hi * 512:(hi + 1) * 512], pap, g1[:, hi * 512:(hi + 1) * 512], ALU.mult, [pt, g1], [t]))
                L.append(lambda: op("dve", lambda: V.scalar_tensor_tensor(out=t[:], in0=xres[:], scalar=ALPHA, in1=t[:], op0=ALU.mult, op1=ALU.add),
                                    reads=[xres, t], writes=[t]))
                L.append(lambda: op("dve", lambda: V.bn_stats(out=st[:, 0, :], in_=t[:, 0:512]), reads=[t], writes=[st]))
                L.append(lambda: op("dve", lambda: V.bn_stats(out=st[:, 1, :], in_=t[:, 512:1024]), reads=[t, st], writes=[st]))
                L.append(lambda: op("dve", lambda: V.bn_aggr(out=mv[:, 0:2], in_=st[:].rearrange("p a b -> p (a b)")), reads=[st], writes=[mv]))
                L.append(lambda: op("dve", lambda: V.tensor_scalar_add(out=mv[:, 2:3], in0=mv[:, 1:2], scalar1=EPS), reads=[mv], writes=[mv]))
                L.append(lambda: op("pool", lambda: G.tensor_tensor(out=mv[:, 3:4], in0=mv[:, 2:3], in1=mhalf[:, 0:1], op=ALU.pow),
                                    reads=[mv, mhalf], writes=[mv]))
                L.append(lambda: op("dve", lambda: V.tensor_scalar(out=mv[:, 4:5], in0=mv[:, 0:1], scalar1=mv[:, 3:4], scalar2=-1.0,
                                                                   op0=ALU.mult, op1=ALU.mult), reads=[mv], writes=[mv]))
                L.append(lambda: op("act", lambda: A.activation(out=xo[:], in_=t[:], func=AF.Identity, bias=mv[:, 4:5], scale=mv[:, 3:4]),
                                    reads=[t, mv], writes=[xo]))
                L.append(lambda: tt("pool", xo[:], xo[:], lng[:], ALU.mult, [xo, lng], [xo]))
                L.append(lambda: tt("pool", xo[:], xo[:], lnb[:], ALU.add, [xo, lnb], [xo]))
                L.append(lambda: fw.dma("sp", out_ap, xo[:], xo, reads=[xo], writes=[out_db]))
                return L

            def run(self, halves, xres, out_ap, out_db):
                for f in self.steps(halves, xres, out_ap, out_db):
                    f()

        def load_mod(ls, b, want):
            for tile, which in want:
                load_bc(tile, mod_d[ls, b, which * D:(which + 1) * D])

        def transpose_small(dst_ap, dstT, src_ap, srcT, npart, pst, pst_ap):
            op("pe", lambda: PE.transpose(pst_ap, src_ap, identf[0:npart, 0:npart]), reads=[srcT, identf], writes=[pst])
            op("dve", lambda: V.tensor_copy(out=dst_ap, in_=pst_ap), reads=[pst], writes=[dstT])

        fw.push()
        if True:
            csb_ = fw.sb("csb", [NB, D], F32, dma=True)
            scs = fw.sb("scs", [NB, D], F32)
            scT = fw.sb("scT", [128, 8, NB], F32)
            stage = [fw.sb("adastage%d" % i, [128, 3 * D], F32, dma=True) for i in range(2)]
            adab = fw.sb("adab", [NB, 3 * D], F32, dma=True)
            msb = fw.sb("msb", [NB, 3 * D], F32, dma=True)
            psm = [fw.ps("psm%d" % i, [NB, 512], F32) for i in range(6)]
            pst = fw.ps("pst_p0", [128, 8, NB], F32)
            fw.dma("sp", csb_[:], c_d, csb_, writes=[csb_])
            op("act", lambda: A.activation(out=scs[:], in_=csb_[:], func=AF.Silu), reads=[csb_], writes=[scs])
            for k in range(8):
                op("pe", lambda: PE.transpose(pst[:, k, :], scs[:, k * 128:(k + 1) * 128], identf[0:NB, 0:NB]), reads=[scs, identf], writes=[pst])
            op("dve", lambda: V.tensor_copy(out=scT[:], in_=pst[:]), reads=[pst], writes=[scT])
            si = 0
            for ls in range(4):
                fw.dma("sp", adab[:], adab_d[ls].partition_broadcast(NB), adab, writes=[adab])
                for k in range(8):
                    stg = stage[si % 2]; si += 1
                    fw.dma("sp", stg[:], adaw_d[ls, k * 128:(k + 1) * 128, :], stg, writes=[stg])
                    for ct in range(6):
                        op("pe", lambda: PE.matmul(psm[ct][:], lhsT=scT[:, k, :], rhs=stg[:, ct * 512:(ct + 1) * 512],
                                                   start=(k == 0), stop=(k == 7)), reads=[scT, stg], writes=[psm[ct]])
                for ct in range(6):
                    tt("dve", msb[:, ct * 512:(ct + 1) * 512], psm[ct][:], adab[:, ct * 512:(ct + 1) * 512], ALU.add,
                       [psm[ct], adab], [msb])
                op("dve", lambda: V.tensor_scalar_add(out=msb[:, D:3 * D], in0=msb[:, D:3 * D], scalar1=1.0), reads=[msb], writes=[msb])
                fw.dma("sp", mod_d[ls], msb[:], msb, reads=[msb], writes=[dbuf("mod")])
        fw.pop()
        if upto < 1:
            return nc

        fw.push()
        if True:
            WsT = fw.sb("WsT", [128, 8, 128], BF16)
            bsT = fw.sb("bsT", [128, 8], F32)
            ngb = fw.sb("ngb", [128, 512], F32, dma=True)
            load_bc(ngb, gng_d)
            cbias = fw.sb("cbias", [128, 2], F32)
            w2 = fw.sb("cw2", [128, 2, 64], BF16, dma="sw")
            fw.dma("pool", w2[:], w2_d.rearrange("kv h d -> h kv d"), w2, writes=[w2])
            w2pad = fw.sb("cw2pad", [128, 2, 128], BF16)
            op("pool", lambda: G.memset(w2pad[:], 0.0), writes=[w2pad])
            for g in range(2):
                op("pool", lambda: G.tensor_copy(out=w2pad[:, g, g * 64:(g + 1) * 64], in_=w2[:, 0, :]), reads=[w2, w2pad], writes=[w2pad])
            McNeg = fw.sb("McNeg", [128, 128], BF16)
            MwNeg = fw.sb("MwNeg", [128, 128], BF16)
            cm_all = fw.sb("cm_all", [128, 33, 128], BF16)
            TMm = fw.sb("TMm", [128, 126], F32); TB = fw.sb("TB", [128, 126], F32)
            qT = fw.sb("qT", [128, 4, S], BF16)
            ksE = [fw.sb("ksE%d" % g_, [128, S], BF16) for g_ in range(2)]; kwT = fw.sb("kwT", [128, S], BF16)
            kcT = fw.sb("kcT", [128, S], BF16); vcT = fw.sb("vcT", [128, S], BF16)
            vsA = fw.sb("vsA", [128, NT, 2, 65], BF16); vwA = fw.sb("vwA", [128, NT, 2, 65], BF16)
            gat = fw.sb("gat", [128, NT, 24], F32)
            kcmpT = fw.sb("kcmpT", [128, 256], BF16)
            vcx = fw.sb("vcx", [128, 2, 2, 129], BF16)
            cmpmask = {}
            if P1_MT < 8:
                for tcache in (qT, ksE[0], ksE[1], kwT, kcT, vcT, gat):
                    op("pool", lambda: G.memset(tcache[:], 0.0), writes=[tcache])
                op("pool", lambda: G.memset(vsA[:], 0.0), writes=[vsA]); op("pool", lambda: G.memset(vwA[:], 0.0), writes=[vwA])
            fw.push()
            if True:
                w1t = fw.sb("cw1t", [64, 2, 32, 128], BF16, dma="sw")
                for kv in range(2):
                    fw.dma("pool", w1t[:, kv, :, :], w1_d[kv].rearrange("(l d) h -> d l h", d=64), w1t, writes=[w1t])
                posf = fw.sb("posf", [32, 2, 64], F32, dma=True)
                fw.dma("sp", posf[:], pos_d.rearrange("kv l d -> l kv d"), posf, writes=[posf])
                posT = fw.sb("posT", [64, 2, 32], BF16)
                pss = fw.ps("pss", [128, 64], F32)
                for kv in range(2):
                    transpose_small(posT[:, kv, :], posT, posf[:, kv, :], posf, 32, pss, pss[0:64, 0:32])
                psb = fw.ps("psb", [128, 2], F32)
                for kv in range(2):
                    for l in range(32):
                        op("pe", lambda: PE.matmul(psb[:, kv:kv + 1], lhsT=w1t[:, kv, l, :], rhs=posT[:, kv, l:l + 1],
                                                   start=(l == 0), stop=(l == 31)), reads=[w1t, posT], writes=[psb])
                op("dve", lambda: V.tensor_copy(out=cbias[:], in_=psb[:]), reads=[psb], writes=[cbias])
                gbsf = fw.sb("gbsf", [8, 128], F32, dma=True)
                fw.dma("sp", gbsf[:], gbs_d, gbsf, writes=[gbsf])
                transpose_small(bsT[:], bsT, gbsf[:], gbsf, 8, pss, pss[:, 0:8])
                wsf = fw.sb("wsf", [128, 8, 128], F32, dma=True)
                wsb = fw.sb("wsb", [128, 8, 128], BF16)
                fw.dma("sp", wsf[:], gws_d.rearrange("g t s -> t g s"), wsf, writes=[wsf])
                op("pool", lambda: G.affine_select(out=wsf[:], in_=wsf[:], pattern=[[0, 8], [-1, 128]], compare_op=ALU.is_ge,
                                                   fill=0.0, base=0, channel_multiplier=1), reads=[wsf], writes=[wsf])
                op("pool", lambda: G.tensor_copy(out=wsb[:], in_=wsf[:]), reads=[wsf], writes=[wsb])
                pst = fw.ps("pst0", [128, 8, 128], BF16)
                for gm in range(8):
                    op("pe", lambda: PE.transpose(pst[:, gm, :], wsb[:, gm, :], ident[:]), reads=[wsb, ident], writes=[pst])
                op("dve", lambda: V.tensor_copy(out=WsT[:], in_=pst[:]), reads=[pst], writes=[WsT])
                zf = fw.sb("zf", [128, 128], F32)
                mtmp = fw.sb("mtmp", [128, 128], F32)
                op("pool", lambda: G.memset(zf[:], 0.0), writes=[zf])
                op("pool", lambda: G.affine_select(out=mtmp[:], in_=zf[:], pattern=[[1, 128]], compare_op=ALU.is_ge, fill=NEG,
                                                   base=0, channel_multiplier=-1), reads=[zf], writes=[mtmp])
                op("pool", lambda: G.tensor_copy(out=McNeg[:], in_=mtmp[:]), reads=[mtmp], writes=[McNeg])
                op("pool", lambda: G.affine_select(out=mtmp[:], in_=zf[:], pattern=[[-1, 128]], compare_op=ALU.is_gt, fill=NEG,
                                                   base=0, channel_multiplier=1), reads=[zf], writes=[mtmp])
                op("pool", lambda: G.tensor_copy(out=MwNeg[:], in_=mtmp[:]), reads=[mtmp], writes=[MwNeg])
                mi = 0
                for c in range(2):
                    for i in (range(0, 17) if c == 0 else range(16, 32)):
                        op("pool", lambda: G.affine_select(out=mtmp[:], in_=zf[:], pattern=[[1, 128]], compare_op=ALU.is_ge, fill=NEG,
                                                           base=128 * i - 31 - 2048 * c, channel_multiplier=-16),
                           reads=[zf], writes=[mtmp])
                        op("pool", lambda: G.tensor_copy(out=cm_all[:, mi, :], in_=mtmp[:]), reads=[mtmp], writes=[cm_all])
                        cmpmask[(c, i)] = mi; mi += 1
                ebf = fw.sb("ebf", [64, S], F32)
                op("pool", lambda: G.memset(ebf[:], 1.0), writes=[ebf])
                op("pool", lambda: G.affine_select(out=ebf[:], in_=ebf[:], pattern=[[1, S]], compare_op=ALU.is_ge, fill=0.0,
                                                   base=0, channel_multiplier=-64), reads=[ebf], writes=[ebf])
                op("pool", lambda: G.affine_select(out=ebf[:], in_=ebf[:], pattern=[[-1, S]], compare_op=ALU.is_ge, fill=0.0,
                                                   base=63, channel_multiplier=64), reads=[ebf], writes=[ebf])
                op("pool", lambda: G.tensor_copy(out=ksE[1][0:64, :], in_=ebf[:]), reads=[ebf, ksE[1]], writes=[ksE[1]])
                ebf2 = fw.sb("ebf2", [128, S], F32)
                op("pool", lambda: G.memset(ebf2[:], 1.0), writes=[ebf2])
                op("pool", lambda: G.affine_select(out=ebf2[:], in_=ebf2[:], pattern=[[1, S]], compare_op=ALU.is_ge, fill=0.0,
                                                   base=4096, channel_multiplier=-64), reads=[ebf2], writes=[ebf2])
                op("pool", lambda: G.affine_select(out=ebf2[:], in_=ebf2[:], pattern=[[-1, S]], compare_op=ALU.is_ge, fill=0.0,
                                                   base=-4033, channel_multiplier=64), reads=[ebf2], writes=[ebf2])
                op("pool", lambda: G.tensor_copy(out=ksE[0][64:128, :], in_=ebf2[64:128, :]), reads=[ebf2, ksE[0]], writes=[ksE[0]])
                for hh in range(2):
                    hs = slice(hh * 64, (hh + 1) * 64)
                    op("pool", lambda: G.memset(TMm[hs, 0:61 + hh], 1.0), reads=[TMm], writes=[TMm])
                    op("pool", lambda: G.memset(TMm[hs, 61 + hh:126], 0.0), reads=[TMm], writes=[TMm])
                    op("pool", lambda: G.memset(TB[hs, 0:61 + hh], 0.0), reads=[TB], writes=[TB])
                    op("pool", lambda: G.memset(TB[hs, 61 + hh:63 + hh], 1e4), reads=[TB], writes=[TB])
                    op("pool", lambda: G.memset(TB[hs, 63 + hh:126], -1e4), reads=[TB], writes=[TB])
                op("pool", lambda: G.memset(vsA[:, :, :, 64:65], 1.0), writes=[vsA])
                op("pool", lambda: G.memset(vwA[:, :, :, 64:65], 1.0), writes=[vwA])
                op("pool", lambda: G.memset(vcx[:], 0.0), writes=[vcx])
                op("pool", lambda: G.memset(vcx[:, :, :, 64:65], 1.0), reads=[vcx], writes=[vcx])
                covf = fw.sb("covf", [128, 2, 64], F32)
                op("pool", lambda: G.memset(covf[:], 1.0), writes=[covf])
                op("pool", lambda: G.affine_select(out=covf[:], in_=covf[:], pattern=[[-128, 2], [4, 64]], compare_op=ALU.is_ge,
                                                   fill=0.0, base=3, channel_multiplier=-1), reads=[covf], writes=[covf])
                op("pool", lambda: G.affine_select(out=covf[:], in_=covf[:], pattern=[[128, 2], [-4, 64]], compare_op=ALU.is_ge,
                                                   fill=0.0, base=1, channel_multiplier=1), reads=[covf], writes=[covf])
                for g in range(2):
                    op("pool", lambda: G.tensor_copy(out=vcx[:, :, g, 65:129], in_=covf[:]), reads=[covf, vcx], writes=[vcx])
                op("pool", lambda: G.memset(kcmpT[:], 0.0), writes=[kcmpT])
            fw.pop()

            v3 = lambda ap: ap.rearrange("p (g d) -> p g d", d=64)
            b8 = lambda ap: ap.unsqueeze(2).to_broadcast([128, 8, 64])

            for b in range(NB_RUN):
                fw.push()
                if True:
                    W0 = fw.sb("W0", [128, 8, 2328], BF16, dma="sw")
                    Wq = fw.sb("Wq", [128, 8, 512], BF16, dma="sw")
                    fw.dma_group("pool", [(W0[:, k, :], win0_d[k * 128:(k + 1) * 128, :]) for k in range(8)], W0, writes=[W0])
                    fw.dma_group("pool", [(Wq[:, k, r * 128:(r + 1) * 128].rearrange("p (g d) -> p g d", g=2),
                                           win0_d[k * 128:(k + 1) * 128, 0:512].rearrange("p (g r d) -> p g r d", g=2, r=4)[:, :, r, :])
                                          for k in range(8) for r in range(4)], Wq, writes=[Wq])
                    sc1 = fw.sb("sc1", [128, D], F32, dma=True); sh = fw.sb("sh", [128, D], F32, dma=True)
                    load_mod(0, b, [(sh, 0), (sc1, 1)])
                    xin = [fw.sb("xin%d" % i, [128, D], F32, dma=True) for i in range(2)]
                    hT = [fw.sb("hT%d" % i, [128, 8, 512], BF16) for i in range(2)]
                    hp = HPrep(sc1, sh)
                    psA = fw.ps("psA", [128, 512], F32); psM = fw.ps("psM", [128, 512], F32)
                    psBs = [fw.ps("psB%d" % i, [128, 512], F32) for i in range(2)]
                    psCs = [fw.ps("psC%d" % i, [128, 512], F32) for i in range(2)]
                    psF = [psBs[1], psCs[1]]
                    ugs = [fw.sb("ug%d" % i, [128, 512], F32) for i in range(2)]; vgs = [fw.sb("vg%d" % i, [128, 512], F32) for i in range(2)]
                    sq = fw.sb("sq", [128, 512], F32)
                    gst = fw.sb("gst", [128, 6, 8], F32)
                    vn = fw.sb("vn", [128, 512], BF16)
                    og = [fw.sb("og%d" % i, [128, 512], BF16, dma=True) for i in range(2)]
                    xcount = [0]
                    pend = {}

                    def p1_prep_a(mt_, sub_):
                        if mt_ >= P1_MT:
                            return
                        i_ = mt_ * 4 + sub_
                        xt = xin[xcount[0] % 2]; xcount[0] += 1
                        fw.dma("sp", xt[:], x_d[b, i_ * 128:(i_ + 1) * 128, :], xt, writes=[xt])
                        pend[(mt_, sub_)] = hp.run_a(xt)

                    def p1_prep_b(mt_, sub_):
                        if mt_ >= P1_MT:
                            return
                        hp.run_b(pend.pop((mt_, sub_)), hT[mt_ % 2], sub_ * 128)

                    for sub in range(4):
                        p1_prep_a(0, sub); p1_prep_b(0, sub)
                    for mt in range(P1_MT):
                        hTm = hT[mt % 2]
                        fm = []
                        for r in range(4):
                            fm.append((Wq[:, :, r * 128:(r + 1) * 128], qT[:, r, mt * 512:(mt + 1) * 512], qT))
                        fm.append((W0[:, :, 512:640], kcT[:, mt * 512:(mt + 1) * 512], kcT))
                        fm.append((W0[:, :, 640:768], vcT[:, mt * 512:(mt + 1) * 512], vcT))
                        fm.append((W0[:, :, 768:896], None, "ks"))
                        fm.append((W0[:, :, 1024:1152], kwT[:, mt * 512:(mt + 1) * 512], kwT))
                        for fi, (wap, dst, dstT) in enumerate(fm):
                            pf = psF[fi % 2]
                            for k in range(8):
                                op("pe", lambda: PE.matmul(pf[:], lhsT=wap[:, k], rhs=hTm[:, k, :], start=(k == 0), stop=(k == 7)),
                                   reads=[W0, Wq, hTm], writes=[pf])
                            if dstT == "ks":
                                op("act", lambda: A.copy(out=ksE[0][0:64, mt * 512:(mt + 1) * 512], in_=pf[0:64, :]), reads=[pf, ksE[0]], writes=[ksE[0]])
                                op("dve", lambda: V.tensor_copy(out=ksE[1][64:128, mt * 512:(mt + 1) * 512], in_=pf[64:128, :]), reads=[pf, ksE[1]], writes=[ksE[1]])
                            elif fi % 2 == 0:
                                op("act", lambda: A.copy(out=dst, in_=pf[:]), reads=[pf], writes=[dstT])
                            else:
                                op("dve", lambda: V.tensor_copy(out=dst, in_=pf[:]), reads=[pf], writes=[dstT])
                        def p1_abc(sub_):
                            for k in range(8):
                                op("pe", lambda: PE.matmul(psA[:, 0:408], lhsT=hTm[:, k, sub_ * 128:(sub_ + 1) * 128], rhs=W0[:, k, 896:1304], start=(k == 0), stop=(k == 7)),
                                   reads=[W0, hTm], writes=[psA])
                            for k in range(8):
                                op("pe", lambda: PE.matmul(psBs[sub_ % 2][:], lhsT=hTm[:, k, sub_ * 128:(sub_ + 1) * 128], rhs=W0[:, k, 1304:1816], start=(k == 0), stop=(k == 7)),
                                   reads=[W0, hTm], writes=[psBs[sub_ % 2]])
                            for k in range(8):
                                op("pe", lambda: PE.matmul(psCs[sub_ % 2][:], lhsT=hTm[:, k, sub_ * 128:(sub_ + 1) * 128], rhs=W0[:, k, 1816:2328], start=(k == 0), stop=(k == 7)),
                                   reads=[W0, hTm], writes=[psCs[sub_ % 2]])

                        p1_abc(0)
                        for sub in range(4):
                            i = mt * 4 + sub
                            psB = psBs[sub % 2]; psC = psCs[sub % 2]; ug = ugs[sub % 2]; vg = vgs[sub % 2]
                            p1_prep_a(mt + 1, sub)
                            op("dve", lambda: V.tensor_copy(out=vsA[:, i, :, 0:64], in_=v3(psA[:, 0:128])), reads=[psA], writes=[vsA])
                            op("dve", lambda: V.tensor_copy(out=vwA[:, i, :, 0:64], in_=v3(psA[:, 256:384])), reads=[psA], writes=[vwA])
                            op("dve", lambda: V.tensor_copy(out=gat[:, i, :], in_=psA[:, 384:408]), reads=[psA], writes=[gat])
                            op("act", lambda: A.activation(out=ug[:], in_=psB[:], func=AF.Gelu_apprx_tanh), reads=[psB], writes=[ug])
                            op("act", lambda: A.activation(out=vg[:], in_=psC[:], func=AF.Gelu_apprx_tanh), reads=[psC], writes=[vg])
                            if sub < 3:
                                p1_abc(sub + 1)
                            op("dve", lambda: V.tensor_reduce(out=gst[:, 0, :], in_=v3(vg[:]), axis=AX.X, op=ALU.add), reads=[vg], writes=[gst])
                            tt("dve", sq[:], vg[:], vg[:], ALU.mult, [vg], [sq])
                            op("dve", lambda: V.tensor_reduce(out=gst[:, 1, :], in_=v3(sq[:]), axis=AX.X, op=ALU.add), reads=[sq, gst], writes=[gst])
                            op("dve", lambda: V.tensor_scalar(out=gst[:, 2, :], in0=gst[:, 0, :], scalar1=1.0 / 64, scalar2=None, op0=ALU.mult),
                               reads=[gst], writes=[gst])
                            tt("dve", gst[:, 5, :], gst[:, 2, :], gst[:, 2, :], ALU.mult, [gst], [gst])
                            op("dve", lambda: V.scalar_tensor_tensor(out=gst[:, 3, :], in0=gst[:, 1, :], scalar=1.0 / 64, in1=gst[:, 5, :],
                                                                     op0=ALU.mult, op1=ALU.subtract), reads=[gst], writes=[gst])
                            op("dve", lambda: V.tensor_scalar_add(out=gst[:, 3, :], in0=gst[:, 3, :], scalar1=EPS), reads=[gst], writes=[gst])
                            op("pool", lambda: G.tensor_tensor(out=gst[:, 4, :], in0=gst[:, 3, :], in1=mhalf[:], op=ALU.pow),
                               reads=[gst, mhalf], writes=[gst])
                            tt("dve", v3(sq[:]), v3(vg[:]), b8(gst[:, 2, :]), ALU.subtract, [vg, gst], [sq])
                            tt("dve", v3(sq[:]), v3(sq[:]), b8(gst[:, 4, :]), ALU.mult, [sq, gst], [sq])
                            tt("dve", vn[:], sq[:], ngb[:], ALU.mult, [sq, ngb], [vn])
                            for gm in range(8):
                                op("pe", lambda: PE.matmul(psM[:, gm * 64:(gm + 1) * 64], lhsT=WsT[:, gm, :], rhs=vn[:, gm * 64:(gm + 1) * 64],
                                                           start=True, stop=True), reads=[WsT, vn], writes=[psM])
                            p1_prep_b(mt + 1, sub)
                            tt("dve", v3(sq[:]), v3(psM[:]), b8(bsT[:]), ALU.add, [psM, bsT], [sq])
                            ogt = og[i % 2]
                            tt("dve", ogt[:], sq[:], ug[:], ALU.mult, [sq, ug], [ogt])
                            fw.dma("sp", ogm_d[b, i * 128:(i + 1) * 128, :], ogt[:], ogt, reads=[ogt], writes=[dbuf("ogm", b, i)])
                fw.pop()
                if upto < 2:
                    continue
                fw.push()
                if True:
                    w1 = fw.sb("cw1", [128, 2, 32, 128], BF16, dma="sw")
                    for hh in range(2):
                        for kv in range(2):
                            fw.dma("pool", w1[hh * 64:(hh + 1) * 64, kv, :, :], w1_d[kv].rearrange("(l d) h -> d l h", d=64), w1, writes=[w1])
                    psH = [fw.ps("psH%d" % i, [128, 256], F32) for i in range(2)]
                    psK = fw.ps("psK", [128, 256], F32); psV = fw.ps("psV", [128, 256], F32)
                    for kv in range(2):
                        src = kcT if kv == 0 else vcT
                        hbs = []
                        for g in range(2):
                            ph = psH[g]
                            for l in range(32):
                                op("pe", lambda: PE.matmul(ph[:, 0:255], lhsT=w1[g * 64:(g + 1) * 64, kv, l, :],
                                                           rhs=src[g * 64:(g + 1) * 64, l:l + 16 * 254 + 1:16], start=(l == 0), stop=(l == 31)),
                                   reads=[w1, src], writes=[ph])
                            hb_g = fw.sb("hbg_%d%d" % (kv, g), [128, 256], BF16)
                            op("pool", lambda: G.memset(hb_g[:], 0.0), writes=[hb_g])
                            op("act", lambda: A.activation(out=hb_g[:, 0:255], in_=ph[:, 0:255], func=AF.Gelu_apprx_tanh,
                                                           bias=cbias[:, kv:kv + 1], scale=1.0), reads=[ph, cbias, hb_g], writes=[hb_g])
                            hbs.append(hb_g)
                        if kv == 0:
                            for g in range(2):
                                op("pe", lambda: PE.matmul(psK[:], lhsT=w2pad[:, g, :], rhs=hbs[g][:], start=(g == 0), stop=(g == 1)),
                                   reads=[w2pad, hbs[g]], writes=[psK])
                            op("dve", lambda: V.tensor_copy(out=kcmpT[:], in_=psK[:]), reads=[psK], writes=[kcmpT])
                            if debug:
                                kd = fw.sb("kd", [128, 256], F32, dma=True)
                                op("dve", lambda: V.tensor_copy(out=kd[:], in_=psK[:]), reads=[psK], writes=[kd])
                                fw.dma("sp", dbg["kcc"][b], kd[:], kd, reads=[kd], writes=[dbuf("kcc", b)])
                        else:
                            for g in range(2):
                                for c in range(2):
                                    op("pe", lambda: PE.matmul(psV[:, (g * 2 + c) * 64:(g * 2 + c + 1) * 64], lhsT=hbs[g][:, c * 128:(c + 1) * 128],
                                                               rhs=w2[:, 1, :], start=True, stop=True), reads=[w2, hbs[g]], writes=[psV])
                            for g in range(2):
                                op("dve", lambda: V.tensor_copy(out=vcx[:, :, g, 0:64],
                                                                in_=psV[:, g * 128:(g + 1) * 128].rearrange("p (c d) -> p c d", d=64)),
                                   reads=[psV], writes=[vcx])
                fw.pop()
                if upto < 3:
                    continue
                fw.push()
                if True:
                    Wo0 = fw.sb("Wo0", [128, 8, D], BF16, dma="sw")
                    fw.dma_group("pool", [(Wo0[:, k, :], wout0_d[k * 128:(k + 1) * 128, :]) for k in range(8)], Wo0, writes=[Wo0])
                    lng = fw.sb("lng", [128, D], F32, dma=True); lnb = fw.sb("lnb", [128, D], F32, dma=True)
                    load_bc(lng, lng_d[0]); load_bc(lnb, lnb_d[0])
                    g1 = fw.sb("g1", [128, D], F32, dma=True)
                    load_mod(0, b, [(g1, 2)])
                    ep = Epilogue(g1, lng, lnb)
                    op("act", lambda: A.activation(out=gat[:].rearrange("p a b -> p (a b)"), in_=gat[:].rearrange("p a b -> p (a b)"), func=AF.Sigmoid),
                       reads=[gat], writes=[gat])
                    psS = [fw.ps("psS%d" % i, [128, 512], F32) for i in range(3)]
                    psTt = fw.ps("psTt", [128, 8, 128], BF16)
                    psOc = fw.ps("psOc", [128, 512], F32); psImp = fw.ps("psImp", [128, 512], F32)
                    psOs = fw.ps("psOs", [128, 512], F32); psOw = fw.ps("psOw", [128, 512], F32)
                    o4 = lambda ps_: ps_[:, 0:260].rearrange("p (r e) -> p r e", e=65)
                    ebuf = [fw.sb("ebuf%d" % i, [128, 4, 128], BF16) for i in range(4)]
                    ecount = [0]
                    cats = [fw.sb("cat%d" % i, [128, D], BF16, dma=True) for i in range(2)]
                    ocS = [fw.sb("ocS%d" % i, [128, 4, 65], F32) for i in range(2)]
                    nsel2s = [fw.sb("nsel2_%d" % i, [128, 2, 64], BF16) for i in range(2)]
                    catT = fw.sb("catT", [128, 8, 128], BF16)
                    sm = fw.sb("sm", [128, 3, 4], F32)
                    ff = fw.sb("ff", [128, 3, 4], F32)
                    rs4 = fw.sb("rs4", [128, 4], F32)
                    impn = fw.sb("impn", [128, 4, 64], F32)
                    imp = fw.sb("imp", [128, 64], F32)
                    impa = fw.sb("impa", [128, 64], F32)
                    wk = fw.sb("wk", [128, 64], F32)
                    m8 = fw.sb("m8", [128, 8], F32)
                    nsel = fw.sb("nsel", [128, 64], BF16)
                    nsel2 = fw.sb("nsel2", [128, 2, 64], BF16)
                    qzs = [[fw.sb("qzs%d_%d" % (g_, k_), [128, 4, 128], BF16) for k_ in range(2)] for g_ in range(2)]
                    qz = [[fw.sb("qz%d_%d" % (g_, k_), [128, 4, 128], BF16) for k_ in range(2)] for g_ in range(2)]
                    for g_ in range(2):
                        for k_ in range(2):
                            op("pool", lambda: G.memset(qz[g_][k_][:], 0.0), writes=[qz[g_][k_]])
                    o1 = fw.sb("o1", [128, 4, 64], F32); o2 = fw.sb("o2", [128, 4, 64], F32)
                    xres = [fw.sb("xres%d" % i, [128, D], F32, dma=True) for i in range(2)]
                    if debug:
                        ond = fw.sb("ond", [128, 512], F32, dma=True)
                    bc4 = lambda ap: ap.unsqueeze(1).to_broadcast([ap.shape[0], 4, 128])

                    qzt_cur = [None]

                    def emit_S(job):
                        if job.get("pre"):
                            job["pre"]()
                        pS = psS[job["k"] % 3]
                        pS4 = pS[:].rearrange("p (r t) -> p r t", r=4)
                        extra = job["extra"]
                        op("pe", lambda: PE.matmul(pS4, lhsT=job["kT_ap"], rhs=job["qs"], start=True, stop=(len(extra) == 0)),
                           reads=[job["kT_t"], job["qzt"]], writes=[pS])
                        for xi, (la, ra, rd) in enumerate(extra):
                            op("pe", lambda: PE.matmul(pS4, lhsT=la, rhs=ra, start=False, stop=(xi == len(extra) - 1)),
                               reads=rd, writes=[pS])

                    def emit_EXP_PV(job):
                        if job.get("pre_pv"):
                            job["pre_pv"]()
                        pS = psS[job["k"] % 3]; e = ebuf[job["k"] % 4]
                        op("act", lambda: A.activation(out=e[:].rearrange("p r t -> p (r t)"), in_=pS[:], func=AF.Exp, scale=0.125),
                           reads=[pS], writes=[e])
                        for r in range(4):
                            for (va, po_fn, pot) in job["pvs"]:
                                op("pe", lambda: PE.matmul(po_fn(r), lhsT=e[:, r, :], rhs=va, start=(job["first"] and r == 0), stop=job["last"],
                                                           skip_group_check=True), reads=[e, job["vaug_t"]], writes=[pot])
                        if job.get("post"):
                            job["post"]()

                    def run_jobs(jobs):
                        n = len(jobs)
                        for idx, job in enumerate(jobs):
                            job["k"] = ecount[0] + idx
                        ecount[0] += n
                        nextS = 0
                        for k in range(n):
                            while nextS < n and nextS <= k + 2:
                                jb = jobs[nextS]
                                if jb.get("pre") and jb.get("needs", -1) > k - 1:
                                    break
                                emit_S(jb); nextS += 1
                            assert nextS > k
                            emit_EXP_PV(jobs[k])
                            for _ in range(3):
                                if deferred:
                                    deferred.pop(0)()

                    deferred = []

                    def topk_dve(i, g):
                        oc = ocS[g]; nsel2 = nsel2s[g]
                        op("dve", lambda: V.tensor_copy(out=oc[:], in_=o4(psOc)), reads=[psOc], writes=[oc])
                        op("dve", lambda: V.tensor_scalar_max(out=rs4[:], in0=oc[:, :, 64], scalar1=1e-30), reads=[oc], writes=[rs4])
                        op("dve", lambda: V.reciprocal(out=rs4[:], in_=rs4[:]), reads=[rs4], writes=[rs4])
                        tt("dve", impn[:], psImp[:, 0:256].rearrange("p (r j) -> p r j", j=64), rs4[:].unsqueeze(2).to_broadcast([128, 4, 64]),
                           ALU.mult, [psImp, rs4], [impn])
                        op("dve", lambda: V.tensor_reduce(out=imp[:], in_=impn[:].rearrange("p r j -> p j r"), axis=AX.X, op=ALU.add),
                           reads=[impn], writes=[imp])
                        so = 62 - 2 * i
                        tt("dve", impa[:], imp[:], TMm[:, so:so + 64], ALU.mult, [imp, TMm], [impa])
                        tt("dve", impa[:], impa[:], TB[:, so:so + 64], ALU.add, [impa, TB], [impa])
                        op("dve", lambda: V.memset(impa[:, 0:1], 1e4), reads=[impa], writes=[impa])
                        op("dve", lambda: V.max(out=m8[:], in_=impa[:]), reads=[impa], writes=[m8])
                        op("dve", lambda: V.match_replace(out=wk[:], in_to_replace=m8[:], in_values=impa[:], imm_value=-1e9),
                           reads=[m8, impa], writes=[wk])
                        op("dve", lambda: V.max(out=m8[:], in_=wk[:]), reads=[wk], writes=[m8])
                        op("dve", lambda: V.tensor_scalar(out=wk[:], in0=impa[:], scalar1=m8[:, 7:8], scalar2=-NEG, op0=ALU.is_ge, op1=ALU.mult),
                           reads=[impa, m8], writes=[wk])
                        op("dve", lambda: V.tensor_scalar_add(out=nsel2[:], in0=wk[:].unsqueeze(1).to_broadcast([128, 2, 64]), scalar1=NEG),
                           reads=[wk], writes=[nsel2])

                    def negT_pe(g, qzst):
                        og_ = slice((1 - g) * 64, (2 - g) * 64)
                        nsel2 = nsel2s[g]
                        op("pe", lambda: PE.transpose(psTt[:, 0, :], nsel2[:].rearrange("p a b -> p (a b)"), ident[:]), reads=[nsel2, ident], writes=[psTt])
                        op("dve", lambda: V.tensor_copy(out=qzst[og_, :, :], in_=psTt[og_, 0, :].unsqueeze(1).to_broadcast([64, 4, 128])),
                           reads=[psTt, qzst], writes=[qzst])

                    def tile_start(i):
                        xr = xres[i % 2]; cat = cats[i % 2]
                        fw.dma("sp", xr[:], x_d[b, i * 128:(i + 1) * 128, :], xr, writes=[xr])
                        fw.dma("sp", cat[:, 512:1024], ogm_d[b, i * 128:(i + 1) * 128, :], cat, reads=[dbuf("ogm", b, i)], writes=[cat])
                        for ii in ([0, 1] if i == 0 else [i + 1]):
                            if ii >= P3_TILES:
                                continue
                            for g_ in range(2):
                                gs_ = slice(g_ * 64, (g_ + 1) * 64)
                                a_ = qz[g_][ii % 2]; b_ = qzs[g_][ii % 2]
                                op("dve", lambda: V.tensor_copy(out=a_[gs_, :, :], in_=qT[gs_, :, ii * 128:(ii + 1) * 128]), reads=[qT, a_], writes=[a_])
                                op("pool", lambda: G.tensor_copy(out=b_[gs_, :, :], in_=qT[gs_, :, ii * 128:(ii + 1) * 128]), reads=[qT, b_], writes=[b_])

                    def combine(i, g):
                        oc = ocS[g]; cat = cats[i % 2]
                        op("dve", lambda: V.tensor_scalar_max(out=sm[:, 0, :], in0=oc[:, :, 64], scalar1=1e-30), reads=[oc, sm], writes=[sm])
                        for br, pso in ((1, psOs), (2, psOw)):
                            op("dve", lambda: V.tensor_scalar_max(out=sm[:, br, :], in0=o4(pso)[:, :, 64], scalar1=1e-30), reads=[pso, sm], writes=[sm])
                        op("dve", lambda: V.reciprocal(out=sm[:], in_=sm[:]), reads=[sm], writes=[sm])
                        tt("dve", ff[:], sm[:], gat[:, i, g * 12:(g + 1) * 12].rearrange("p (r b) -> p b r", b=3), ALU.mult, [sm, gat], [ff])
                        fb = lambda br: ff[:, br, :].unsqueeze(2).to_broadcast([128, 4, 64])
                        tt("dve", o2[:], o4(psOw)[:, :, 0:64], fb(2), ALU.mult, [psOw, ff], [o2])
                        tt("dve", o1[:], o4(psOs)[:, :, 0:64], fb(1), ALU.mult, [psOs, ff], [o1])
                        tt("dve", o1[:], o1[:], o2[:], ALU.add, [o1, o2], [o1])
                        tt("dve", o2[:], oc[:, :, 0:64], fb(0), ALU.mult, [oc, ff], [o2])
                        if debug:
                            tt("pool", ond[:, g * 256:(g + 1) * 256].rearrange("p (r d) -> p r d", d=64), o1[:], o2[:], ALU.add, [o1, o2], [ond])
                            if g == 1:
                                fw.dma("sp", dbg["onsa"][b, i * 128:(i + 1) * 128, :], ond[:], ond, reads=[ond], writes=[dbuf("onsa", b, i)])
                        tt("dve", cat[:, g * 256:(g + 1) * 256].rearrange("p (r d) -> p r d", d=64), o1[:], o2[:], ALU.add, [o1, o2], [cat])

                    def outproj(i):
                        while deferred:
                            deferred.pop(0)()
                        cat = cats[i % 2]; xr = xres[i % 2]
                        for c in range(8):
                            op("pe", lambda: PE.transpose(psTt[:, c, :], cat[:, c * 128:(c + 1) * 128], ident[:]), reads=[cat, ident], writes=[psTt])
                        op("act", lambda: A.copy(out=catT[:], in_=psTt[:]), reads=[psTt], writes=[catT])
                        halves = [(psOs, psOs[:]), (psOw, psOw[:])]
                        for half in range(2):
                            pyt, pya = halves[half]
                            for c in range(8):
                                op("pe", lambda: PE.matmul(pya, lhsT=catT[:, c, :], rhs=Wo0[:, c, half * 512:(half + 1) * 512],
                                                           start=(c == 0), stop=(c == 7)), reads=[catT, Wo0], writes=[pyt])
                        st_ = ep.steps(halves, xr, x1_d[b, i * 128:(i + 1) * 128, :], dbuf("x1", b, i))
                        for f_ in st_[:3]:
                            f_()
                        deferred.extend(st_[3:])

                    jobs = []
                    last_sel_g1 = -1
                    for i in range(P3_TILES):
                        def mk(kT_ap, kT_t, extra, pvs, vaug_t, first, last, qt):
                            return dict(kT_ap=kT_ap, kT_t=kT_t, qs=qt[:], qzt=qt, extra=extra, pvs=pvs, vaug_t=vaug_t, first=first, last=last)
                        chunks = [0] if i < 16 else [0, 1]
                        last_cmp = {}
                        for g in range(2):
                            qzt = qz[g][i % 2]
                            for ci, c in enumerate(chunks):
                                extra = []
                                if (c, i) in cmpmask:
                                    extra.append((ident[:], bc4(cm_all[:, cmpmask[(c, i)], :]), [ident, cm_all]))
                                jb = mk(kcmpT[:, c * 128:(c + 1) * 128], kcmpT, extra,
                                        [(vcx[:, c, g, 0:65], lambda r: o4(psOc)[:, r, :], psOc),
                                         (vcx[:, c, g, 65:129], lambda r: psImp[:, r * 64:(r + 1) * 64], psImp)],
                                        vcx, ci == 0, ci == len(chunks) - 1, qzt)
                                if g == 0 and ci == 0:
                                    jb["pre"] = (lambda i_=i: tile_start(i_)); jb["needs"] = -1
                                jobs.append(jb)
                            jobs[-1]["post"] = (lambda i_=i, g_=g: topk_dve(i_, g_))
                            last_cmp[g] = len(jobs) - 1
                        for g in range(2):
                            qzt = qz[g][i % 2]; qzst = qzs[g][i % 2]
                            j0 = max(0, i - 4)
                            for j in range(j0, i + 1):
                                extra = []
                                if j == i:
                                    extra.append((ident[:], bc4(McNeg[:]), [ident, McNeg]))
                                elif j == i - 4:
                                    extra.append((ident[:], bc4(MwNeg[:]), [ident, MwNeg]))
                                jb = mk(kwT[:, j * 128:(j + 1) * 128], kwT, extra,
                                        [(vwA[:, j, g, :], lambda r: o4(psOw)[:, r, :], psOw)], vwA, j == j0, j == i, qzt)
                                if g == 0 and j == j0 and i > 0:
                                    jb["pre_pv"] = (lambda i_=i: outproj(i_ - 1))
                                jobs.append(jb)
                            for j in range(i + 1):
                                extra = []
                                if j == i:
                                    extra.append((ident[:], bc4(McNeg[:]), [ident, McNeg]))
                                jb = mk(ksE[g][:, j * 128:(j + 1) * 128], ksE[g], extra,
                                        [(vsA[:, j, g, :], lambda r: o4(psOs)[:, r, :], psOs)], vsA, j == 0, j == i, qzst)
                                if j == 0:
                                    jb["pre"] = (lambda g_=g, q_=qzst: negT_pe(g_, q_)); jb["needs"] = last_cmp[g]
                                jobs.append(jb)
                            jobs[-1]["post"] = (lambda i_=i, g_=g: combine(i_, g_))
                            if g == 1:
                                last_sel_g1 = len(jobs) - 1
                    run_jobs(jobs)
                    outproj(P3_TILES - 1)
                    while deferred:
                        deferred.pop(0)()
                fw.pop()
        fw.pop()
        if upto < 4:
            return nc

        def ffn_phase(ls, layer, src_d, src_name, dst_d, dst_name):
            fw.push()
            if True:
                Wi = fw.sb("Wi", [128, 8, 2 * FH], BF16, dma="sw")
                Wo = fw.sb("Wo", [128, 22, D], BF16, dma="sw")
                fw.dma_group("pool", [(Wi[:, k, :], fwi_d[layer, k * 128:(k + 1) * 128, :]) for k in range(8)], Wi, writes=[Wi])
                fw.dma_group("pool", [(Wo[:, k, :], fwo_d[layer, k * 128:(k + 1) * 128, :]) for k in range(22)], Wo, writes=[Wo])
                lng = fw.sb("lng", [128, D], F32, dma=True); lnb = fw.sb("lnb", [128, D], F32, dma=True)
                load_bc(lng, lng_d[ls]); load_bc(lnb, lnb_d[ls])
                sc1 = fw.sb("sc1", [128, D], F32, dma=True); sh = fw.sb("sh", [128, D], F32, dma=True); g1 = fw.sb("g1", [128, D], F32, dma=True)
                NX = 4
                xin = [fw.sb("xin%d" % i, [128, D], F32, dma=True) for i in range(NX)]
                hT = [fw.sb("hT%d" % i, [128, 8, 256], BF16) for i in range(2)]
                actT = fw.sb("actT", [128, 22, 256], BF16)
                sg = [fw.sb("sg%d" % i, [128, 256], BF16) for i in range(2)]
                hp = HPrep(sc1, sh, nhb=2, tmp_dt=BF16)
                ep = Epilogue(g1, lng, lnb)
                psGU = [fw.ps("psGU%d" % i, [128, 2, 256], F32) for i in range(2)]
                psYs = [fw.ps("psY%d" % i, [128, 2, 512], F32) for i in range(2)]
                tiles = [(b, mt) for b in range(NB) for mt in range(S // 256)]
                state = {}
                xc = [0]

                def prep_a(idx):
                    b, mt = tiles[idx]
                    if mt == 0:
                        load_mod(ls, b, [(sh, 0), (sc1, 1)])
                    hTm = hT[idx % 2]
                    xts = []; hbs = []
                    for sub in range(2):
                        i = mt * 2 + sub
                        xt = xin[xc[0] % NX]; xc[0] += 1
                        fw.dma("sp", xt[:], src_d[b, i * 128:(i + 1) * 128, :], xt, reads=[dbuf(src_name, b, i)], writes=[xt])
                        hbs.append(hp.run_a(xt))
                        xts.append(xt)
                    state[idx] = (hTm, xts, hbs)

                def prep_b(idx):
                    hTm, xts, hbs = state[idx]
                    for sub in range(2):
                        hp.run_b(hbs[sub], hTm, sub * 128)

                def up(idx, hcs):
                    hTm = state[idx][0]
                    for hc in hcs:
                        pg = psGU[hc % 2]; sgt = sg[hc % 2]
                        for k in range(8):
                            op("pe", lambda: PE.matmul(pg[:, 0, :], lhsT=Wi[:, k, hc * 128:(hc + 1) * 128], rhs=hTm[:, k, :],
                                                       start=(k == 0), stop=(k == 7)), reads=[Wi, hTm], writes=[pg])
                        for k in range(8):
                            op("pe", lambda: PE.matmul(pg[:, 1, :], lhsT=Wi[:, k, FH + hc * 128:FH + (hc + 1) * 128], rhs=hTm[:, k, :],
                                                       start=(k == 0), stop=(k == 7)), reads=[Wi, hTm], writes=[pg])
                        op("act", lambda: A.activation(out=sgt[:], in_=pg[:, 0, :], func=AF.Silu), reads=[pg], writes=[sgt])
                        tt("dve", actT[:, hc, :], pg[:, 1, :], sgt[:], ALU.mult, [pg, sgt], [actT])
                        for _ in range(2):
                            if deferred:
                                deferred.pop(0)()

                def down_ep(idx):
                    b, mt = tiles[idx]
                    xts = state.pop(idx)[1]
                    if mt == 0:
                        load_mod(ls, b, [(g1, 2)])
                    for sub in range(2):
                        i = mt * 2 + sub
                        psY = psYs[sub]
                        for half in range(2):
                            for hc in range(22):
                                op("pe", lambda: PE.matmul(psY[:, half, :], lhsT=actT[:, hc, sub * 128:(sub + 1) * 128],
                                                           rhs=Wo[:, hc, half * 512:(half + 1) * 512], start=(hc == 0), stop=(hc == 21)),
                                   reads=[actT, Wo], writes=[psY])
                        st_ = ep.steps([(psY, psY[:, 0, :]), (psY, psY[:, 1, :])], xts[sub], dst_d[b, i * 128:(i + 1) * 128, :], dbuf(dst_name, b, i))
                        nim = 3 if sub == 1 else len(st_)
                        for f_ in st_[:nim]:
                            f_()
                        deferred.extend(st_[nim:])

                deferred = []
                prep_a(0); prep_b(0)
                for idx in range(len(tiles)):
                    up(idx, range(0, 2))
                    if idx + 1 < len(tiles):
                        prep_a(idx + 1)
                    up(idx, range(2, 12))
                    if idx + 1 < len(tiles):
                        prep_b(idx + 1)
                    up(idx, range(12, 22))
                    while deferred:
                        deferred.pop(0)()
                    down_ep(idx)
                while deferred:
                    deferred.pop(0)()
            fw.pop()

        ffn_phase(1, 0, x1_d, "x1", x2_d, "x2")
        if upto < 5:
            return nc

        fw.push()
        if True:
            W1 = fw.sb("W1", [128, 8, 3 * D], BF16, dma="sw")
            Wo1 = fw.sb("Wo1", [128, 8, D], BF16, dma="sw")
            fw.dma_group("pool", [(W1[:, k, :], win1_d[k * 128:(k + 1) * 128, :]) for k in range(8)], W1, writes=[W1])
            fw.dma_group("pool", [(Wo1[:, k, :], wout1_d[k * 128:(k + 1) * 128, :]) for k in range(8)], Wo1, writes=[Wo1])
            cwf = fw.sb("cwf", [3, D], F32, dma=True)
            fw.dma("sp", cwf[:], cw_d, cwf, writes=[cwf])
            cw = fw.sb("cw", [128, 8, 3], F32)
            pscw = fw.ps("pscw", [128, 8, 3], F32)
            for cc in range(8):
                op("pe", lambda: PE.transpose(pscw[:, cc, :], cwf[:, cc * 128:(cc + 1) * 128], identf[0:3, 0:3]), reads=[cwf, identf], writes=[pscw])
            op("dve", lambda: V.tensor_copy(out=cw[:], in_=pscw[:]), reads=[pscw], writes=[cw])
            lng = fw.sb("lng", [128, D], F32, dma=True); lnb = fw.sb("lnb", [128, D], F32, dma=True)
            load_bc(lng, lng_d[2]); load_bc(lnb, lnb_d[2])
            sc1 = fw.sb("sc1", [128, D], F32, dma=True); sh = fw.sb("sh", [128, D], F32, dma=True); g1 = fw.sb("g1", [128, D], F32, dma=True)
            xin = [fw.sb("xin%d" % i, [128, D], F32, dma=True) for i in range(4)]
            hT = [fw.sb("hT%d" % i, [128, 8, 256], BF16) for i in range(2)]
            cz = fw.sb("cz", [128, 8, 258], F32)
            csb = [fw.sb("csb%d" % i, [128, 256], F32) for i in range(2)]
            y1 = [fw.sb("y1_%d" % i, [128, 256], F32) for i in range(2)]
            byT = fw.sb("byT", [128, 8, 256], BF16)
            hp = HPrep(sc1, sh)
            ep = Epilogue(g1, lng, lnb, nt=2)
            psP = [fw.ps("psP%d" % i, [128, 4, 256], F32) for i in range(2)]
            psY_ = fw.ps("psY", [128, 2, 512], F32)
            psYs = [psY_, psY_]
            tiles = [(b, mt) for b in range(NB) for mt in range(S // 256)]
            state = {}
            xc = [0]

            def prep_a(idx):
                b, mt = tiles[idx]
                if mt == 0:
                    load_mod(2, b, [(sh, 0), (sc1, 1)])
                hTm = hT[idx % 2]
                xts = []; hbs = []
                for sub in range(2):
                    i = mt * 2 + sub
                    xt = xin[xc[0] % 4]; xc[0] += 1
                    fw.dma("sp", xt[:], x2_d[b, i * 128:(i + 1) * 128, :], xt, reads=[dbuf("x2", b, i)], writes=[xt])
                    hbs.append(hp.run_a(xt))
                    xts.append(xt)
                state[idx] = (hTm, xts, hbs)

            def prep_b(idx):
                hTm, xts, hbs = state[idx]
                for sub in range(2):
                    hp.run_b(hbs[sub], hTm, sub * 128)

            deferred = []

            def compute(idx, ccs):
                b, mt = tiles[idx]
                hTm = state[idx][0]
                for cc in ccs:
                    if cc == 0:
                        if mt == 0:
                            op("pool", lambda: G.memset(cz[:, :, 0:2], 0.0), reads=[cz], writes=[cz])
                        else:
                            op("pool", lambda: G.tensor_copy(out=cz[:, :, 0:2], in_=cz[:, :, 256:258]), reads=[cz], writes=[cz])
                    pp = psP[cc % 2]; cs = csb[cc % 2]; yt = y1[cc % 2]
                    for part in range(3):
                        for k in range(8):
                            op("pe", lambda: PE.matmul(pp[:, part, :], lhsT=W1[:, k, part * D + cc * 128:part * D + (cc + 1) * 128],
                                                       rhs=hTm[:, k, :], start=(k == 0), stop=(k == 7)), reads=[W1, hTm], writes=[pp])
                    op("act", lambda: A.copy(out=cs[:], in_=pp[:, 1, :]), reads=[pp], writes=[cs])
                    tt("dve", cz[:, cc, 2:258], pp[:, 2, :], cs[:], ALU.mult, [pp, cs, cz], [cz])
                    op("act", lambda: A.activation(out=yt[:], in_=cz[:, cc, 0:256], func=AF.Copy, scale=cw[:, cc, 0:1]),
                       reads=[cz, cw], writes=[yt])
                    op("dve", lambda: V.scalar_tensor_tensor(out=yt[:], in0=cz[:, cc, 1:257], scalar=cw[:, cc, 1:2], in1=yt[:],
                                                             op0=ALU.mult, op1=ALU.add), reads=[cz, cw, yt], writes=[yt])
                    op("dve", lambda: V.scalar_tensor_tensor(out=yt[:], in0=cz[:, cc, 2:258], scalar=cw[:, cc, 2:3], in1=yt[:],
                                                             op0=ALU.mult, op1=ALU.add), reads=[cz, cw, yt], writes=[yt])
                    tt("dve", byT[:, cc, :], pp[:, 0, :], yt[:], ALU.mult, [pp, yt], [byT])
                    for _ in range(4):
                        if deferred:
                            deferred.pop(0)()

            def out_ep(idx):
                b, mt = tiles[idx]
                xts = state.pop(idx)[1]
                if mt == 0:
                    load_mod(2, b, [(g1, 2)])
                for sub in range(2):
                    i = mt * 2 + sub
                    psY = psYs[sub]
                    for half in range(2):
                        for cc in range(8):
                            op("pe", lambda: PE.matmul(psY[:, half, :], lhsT=byT[:, cc, sub * 128:(sub + 1) * 128],
                                                       rhs=Wo1[:, cc, half * 512:(half + 1) * 512], start=(cc == 0), stop=(cc == 7)),
                               reads=[byT, Wo1], writes=[psY])
                    st_ = ep.steps([(psY, psY[:, 0, :]), (psY, psY[:, 1, :])], xts[sub], x3_d[b, i * 128:(i + 1) * 128, :], dbuf("x3", b, i))
                    nim = 3
                    for f_ in st_[:nim]:
                        f_()
                    deferred.extend(st_[nim:])

            prep_a(0); prep_b(0)
            for idx in range(len(tiles)):
                compute(idx, range(0, 1))
                if idx + 1 < len(tiles):
                    prep_a(idx + 1)
                compute(idx, range(1, 5))
                if idx + 1 < len(tiles):
                    prep_b(idx + 1)
                compute(idx, range(5, 8))
                while deferred:
                    deferred.pop(0)()
                out_ep(idx)
            while deferred:
                deferred.pop(0)()
        fw.pop()
        if upto < 6:
            return nc

        ffn_phase(3, 1, x3_d, "x3", out_d, "out")
        fw.barrier()
    return nc


_CACHE = {}


def _prep_inputs(inputs):
    f = lambda a: np.ascontiguousarray(np.asarray(a, dtype=np.float32))
    common = {
        "ada_w": f(inputs["ada_w"]).reshape(4, D, 3 * D), "ada_b": f(inputs["ada_b"]).reshape(4, 3 * D),
        "ln_g": f(inputs["ln_g"]).reshape(4, D), "ln_b": f(inputs["ln_b"]).reshape(4, D),
        "even_w_in": f(inputs["even_w_in"])[0], "even_cmp_pos": f(inputs["even_cmp_pos"])[0],
        "even_cmp_w1": f(inputs["even_cmp_w1"])[0], "even_cmp_w2": f(inputs["even_cmp_w2"])[0],
        "even_gmlp_norm_g": f(inputs["even_gmlp_norm_g"])[0].reshape(512), "even_gmlp_ws": f(inputs["even_gmlp_ws"])[0],
        "even_gmlp_bs": f(inputs["even_gmlp_bs"])[0], "even_w_out": f(inputs["even_w_out"])[0],
        "odd_w_in": f(inputs["odd_w_in"])[0], "odd_conv_w": f(inputs["odd_conv_w"])[0], "odd_w_out": f(inputs["odd_w_out"])[0],
        "ffn_w_in": f(inputs["ffn_w_in"]), "ffn_w_out": f(inputs["ffn_w_out"]),
    }
    x = f(inputs["x"]); c = f(inputs["c"])
    maps = []
    for i in range(N_CORES):
        m = dict(common)
        m["x"] = x[NB * i:NB * (i + 1)]; m["c"] = c[NB * i:NB * (i + 1)]
        maps.append(m)
    return maps


def kernel(**inputs):
    if "nc" not in _CACHE:
        _CACHE["nc"] = build_program()
    nc = _CACHE["nc"]
    maps = _prep_inputs(inputs)
    res = run_bass_kernel_spmd(nc, maps, core_ids=list(range(N_CORES)))
    return np.concatenate([r["out"] for r in res.results], axis=0).astype(np.float32)
```

```python
import numpy as np
import concourse.bass as bass
import concourse.mybir as mybir
from concourse.bass_utils import run_bass_kernel_spmd
from contextlib import ExitStack

F32 = mybir.dt.float32; BF16 = mybir.dt.bfloat16; I32 = mybir.dt.int32
ALU = mybir.AluOpType; AF = mybir.ActivationFunctionType; AX = mybir.AxisListType

S = 4096; D = 1024; NB = 2; NT = S // 128
FH = 2816
NEG = -30000.0
ALPHA = 4.0 ** 0.25
EPS = 1e-5
N_CORES = 8
P3_TILES = NT
P1_MT = 8
SELMODE = 0
NB_RUN = NB
P3_PARTS = ('cmp', 'topk', 'sel', 'win', 'comb', 'out')


class Buf:
    __slots__ = ("name", "w", "r", "dsem")

    def __init__(self, name):
        self.name = name; self.w = {}; self.r = {}; self.dsem = None


class T:
    def __init__(self, h, buf):
        self.h = h; self.b = buf

    def __getitem__(self, k):
        return self.h[k]


class FW:
    def __init__(self, nc, ges, n_dsem=56):
        self.nc = nc; self.es = ges
        self.eng = {"pe": nc.tensor, "act": nc.scalar, "dve": nc.vector, "pool": nc.gpsimd, "sp": nc.sync}
        self.sems = {}; self.cnt = {}
        for e in ("pe", "act", "dve", "pool"):
            self.sems[e] = ges.enter_context(nc.semaphore("s_" + e)); self.cnt[e] = 0
        self.dpool = []; self.dpool_sw = []
        for i in range(n_dsem):
            k = "d%d" % i
            self.sems[k] = ges.enter_context(nc.semaphore("s_" + k)); self.cnt[k] = 0
            (self.dpool_sw if i < 10 else self.dpool).append(k)
        self.checked = []
        self.scopes = []
        self.seen = {e: {} for e in self.eng}
        self.uid = 0

    def push(self):
        st = ExitStack(); st.__enter__()
        self.scopes.append((self.es, self.checked, st))
        self.es = st; self.checked = []

    def pop(self):
        self.barrier()
        old_es, old_checked, st = self.scopes.pop()
        st.__exit__(None, None, None)
        for k in self.checked:
            (self.dpool_sw if int(k[1:]) < 10 else self.dpool).append(k)
        self.es = old_es; self.checked = old_checked

    def sb(self, name, shape, dt, dma=False):
        self.uid += 1
        h = self.es.enter_context(self.nc.sbuf_tensor("%s_%d" % (name, self.uid), shape, dt))
        b = Buf(name)
        if dma:
            b.dsem = (self.dpool_sw if dma == "sw" else self.dpool).pop(0); self.checked.append(b.dsem)
        return T(h, b)

    def ps(self, name, shape, dt):
        self.uid += 1
        h = self.es.enter_context(self.nc.psum_tensor("%s_%d" % (name, self.uid), shape, dt))
        return T(h, Buf(name))

    def _deps(self, reads, writes):
        deps = {}
        for b in reads:
            for k, v in b.w.items():
                if deps.get(k, 0) < v: deps[k] = v
        for b in writes:
            for k, v in b.w.items():
                if deps.get(k, 0) < v: deps[k] = v
            for k, v in b.r.items():
                if deps.get(k, 0) < v: deps[k] = v
        return deps

    def _wait(self, e, deps):
        seen = self.seen[e]; eng = self.eng[e]
        for k, v in deps.items():
            if k == "pe" and e == "pe":
                continue
            if seen.get(k, 0) < v:
                eng.wait_ge(self.sems[k], v); seen[k] = v

    def _record(self, reads, writes, key, val):
        for b in writes:
            b.w = {key: val}; b.r = {}
        for b in reads:
            if b.r.get(key, 0) < val: b.r[key] = val

    @staticmethod
    def _bl(ts):
        return [t.b if isinstance(t, T) else t for t in ts]

    def op(self, e, fn, reads=(), writes=()):
        reads = self._bl(reads); writes = self._bl(writes)
        self._wait(e, self._deps(reads, writes))
        ins = fn()
        self.cnt[e] += 1
        ins.then_inc(self.sems[e], 1)
        self._record(reads, writes, e, self.cnt[e])
        return ins

    def dma(self, q, out, in_, sbuf_t, reads=(), writes=()):
        reads = self._bl(reads); writes = self._bl(writes)
        sbb = sbuf_t.b if isinstance(sbuf_t, T) else sbuf_t
        assert sbb.dsem is not None, sbb.name
        self._wait(q, self._deps(reads, writes))
        ins = self.eng[q].dma_start(out=out, in_=in_)
        key = sbb.dsem
        self.cnt[key] += 16
        ins.then_inc(self.sems[key], 16)
        self._record(reads, writes, key, self.cnt[key])
        return ins

    def dma_group(self, q, pairs, sbuf_t, reads=(), writes=()):
        reads = self._bl(reads); writes = self._bl(writes)
        sbb = sbuf_t.b if isinstance(sbuf_t, T) else sbuf_t
        self._wait(q, self._deps(reads, writes))
        key = sbb.dsem
        base = self.cnt[key]
        for n_, (o, i_) in enumerate(pairs):
            if n_ >= 8 and n_ % 8 == 0:
                self.eng[q].wait_ge(self.sems[key], base + 16 * n_)
            ins = self.eng[q].dma_start(out=o, in_=i_)
            self.cnt[key] += 16
            ins.then_inc(self.sems[key], 16)
        self._record(reads, writes, key, self.cnt[key])

    def barrier(self):
        deps = {k: v for k, v in self.cnt.items() if v > 0}
        for e in self.eng:
            seen = self.seen[e]
            for k, v in deps.items():
                if seen.get(k, 0) < v:
                    self.eng[e].wait_ge(self.sems[k], v); seen[k] = v


def build_program(upto=99, debug=False):
    nc = bass.Bass("TRN2", target_bir_lowering=False)
    kS = "ExternalOutput" if debug else "Internal"

    def din(name, shape):
        return nc.dram_tensor(name, shape, F32, kind="ExternalInput").ap()

    x_d = din("x", [NB, S, D]); c_d = din("c", [NB, D])
    adaw_d = din("ada_w", [4, D, 3 * D]); adab_d = din("ada_b", [4, 3 * D])
    lng_d = din("ln_g", [4, D]); lnb_d = din("ln_b", [4, D])
    win0_d = din("even_w_in", [D, 2328]); pos_d = din("even_cmp_pos", [2, 32, 64])
    w1_d = din("even_cmp_w1", [2, 2048, 128]); w2_d = din("even_cmp_w2", [2, 128, 64])
    gng_d = din("even_gmlp_norm_g", [512]); gws_d = din("even_gmlp_ws", [8, 128, 128]); gbs_d = din("even_gmlp_bs", [8, 128])
    wout0_d = din("even_w_out", [D, D])
    win1_d = din("odd_w_in", [D, 3 * D]); cw_d = din("odd_conv_w", [3, D]); wout1_d = din("odd_w_out", [D, D])
    fwi_d = din("ffn_w_in", [2, D, 2 * FH]); fwo_d = din("ffn_w_out", [2, FH, D])
    out_d = nc.dram_tensor("out", [NB, S, D], F32, kind="ExternalOutput").ap()
    mod_d = nc.dram_tensor("mod_s", [4, NB, 3 * D], F32, kind=kS).ap()
    x1_d = nc.dram_tensor("x1_s", [NB, S, D], F32, kind=kS).ap()
    x2_d = nc.dram_tensor("x2_s", [NB, S, D], F32, kind=kS).ap()
    x3_d = nc.dram_tensor("x3_s", [NB, S, D], F32, kind=kS).ap()
    ogm_d = nc.dram_tensor("ogm_s", [NB, S, 512], BF16, kind=kS).ap()
    dbg = {}
    if debug:
        dbg["onsa"] = nc.dram_tensor("onsa_s", [NB, S, 512], F32, kind=kS).ap()
        dbg["kcc"] = nc.dram_tensor("kcc_s", [NB, 128, 256], F32, kind=kS).ap()

    dbufs = {}

    def dbuf(*key):
        if key not in dbufs:
            dbufs[key] = Buf(str(key))
        return dbufs[key]

    with ExitStack() as ges, nc.allow_non_contiguous_dma(reason="small param layouts"):
        fw = FW(nc, ges)
        op = fw.op
        V = nc.vector; A = nc.scalar; G = nc.gpsimd; PE = nc.tensor
        VENG = {"dve": V, "pool": G}

        ident = fw.sb("ident", [128, 128], BF16)
        identf = fw.sb("identf", [128, 128], F32)
        epsb = fw.sb("epsb", [128, 1], F32)
        op("pool", lambda: G.memset(identf[:], 1.0), writes=[identf])
        op("pool", lambda: G.affine_select(out=identf[:], in_=identf[:], pattern=[[1, 128]], compare_op=ALU.is_equal,
                                           fill=0.0, base=0, channel_multiplier=-1), reads=[identf], writes=[identf])
        op("pool", lambda: G.tensor_copy(out=ident[:], in_=identf[:]), reads=[identf], writes=[ident])
        op("pool", lambda: G.memset(epsb[:], EPS), writes=[epsb])
        mhalf = fw.sb("mhalf", [128, 8], F32)
        op("pool", lambda: G.memset(mhalf[:], -0.5), writes=[mhalf])

        def tt(e, out, in0, in1, o, rd, wr):
            return op(e, lambda: VENG[e].tensor_tensor(out=out, in0=in0, in1=in1, op=o), reads=rd, writes=wr)

        def load_bc(dst, src_1d, q="sp"):
            fw.dma(q, dst[:], src_1d.partition_broadcast(128), dst, writes=[dst])

        def load_w_bf16(dst, kidx, src2d):
            fw.dma("pool", dst[:, kidx, :], src2d, dst, writes=[dst])

        class HPrep:
            def __init__(self, sc1, sh, nhb=2, tmp_dt=F32):
                self.sc1 = sc1; self.sh = sh; self.nhb = nhb
                self.tmp = [fw.sb("hp_tmp%d" % i, [128, D], tmp_dt) for i in range(1)]
                self.hb = [fw.sb("hp_hb%d" % i, [128, D], BF16) for i in range(nhb)]
                self.psT = fw.ps("hp_psT", [128, 8, 128], BF16)
                self.n = 0

            def run_a(self, xt):
                tmp = self.tmp[0]; hb = self.hb[self.n % self.nhb]; self.n += 1
                tt("dve", tmp[:], xt[:], self.sc1[:], ALU.mult, [xt, self.sc1], [tmp])
                tt("pool", hb[:], tmp[:], self.sh[:], ALU.add, [tmp, self.sh], [hb])
                return hb

            def run_b(self, hb, hT, off):
                psT = self.psT
                for c in range(8):
                    op("pe", lambda: PE.transpose(psT[:, c, :], hb[:, c * 128:(c + 1) * 128], ident[:]),
                       reads=[hb, ident], writes=[psT])
                op("act", lambda: A.copy(out=hT[:, :, off:off + 128], in_=psT[:]), reads=[psT], writes=[hT])

            def run(self, xt, hT, off):
                self.run_b(self.run_a(xt), hT, off)

        class Epilogue:
            def __init__(self, g1, lng, lnb, nt=1):
                self.g1 = g1; self.lng = lng; self.lnb = lnb
                self.ts = [fw.sb("ep_t%d" % i, [128, D], F32) for i in range(nt)]
                self.xo = [fw.sb("ep_xo%d" % i, [128, D], F32, dma=True) for i in range(2)]
                self.sts = [fw.sb("ep_st%d" % i, [128, 2, 6], F32) for i in range(nt)]
                self.mvs = [fw.sb("ep_mv%d" % i, [128, 8], F32) for i in range(nt)]
                self.k = 0

            def steps(self, halves, xres, out_ap, out_db):
                t = self.ts[self.k % len(self.ts)]; st = self.sts[self.k % len(self.ts)]; mv = self.mvs[self.k % len(self.ts)]
                xo = self.xo[self.k % 2]; self.k += 1
                g1 = self.g1; lng = self.lng; lnb = self.lnb
                L = []
                for hi, (pt, pap) in enumerate(halves):
                    L.append(lambda hi=hi, pt=pt, pap=pap: tt("dve", t[:, hi * 512:(hi + 1) * 512], pap, g1[:, hi * 512:(hi + 1) * 512], ALU.mult, [pt, g1], [t]))
                L.append(lambda: op("dve", lambda: V.scalar_tensor_tensor(out=t[:], in0=xres[:], scalar=ALPHA, in1=t[:], op0=ALU.mult, op1=ALU.add),
                                    reads=[xres, t], writes=[t]))
                L.append(lambda: op("dve", lambda: V.bn_stats(out=st[:, 0, :], in_=t[:, 0:512]), reads=[t], writes=[st]))
                L.append(lambda: op("dve", lambda: V.bn_stats(out=st[:, 1, :], in_=t[:, 512:1024]), reads=[t, st], writes=[st]))
                L.append(lambda: op("dve", lambda: V.bn_aggr(out=mv[:, 0:2], in_=st[:].rearrange("p a b -> p (a b)")), reads=[st], writes=[mv]))
                L.append(lambda: op("dve", lambda: V.tensor_scalar_add(out=mv[:, 2:3], in0=mv[:, 1:2], scalar1=EPS), reads=[mv], writes=[mv]))
                L.append(lambda: op("pool", lambda: G.tensor_tensor(out=mv[:, 3:4], in0=mv[:, 2:3], in1=mhalf[:, 0:1], op=ALU.pow),
                                    reads=[mv, mhalf], writes=[mv]))
                L.append(lambda: op("dve", lambda: V.tensor_scalar(out=mv[:, 4:5], in0=mv[:, 0:1], scalar1=mv[:, 3:4], scalar2=-1.0,
                                                                   op0=ALU.mult, op1=ALU.mult), reads=[mv], writes=[mv]))
                L.append(lambda: op("act", lambda: A.activation(out=xo[:], in_=t[:], func=AF.Identity, bias=mv[:, 4:5], scale=mv[:, 3:4]),
                                    reads=[t, mv], writes=[xo]))
                L.append(lambda: tt("pool", xo[:], xo[:], lng[:], ALU.mult, [xo, lng], [xo]))
                L.append(lambda: tt("pool", xo[:], xo[:], lnb[:], ALU.add, [xo, lnb], [xo]))
                L.append(lambda: fw.dma("sp", out_ap, xo[:], xo, reads=[xo], writes=[out_db]))
                return L

            def run(self, halves, xres, out_ap, out_db):
                for f in self.steps(halves, xres, out_ap, out_db):
                    f()

        def load_mod(ls, b, want):
            for tile, which in want:
                load_bc(tile, mod_d[ls, b, which * D:(which + 1) * D])

        def transpose_small(dst_ap, dstT, src_ap, srcT, npart, pst, pst_ap):
            op("pe", lambda: PE.transpose(pst_ap, src_ap, identf[0:npart, 0:npart]), reads=[srcT, identf], writes=[pst])
            op("dve", lambda: V.tensor_copy(out=dst_ap, in_=pst_ap), reads=[pst], writes=[dstT])

        fw.push()
        if True:
            csb_ = fw.sb("csb", [NB, D], F32, dma=True)
            scs = fw.sb("scs", [NB, D], F32)
            scT = fw.sb("scT", [128, 8, NB], F32)
            stage = [fw.sb("adastage%d" % i, [128, 3 * D], F32, dma=True) for i in range(2)]
            adab = fw.sb("adab", [NB, 3 * D], F32, dma=True)
            msb = fw.sb("msb", [NB, 3 * D], F32, dma=True)
            psm = [fw.ps("psm%d" % i, [NB, 512], F32) for i in range(6)]
            pst = fw.ps("pst_p0", [128, 8, NB], F32)
            fw.dma("sp", csb_[:], c_d, csb_, writes=[csb_])
            op("act", lambda: A.activation(out=scs[:], in_=csb_[:], func=AF.Silu), reads=[csb_], writes=[scs])
            for k in range(8):
                op("pe", lambda: PE.transpose(pst[:, k, :], scs[:, k * 128:(k + 1) * 128], identf[0:NB, 0:NB]), reads=[scs, identf], writes=[pst])
            op("dve", lambda: V.tensor_copy(out=scT[:], in_=pst[:]), reads=[pst], writes=[scT])
            si = 0
            for ls in range(4):
                fw.dma("sp", adab[:], adab_d[ls].partition_broadcast(NB), adab, writes=[adab])
                for k in range(8):
                    stg = stage[si % 2]; si += 1
                    fw.dma("sp", stg[:], adaw_d[ls, k * 128:(k + 1) * 128, :], stg, writes=[stg])
                    for ct in range(6):
                        op("pe", lambda: PE.matmul(psm[ct][:], lhsT=scT[:, k, :], rhs=stg[:, ct * 512:(ct + 1) * 512],
                                                   start=(k == 0), stop=(k == 7)), reads=[scT, stg], writes=[psm[ct]])
                for ct in range(6):
                    tt("dve", msb[:, ct * 512:(ct + 1) * 512], psm[ct][:], adab[:, ct * 512:(ct + 1) * 512], ALU.add,
                       [psm[ct], adab], [msb])
                op("dve", lambda: V.tensor_scalar_add(out=msb[:, D:3 * D], in0=msb[:, D:3 * D], scalar1=1.0), reads=[msb], writes=[msb])
                fw.dma("sp", mod_d[ls], msb[:], msb, reads=[msb], writes=[dbuf("mod")])
        fw.pop()
        if upto < 1:
            return nc

        fw.push()
        if True:
            WsT = fw.sb("WsT", [128, 8, 128], BF16)
            bsT = fw.sb("bsT", [128, 8], F32)
            ngb = fw.sb("ngb", [128, 512], F32, dma=True)
            load_bc(ngb, gng_d)
            cbias = fw.sb("cbias", [128, 2], F32)
            w2 = fw.sb("cw2", [128, 2, 64], BF16, dma="sw")
            fw.dma("pool", w2[:], w2_d.rearrange("kv h d -> h kv d"), w2, writes=[w2])
            w2pad = fw.sb("cw2pad", [128, 2, 128], BF16)
            op("pool", lambda: G.memset(w2pad[:], 0.0), writes=[w2pad])
            for g in range(2):
                op("pool", lambda: G.tensor_copy(out=w2pad[:, g, g * 64:(g + 1) * 64], in_=w2[:, 0, :]), reads=[w2, w2pad], writes=[w2pad])
            McNeg = fw.sb("McNeg", [128, 128], BF16)
            MwNeg = fw.sb("MwNeg", [128, 128], BF16)
            cm_all = fw.sb("cm_all", [128, 33, 128], BF16)
            TMm = fw.sb("TMm", [128, 126], F32); TB = fw.sb("TB", [128, 126], F32)
            qT = fw.sb("qT", [128, 4, S], BF16)
            ksE = [fw.sb("ksE%d" % g_, [128, S], BF16) for g_ in range(2)]; kwT = fw.sb("kwT", [128, S], BF16)
            kcT = fw.sb("kcT", [128, S], BF16); vcT = fw.sb("vcT", [128, S], BF16)
            vsA = fw.sb("vsA", [128, NT, 2, 65], BF16); vwA = fw.sb("vwA", [128, NT, 2, 65], BF16)
            gat = fw.sb("gat", [128, NT, 24], F32)
            kcmpT = fw.sb("kcmpT", [128, 256], BF16)
            vcx = fw.sb("vcx", [128, 2, 2, 129], BF16)
            cmpmask = {}
            if P1_MT < 8:
                for tcache in (qT, ksE[0], ksE[1], kwT, kcT, vcT, gat):
                    op("pool", lambda: G.memset(tcache[:], 0.0), writes=[tcache])
                op("pool", lambda: G.memset(vsA[:], 0.0), writes=[vsA]); op("pool", lambda: G.memset(vwA[:], 0.0), writes=[vwA])
            fw.push()
            if True:
                w1t = fw.sb("cw1t", [64, 2, 32, 128], BF16, dma="sw")
                for kv in range(2):
                    fw.dma("pool", w1t[:, kv, :, :], w1_d[kv].rearrange("(l d) h -> d l h", d=64), w1t, writes=[w1t])
                posf = fw.sb("posf", [32, 2, 64], F32, dma=True)
                fw.dma("sp", posf[:], pos_d.rearrange("kv l d -> l kv d"), posf, writes=[posf])
                posT = fw.sb("posT", [64, 2, 32], BF16)
                pss = fw.ps("pss", [128, 64], F32)
                for kv in range(2):
                    transpose_small(posT[:, kv, :], posT, posf[:, kv, :], posf, 32, pss, pss[0:64, 0:32])
                psb = fw.ps("psb", [128, 2], F32)
                for kv in range(2):
                    for l in range(32):
                        op("pe", lambda: PE.matmul(psb[:, kv:kv + 1], lhsT=w1t[:, kv, l, :], rhs=posT[:, kv, l:l + 1],
                                                   start=(l == 0), stop=(l == 31)), reads=[w1t, posT], writes=[psb])
                op("dve", lambda: V.tensor_copy(out=cbias[:], in_=psb[:]), reads=[psb], writes=[cbias])
                gbsf = fw.sb("gbsf", [8, 128], F32, dma=True)
                fw.dma("sp", gbsf[:], gbs_d, gbsf, writes=[gbsf])
                transpose_small(bsT[:], bsT, gbsf[:], gbsf, 8, pss, pss[:, 0:8])
                wsf = fw.sb("wsf", [128, 8, 128], F32, dma=True)
                wsb = fw.sb("wsb", [128, 8, 128], BF16)
                fw.dma("sp", wsf[:], gws_d.rearrange("g t s -> t g s"), wsf, writes=[wsf])
                op("pool", lambda: G.affine_select(out=wsf[:], in_=wsf[:], pattern=[[0, 8], [-1, 128]], compare_op=ALU.is_ge,
                                                   fill=0.0, base=0, channel_multiplier=1), reads=[wsf], writes=[wsf])
                op("pool", lambda: G.tensor_copy(out=wsb[:], in_=wsf[:]), reads=[wsf], writes=[wsb])
                pst = fw.ps("pst0", [128, 8, 128], BF16)
                for gm in range(8):
                    op("pe", lambda: PE.transpose(pst[:, gm, :], wsb[:, gm, :], ident[:]), reads=[wsb, ident], writes=[pst])
                op("dve", lambda: V.tensor_copy(out=WsT[:], in_=pst[:]), reads=[pst], writes=[WsT])
                zf = fw.sb("zf", [128, 128], F32)
                mtmp = fw.sb("mtmp", [128, 128], F32)
                op("pool", lambda: G.memset(zf[:], 0.0), writes=[zf])
                op("pool", lambda: G.affine_select(out=mtmp[:], in_=zf[:], pattern=[[1, 128]], compare_op=ALU.is_ge, fill=NEG,
                                                   base=0, channel_multiplier=-1), reads=[zf], writes=[mtmp])
                op("pool", lambda: G.tensor_copy(out=McNeg[:], in_=mtmp[:]), reads=[mtmp], writes=[McNeg])
                op("pool", lambda: G.affine_select(out=mtmp[:], in_=zf[:], pattern=[[-1, 128]], compare_op=ALU.is_gt, fill=NEG,
                                                   base=0, channel_multiplier=1), reads=[zf], writes=[mtmp])
                op("pool", lambda: G.tensor_copy(out=MwNeg[:], in_=mtmp[:]), reads=[mtmp], writes=[MwNeg])
                mi = 0
                for c in range(2):
                    for i in (range(0, 17) if c == 0 else range(16, 32)):
                        op("pool", lambda: G.affine_select(out=mtmp[:], in_=zf[:], pattern=[[1, 128]], compare_op=ALU.is_ge, fill=NEG,
                                                           base=128 * i - 31 - 2048 * c, channel_multiplier=-16),
                           reads=[zf], writes=[mtmp])
                        op("pool", lambda: G.tensor_copy(out=cm_all[:, mi, :], in_=mtmp[:]), reads=[mtmp], writes=[cm_all])
                        cmpmask[(c, i)] = mi; mi += 1
                ebf = fw.sb("ebf", [64, S], F32)
                op("pool", lambda: G.memset(ebf[:], 1.0), writes=[ebf])
                op("pool", lambda: G.affine_select(out=ebf[:], in_=ebf[:], pattern=[[1, S]], compare_op=ALU.is_ge, fill=0.0,
                                                   base=0, channel_multiplier=-64), reads=[ebf], writes=[ebf])
                op("pool", lambda: G.affine_select(out=ebf[:], in_=ebf[:], pattern=[[-1, S]], compare_op=ALU.is_ge, fill=0.0,
                                                   base=63, channel_multiplier=64), reads=[ebf], writes=[ebf])
                op("pool", lambda: G.tensor_copy(out=ksE[1][0:64, :], in_=ebf[:]), reads=[ebf, ksE[1]], writes=[ksE[1]])
                ebf2 = fw.sb("ebf2", [128, S], F32)
                op("pool", lambda: G.memset(ebf2[:], 1.0), writes=[ebf2])
                op("pool", lambda: G.affine_select(out=ebf2[:], in_=ebf2[:], pattern=[[1, S]], compare_op=ALU.is_ge, fill=0.0,
                                                   base=4096, channel_multiplier=-64), reads=[ebf2], writes=[ebf2])
                op("pool", lambda: G.affine_select(out=ebf2[:], in_=ebf2[:], pattern=[[-1, S]], compare_op=ALU.is_ge, fill=0.0,
                                                   base=-4033, channel_multiplier=64), reads=[ebf2], writes=[ebf2])
                op("pool", lambda: G.tensor_copy(out=ksE[0][64:128, :], in_=ebf2[64:128, :]), reads=[ebf2, ksE[0]], writes=[ksE[0]])
                for hh in range(2):
                    hs = slice(hh * 64, (hh + 1) * 64)
                    op("pool", lambda: G.memset(TMm[hs, 0:61 + hh], 1.0), reads=[TMm], writes=[TMm])
                    op("pool", lambda: G.memset(TMm[hs, 61 + hh:126], 0.0), reads=[TMm], writes=[TMm])
                    op("pool", lambda: G.memset(TB[hs, 0:61 + hh], 0.0), reads=[TB], writes=[TB])
                    op("pool", lambda: G.memset(TB[hs, 61 + hh:63 + hh], 1e4), reads=[TB], writes=[TB])
                    op("pool", lambda: G.memset(TB[hs, 63 + hh:126], -1e4), reads=[TB], writes=[TB])
                op("pool", lambda: G.memset(vsA[:, :, :, 64:65], 1.0), writes=[vsA])
                op("pool", lambda: G.memset(vwA[:, :, :, 64:65], 1.0), writes=[vwA])
                op("pool", lambda: G.memset(vcx[:], 0.0), writes=[vcx])
                op("pool", lambda: G.memset(vcx[:, :, :, 64:65], 1.0), reads=[vcx], writes=[vcx])
                covf = fw.sb("covf", [128, 2, 64], F32)
                op("pool", lambda: G.memset(covf[:], 1.0), writes=[covf])
                op("pool", lambda: G.affine_select(out=covf[:], in_=covf[:], pattern=[[-128, 2], [4, 64]], compare_op=ALU.is_ge,
                                                   fill=0.0, base=3, channel_multiplier=-1), reads=[covf], writes=[covf])
                op("pool", lambda: G.affine_select(out=covf[:], in_=covf[:], pattern=[[128, 2], [-4, 64]], compare_op=ALU.is_ge,
                                                   fill=0.0, base=1, channel_multiplier=1), reads=[covf], writes=[covf])
                for g in range(2):
                    op("pool", lambda: G.tensor_copy(out=vcx[:, :, g, 65:129], in_=covf[:]), reads=[covf, vcx], writes=[vcx])
                op("pool", lambda: G.memset(kcmpT[:], 0.0), writes=[kcmpT])
            fw.pop()

            v3 = lambda ap: ap.rearrange("p (g d) -> p g d", d=64)
            b8 = lambda ap: ap.unsqueeze(2).to_broadcast([128, 8, 64])

            for b in range(NB_RUN):
                fw.push()
                if True:
                    W0 = fw.sb("W0", [128, 8, 2328], BF16, dma="sw")
                    Wq = fw.sb("Wq", [128, 8, 512], BF16, dma="sw")
                    fw.dma_group("pool", [(W0[:, k, :], win0_d[k * 128:(k + 1) * 128, :]) for k in range(8)], W0, writes=[W0])
                    fw.dma_group("pool", [(Wq[:, k, r * 128:(r + 1) * 128].rearrange("p (g d) -> p g d", g=2),
                                           win0_d[k * 128:(k + 1) * 128, 0:512].rearrange("p (g r d) -> p g r d", g=2, r=4)[:, :, r, :])
                                          for k in range(8) for r in range(4)], Wq, writes=[Wq])
                    sc1 = fw.sb("sc1", [128, D], F32, dma=True); sh = fw.sb("sh", [128, D], F32, dma=True)
                    load_mod(0, b, [(sh, 0), (sc1, 1)])
                    xin = [fw.sb("xin%d" % i, [128, D], F32, dma=True) for i in range(3)]
                    hT = [fw.sb("hT%d" % i, [128, 8, 512], BF16) for i in range(2)]
                    hp = HPrep(sc1, sh)
                    psF = [fw.ps("psF%d" % i, [128, 512], F32) for i in range(2)]
                    psA = fw.ps("psA", [128, 512], F32); psB = fw.ps("psB", [128, 512], F32)
                    psC = fw.ps("psC", [128, 512], F32); psM = fw.ps("psM", [128, 512], F32)
                    ug = fw.sb("ug", [128, 512], F32); vg = fw.sb("vg", [128, 512], F32)
                    sq = fw.sb("sq", [128, 512], F32)
                    gst = fw.sb("gst", [128, 6, 8], F32)
                    vn = fw.sb("vn", [128, 512], BF16)
                    og = [fw.sb("og%d" % i, [128, 512], BF16, dma=True) for i in range(2)]
                    xcount = [0]
                    pend = {}

                    def p1_prep_a(mt_, sub_):
                        if mt_ >= P1_MT:
                            return
                        i_ = mt_ * 4 + sub_
                        xt = xin[xcount[0] % 3]; xcount[0] += 1
                        fw.dma("sp", xt[:], x_d[b, i_ * 128:(i_ + 1) * 128, :], xt, writes=[xt])
                        pend[(mt_, sub_)] = hp.run_a(xt)

                    def p1_prep_b(mt_, sub_):
                        if mt_ >= P1_MT:
                            return
                        hp.run_b(pend.pop((mt_, sub_)), hT[mt_ % 2], sub_ * 128)

                    for sub in range(4):
                        p1_prep_a(0, sub); p1_prep_b(0, sub)
                    for mt in range(P1_MT):
                        hTm = hT[mt % 2]
                        fm = []
                        for r in range(4):
                            fm.append((Wq[:, :, r * 128:(r + 1) * 128], qT[:, r, mt * 512:(mt + 1) * 512], qT))
                        fm.append((W0[:, :, 512:640], kcT[:, mt * 512:(mt + 1) * 512], kcT))
                        fm.append((W0[:, :, 640:768], vcT[:, mt * 512:(mt + 1) * 512], vcT))
                        fm.append((W0[:, :, 768:896], None, "ks"))
                        fm.append((W0[:, :, 1024:1152], kwT[:, mt * 512:(mt + 1) * 512], kwT))
                        for fi, (wap, dst, dstT) in enumerate(fm):
                            pf = psF[fi % 2]
                            for k in range(8):
                                op("pe", lambda: PE.matmul(pf[:], lhsT=wap[:, k], rhs=hTm[:, k, :], start=(k == 0), stop=(k == 7)),
                                   reads=[W0, Wq, hTm], writes=[pf])
                            if dstT == "ks":
                                op("act", lambda: A.copy(out=ksE[0][0:64, mt * 512:(mt + 1) * 512], in_=pf[0:64, :]), reads=[pf, ksE[0]], writes=[ksE[0]])
                                op("dve", lambda: V.tensor_copy(out=ksE[1][64:128, mt * 512:(mt + 1) * 512], in_=pf[64:128, :]), reads=[pf, ksE[1]], writes=[ksE[1]])
                            elif fi % 2 == 0:
                                op("act", lambda: A.copy(out=dst, in_=pf[:]), reads=[pf], writes=[dstT])
                            else:
                                op("dve", lambda: V.tensor_copy(out=dst, in_=pf[:]), reads=[pf], writes=[dstT])
                        for sub in range(4):
                            i = mt * 4 + sub
                            lh = lambda k: hTm[:, k, sub * 128:(sub + 1) * 128]
                            for k in range(8):
                                op("pe", lambda: PE.matmul(psA[:, 0:408], lhsT=lh(k), rhs=W0[:, k, 896:1304], start=(k == 0), stop=(k == 7)),
                                   reads=[W0, hTm], writes=[psA])
                            for k in range(8):
                                op("pe", lambda: PE.matmul(psB[:], lhsT=lh(k), rhs=W0[:, k, 1304:1816], start=(k == 0), stop=(k == 7)),
                                   reads=[W0, hTm], writes=[psB])
                            for k in range(8):
                                op("pe", lambda: PE.matmul(psC[:], lhsT=lh(k), rhs=W0[:, k, 1816:2328], start=(k == 0), stop=(k == 7)),
                                   reads=[W0, hTm], writes=[psC])
                            p1_prep_a(mt + 1, sub)
                            op("dve", lambda: V.tensor_copy(out=vsA[:, i, :, 0:64], in_=v3(psA[:, 0:128])), reads=[psA], writes=[vsA])
                            op("dve", lambda: V.tensor_copy(out=vwA[:, i, :, 0:64], in_=v3(psA[:, 256:384])), reads=[psA], writes=[vwA])
                            op("dve", lambda: V.tensor_copy(out=gat[:, i, :], in_=psA[:, 384:408]), reads=[psA], writes=[gat])
                            op("act", lambda: A.activation(out=ug[:], in_=psB[:], func=AF.Gelu_apprx_tanh), reads=[psB], writes=[ug])
                            op("act", lambda: A.activation(out=vg[:], in_=psC[:], func=AF.Gelu_apprx_tanh), reads=[psC], writes=[vg])
                            op("dve", lambda: V.tensor_reduce(out=gst[:, 0, :], in_=v3(vg[:]), axis=AX.X, op=ALU.add), reads=[vg], writes=[gst])
                            tt("dve", sq[:], vg[:], vg[:], ALU.mult, [vg], [sq])
                            op("dve", lambda: V.tensor_reduce(out=gst[:, 1, :], in_=v3(sq[:]), axis=AX.X, op=ALU.add), reads=[sq, gst], writes=[gst])
                            op("dve", lambda: V.tensor_scalar(out=gst[:, 2, :], in0=gst[:, 0, :], scalar1=1.0 / 64, scalar2=None, op0=ALU.mult),
                               reads=[gst], writes=[gst])
                            tt("dve", gst[:, 5, :], gst[:, 2, :], gst[:, 2, :], ALU.mult, [gst], [gst])
                            op("dve", lambda: V.scalar_tensor_tensor(out=gst[:, 3, :], in0=gst[:, 1, :], scalar=1.0 / 64, in1=gst[:, 5, :],
                                                                     op0=ALU.mult, op1=ALU.subtract), reads=[gst], writes=[gst])
                            op("dve", lambda: V.tensor_scalar_add(out=gst[:, 3, :], in0=gst[:, 3, :], scalar1=EPS), reads=[gst], writes=[gst])
                            op("pool", lambda: G.tensor_tensor(out=gst[:, 4, :], in0=gst[:, 3, :], in1=mhalf[:], op=ALU.pow),
                               reads=[gst, mhalf], writes=[gst])
                            tt("dve", v3(sq[:]), v3(vg[:]), b8(gst[:, 2, :]), ALU.subtract, [vg, gst], [sq])
                            tt("dve", v3(sq[:]), v3(sq[:]), b8(gst[:, 4, :]), ALU.mult, [sq, gst], [sq])
                            tt("dve", vn[:], sq[:], ngb[:], ALU.mult, [sq, ngb], [vn])
                            for gm in range(8):
                                op("pe", lambda: PE.matmul(psM[:, gm * 64:(gm + 1) * 64], lhsT=WsT[:, gm, :], rhs=vn[:, gm * 64:(gm + 1) * 64],
                                                           start=True, stop=True), reads=[WsT, vn], writes=[psM])
                            p1_prep_b(mt + 1, sub)
                            tt("dve", v3(sq[:]), v3(psM[:]), b8(bsT[:]), ALU.add, [psM, bsT], [sq])
                            ogt = og[i % 2]
                            tt("dve", ogt[:], sq[:], ug[:], ALU.mult, [sq, ug], [ogt])
                            fw.dma("sp", ogm_d[b, i * 128:(i + 1) * 128, :], ogt[:], ogt, reads=[ogt], writes=[dbuf("ogm", b, i)])
                fw.pop()
                if upto < 2:
                    continue
                fw.push()
                if True:
                    w1 = fw.sb("cw1", [128, 2, 32, 128], BF16, dma="sw")
                    for hh in range(2):
                        for kv in range(2):
                            fw.dma("pool", w1[hh * 64:(hh + 1) * 64, kv, :, :], w1_d[kv].rearrange("(l d) h -> d l h", d=64), w1, writes=[w1])
                    psH = [fw.ps("psH%d" % i, [128, 256], F32) for i in range(2)]
                    psK = fw.ps("psK", [128, 256], F32); psV = fw.ps("psV", [128, 256], F32)
                    for kv in range(2):
                        src = kcT if kv == 0 else vcT
                        hbs = []
                        for g in range(2):
                            ph = psH[g]
                            for l in range(32):
                                op("pe", lambda: PE.matmul(ph[:, 0:255], lhsT=w1[g * 64:(g + 1) * 64, kv, l, :],
                                                           rhs=src[g * 64:(g + 1) * 64, l:l + 16 * 254 + 1:16], start=(l == 0), stop=(l == 31)),
                                   reads=[w1, src], writes=[ph])
                            hb_g = fw.sb("hbg_%d%d" % (kv, g), [128, 256], BF16)
                            op("pool", lambda: G.memset(hb_g[:], 0.0), writes=[hb_g])
                            op("act", lambda: A.activation(out=hb_g[:, 0:255], in_=ph[:, 0:255], func=AF.Gelu_apprx_tanh,
                                                           bias=cbias[:, kv:kv + 1], scale=1.0), reads=[ph, cbias, hb_g], writes=[hb_g])
                            hbs.append(hb_g)
                        if kv == 0:
                            for g in range(2):
                                op("pe", lambda: PE.matmul(psK[:], lhsT=w2pad[:, g, :], rhs=hbs[g][:], start=(g == 0), stop=(g == 1)),
                                   reads=[w2pad, hbs[g]], writes=[psK])
                            op("dve", lambda: V.tensor_copy(out=kcmpT[:], in_=psK[:]), reads=[psK], writes=[kcmpT])
                            if debug:
                                kd = fw.sb("kd", [128, 256], F32, dma=True)
                                op("dve", lambda: V.tensor_copy(out=kd[:], in_=psK[:]), reads=[psK], writes=[kd])
                                fw.dma("sp", dbg["kcc"][b], kd[:], kd, reads=[kd], writes=[dbuf("kcc", b)])
                        else:
                            for g in range(2):
                                for c in range(2):
                                    op("pe", lambda: PE.matmul(psV[:, (g * 2 + c) * 64:(g * 2 + c + 1) * 64], lhsT=hbs[g][:, c * 128:(c + 1) * 128],
                                                               rhs=w2[:, 1, :], start=True, stop=True), reads=[w2, hbs[g]], writes=[psV])
                            for g in range(2):
                                op("dve", lambda: V.tensor_copy(out=vcx[:, :, g, 0:64],
                                                                in_=psV[:, g * 128:(g + 1) * 128].rearrange("p (c d) -> p c d", d=64)),
                                   reads=[psV], writes=[vcx])
                fw.pop()
                if upto < 3:
                    continue
                fw.push()
                if True:
                    Wo0 = fw.sb("Wo0", [128, 8, D], BF16, dma="sw")
                    fw.dma_group("pool", [(Wo0[:, k, :], wout0_d[k * 128:(k + 1) * 128, :]) for k in range(8)], Wo0, writes=[Wo0])
                    lng = fw.sb("lng", [128, D], F32, dma=True); lnb = fw.sb("lnb", [128, D], F32, dma=True)
                    load_bc(lng, lng_d[0]); load_bc(lnb, lnb_d[0])
                    g1 = fw.sb("g1", [128, D], F32, dma=True)
                    load_mod(0, b, [(g1, 2)])
                    ep = Epilogue(g1, lng, lnb)
                    op("act", lambda: A.activation(out=gat[:].rearrange("p a b -> p (a b)"), in_=gat[:].rearrange("p a b -> p (a b)"), func=AF.Sigmoid),
                       reads=[gat], writes=[gat])
                    psS = [fw.ps("psS%d" % i, [128, 512], F32) for i in range(3)]
                    psTt = fw.ps("psTt", [128, 8, 128], BF16)
                    psOc = fw.ps("psOc", [128, 512], F32); psImp = fw.ps("psImp", [128, 512], F32)
                    psOs = fw.ps("psOs", [128, 512], F32); psOw = fw.ps("psOw", [128, 512], F32)
                    o4 = lambda ps_: ps_[:, 0:260].rearrange("p (r e) -> p r e", e=65)
                    ebuf = [fw.sb("ebuf%d" % i, [128, 4, 128], BF16) for i in range(4)]
                    ecount = [0]
                    cats = [fw.sb("cat%d" % i, [128, D], BF16, dma=True) for i in range(2)]
                    ocS = [fw.sb("ocS%d" % i, [128, 4, 65], F32) for i in range(2)]
                    nsel2s = [fw.sb("nsel2_%d" % i, [128, 2, 64], BF16) for i in range(2)]
                    catT = fw.sb("catT", [128, 8, 128], BF16)
                    sm = fw.sb("sm", [128, 3, 4], F32)
                    ff = fw.sb("ff", [128, 3, 4], F32)
                    rs4 = fw.sb("rs4", [128, 4], F32)
                    impn = fw.sb("impn", [128, 4, 64], F32)
                    imp = fw.sb("imp", [128, 64], F32)
                    impa = fw.sb("impa", [128, 64], F32)
                    wk = fw.sb("wk", [128, 64], F32)
                    m8 = fw.sb("m8", [128, 8], F32)
                    nsel = fw.sb("nsel", [128, 64], BF16)
                    nsel2 = fw.sb("nsel2", [128, 2, 64], BF16)
                    qzs = [[fw.sb("qzs%d_%d" % (g_, k_), [128, 4, 128], BF16) for k_ in range(2)] for g_ in range(2)]
                    qz = [[fw.sb("qz%d_%d" % (g_, k_), [128, 4, 128], BF16) for k_ in range(2)] for g_ in range(2)]
                    for g_ in range(2):
                        for k_ in range(2):
                            op("pool", lambda: G.memset(qz[g_][k_][:], 0.0), writes=[qz[g_][k_]])
                    o1 = fw.sb("o1", [128, 4, 64], F32); o2 = fw.sb("o2", [128, 4, 64], F32)
                    xres = [fw.sb("xres%d" % i, [128, D], F32, dma=True) for i in range(2)]
                    if debug:
                        ond = fw.sb("ond", [128, 512], F32, dma=True)
                    bc4 = lambda ap: ap.unsqueeze(1).to_broadcast([ap.shape[0], 4, 128])

                    qzt_cur = [None]

                    def emit_S(job):
                        if job.get("pre"):
                            job["pre"]()
                        pS = psS[job["k"] % 3]
                        pS4 = pS[:].rearrange("p (r t) -> p r t", r=4)
                        extra = job["extra"]
                        op("pe", lambda: PE.matmul(pS4, lhsT=job["kT_ap"], rhs=job["qs"], start=True, stop=(len(extra) == 0)),
                           reads=[job["kT_t"], job["qzt"]], writes=[pS])
                        for xi, (la, ra, rd) in enumerate(extra):
                            op("pe", lambda: PE.matmul(pS4, lhsT=la, rhs=ra, start=False, stop=(xi == len(extra) - 1)),
                               reads=rd, writes=[pS])

                    def emit_EXP_PV(job):
                        if job.get("pre_pv"):
                            job["pre_pv"]()
                        pS = psS[job["k"] % 3]; e = ebuf[job["k"] % 4]
                        op("act", lambda: A.activation(out=e[:].rearrange("p r t -> p (r t)"), in_=pS[:], func=AF.Exp, scale=0.125),
                           reads=[pS], writes=[e])
                        for r in range(4):
                            for (va, po_fn, pot) in job["pvs"]:
                                op("pe", lambda: PE.matmul(po_fn(r), lhsT=e[:, r, :], rhs=va, start=(job["first"] and r == 0), stop=job["last"],
                                                           skip_group_check=True), reads=[e, job["vaug_t"]], writes=[pot])
                        if job.get("post"):
                            job["post"]()

                    def run_jobs(jobs):
                        n = len(jobs)
                        for idx, job in enumerate(jobs):
                            job["k"] = ecount[0] + idx
                        ecount[0] += n
                        nextS = 0
                        for k in range(n):
                            while nextS < n and nextS <= k + 2:
                                jb = jobs[nextS]
                                if jb.get("pre") and jb.get("needs", -1) > k - 1:
                                    break
                                emit_S(jb); nextS += 1
                            assert nextS > k
                            emit_EXP_PV(jobs[k])
                            for _ in range(3):
                                if deferred:
                                    deferred.pop(0)()

                    deferred = []

                    def topk_dve(i, g):
                        oc = ocS[g]; nsel2 = nsel2s[g]
                        op("dve", lambda: V.tensor_copy(out=oc[:], in_=o4(psOc)), reads=[psOc], writes=[oc])
                        op("dve", lambda: V.tensor_scalar_max(out=rs4[:], in0=oc[:, :, 64], scalar1=1e-30), reads=[oc], writes=[rs4])
                        op("dve", lambda: V.reciprocal(out=rs4[:], in_=rs4[:]), reads=[rs4], writes=[rs4])
                        tt("dve", impn[:], psImp[:, 0:256].rearrange("p (r j) -> p r j", j=64), rs4[:].unsqueeze(2).to_broadcast([128, 4, 64]),
                           ALU.mult, [psImp, rs4], [impn])
                        op("dve", lambda: V.tensor_reduce(out=imp[:], in_=impn[:].rearrange("p r j -> p j r"), axis=AX.X, op=ALU.add),
                           reads=[impn], writes=[imp])
                        so = 62 - 2 * i
                        tt("dve", impa[:], imp[:], TMm[:, so:so + 64], ALU.mult, [imp, TMm], [impa])
                        tt("dve", impa[:], impa[:], TB[:, so:so + 64], ALU.add, [impa, TB], [impa])
                        op("dve", lambda: V.memset(impa[:, 0:1], 1e4), reads=[impa], writes=[impa])
                        op("dve", lambda: V.max(out=m8[:], in_=impa[:]), reads=[impa], writes=[m8])
                        op("dve", lambda: V.match_replace(out=wk[:], in_to_replace=m8[:], in_values=impa[:], imm_value=-1e9),
                           reads=[m8, impa], writes=[wk])
                        op("dve", lambda: V.max(out=m8[:], in_=wk[:]), reads=[wk], writes=[m8])
                        op("dve", lambda: V.tensor_scalar(out=wk[:], in0=impa[:], scalar1=m8[:, 7:8], scalar2=-NEG, op0=ALU.is_ge, op1=ALU.mult),
                           reads=[impa, m8], writes=[wk])
                        op("dve", lambda: V.tensor_scalar_add(out=nsel2[:], in0=wk[:].unsqueeze(1).to_broadcast([128, 2, 64]), scalar1=NEG),
                           reads=[wk], writes=[nsel2])

                    def negT_pe(g, qzst):
                        og_ = slice((1 - g) * 64, (2 - g) * 64)
                        nsel2 = nsel2s[g]
                        op("pe", lambda: PE.transpose(psTt[:, 0, :], nsel2[:].rearrange("p a b -> p (a b)"), ident[:]), reads=[nsel2, ident], writes=[psTt])
                        op("dve", lambda: V.tensor_copy(out=qzst[og_, :, :], in_=psTt[og_, 0, :].unsqueeze(1).to_broadcast([64, 4, 128])),
                           reads=[psTt, qzst], writes=[qzst])

                    def tile_start(i):
                        xr = xres[i % 2]; cat = cats[i % 2]
                        fw.dma("sp", xr[:], x_d[b, i * 128:(i + 1) * 128, :], xr, writes=[xr])
                        fw.dma("sp", cat[:, 512:1024], ogm_d[b, i * 128:(i + 1) * 128, :], cat, reads=[dbuf("ogm", b, i)], writes=[cat])
                        for ii in ([0, 1] if i == 0 else [i + 1]):
                            if ii >= P3_TILES:
                                continue
                            for g_ in range(2):
                                gs_ = slice(g_ * 64, (g_ + 1) * 64)
                                a_ = qz[g_][ii % 2]; b_ = qzs[g_][ii % 2]
                                op("dve", lambda: V.tensor_copy(out=a_[gs_, :, :], in_=qT[gs_, :, ii * 128:(ii + 1) * 128]), reads=[qT, a_], writes=[a_])
                                op("pool", lambda: G.tensor_copy(out=b_[gs_, :, :], in_=qT[gs_, :, ii * 128:(ii + 1) * 128]), reads=[qT, b_], writes=[b_])

                    def combine(i, g):
                        oc = ocS[g]; cat = cats[i % 2]
                        op("dve", lambda: V.tensor_scalar_max(out=sm[:, 0, :], in0=oc[:, :, 64], scalar1=1e-30), reads=[oc, sm], writes=[sm])
                        for br, pso in ((1, psOs), (2, psOw)):
                            op("dve", lambda: V.tensor_scalar_max(out=sm[:, br, :], in0=o4(pso)[:, :, 64], scalar1=1e-30), reads=[pso, sm], writes=[sm])
                        op("dve", lambda: V.reciprocal(out=sm[:], in_=sm[:]), reads=[sm], writes=[sm])
                        tt("dve", ff[:], sm[:], gat[:, i, g * 12:(g + 1) * 12].rearrange("p (r b) -> p b r", b=3), ALU.mult, [sm, gat], [ff])
                        fb = lambda br: ff[:, br, :].unsqueeze(2).to_broadcast([128, 4, 64])
                        tt("dve", o2[:], o4(psOw)[:, :, 0:64], fb(2), ALU.mult, [psOw, ff], [o2])
                        tt("dve", o1[:], o4(psOs)[:, :, 0:64], fb(1), ALU.mult, [psOs, ff], [o1])
                        tt("dve", o1[:], o1[:], o2[:], ALU.add, [o1, o2], [o1])
                        tt("dve", o2[:], oc[:, :, 0:64], fb(0), ALU.mult, [oc, ff], [o2])
                        if debug:
                            tt("pool", ond[:, g * 256:(g + 1) * 256].rearrange("p (r d) -> p r d", d=64), o1[:], o2[:], ALU.add, [o1, o2], [ond])
                            if g == 1:
                                fw.dma("sp", dbg["onsa"][b, i * 128:(i + 1) * 128, :], ond[:], ond, reads=[ond], writes=[dbuf("onsa", b, i)])
                        tt("dve", cat[:, g * 256:(g + 1) * 256].rearrange("p (r d) -> p r d", d=64), o1[:], o2[:], ALU.add, [o1, o2], [cat])

                    def outproj(i):
                        while deferred:
                            deferred.pop(0)()
                        cat = cats[i % 2]; xr = xres[i % 2]
                        for c in range(8):
                            op("pe", lambda: PE.transpose(psTt[:, c, :], cat[:, c * 128:(c + 1) * 128], ident[:]), reads=[cat, ident], writes=[psTt])
                        op("act", lambda: A.copy(out=catT[:], in_=psTt[:]), reads=[psTt], writes=[catT])
                        halves = [(psOs, psOs[:]), (psOw, psOw[:])]
                        for half in range(2):
                            pyt, pya = halves[half]
                            for c in range(8):
                                op("pe", lambda: PE.matmul(pya, lhsT=catT[:, c, :], rhs=Wo0[:, c, half * 512:(half + 1) * 512],
                                                           start=(c == 0), stop=(c == 7)), reads=[catT, Wo0], writes=[pyt])
                        st_ = ep.steps(halves, xr, x1_d[b, i * 128:(i + 1) * 128, :], dbuf("x1", b, i))
                        for f_ in st_[:3]:
                            f_()
                        deferred.extend(st_[3:])

                    jobs = []
                    last_sel_g1 = -1
                    for i in range(P3_TILES):
                        def mk(kT_ap, kT_t, extra, pvs, vaug_t, first, last, qt):
                            return dict(kT_ap=kT_ap, kT_t=kT_t, qs=qt[:], qzt=qt, extra=extra, pvs=pvs, vaug_t=vaug_t, first=first, last=last)
                        chunks = [0] if i < 16 else [0, 1]
                        last_cmp = {}
                        for g in range(2):
                            qzt = qz[g][i % 2]
                            for ci, c in enumerate(chunks):
                                extra = []
                                if (c, i) in cmpmask:
                                    extra.append((ident[:], bc4(cm_all[:, cmpmask[(c, i)], :]), [ident, cm_all]))
                                jb = mk(kcmpT[:, c * 128:(c + 1) * 128], kcmpT, extra,
                                        [(vcx[:, c, g, 0:65], lambda r: o4(psOc)[:, r, :], psOc),
                                         (vcx[:, c, g, 65:129], lambda r: psImp[:, r * 64:(r + 1) * 64], psImp)],
                                        vcx, ci == 0, ci == len(chunks) - 1, qzt)
                                if g == 0 and ci == 0:
                                    jb["pre"] = (lambda i_=i: tile_start(i_)); jb["needs"] = -1
                                jobs.append(jb)
                            jobs[-1]["post"] = (lambda i_=i, g_=g: topk_dve(i_, g_))
                            last_cmp[g] = len(jobs) - 1
                        for g in range(2):
                            qzt = qz[g][i % 2]; qzst = qzs[g][i % 2]
                            j0 = max(0, i - 4)
                            for j in range(j0, i + 1):
                                extra = []
                                if j == i:
                                    extra.append((ident[:], bc4(McNeg[:]), [ident, McNeg]))
                                elif j == i - 4:
                                    extra.append((ident[:], bc4(MwNeg[:]), [ident, MwNeg]))
                                jb = mk(kwT[:, j * 128:(j + 1) * 128], kwT, extra,
                                        [(vwA[:, j, g, :], lambda r: o4(psOw)[:, r, :], psOw)], vwA, j == j0, j == i, qzt)
                                if g == 0 and j == j0 and i > 0:
                                    jb["pre_pv"] = (lambda i_=i: outproj(i_ - 1))
                                jobs.append(jb)
                            for j in range(i + 1):
                                extra = []
                                if j == i:
                                    extra.append((ident[:], bc4(McNeg[:]), [ident, McNeg]))
                                jb = mk(ksE[g][:, j * 128:(j + 1) * 128], ksE[g], extra,
                                        [(vsA[:, j, g, :], lambda r: o4(psOs)[:, r, :], psOs)], vsA, j == 0, j == i, qzst)
                                if j == 0:
                                    jb["pre"] = (lambda g_=g, q_=qzst: negT_pe(g_, q_)); jb["needs"] = last_cmp[g]
                                jobs.append(jb)
                            jobs[-1]["post"] = (lambda i_=i, g_=g: combine(i_, g_))
                            if g == 1:
                                last_sel_g1 = len(jobs) - 1
                    run_jobs(jobs)
                    outproj(P3_TILES - 1)
                    while deferred:
                        deferred.pop(0)()
                fw.pop()
        fw.pop()
        if upto < 4:
            return nc

        def ffn_phase(ls, layer, src_d, src_name, dst_d, dst_name):
            fw.push()
            if True:
                Wi = fw.sb("Wi", [128, 8, 2 * FH], BF16, dma="sw")
                Wo = fw.sb("Wo", [128, 22, D], BF16, dma="sw")
                fw.dma_group("pool", [(Wi[:, k, :], fwi_d[layer, k * 128:(k + 1) * 128, :]) for k in range(8)], Wi, writes=[Wi])
                fw.dma_group("pool", [(Wo[:, k, :], fwo_d[layer, k * 128:(k + 1) * 128, :]) for k in range(22)], Wo, writes=[Wo])
                lng = fw.sb("lng", [128, D], F32, dma=True); lnb = fw.sb("lnb", [128, D], F32, dma=True)
                load_bc(lng, lng_d[ls]); load_bc(lnb, lnb_d[ls])
                sc1 = fw.sb("sc1", [128, D], F32, dma=True); sh = fw.sb("sh", [128, D], F32, dma=True); g1 = fw.sb("g1", [128, D], F32, dma=True)
                NX = 4
                xin = [fw.sb("xin%d" % i, [128, D], F32, dma=True) for i in range(NX)]
                hT = [fw.sb("hT%d" % i, [128, 8, 256], BF16) for i in range(2)]
                actT = fw.sb("actT", [128, 22, 256], BF16)
                sg = [fw.sb("sg%d" % i, [128, 256], BF16) for i in range(2)]
                hp = HPrep(sc1, sh, nhb=2, tmp_dt=BF16)
                ep = Epilogue(g1, lng, lnb)
                psGU = [fw.ps("psGU%d" % i, [128, 2, 256], F32) for i in range(2)]
                psYs = [fw.ps("psY%d" % i, [128, 2, 512], F32) for i in range(2)]
                tiles = [(b, mt) for b in range(NB) for mt in range(S // 256)]
                state = {}
                xc = [0]

                def prep_a(idx):
                    b, mt = tiles[idx]
                    if mt == 0:
                        load_mod(ls, b, [(sh, 0), (sc1, 1)])
                    hTm = hT[idx % 2]
                    xts = []; hbs = []
                    for sub in range(2):
                        i = mt * 2 + sub
                        xt = xin[xc[0] % NX]; xc[0] += 1
                        fw.dma("sp", xt[:], src_d[b, i * 128:(i + 1) * 128, :], xt, reads=[dbuf(src_name, b, i)], writes=[xt])
                        hbs.append(hp.run_a(xt))
                        xts.append(xt)
                    state[idx] = (hTm, xts, hbs)

                def prep_b(idx):
                    hTm, xts, hbs = state[idx]
                    for sub in range(2):
                        hp.run_b(hbs[sub], hTm, sub * 128)

                def up(idx, hcs):
                    hTm = state[idx][0]
                    for hc in hcs:
                        pg = psGU[hc % 2]; sgt = sg[hc % 2]
                        for k in range(8):
                            op("pe", lambda: PE.matmul(pg[:, 0, :], lhsT=Wi[:, k, hc * 128:(hc + 1) * 128], rhs=hTm[:, k, :],
                                                       start=(k == 0), stop=(k == 7)), reads=[Wi, hTm], writes=[pg])
                        for k in range(8):
                            op("pe", lambda: PE.matmul(pg[:, 1, :], lhsT=Wi[:, k, FH + hc * 128:FH + (hc + 1) * 128], rhs=hTm[:, k, :],
                                                       start=(k == 0), stop=(k == 7)), reads=[Wi, hTm], writes=[pg])
                        op("act", lambda: A.activation(out=sgt[:], in_=pg[:, 0, :], func=AF.Silu), reads=[pg], writes=[sgt])
                        tt("dve", actT[:, hc, :], pg[:, 1, :], sgt[:], ALU.mult, [pg, sgt], [actT])
                        for _ in range(2):
                            if deferred:
                                deferred.pop(0)()

                def down_ep(idx):
                    b, mt = tiles[idx]
                    xts = state.pop(idx)[1]
                    if mt == 0:
                        load_mod(ls, b, [(g1, 2)])
                    for sub in range(2):
                        i = mt * 2 + sub
                        psY = psYs[sub]
                        for half in range(2):
                            for hc in range(22):
                                op("pe", lambda: PE.matmul(psY[:, half, :], lhsT=actT[:, hc, sub * 128:(sub + 1) * 128],
                                                           rhs=Wo[:, hc, half * 512:(half + 1) * 512], start=(hc == 0), stop=(hc == 21)),
                                   reads=[actT, Wo], writes=[psY])
                        st_ = ep.steps([(psY, psY[:, 0, :]), (psY, psY[:, 1, :])], xts[sub], dst_d[b, i * 128:(i + 1) * 128, :], dbuf(dst_name, b, i))
                        nim = 3 if sub == 1 else len(st_)
                        for f_ in st_[:nim]:
                            f_()
                        deferred.extend(st_[nim:])

                deferred = []
                prep_a(0); prep_b(0)
                for idx in range(len(tiles)):
                    up(idx, range(0, 2))
                    if idx + 1 < len(tiles):
                        prep_a(idx + 1)
                    up(idx, range(2, 12))
                    if idx + 1 < len(tiles):
                        prep_b(idx + 1)
                    up(idx, range(12, 22))
                    while deferred:
                        deferred.pop(0)()
                    down_ep(idx)
                while deferred:
                    deferred.pop(0)()
            fw.pop()

        ffn_phase(1, 0, x1_d, "x1", x2_d, "x2")
        if upto < 5:
            return nc

        fw.push()
        if True:
            W1 = fw.sb("W1", [128, 8, 3 * D], BF16, dma="sw")
            Wo1 = fw.sb("Wo1", [128, 8, D], BF16, dma="sw")
            fw.dma_group("pool", [(W1[:, k, :], win1_d[k * 128:(k + 1) * 128, :]) for k in range(8)], W1, writes=[W1])
            fw.dma_group("pool", [(Wo1[:, k, :], wout1_d[k * 128:(k + 1) * 128, :]) for k in range(8)], Wo1, writes=[Wo1])
            cwf = fw.sb("cwf", [3, D], F32, dma=True)
            fw.dma("sp", cwf[:], cw_d, cwf, writes=[cwf])
            cw = fw.sb("cw", [128, 8, 3], F32)
            pscw = fw.ps("pscw", [128, 8, 3], F32)
            for cc in range(8):
                op("pe", lambda: PE.transpose(pscw[:, cc, :], cwf[:, cc * 128:(cc + 1) * 128], identf[0:3, 0:3]), reads=[cwf, identf], writes=[pscw])
            op("dve", lambda: V.tensor_copy(out=cw[:], in_=pscw[:]), reads=[pscw], writes=[cw])
            lng = fw.sb("lng", [128, D], F32, dma=True); lnb = fw.sb("lnb", [128, D], F32, dma=True)
            load_bc(lng, lng_d[2]); load_bc(lnb, lnb_d[2])
            sc1 = fw.sb("sc1", [128, D], F32, dma=True); sh = fw.sb("sh", [128, D], F32, dma=True); g1 = fw.sb("g1", [128, D], F32, dma=True)
            xin = [fw.sb("xin%d" % i, [128, D], F32, dma=True) for i in range(4)]
            hT = [fw.sb("hT%d" % i, [128, 8, 256], BF16) for i in range(2)]
            cz = fw.sb("cz", [128, 8, 258], F32)
            csb = [fw.sb("csb%d" % i, [128, 256], F32) for i in range(2)]
            y1 = [fw.sb("y1_%d" % i, [128, 256], F32) for i in range(2)]
            byT = fw.sb("byT", [128, 8, 256], BF16)
            hp = HPrep(sc1, sh)
            ep = Epilogue(g1, lng, lnb, nt=2)
            psP = [fw.ps("psP%d" % i, [128, 4, 256], F32) for i in range(2)]
            psY_ = fw.ps("psY", [128, 2, 512], F32)
            psYs = [psY_, psY_]
            tiles = [(b, mt) for b in range(NB) for mt in range(S // 256)]
            state = {}
            xc = [0]

            def prep_a(idx):
                b, mt = tiles[idx]
                if mt == 0:
                    load_mod(2, b, [(sh, 0), (sc1, 1)])
                hTm = hT[idx % 2]
                xts = []; hbs = []
                for sub in range(2):
                    i = mt * 2 + sub
                    xt = xin[xc[0] % 4]; xc[0] += 1
                    fw.dma("sp", xt[:], x2_d[b, i * 128:(i + 1) * 128, :], xt, reads=[dbuf("x2", b, i)], writes=[xt])
                    hbs.append(hp.run_a(xt))
                    xts.append(xt)
                state[idx] = (hTm, xts, hbs)

            def prep_b(idx):
                hTm, xts, hbs = state[idx]
                for sub in range(2):
                    hp.run_b(hbs[sub], hTm, sub * 128)

            deferred = []

            def compute(idx, ccs):
                b, mt = tiles[idx]
                hTm = state[idx][0]
                for cc in ccs:
                    if cc == 0:
                        if mt == 0:
                            op("pool", lambda: G.memset(cz[:, :, 0:2], 0.0), reads=[cz], writes=[cz])
                        else:
                            op("pool", lambda: G.tensor_copy(out=cz[:, :, 0:2], in_=cz[:, :, 256:258]), reads=[cz], writes=[cz])
                    pp = psP[cc % 2]; cs = csb[cc % 2]; yt = y1[cc % 2]
                    for part in range(3):
                        for k in range(8):
                            op("pe", lambda: PE.matmul(pp[:, part, :], lhsT=W1[:, k, part * D + cc * 128:part * D + (cc + 1) * 128],
                                                       rhs=hTm[:, k, :], start=(k == 0), stop=(k == 7)), reads=[W1, hTm], writes=[pp])
                    op("act", lambda: A.copy(out=cs[:], in_=pp[:, 1, :]), reads=[pp], writes=[cs])
                    tt("dve", cz[:, cc, 2:258], pp[:, 2, :], cs[:], ALU.mult, [pp, cs, cz], [cz])
                    op("act", lambda: A.activation(out=yt[:], in_=cz[:, cc, 0:256], func=AF.Copy, scale=cw[:, cc, 0:1]),
                       reads=[cz, cw], writes=[yt])
                    op("dve", lambda: V.scalar_tensor_tensor(out=yt[:], in0=cz[:, cc, 1:257], scalar=cw[:, cc, 1:2], in1=yt[:],
                                                             op0=ALU.mult, op1=ALU.add), reads=[cz, cw, yt], writes=[yt])
                    op("dve", lambda: V.scalar_tensor_tensor(out=yt[:], in0=cz[:, cc, 2:258], scalar=cw[:, cc, 2:3], in1=yt[:],
                                                             op0=ALU.mult, op1=ALU.add), reads=[cz, cw, yt], writes=[yt])
                    tt("dve", byT[:, cc, :], pp[:, 0, :], yt[:], ALU.mult, [pp, yt], [byT])
                    for _ in range(4):
                        if deferred:
                            deferred.pop(0)()

            def out_ep(idx):
                b, mt = tiles[idx]
                xts = state.pop(idx)[1]
                if mt == 0:
                    load_mod(2, b, [(g1, 2)])
                for sub in range(2):
                    i = mt * 2 + sub
                    psY = psYs[sub]
                    for half in range(2):
                        for cc in range(8):
                            op("pe", lambda: PE.matmul(psY[:, half, :], lhsT=byT[:, cc, sub * 128:(sub + 1) * 128],
                                                       rhs=Wo1[:, cc, half * 512:(half + 1) * 512], start=(cc == 0), stop=(cc == 7)),
                               reads=[byT, Wo1], writes=[psY])
                    st_ = ep.steps([(psY, psY[:, 0, :]), (psY, psY[:, 1, :])], xts[sub], x3_d[b, i * 128:(i + 1) * 128, :], dbuf("x3", b, i))
                    nim = 3
                    for f_ in st_[:nim]:
                        f_()
                    deferred.extend(st_[nim:])

            prep_a(0); prep_b(0)
            for idx in range(len(tiles)):
                compute(idx, range(0, 1))
                if idx + 1 < len(tiles):
                    prep_a(idx + 1)
                compute(idx, range(1, 5))
                if idx + 1 < len(tiles):
                    prep_b(idx + 1)
                compute(idx, range(5, 8))
                while deferred:
                    deferred.pop(0)()
                out_ep(idx)
            while deferred:
                deferred.pop(0)()
        fw.pop()
        if upto < 6:
            return nc

        ffn_phase(3, 1, x3_d, "x3", out_d, "out")
        fw.barrier()
    return nc


_CACHE = {}


def _prep_inputs(inputs):
    f = lambda a: np.ascontiguousarray(np.asarray(a, dtype=np.float32))
    common = {
        "ada_w": f(inputs["ada_w"]).reshape(4, D, 3 * D), "ada_b": f(inputs["ada_b"]).reshape(4, 3 * D),
        "ln_g": f(inputs["ln_g"]).reshape(4, D), "ln_b": f(inputs["ln_b"]).reshape(4, D),
        "even_w_in": f(inputs["even_w_in"])[0], "even_cmp_pos": f(inputs["even_cmp_pos"])[0],
        "even_cmp_w1": f(inputs["even_cmp_w1"])[0], "even_cmp_w2": f(inputs["even_cmp_w2"])[0],
        "even_gmlp_norm_g": f(inputs["even_gmlp_norm_g"])[0].reshape(512), "even_gmlp_ws": f(inputs["even_gmlp_ws"])[0],
        "even_gmlp_bs": f(inputs["even_gmlp_bs"])[0], "even_w_out": f(inputs["even_w_out"])[0],
        "odd_w_in": f(inputs["odd_w_in"])[0], "odd_conv_w": f(inputs["odd_conv_w"])[0], "odd_w_out": f(inputs["odd_w_out"])[0],
        "ffn_w_in": f(inputs["ffn_w_in"]), "ffn_w_out": f(inputs["ffn_w_out"]),
    }
    x = f(inputs["x"]); c = f(inputs["c"])
    maps = []
    for i in range(N_CORES):
        m = dict(common)
        m["x"] = x[NB * i:NB * (i + 1)]; m["c"] = c[NB * i:NB * (i + 1)]
        maps.append(m)
    return maps


def kernel(**inputs):
    if "nc" not in _CACHE:
        _CACHE["nc"] = build_program()
    nc = _CACHE["nc"]
    maps = _prep_inputs(inputs)
    res = run_bass_kernel_spmd(nc, maps, core_ids=list(range(N_CORES)))
    return np.concatenate([r["out"] for r in res.results], axis=0).astype(np.float32)
```
